# Optimizing a Trainium2 kernel written in Bass

```python
import math
import jax, jax.numpy as jnp
from jax import lax
import numpy as np


D_MODEL = 1024
BATCH = 32
SEQ = 2048
DEPTH = 4

D_MIX = D_MODEL
DN_HEADS = 4
DN_HEAD_DIM = 128
DN_WIDTH = DN_HEADS * DN_HEAD_DIM
LRU_WIDTH = D_MIX - DN_WIDTH
LRU_BLOCKS = 8
LRU_BLOCK = LRU_WIDTH // LRU_BLOCKS
LRU_C = 8.0
SHORT_CONV = 4
SHORT_CONV_LEFT = 2
FFN_CONV = 3
FFN_CONV_LEFT = 1
D_FF = 2816
PLE_DIM = 256
CHUNK = 64
EPS = 1e-6

Q_OFF = 0
K_OFF = DN_WIDTH
V_OFF = 2 * DN_WIDTH
Z_OFF = 3 * DN_WIDTH
BETA_OFF = 4 * DN_WIDTH
ALPHA_OFF = BETA_OFF + 2 * DN_HEADS
LX_OFF = ALPHA_OFF + 2 * DN_HEADS
LG_OFF = LX_OFF + LRU_WIDTH
IN_COLS = LG_OFF + LRU_WIDTH

kernel_name = 'hymba_gdn_rglru_convglu_ple_encoder'


def rmsnorm(x, g):
    x32 = x.astype(jnp.float32)
    y = x32 * lax.rsqrt(jnp.mean(x32 * x32, axis=-1, keepdims=True) + EPS)
    return (y * g.astype(jnp.float32)).astype(x.dtype)


def l2norm(x):
    x32 = x.astype(jnp.float32)
    return (x32 * lax.rsqrt(jnp.sum(x32 * x32, axis=-1, keepdims=True) + EPS)).astype(x.dtype)


def dwconv(x, w, left):
    k = w.shape[0]
    s = x.shape[1]
    xp = jnp.pad(x, ((0, 0), (left, k - 1 - left), (0, 0)))
    out = xp[:, 0:s] * w[0]
    for j in range(1, k):
        out = out + xp[:, j:j + s] * w[j]
    return out


def flip(t):
    return jnp.flip(t, axis=1)


def gated_delta_chunked(q, k, v, g, beta):
    B, S, H, Dk = q.shape
    Dv = v.shape[-1]
    N = S // CHUNK
    f32 = jnp.float32

    def chunks(t):
        t = t.astype(f32).reshape((B, N, CHUNK) + t.shape[2:])
        return jnp.moveaxis(t, 3, 1)

    qc = chunks(q) * (Dk ** -0.5)
    kc, vc, gc, bc = chunks(k), chunks(v), chunks(g), chunks(beta)
    gcum = jnp.cumsum(gc, axis=-1)
    idx = jnp.arange(CHUNK)
    incl = idx[:, None] >= idx[None, :]
    strict = idx[:, None] > idx[None, :]
    decay = jnp.exp(jnp.where(incl, gcum[..., :, None] - gcum[..., None, :], -jnp.inf))
    kb = kc * bc[..., None]
    a_mat = jnp.where(strict, jnp.einsum('bhnid,bhnjd->bhnij', kb, kc) * decay, 0.0)
    rhs = jnp.concatenate([vc * bc[..., None], kb * jnp.exp(gcum)[..., None]], axis=-1)
    sol = lax.linalg.triangular_solve(a_mat, rhs, left_side=True, lower=True, unit_diagonal=True)
    u, w = sol[..., :Dv], sol[..., Dv:]
    attn = jnp.einsum('bhnid,bhnjd->bhnij', qc, kc) * decay
    glast = gcum[..., -1:]
    q_dec = qc * jnp.exp(gcum)[..., None]
    k_dec = kc * jnp.exp(glast - gcum)[..., None]
    cdec = jnp.exp(glast[..., 0])
    xs = tuple(jnp.moveaxis(t, 2, 0) for t in (q_dec, k_dec, w, u, attn, cdec))

    def step(state, inp):
        qd, kd, wi, ui, ai, cd = inp
        v_new = ui - jnp.einsum('bhcd,bhde->bhce', wi, state)
        o = jnp.einsum('bhcd,bhde->bhce', qd, state) + jnp.einsum('bhij,bhje->bhie', ai, v_new)
        state = state * cd[..., None, None] + jnp.einsum('bhcd,bhce->bhde', kd, v_new)
        return state, o

    state0 = jnp.zeros((B, H, Dk, Dv), f32)
    _, o = lax.scan(step, state0, xs)
    o = jnp.transpose(o, (1, 0, 3, 2, 4)).reshape(B, S, H, Dv)
    return o.astype(v.dtype)


def rglru(x, wa, ba, wx, bx, lam):
    B, S, W = x.shape
    xr = x.reshape(B, S, LRU_BLOCKS, LRU_BLOCK)
    r = jax.nn.sigmoid(jnp.einsum('bsnc,ncd->bsnd', xr, wa).reshape(B, S, W) + ba)
    ig = jax.nn.sigmoid(jnp.einsum('bsnc,ncd->bsnd', xr, wx).reshape(B, S, W) + bx)
    log_a = -LRU_C * r.astype(jnp.float32) * jax.nn.softplus(-lam.astype(jnp.float32))
    a = jnp.exp(log_a)
    b = jnp.sqrt(-jnp.expm1(2.0 * log_a)) * (ig * x).astype(jnp.float32)

    def combine(e1, e2):
        a1, b1 = e1
        a2, b2 = e2
        return a1 * a2, a2 * b1 + b2

    _, h = lax.associative_scan(combine, (a, b), axis=1)
    return h.astype(x.dtype)


def setup_inputs(seed: int = 0) -> dict:
    key = jax.random.key(seed)
    ks = iter(jax.random.split(key, 32))
    f32 = jnp.float32

    def nrm(shape, scale):
        return scale * jax.random.normal(next(ks), shape, f32)

    def gain(shape):
        return 1.0 + nrm(shape, 0.02)

    L, H = DEPTH, DN_HEADS
    x = nrm((BATCH, SEQ, D_MODEL), 1.0)
    p = nrm((DEPTH, BATCH, SEQ, PLE_DIM), 1.0)
    norm1_g = gain((L, D_MODEL))
    w_in = nrm((L, D_MODEL, IN_COLS), D_MODEL ** -0.5)
    dn_conv_w = nrm((L, SHORT_CONV, 3 * DN_WIDTH), SHORT_CONV ** -0.5)
    dn_a_log = jnp.log(jax.random.uniform(next(ks), (L, 2, H), f32, 1.0, 16.0))
    dt = jnp.exp(jax.random.uniform(next(ks), (L, 2, H), f32, math.log(1e-3), math.log(1e-1)))
    dn_dt_bias = dt + jnp.log(-jnp.expm1(-dt))
    dn_norm_g = gain((L, DN_HEAD_DIM))
    lru_conv_w = nrm((L, SHORT_CONV, LRU_WIDTH), SHORT_CONV ** -0.5)
    lru_conv_b = nrm((L, LRU_WIDTH), 0.02)
    lru_wa = nrm((L, 2, LRU_BLOCKS, LRU_BLOCK, LRU_BLOCK), LRU_BLOCK ** -0.5)
    lru_ba = nrm((L, 2, LRU_WIDTH), 0.02)
    lru_wx = nrm((L, 2, LRU_BLOCKS, LRU_BLOCK, LRU_BLOCK), LRU_BLOCK ** -0.5)
    lru_bx = nrm((L, 2, LRU_WIDTH), 0.02)
    a0 = jax.random.uniform(next(ks), (L, 2, LRU_WIDTH), f32, 0.9, 0.999) ** (1.0 / LRU_C)
    lru_lambda = jnp.log(a0) - jnp.log1p(-a0)
    lru_norm_g = gain((L, LRU_WIDTH))
    w_out = nrm((L, D_MIX, D_MODEL), D_MIX ** -0.5)
    norm2_g = gain((L, D_MODEL))
    ffn_wg = nrm((L, D_MODEL, D_FF), D_MODEL ** -0.5)
    ffn_wu = nrm((L, D_MODEL, D_FF), D_MODEL ** -0.5)
    ffn_conv_w = nrm((L, FFN_CONV, D_FF), FFN_CONV ** -0.5)
    ffn_conv_b = nrm((L, D_FF), 0.02)
    ffn_wd = nrm((L, D_FF, D_MODEL), D_FF ** -0.5)
    ple_norm_g = gain((L, D_MODEL))
    ple_wg = nrm((L, D_MODEL, D_MODEL), D_MODEL ** -0.5)
    ple_bg = nrm((L, D_MODEL), 0.02)
    ple_wp = nrm((L, PLE_DIM, D_MODEL), PLE_DIM ** -0.5)
    final_g = gain((D_MODEL,))
    return {'x': x, 'p': p, 'norm1_g': norm1_g, 'w_in': w_in, 'dn_conv_w': dn_conv_w,
            'dn_a_log': dn_a_log, 'dn_dt_bias': dn_dt_bias, 'dn_norm_g': dn_norm_g,
            'lru_conv_w': lru_conv_w, 'lru_conv_b': lru_conv_b, 'lru_wa': lru_wa, 'lru_ba': lru_ba,
            'lru_wx': lru_wx, 'lru_bx': lru_bx, 'lru_lambda': lru_lambda, 'lru_norm_g': lru_norm_g,
            'w_out': w_out, 'norm2_g': norm2_g, 'ffn_wg': ffn_wg, 'ffn_wu': ffn_wu,
            'ffn_conv_w': ffn_conv_w, 'ffn_conv_b': ffn_conv_b, 'ffn_wd': ffn_wd,
            'ple_norm_g': ple_norm_g, 'ple_wg': ple_wg, 'ple_bg': ple_bg, 'ple_wp': ple_wp,
            'final_g': final_g}


def reference(x, p, norm1_g, w_in, dn_conv_w, dn_a_log, dn_dt_bias, dn_norm_g,
              lru_conv_w, lru_conv_b, lru_wa, lru_ba, lru_wx, lru_bx, lru_lambda, lru_norm_g,
              w_out, norm2_g, ffn_wg, ffn_wu, ffn_conv_w, ffn_conv_b, ffn_wd,
              ple_norm_g, ple_wg, ple_bg, ple_wp, final_g):
    B, S, _ = x.shape
    H, Dh = DN_HEADS, DN_HEAD_DIM
    r = x
    for i in range(DEPTH):
        h = rmsnorm(r, norm1_g[i])
        proj = h @ w_in[i]
        qkv = jax.nn.silu(dwconv(proj[..., Q_OFF:Z_OFF], dn_conv_w[i], SHORT_CONV_LEFT))
        q = l2norm(qkv[..., Q_OFF:K_OFF].reshape(B, S, H, Dh))
        k = l2norm(qkv[..., K_OFF:V_OFF].reshape(B, S, H, Dh))
        v = qkv[..., V_OFF:Z_OFF].reshape(B, S, H, Dh)
        z = proj[..., Z_OFF:BETA_OFF].reshape(B, S, H, Dh)
        beta = jax.nn.sigmoid(proj[..., BETA_OFF:ALPHA_OFF].reshape(B, S, 2, H))
        alpha = proj[..., ALPHA_OFF:LX_OFF].reshape(B, S, 2, H).astype(jnp.float32)
        g = -jnp.exp(dn_a_log[i].astype(jnp.float32)) * jax.nn.softplus(alpha + dn_dt_bias[i].astype(jnp.float32))
        o_f = gated_delta_chunked(q, k, v, g[:, :, 0], beta[:, :, 0])
        o_b = flip(gated_delta_chunked(flip(q), flip(k), flip(v), flip(g[:, :, 1]), flip(beta[:, :, 1])))
        dn_out = (rmsnorm(o_f + o_b, dn_norm_g[i]) * jax.nn.silu(z)).reshape(B, S, DN_WIDTH)
        xc = dwconv(proj[..., LX_OFF:LG_OFF], lru_conv_w[i], SHORT_CONV_LEFT) + lru_conv_b[i]
        h_f = rglru(xc, lru_wa[i, 0], lru_ba[i, 0], lru_wx[i, 0], lru_bx[i, 0], lru_lambda[i, 0])
        h_b = flip(rglru(flip(xc), lru_wa[i, 1], lru_ba[i, 1], lru_wx[i, 1], lru_bx[i, 1], lru_lambda[i, 1]))
        lru_out = rmsnorm(jax.nn.gelu(proj[..., LG_OFF:IN_COLS]) * (h_f + h_b), lru_norm_g[i])
        r = r + jnp.concatenate([dn_out, lru_out], axis=-1) @ w_out[i]
        h2 = rmsnorm(r, norm2_g[i])
        gate = dwconv(h2 @ ffn_wg[i], ffn_conv_w[i], FFN_CONV_LEFT) + ffn_conv_b[i]
        r = r + (jax.nn.gelu(gate) * (h2 @ ffn_wu[i])) @ ffn_wd[i]
        pg = jax.nn.sigmoid(rmsnorm(r, ple_norm_g[i]) @ ple_wg[i] + ple_bg[i])
        r = r + pg * (p[i] @ ple_wp[i])
    return rmsnorm(r, final_g)
```

```python
import math
import numpy as np
from contextlib import ExitStack
import concourse.bass as bass
import concourse.mybir as mybir
from concourse.bass_utils import run_bass_kernel_spmd

F32 = mybir.dt.float32
BF16 = mybir.dt.bfloat16
AF = mybir.ActivationFunctionType
ALU = mybir.AluOpType

D = 1024
S_LEN = 2048
L = 4
NCORE = 8
SEQ_PER_CORE = 4
DFF = 2816
NFC = 22
PLE = 256
EPS = 1e-6
NT = 4
TK = 16
GC1 = math.sqrt(2.0 / math.pi)
GC2 = GC1 * 0.044715

O_WIN = 0
O_WAB = O_WIN + 24 * 1024
O_LRUW = O_WAB + 128
O_WOUT = O_LRUW + 2048
O_WG = O_WOUT + 8192
O_WU = O_WG + 22528
O_WD = O_WU + 22528
O_PWG = O_WD + 22528
O_PWP = O_PWG + 8192
WTOT = O_PWP + 2048

V_G1, V_G2, V_GP, V_BGP = 0, 8, 16, 24
V_DNCW = 32
V_LCW = 80
V_LCB = 96
V_LBA = 100
V_LBX = 108
V_LLAM = 116
V_LNG = 124
V_DNG = 128
V_FCW = 129
V_FCB = 195
V_ALOG = 217
V_DTB = 225
NV = 240


class Sched:
    NDS = 24

    def __init__(self, nc, es):
        self.nc = nc
        self.E = {}
        for name, h in (("pe", nc.tensor), ("act", nc.scalar), ("dve", nc.vector), ("pool", nc.gpsimd), ("sp", nc.sync)):
            sem = es.enter_context(nc.semaphore("s_" + name))
            self.E[name] = dict(h=h, sem=sem, n=0, seen={}, seend={})
        self.dsem = [es.enter_context(nc.semaphore(f"sd{i}")) for i in range(self.NDS)]
        self.dcnt = [0] * self.NDS
        self.dnext = 0
        self.lastw = {}
        self.readers = {}
        self.ninst = 0

    def _wait(self, eng, tok, same_ok):
        X = self.E[eng]
        if tok[0] == "e":
            _, p, c = tok
            if p == eng and (eng == "pe" or not same_ok):
                return
            if X["seen"].get(p, 0) >= c:
                return
            X["h"].wait_ge(self.E[p]["sem"], c)
            X["seen"][p] = c
        else:
            _, i, v = tok
            if X["seend"].get(i, 0) >= v:
                return
            X["h"].wait_ge(self.dsem[i], v)
            X["seend"][i] = v

    def _deps(self, eng, r, w):
        for k in r:
            t = self.lastw.get(k)
            if t is not None:
                self._wait(eng, t, True)
            if k[0] == "ps":
                for t in self.readers.get(k, {}).values():
                    self._wait(eng, t, False)
        for k in w:
            t = self.lastw.get(k)
            if t is not None:
                self._wait(eng, t, True)
            for t in self.readers.get(k, {}).values():
                self._wait(eng, t, False)

    def _record(self, tok, r, w):
        for k in r:
            self.readers.setdefault(k, {})[tok[1]] = tok
        for k in w:
            self.lastw[k] = tok
            self.readers[k] = {}

    def op(self, eng, emit, r=(), w=()):
        X = self.E[eng]
        self._deps(eng, r, w)
        inst = emit(X["h"])
        X["n"] += 1
        inst.then_inc(X["sem"], 1)
        self.ninst += 1
        self._record(("e", eng, X["n"]), r, w)

    def dma(self, out, in_, r=(), w=()):
        self._deps("sp", r, w)
        i = self.dnext
        self.dnext = (i + 1) % self.NDS
        if self.dcnt[i] > 0:
            self._wait("sp", ("d", i, 16 * self.dcnt[i]), True)
        inst = self.nc.sync.dma_start(out=out, in_=in_)
        self.dcnt[i] += 1
        inst.then_inc(self.dsem[i], 16)
        self.ninst += 1
        self._record(("d", i, 16 * self.dcnt[i]), r, w)

    def barrier(self):
        comp = ("pe", "act", "dve", "pool")
        for e in comp:
            for p in comp:
                if p != e and self.E[p]["n"] > 0:
                    self._wait(e, ("e", p, self.E[p]["n"]), True)
            for i in range(self.NDS):
                if self.dcnt[i] > 0:
                    self._wait(e, ("d", i, 16 * self.dcnt[i]), True)
        for p in comp:
            if self.E[p]["n"] > 0:
                self._wait("sp", ("e", p, self.E[p]["n"]), True)
        for i in range(self.NDS):
            if self.dcnt[i] > 0:
                self._wait("sp", ("d", i, 16 * self.dcnt[i]), True)
        self.lastw = {}
        self.readers = {}


def rev_ap(ap2d):
    a = ap2d.ap
    n = a[-1][1]
    st = a[-1][0]
    return bass.AP(ap2d.tensor, ap2d.offset + (n - 1) * st, [list(x) for x in a[:-1]] + [[-st, n]])


DBG = dict(dn_stop=99, heads=4, gdn_m=8)


def interleave(gens):
    gens = list(gens)
    while gens:
        for g in list(gens):
            try:
                next(g)
            except StopIteration:
                gens.remove(g)


def build_nc(nseq=SEQ_PER_CORE, nlayers=L, dump_r=False, do_prepass=True, phases=("lru", "dn", "ffn", "ple"), ndbg=0):
    nc = bass.Bass("TRN2", target_bir_lowering=False)
    xT = nc.dram_tensor("xT", [SEQ_PER_CORE, D, S_LEN], F32, kind="ExternalInput").ap()
    pT = nc.dram_tensor("pT", [L, SEQ_PER_CORE, PLE, S_LEN], F32, kind="ExternalInput").ap()
    wpk = nc.dram_tensor("wpk", [L, 128, WTOT], F32, kind="ExternalInput").ap()
    vecs = nc.dram_tensor("vecs", [128, L * NV + 8], F32, kind="ExternalInput").ap()
    yT = nc.dram_tensor("yT", [SEQ_PER_CORE, D, S_LEN], F32, kind="ExternalOutput").ap()
    wsc = nc.dram_tensor("wsc", [L, 128, WTOT], BF16, kind="Internal").ap()
    dbg_out = None
    if ndbg:
        dbg_out = nc.dram_tensor("dbg", [ndbg, 128, S_LEN], F32, kind="ExternalOutput").ap()

    es = ExitStack()
    with es:
        def sb(name, shape, dt):
            return es.enter_context(nc.sbuf_tensor(name, shape, dt))

        S = Sched(nc, es)
        R = sb("R", [128, 8, S_LEN], F32)
        HN = sb("HN", [128, 8, S_LEN], BF16)
        ARENA = sb("ARENA", [128, 18560], BF16)
        TT = sb("TT", [128, 8, 512], F32)
        TB = sb("TB", [128, 6, 512], BF16)
        SV = sb("SV", [128, 21, 512], BF16)
        SF = sb("SF", [128, 512], F32)
        WB = sb("WB", [128, 5, 1024], BF16)
        VEC = sb("VEC", [128, L * NV + 8], F32)
        CST = sb("CST", [128, 13, 128], F32)
        CSB = sb("CSB", [128, 6, 128], BF16)
        MISC = sb("MISC", [128, 12, 128], F32)
        LV = sb("LV", [128, 64], F32)
        SST = sb("SST", [128, 2, 128], F32)
        SBF = sb("SBF", [128, 2, 128], BF16)
        CAR = sb("CAR", [128, 4], F32)
        PS = [es.enter_context(nc.psum_tensor(f"ps{i}", [128, 512], F32)) for i in range(8)]
        cnt = dict(ps=0, tt=0, tb=0, wb=0, sv=0, ev=0)

        def psum():
            i = cnt["ps"] % 8
            cnt["ps"] += 1
            return PS[i], ("ps", i)

        def tt():
            i = cnt["tt"] % 8
            cnt["tt"] += 1
            return TT[:, i, :], ("tt", i)

        def tb():
            i = cnt["tb"] % 6
            cnt["tb"] += 1
            return TB[:, i, :], ("tb", i)

        def wslot():
            i = cnt["wb"] % 5
            cnt["wb"] += 1
            return WB[:, i, :], ("wb", i)

        def sv():
            i = 10 + cnt["sv"] % 11
            cnt["sv"] += 1
            return SV[:, i, :], ("sv", i)

        def act(out, in_, func, r, w, bias=0.0, scale=1.0):
            S.op("act", lambda h: h.activation(out=out, in_=in_, func=func, bias=bias, scale=scale), r, w)

        def tsc(eng, out, in0, s1, s2, op0, op1, r, w):
            S.op(eng, lambda h: h.tensor_scalar(out=out, in0=in0, scalar1=s1, scalar2=s2, op0=op0, op1=op1), r, w)

        def ts1(eng, out, in_, s, op, r, w):
            S.op(eng, lambda h: h.tensor_single_scalar(out=out, in_=in_, scalar=s, op=op), r, w)

        def stt(eng, out, in0, s, in1, op0, op1, r, w):
            eng = "dve"
            S.op(eng, lambda h: h.scalar_tensor_tensor(out=out, in0=in0, scalar=s, in1=in1, op0=op0, op1=op1), r, w)

        def tten(eng, out, in0, in1, op, r, w):
            S.op(eng, lambda h: h.tensor_tensor(out=out, in0=in0, in1=in1, op=op), r, w)

        def cpy(eng, out, in_, r, w):
            if eng == "act":
                act(out, in_, AF.Identity, r, w)
            else:
                S.op(eng, lambda h: h.tensor_copy(out=out, in_=in_), r, w)

        def mm(out, lhsT, rhs, start, stop, r, w):
            S.op("pe", lambda h: h.matmul(out, lhsT=lhsT, rhs=rhs, start=start, stop=stop), r, w)

        def mset(eng, ap, val, w):
            S.op(eng, lambda h: h.memset(ap, val), (), w)

        INCF, INCB, OFFD, ONESF, IDF = (CST[:, i, :] for i in range(5))
        NEG4 = CST[:, 5:9, :]
        M16, ML0, ML1, ML2, IDB, ONESB = (CSB[:, i, :] for i in range(6))
        MLV = [ML0, ML1, ML2]
        KC = ("cst",)

        def b4(ap):
            return ap.unsqueeze(1).to_broadcast([128, 4, 128])

        def v4(ap):
            return ap.rearrange("p (e i) -> p e i", e=4)

        def aff(ap, pattern, op, fill, base, cm):
            S.op("pool", lambda h: h.affine_select(out=ap, in_=ap, pattern=pattern, compare_op=op, fill=fill,
                                                   base=base, channel_multiplier=cm), [KC], [KC])

        mset("pool", CST[:, 0:5, :], 1.0, [KC])
        aff(INCF, [[1, 128]], ALU.is_ge, 0.0, 0, -1)
        aff(INCB, [[-1, 128]], ALU.is_ge, 0.0, 0, 1)
        aff(OFFD, [[1, 128]], ALU.not_equal, 0.0, 0, -1)
        aff(IDF, [[1, 128]], ALU.is_equal, 0.0, 0, -1)
        for e in range(4):
            tsc("pool", NEG4[:, e, :], INCF if e < 2 else INCB, -1.0, 1e30, ALU.add, ALU.mult, [KC], [KC])
        mset("pool", CST[:, 9:13, :], 1.0, [KC])
        for bi, b in enumerate((16, 32, 64)):
            nb = 128 // b
            v = CST[:, 9 + bi, :].rearrange("p (k c) -> p k c", c=b)
            aff(v, [[-b, nb], [0, b]], ALU.is_ge, 0.0, 0, 1)
            aff(v, [[b, nb], [0, b]], ALU.is_gt, 0.0, b, -1)
        cpy("pool", M16, CST[:, 9, :], [KC], [KC])
        for k in range(3):
            tten("pool", MLV[k], CST[:, 10 + k, :], CST[:, 9 + k, :], ALU.subtract, [KC], [KC])
        cpy("pool", IDB, IDF, [KC], [KC])
        cpy("pool", ONESB, ONESF, [KC], [KC])
        OFFB = sb("OFFB", [128, 128], BF16)
        cpy("pool", OFFB[:], OFFD, [KC], [KC])
        S.dma(VEC[:], vecs[:, :], [], [("vec",)])
        S.barrier()

        if do_prepass:
            CH = 2048
            st_f = ARENA[:, 0:8192].bitcast(F32).rearrange("p (b c) -> p b c", b=2)
            st_b = ARENA[:, 8192:12288].rearrange("p (b c) -> p b c", b=2)
            n = 0
            for l in range(nlayers):
                c0 = 0
                while c0 < WTOT:
                    cw = min(CH, WTOT - c0)
                    fb = n % 2
                    src = st_f[:, fb, 0:cw]
                    dst = st_b[:, fb, 0:cw]
                    S.dma(src, wpk[l, :, c0:c0 + cw], [], [("pf", fb)])
                    cpy(("dve", "pool", "act")[n % 3], dst, src, [("pf", fb)], [("pb", fb)])
                    S.dma(wsc[l, :, c0:c0 + cw], dst, [("pb", fb)], [("wsc", l)])
                    c0 += cw
                    n += 1
            S.barrier()

        def wload(l, off, width):
            slot, k = wslot()
            S.dma(slot[:, 0:width], wsc[l, :, off:off + width], [], [k])
            return slot, k

        def tsl_(t):
            return slice(t * 512, (t + 1) * 512)

        def rms_stats(t, nch, src, srck, scale, eps):
            ps, kp = psum()
            for ch in range(nch):
                sq, ks = tb()
                act(sq, src(ch, t), AF.Square, [srck(ch, t)], [ks])
                mm(ps[:], ONESB, sq, ch == 0, ch == nch - 1, [ks], [kp])
            ln, kl = tt()
            act(ln, ps[:], AF.Ln, [kp], [kl], bias=eps, scale=scale)
            rs, kr = tt()
            act(rs, ln, AF.Exp, [kl], [kr], scale=-0.5)
            return rs, kr

        def rms_to_hn(gbase):
            for t in range(NT):
                rs, kr = rms_stats(t, 8, lambda ch, t: R[:, ch, tsl_(t)], lambda ch, t: ("R", ch, t), 1.0 / D, EPS)
                for ch in range(8):
                    stt("dve" if ch % 2 == 0 else "pool", HN[:, ch, tsl_(t)], R[:, ch, tsl_(t)],
                        VEC[:, gbase + ch:gbase + ch + 1], rs, ALU.mult, ALU.mult, [("R", ch, t), kr, ("vec",)], [("HN", ch, t)])

        def proj_tile(ps, kp, slab, ks, t):
            for kc in range(8):
                mm(ps[:], slab[:, kc * 128:(kc + 1) * 128], HN[:, kc, tsl_(t)], kc == 0, kc == 7, [ks, ("HN", kc, t)], [kp])

        def radd(m, t, ps, kp):
            tten("dve", R[:, m, tsl_(t)], ps[:], R[:, m, tsl_(t)], ALU.add, [kp, ("R", m, t)], [("R", m, t)])

        def dbg_store(idx, ap2d, keys):
            if dbg_out is not None and idx < ndbg:
                S.dma(dbg_out[idx, :, 0:ap2d.shape[-1]], ap2d, keys, [("dbg", idx)])

        A_QT = ARENA[:, 0:2048]
        A_KT = ARENA[:, 2048:4096]
        A_VT = ARENA[:, 4096:6144]
        A_KTOK = ARENA[:, 6144:8192]
        A_VTOK = ARENA[:, 8192:10240]
        A_OSUM = ARENA[:, 10240:14336].bitcast(F32)
        A_PC = ARENA[:, 14336:18440].bitcast(F32)
        A_XCB = ARENA[:, 0:2048]
        A_GL = ARENA[:, 2048:6144].bitcast(F32)
        A_HS = ARENA[:, 6144:10240].bitcast(F32)
        A_XC = A_OSUM
        YL = SV[:, 0:16, :].rearrange("p (c a) n -> p c (a n)", c=4)

        def pc_pads():
            mset("pool", A_PC[:, 0:2], 0.0, [("pc", "pad")])
            mset("pool", A_PC[:, 2050:2052], 0.0, [("pc", "pad")])

        def conv4(dst, dstk, wbase, bias_ap, vk):
            rk = [("pc", t) for t in range(NT)] + [("pc", "pad"), vk]
            wk = [(dstk, t) for t in range(NT)]
            if bias_ap is None:
                ts1("pool", dst, A_PC[:, 0:2048], VEC[:, wbase:wbase + 1], ALU.mult, rk, wk)
            else:
                tsc("pool", dst, A_PC[:, 0:2048], VEC[:, wbase:wbase + 1], bias_ap, ALU.mult, ALU.add, rk, wk)
            for j in range(1, 4):
                stt("pool" if j % 2 else "dve", dst, A_PC[:, j:j + 2048], VEC[:, wbase + j:wbase + j + 1], dst,
                    ALU.mult, ALU.add, rk + wk, wk)

        def wout_pass(l, kcs, src, srck):
            slabs = [wload(l, O_WOUT + kc * 1024, 1024) for kc in kcs]
            for m in range(8):
                for t in range(NT):
                    ps, kp = psum()
                    for i, (sl, ks) in enumerate(slabs):
                        mm(ps[:], sl[:, m * 128:(m + 1) * 128], src(i, t), i == 0, i == len(slabs) - 1, [ks, srck(i, t)], [kp])
                    radd(m, t, ps, kp)

        def layer_prep(l):
            vb = l * NV
            kl = ("lv",)
            vk = ("vec",)
            act(LV[:, 48:56], VEC[:, vb + V_ALOG:vb + V_ALOG + 8], AF.Exp, [vk], [kl])
            ts1("dve", LV[:, 0:8], LV[:, 48:56], -1.0, ALU.mult, [kl], [kl])
            ts1("dve", LV[:, 8:16], VEC[:, vb + V_LBA:vb + V_LBA + 8], 0.5, ALU.mult, [vk, kl], [kl])
            ts1("dve", LV[:, 16:24], VEC[:, vb + V_LBX:vb + V_LBX + 8], 0.5, ALU.mult, [vk, kl], [kl])
            act(LV[:, 48:56], VEC[:, vb + V_LLAM:vb + V_LLAM + 8], AF.Exp, [vk, kl], [kl], scale=-1.0)
            act(LV[:, 56:64], LV[:, 48:56], AF.Ln, [kl], [kl], bias=1.0)
            ts1("dve", LV[:, 24:32], LV[:, 56:64], -8.0, ALU.mult, [kl], [kl])
            ts1("dve", LV[:, 32:40], LV[:, 56:64], -4.0, ALU.mult, [kl], [kl])
            ts1("dve", LV[:, 40:48], VEC[:, vb + V_BGP:vb + V_BGP + 8], 0.5, ALU.mult, [vk, kl], [kl])

        def m3(i):
            return MISC[:, i, :].rearrange("p (t h) -> p t h", h=8)

        AB = MISC[:, 0:2, :].rearrange("p a b -> p (a b)").rearrange("p (t c) -> p t c", c=16)
        G3, NLNB, CCOL, CLAST, BIASD, NEGEC, BDEC, ECL, TM1, TM2 = (m3(i) for i in range(2, 12))
        KM = ("misc",)

        def ab_phase(l):
            vb = l * NV
            slab, ks = wload(l, O_WAB, 128)
            ps, kp = psum()
            for n in range(TK):
                for kc in range(8):
                    mm(ps[:, n * 16:(n + 1) * 16], HN[:, kc, n * 128:(n + 1) * 128], slab[:, kc * 16:(kc + 1) * 16],
                       kc == 0, kc == 7, [ks, ("HN", kc, n // 4)], [kp])
            cpy("dve", MISC[:, 0:2, :].rearrange("p a b -> p (a b)"), ps[:, 0:256], [kp], [KM])
            r = [KM, ("lv",), ("vec",)]
            act(TM1, AB[:, :, 0:8], AF.Exp, r, [KM], scale=-1.0)
            act(NLNB, TM1, AF.Ln, r, [KM], bias=1.0)
            tten("dve", TM2, AB[:, :, 8:16], VEC[:, vb + V_DTB:vb + V_DTB + 8].unsqueeze(1).to_broadcast([128, 16, 8]), ALU.add, r, [KM])
            act(TM2, TM2, AF.Exp, r, [KM])
            act(TM2, TM2, AF.Ln, r, [KM], bias=1.0)
            tten("dve", G3, TM2, LV[:, 0:8].unsqueeze(1).to_broadcast([128, 16, 8]), ALU.mult, r, [KM])
            ps2, kp2 = psum()
            mm(ps2[:, 0:64].rearrange("p (t h) -> p t h", h=4), INCF, G3[:, :, 0:4], True, True, [KM, KC], [kp2])
            mm(ps2[:, 64:128].rearrange("p (t h) -> p t h", h=4), INCB, G3[:, :, 4:8], True, True, [KM, KC], [kp2])
            mm(ps2[:, 128:256], ONESF, MISC[:, 2, :], True, True, [KM, KC], [kp2])
            cpy("dve", CCOL[:, :, 0:4], ps2[:, 0:64].rearrange("p (t h) -> p t h", h=4), [kp2], [KM])
            cpy("dve", CCOL[:, :, 4:8], ps2[:, 64:128].rearrange("p (t h) -> p t h", h=4), [kp2, KM], [KM])
            cpy("dve", MISC[:, 5, :], ps2[:, 128:256], [kp2, KM], [KM])
            tten("dve", TM1, NLNB, CCOL, ALU.add, r, [KM])
            ts1("dve", BIASD, TM1, -1.0, ALU.mult, r, [KM])
            act(TM2, CCOL, AF.Exp, r, [KM])
            ts1("dve", NEGEC, TM2, -1.0, ALU.mult, r, [KM])
            tten("dve", TM1, BIASD, CLAST, ALU.add, r, [KM])
            act(BDEC, TM1, AF.Exp, r, [KM])
            act(ECL, CLAST, AF.Exp, r, [KM])

        def lru_phase(l):
            vb = l * NV
            vk = ("vec",)
            for c in range(4):
                pc_pads()
                lw, klw = wload(l, O_LRUW + c * 512, 512)
                slab, ks = wload(l, O_WIN + (16 + c) * 1024, 1024)
                for t in range(NT):
                    ps, kp = psum()
                    proj_tile(ps, kp, slab, ks, t)
                    cpy("act", A_PC[:, 2 + t * 512:2 + (t + 1) * 512], ps[:], [kp], [("pc", t)])
                conv4(A_XC, "xc", vb + V_LCW + c * 4, VEC[:, vb + V_LCB + c:vb + V_LCB + c + 1], vk)
                cpy("act", A_XCB, A_XC, [("xc", t) for t in range(NT)], [("xcb", t) for t in range(NT)])
                slab, ks = wload(l, O_WIN + (20 + c) * 1024, 1024)
                for t in range(NT):
                    ps, kp = psum()
                    proj_tile(ps, kp, slab, ks, t)
                    x2, k2 = tt()
                    act(x2, ps[:], AF.Square, [kp], [k2])
                    tsc("dve", x2, x2, GC2, GC1, ALU.mult, ALU.add, [k2], [k2])
                    tten("dve", x2, x2, ps[:], ALU.mult, [k2, kp], [k2])
                    act(x2, x2, AF.Tanh, [k2], [k2])
                    stt("dve", A_GL[:, tsl_(t)], x2, 1.0, ps[:], ALU.add, ALU.mult, [k2, kp], [("gl", t)])
                for d in range(2):
                    order = list(range(NT)) if d == 0 else list(range(NT - 1, -1, -1))
                    lvi = d * 4 + c
                    for ti, t in enumerate(order):
                        tsl = tsl_(t)
                        psr, kpr = psum()
                        mm(psr[:], lw[:, (d * 2 + 0) * 128:(d * 2 + 1) * 128], A_XCB[:, tsl], True, True, [klw, ("xcb", t)], [kpr])
                        psi, kpi = psum()
                        mm(psi[:], lw[:, (d * 2 + 1) * 128:(d * 2 + 2) * 128], A_XCB[:, tsl], True, True, [klw, ("xcb", t)], [kpi])
                        thr, kr_ = tt()
                        act(thr, psr[:], AF.Tanh, [kpr, ("lv",)], [kr_], scale=0.5, bias=LV[:, 8 + lvi:9 + lvi])
                        thi, ki_ = tt()
                        act(thi, psi[:], AF.Tanh, [kpi, ("lv",)], [ki_], scale=0.5, bias=LV[:, 16 + lvi:17 + lvi])
                        a, ka = tt()
                        act(a, thr, AF.Exp, [kr_, ("lv",)], [ka], scale=LV[:, 32 + lvi:33 + lvi], bias=LV[:, 32 + lvi:33 + lvi])
                        a2, ka2 = tt()
                        act(a2, thr, AF.Exp, [kr_, ("lv",)], [ka2], scale=LV[:, 24 + lvi:25 + lvi], bias=LV[:, 24 + lvi:25 + lvi])
                        tl, ktl = tt()
                        act(tl, thr, AF.Tanh, [kr_, ("lv",)], [ktl], scale=LV[:, 32 + lvi:33 + lvi], bias=LV[:, 32 + lvi:33 + lvi])
                        stt("dve", a2, a2, 1.0, tl, ALU.add, ALU.mult, [ka2, ktl], [ka2])
                        act(a2, a2, AF.Ln, [ka2], [ka2], scale=-1.0)
                        act(a2, a2, AF.Exp, [ka2], [ka2], scale=0.5)
                        stt("pool", thi, thi, 1.0, A_XC[:, tsl], ALU.add, ALU.mult, [ki_, ("xc", t)], [ki_])
                        stt("pool", thi, thi, 0.5, a2, ALU.mult, ALU.mult, [ki_, ka2], [ki_])
                        if d == 0:
                            init = 0.0 if ti == 0 else A_HS[:, t * 512 - 1:t * 512]
                            rr = [ka, ki_] + ([("hs", t - 1)] if ti else [])
                            S.op("dve", lambda h, o=A_HS[:, tsl], a_=a, b_=thi, i_=init: h.tensor_tensor_scan(
                                out=o, data0=a_, data1=b_, initial=i_, op0=ALU.mult, op1=ALU.add), rr, [("hs", t)])
                        else:
                            init = 0.0 if ti == 0 else CAR[:, 0:1]
                            S.op("dve", lambda h, o=rev_ap(tl), a_=rev_ap(a), b_=rev_ap(thi), i_=init: h.tensor_tensor_scan(
                                out=o, data0=a_, data1=b_, initial=i_, op0=ALU.mult, op1=ALU.add), [ka, ki_, ktl, ("car",)], [ktl])
                            cpy("dve", CAR[:, 0:1], tl[:, 0:1], [ktl, ("car",)], [("car",)])
                            tten("pool", A_HS[:, tsl], A_HS[:, tsl], tl, ALU.add, [ktl, ("hs", t)], [("hs", t)])
                tten("pool", YL[:, c, :], A_GL, A_HS, ALU.mult, [("gl", t) for t in range(NT)] + [("hs", t) for t in range(NT)],
                     [("yl", c, t) for t in range(NT)])
            for t in range(NT):
                rs, kr = rms_stats(t, 4, lambda c, t: YL[:, c, tsl_(t)], lambda c, t: ("yl", c, t), 1.0 / 512, 4 * EPS)
                for c in range(4):
                    stt("dve", YL[:, c, tsl_(t)], YL[:, c, tsl_(t)], VEC[:, vb + V_LNG + c:vb + V_LNG + c + 1], rs,
                        ALU.mult, ALU.mult, [("yl", c, t), kr, vk], [("yl", c, t)])
            wout_pass(l, [4, 5, 6, 7], lambda i, t: YL[:, i, tsl_(t)], lambda i, t: ("yl", i, t))

        def gdn(l, h):
            KT3 = A_KT.rearrange("p (n i) -> p n i", i=128)
            QT3 = A_QT.rearrange("p (n i) -> p n i", i=128)
            KTOK3 = A_KTOK.rearrange("p (n i) -> p n i", i=128)
            VTOK3 = A_VTOK.rearrange("p (n i) -> p n i", i=128)
            OS3 = A_OSUM.rearrange("p (n i) -> p n i", i=128)
            mset("pool", SST[:], 0.0, [("sst", 0), ("sst", 1)])
            mset("pool", SBF[:], 0.0, [("sbf", 0), ("sbf", 1)])

            def elem(m, e):
                if e < 2:
                    return 2 * m + e, 0, h
                return 15 - 2 * m - (e - 2), 1, 4 + h

            def mm4(lh, klh, rh, krh):
                ps, kp = psum()
                p4 = v4(ps[:])
                l4, r4 = v4(lh), v4(rh)
                for e in range(4):
                    mm(p4[:, e, :], l4[:, e, :], r4[:, e, :], True, True, [klh, krh], [kp])
                return ps, kp

            def evac(ps, kp, dst=None, kd=None):
                if dst is None:
                    dst, kd = sv()
                cnt["ev"] += 1
                cpy("act" if cnt["ev"] % 2 else "dve", dst, ps[:], [kp], [kd])
                return dst, kd

            def plus_i(x, kx):
                g, kg = sv()
                tten("pool", v4(g), v4(x), b4(IDB), ALU.add, [kx, KC], [kg])
                return g, kg

            def prep(m):
                par = m % 2
                els = [elem(m, e) for e in range(4)]
                Z, kZ = SV[:, par * 3 + 0, :], ("svp", par, 0)
                ATT, kATT = SV[:, par * 3 + 1, :], ("svp", par, 1)
                QD, kQD = SV[:, par * 3 + 2, :], ("svp", par, 2)
                BP, kBP = SV[:, 6, :], ("svp", 6)
                AT, kAT = SV[:, 7, :], ("svp", 7)
                DT, kDT = SV[:, 8, :], ("svp", 8)
                ER, kER = SV[:, 9, :], ("svp", 9)
                mg4 = v4(SF[:])
                kmg = ("sf",)
                for e, (tile, d, dh) in enumerate(els):
                    ts1("pool", mg4[:, e, :], INCF if d == 0 else INCB, G3[:, tile, dh:dh + 1], ALU.mult, [KM, KC], [kmg])
                c1, kc1 = psum()
                mm(c1[:], ONESF, SF[:], True, False, [kmg, KC], [kc1])
                mm(v4(c1[:]), IDF, NEG4, False, True, [KC], [kc1])
                c2, kc2 = psum()
                mm(c2[:], ONESF, SF[:], True, True, [kmg, KC], [kc2])
                yield
                kk, kkk = psum()
                qk, kqk = psum()
                for e, (tile, d, dh) in enumerate(els):
                    mm(v4(kk[:])[:, e, :], KT3[:, tile, :], KT3[:, tile, :], True, True, [("kt",)], [kkk])
                    mm(v4(qk[:])[:, e, :], KT3[:, tile, :], QT3[:, tile, :], True, True, [("kt",), ("qt",)], [kqk])
                for e, (tile, d, dh) in enumerate(els):
                    act(v4(DT)[:, e, :], v4(c1[:])[:, e, :], AF.Exp, [kc1, KM], [kDT], bias=BIASD[:, tile, dh:dh + 1])
                act(ER, c2[:], AF.Exp, [kc2], [kER])
                yield
                tten("dve", BP, kk[:], DT, ALU.mult, [kkk, kDT], [kBP])
                tten("pool", v4(BP), v4(BP), b4(OFFB[:]), ALU.mult, [kBP, KC], [kBP])
                tten("dve", ATT, qk[:], DT, ALU.mult, [kqk, kDT], [kATT])
                for e, (tile, d, dh) in enumerate(els):
                    tten("pool", v4(QD)[:, e, :], QT3[:, tile, :], v4(ER)[:, e, :], ALU.mult, [("qt",), kER], [kQD])
                yield
                pst, kpst = psum()
                pstb = pst[:].bitcast(BF16)
                for e in range(4):
                    S.op("pe", lambda hh, o=pstb[:, e * 128:(e + 1) * 128], i_=v4(BP)[:, e, :]: hh.transpose(out=o, in_=i_, identity=IDB),
                         [kBP, KC], [kpst])
                cpy("act", AT, pstb[:, 0:512], [kpst], [kAT])
                x, kx = sv()
                stt("pool", v4(x), v4(BP), -1.0, b4(M16), ALU.mult, ALU.mult, [kBP, KC], [kx])
                xt, kxt = sv()
                stt("pool", v4(xt), v4(AT), -1.0, b4(M16), ALU.mult, ALU.mult, [kAT, KC], [kxt])
                yield
                x2, kx2 = evac(*mm4(xt, kxt, x, kx))
                x2t, kx2t = evac(*mm4(x, kx, xt, kxt))
                g1t, kg1t = plus_i(xt, kxt)
                g2, kg2 = plus_i(x2, kx2)
                yield
                y1, ky1 = evac(*mm4(g1t, kg1t, g2, kg2))
                y1t, ky1t = evac(*mm4(g2, kg2, g1t, kg1t))
                x4, kx4 = evac(*mm4(x2t, kx2t, x2, kx2))
                x4t, kx4t = evac(*mm4(x2, kx2, x2t, kx2t))
                yield
                g4, kg4 = plus_i(x4, kx4)
                y2, ky2 = evac(*mm4(y1t, ky1t, g4, kg4))
                y2t, ky2t = evac(*mm4(g4, kg4, y1t, ky1t))
                x8, kx8 = evac(*mm4(x4t, kx4t, x4, kx4))
                g8, kg8 = plus_i(x8, kx8)
                yield
                z, kz = evac(*mm4(y2t, ky2t, g8, kg8))
                zt, kzt = evac(*mm4(g8, kg8, y2t, ky2t))
                yield
                for k in range(3):
                    ot, kot = sv()
                    tten("pool", v4(ot), v4(AT), b4(MLV[k]), ALU.mult, [kAT, KC], [kot])
                    w1, kw1 = mm4(ot, kot, z, kz)
                    iw, kiw = sv()
                    tten("dve", v4(iw), b4(IDB), v4(w1[:]), ALU.subtract, [kw1, KC], [kiw])
                    zn, kzn = mm4(zt, kzt, iw, kiw)
                    yield
                    if k < 2:
                        z, kz = evac(zn, kzn)
                        pst, kpst = psum()
                        pstb = pst[:].bitcast(BF16)
                        for e in range(4):
                            S.op("pe", lambda hh, o=pstb[:, e * 128:(e + 1) * 128], i_=v4(z)[:, e, :]: hh.transpose(out=o, in_=i_, identity=IDB),
                                 [kz, KC], [kpst])
                        zt, kzt = sv()
                        cpy("act", zt, pstb[:, 0:512], [kpst], [kzt])
                    else:
                        evac(zn, kzn, Z, kZ)
                    yield

            def steps(m):
                par = m % 2
                Z4, kZ = v4(SV[:, par * 3 + 0, :]), ("svp", par, 0)
                ATT4, kATT = v4(SV[:, par * 3 + 1, :]), ("svp", par, 1)
                QD4, kQD = v4(SV[:, par * 3 + 2, :]), ("svp", par, 2)
                for sidx in (2 * m, 2 * m + 1):
                    for d in range(2):
                        tile = sidx if d == 0 else 15 - sidx
                        e = (sidx % 2) + 2 * d
                        dh = h + 4 * d
                        ksp, k1 = psum()
                        mm(ksp[:, 0:128], KT3[:, tile, :], SBF[:, d, :], True, True, [("kt",), ("sbf", d)], [k1])
                        vp, kvp = tb()
                        stt("dve", vp[:, 0:128], ksp[:, 0:128], NEGEC[:, tile, dh:dh + 1], VTOK3[:, tile, :], ALU.mult, ALU.add,
                            [k1, KM, ("vtok",)], [kvp])
                        yield
                        if DBG.get("step_stop", 9) <= 1:
                            continue
                        vrp, k2 = psum()
                        mm(vrp[:, 0:128], Z4[:, e, :], vp[:, 0:128], True, True, [kZ, kvp], [k2])
                        vraw, kvr = tb()
                        cpy("act", vraw[:, 0:128], vrp[:, 0:128], [k2], [kvr])
                        vdec, kvd = tb()
                        if DBG.get("v2"):
                            ts1("dve", vdec[:, 0:128], vraw[:, 0:128], BDEC[:, tile, dh:dh + 1], ALU.mult, [kvr, KM], [kvd])
                        elif not DBG.get("v1"):
                            ts1("dve", vdec[:, 0:128], vrp[:, 0:128], BDEC[:, tile, dh:dh + 1], ALU.mult, [k2, KM], [kvd])
                        yield
                        if DBG.get("step_stop", 9) <= 2:
                            continue
                        op_, k3 = psum()
                        mm(op_[:, 0:128], SBF[:, d, :], QD4[:, e, :], True, False, [("sbf", d), kQD], [k3])
                        mm(op_[:, 0:128], vraw[:, 0:128], ATT4[:, e, :], False, True, [kvr, kATT], [k3])
                        first = (d == 0 and tile < 8) or (d == 1 and tile >= 8)
                        if first:
                            cpy("act", OS3[:, tile, :], op_[:, 0:128], [k3], [("os", tile)])
                        else:
                            tten("dve", OS3[:, tile, :], op_[:, 0:128], OS3[:, tile, :], ALU.add, [k3, ("os", tile)], [("os", tile)])
                        if DBG.get("step_stop", 9) <= 3:
                            continue
                        snp, k4 = psum()
                        mm(snp[:, 0:128], KTOK3[:, tile, :], vdec[:, 0:128], True, True, [("ktok",), kvd], [k4])
                        stt("dve", SST[:, d, :], SST[:, d, :], ECL[:, tile, dh:dh + 1], snp[:, 0:128], ALU.mult, ALU.add,
                            [k4, KM, ("sst", d)], [("sst", d)])
                        cpy("pool", SBF[:, d, :], SST[:, d, :], [("sst", d)], [("sbf", d)])
                        yield

            interleave([prep(0)])
            if DBG["dn_stop"] <= 3:
                return
            for m in range(DBG["gdn_m"]):
                gens = [steps(m)]
                if m + 1 < 8 and not DBG.get("noprep"):
                    gens.insert(0, prep(m + 1))
                interleave(gens)


        def dn_head(l, h):
            vb = l * NV
            vk = ("vec",)
            pc_pads()
            for ci, (dst, dk) in enumerate(((A_QT, "qt"), (A_KT, "kt"), (A_VT, "vt"))):
                slab, ks = wload(l, O_WIN + (ci * 4 + h) * 1024, 1024)
                for t in range(NT):
                    ps, kp = psum()
                    proj_tile(ps, kp, slab, ks, t)
                    cpy("act", A_PC[:, 2 + t * 512:2 + (t + 1) * 512], ps[:], [kp], [("pc", t)])
                conv4(A_OSUM, "cy", vb + V_DNCW + (ci * 4 + h) * 4, None, vk)
                for t in range(NT):
                    tsl = tsl_(t)
                    th, kth = tt()
                    act(th, A_OSUM[:, tsl], AF.Tanh, [("cy", t)], [kth], scale=0.5)
                    if ci == 2:
                        stt("dve", dst[:, tsl], th, 1.0, A_OSUM[:, tsl], ALU.add, ALU.mult, [kth, ("cy", t)], [(dk,)])
                    else:
                        sy, ksy = tt()
                        stt("dve", sy, th, 1.0, A_OSUM[:, tsl], ALU.add, ALU.mult, [kth, ("cy", t)], [ksy])
                        sq, ksq = tb()
                        act(sq, sy, AF.Square, [ksy], [ksq])
                        ps, kp = psum()
                        mm(ps[:], ONESB, sq, True, True, [ksq, KC], [kp])
                        ln, kln = tt()
                        act(ln, ps[:], AF.Ln, [kp], [kln], bias=4 * EPS)
                        act(ln, ln, AF.Exp, [kln], [kln], scale=-0.5, bias=(-0.5 * math.log(128.0) if ci == 0 else 0.0))
                        tten("pool", dst[:, tsl], sy, ln, ALU.mult, [ksy, kln], [(dk,)])
            if DBG["dn_stop"] <= 1:
                return
            for src, sk, dst, dk, sc in ((A_KT, "kt", A_KTOK, "ktok", 1.0), (A_VT, "vt", A_VTOK, "vtok", 0.5)):
                for half in range(2):
                    pst, kpst = psum()
                    pstb = pst[:].bitcast(BF16)
                    for n in range(8):
                        tile = half * 8 + n
                        S.op("pe", lambda hh, o=pstb[:, n * 128:(n + 1) * 128], i_=src[:, tile * 128:(tile + 1) * 128]:
                             hh.transpose(out=o, in_=i_, identity=IDB), [(sk,), KC], [kpst])
                    act(dst[:, half * 1024:(half + 1) * 1024], pstb[:, 0:1024], AF.Identity, [kpst], [(dk,)], scale=sc)
            if DBG["dn_stop"] <= 2:
                return
            S.barrier()
            gdn(l, h)
            S.barrier()
            if DBG["dn_stop"] <= 4:
                return
            zslab, kzs = wload(l, O_WIN + (12 + h) * 1024, 1024)
            for t in range(NT):
                tsl = tsl_(t)
                osk = [("os", n) for n in range(t * 4, t * 4 + 4)]
                sq, ksq = tb()
                act(sq, A_OSUM[:, tsl], AF.Square, osk, [ksq])
                ps, kp = psum()
                mm(ps[:], ONESB, sq, True, True, [ksq, KC], [kp])
                ln, kln = tt()
                act(ln, ps[:], AF.Ln, [kp], [kln], bias=EPS, scale=1.0 / 128)
                act(ln, ln, AF.Exp, [kln], [kln], scale=-0.5)
                tmp, ktmp = tt()
                stt("dve", tmp, A_OSUM[:, tsl], VEC[:, vb + V_DNG:vb + V_DNG + 1], ln, ALU.mult, ALU.mult, osk + [kln, vk], [ktmp])
                psz, kpz = psum()
                proj_tile(psz, kpz, zslab, kzs, t)
                th, kth = tt()
                act(th, psz[:], AF.Tanh, [kpz], [kth], scale=0.5)
                stt("dve", th, th, 1.0, psz[:], ALU.add, ALU.mult, [kth, kpz], [kth])
                stt("pool", A_VT[:, tsl], tmp, 0.5, th, ALU.mult, ALU.mult, [ktmp, kth], [("dno", t), ("vt",)])
            wout_pass(l, [h], lambda i, t: A_VT[:, tsl_(t)], lambda i, t: ("dno", t))

        def ffn_phase(l):
            vb = l * NV
            vk = ("vec",)
            rms_to_hn(vb + V_G2)
            ACTB = ARENA[:, 0:11264].rearrange("p (f n) -> p f n", n=512)
            GP = ARENA[:, 11264:13320].bitcast(F32).rearrange("p (b n) -> p b n", b=2)
            for qt in range(NT):
                t0 = qt * 512
                for fc in range(NFC):
                    sg, ksg = wload(l, O_WG + fc * 1024, 1024)
                    su, ksu = wload(l, O_WU + fc * 1024, 1024)
                    gb = fc % 2
                    gp = GP[:, gb, :]
                    kgp = ("gp", gb)
                    psg, kpg = psum()
                    proj_tile(psg, kpg, sg, ksg, qt)
                    psh, kph = psum()
                    sides = []
                    if qt > 0:
                        sides.append((0, t0 - 1))
                    if qt < NT - 1:
                        sides.append((1, t0 + 512))
                    for (si, col) in sides:
                        for kc in range(8):
                            mm(psh[:, si:si + 1], sg[:, kc * 128:(kc + 1) * 128], HN[:, kc, col:col + 1], kc == 0, kc == 7,
                               [ksg, ("HN", kc, col // 512)], [kph])
                    cpy("act", gp[:, 1:513], psg[:], [kpg], [kgp])
                    if qt > 0:
                        cpy("dve", gp[:, 0:1], psh[:, 0:1], [kph, kgp], [kgp])
                    else:
                        mset("dve", gp[:, 0:1], 0.0, [kgp])
                    if qt < NT - 1:
                        cpy("dve", gp[:, 513:514], psh[:, 1:2], [kph, kgp], [kgp])
                    else:
                        mset("dve", gp[:, 513:514], 0.0, [kgp])
                    cg, kcg = tt()
                    wb_ = vb + V_FCW + fc * 3
                    tsc("pool", cg, gp[:, 0:512], VEC[:, wb_:wb_ + 1], VEC[:, vb + V_FCB + fc:vb + V_FCB + fc + 1], ALU.mult, ALU.add,
                        [kgp, vk], [kcg])
                    stt("pool", cg, gp[:, 1:513], VEC[:, wb_ + 1:wb_ + 2], cg, ALU.mult, ALU.add, [kgp, vk, kcg], [kcg])
                    stt("pool", cg, gp[:, 2:514], VEC[:, wb_ + 2:wb_ + 3], cg, ALU.mult, ALU.add, [kgp, vk, kcg], [kcg])
                    x2, k2 = tt()
                    act(x2, cg, AF.Square, [kcg], [k2])
                    tsc("dve", x2, x2, GC2, GC1, ALU.mult, ALU.add, [k2], [k2])
                    tten("pool", x2, x2, cg, ALU.mult, [k2, kcg], [k2])
                    act(x2, x2, AF.Tanh, [k2], [k2])
                    stt("pool", cg, x2, 1.0, cg, ALU.add, ALU.mult, [k2, kcg], [kcg])
                    psu, kpu = psum()
                    proj_tile(psu, kpu, su, ksu, qt)
                    stt("dve", ACTB[:, fc, :], cg, 0.5, psu[:], ALU.mult, ALU.mult, [kcg, kpu], [("actb", fc)])
                for m in range(8):
                    ps, kp = psum()
                    pieces = ((0, 8), (8, 16), (16, 22))
                    for (f0, f1) in pieces:
                        sl, ksl = wload(l, O_WD + m * 2816 + f0 * 128, (f1 - f0) * 128)
                        for fc in range(f0, f1):
                            mm(ps[:], sl[:, (fc - f0) * 128:(fc - f0 + 1) * 128], ACTB[:, fc, :], fc == 0, fc == NFC - 1,
                               [ksl, ("actb", fc)], [kp])
                    radd(m, qt, ps, kp)

        def ple_phase(l, s):
            vb = l * NV
            rms_to_hn(vb + V_GP)
            PB = ARENA[:, 0:4096].rearrange("p (k n) -> p k n", k=2)
            PF = ARENA[:, 4096:8192].bitcast(F32).rearrange("p (b k n) -> p b k n", b=2, k=2)
            psrc = pT[l, s].rearrange("(k p) n -> p k n", p=128)
            for t in range(NT):
                S.dma(PF[:, t % 2], psrc[:, :, tsl_(t)], [], [("pf", t % 2)])
                cpy("pool", PB[:, :, tsl_(t)], PF[:, t % 2], [("pf", t % 2)], [("pb", t)])
            for m in range(8):
                sg, ksg = wload(l, O_PWG + m * 1024, 1024)
                sp_, ksp_ = wload(l, O_PWP + m * 256, 256)
                for t in range(NT):
                    ps1, kp1 = psum()
                    proj_tile(ps1, kp1, sg, ksg, t)
                    th, kth = tt()
                    act(th, ps1[:], AF.Tanh, [kp1, ("lv",)], [kth], scale=0.5, bias=LV[:, 40 + m:41 + m])
                    ps2, kp2 = psum()
                    for k2 in range(2):
                        mm(ps2[:], sp_[:, k2 * 128:(k2 + 1) * 128], PB[:, k2, tsl_(t)], k2 == 0, k2 == 1, [ksp_, ("pb", t)], [kp2])
                    stt("dve", th, th, 1.0, ps2[:], ALU.add, ALU.mult, [kth, kp2], [kth])
                    stt("dve", R[:, m, tsl_(t)], th, 0.5, R[:, m, tsl_(t)], ALU.mult, ALU.add, [kth, ("R", m, t)], [("R", m, t)])


        for s in range(nseq):
            for ch in range(8):
                S.dma(R[:, ch, :], xT[s, ch * 128:(ch + 1) * 128, :], [], [("R", ch, t) for t in range(NT)])
            for l in range(nlayers):
                layer_prep(l)
                rms_to_hn(l * NV + V_G1)
                if "lru" in phases or "dn" in phases:
                    ab_phase(l)
                S.barrier()
                if "lru" in phases:
                    lru_phase(l)
                    S.barrier()
                if "dn" in phases:
                    for h in range(DBG["heads"]):
                        dn_head(l, h)
                        S.barrier()
                if "ffn" in phases:
                    ffn_phase(l)
                    S.barrier()
                if "ple" in phases:
                    ple_phase(l, s)
                    S.barrier()
            S.barrier()
            if dump_r:
                for ch in range(8):
                    S.dma(yT[s, ch * 128:(ch + 1) * 128, :], R[:, ch, :], [("R", ch, t) for t in range(NT)], [("y", s, ch)])
            else:
                for t in range(NT):
                    rs, kr = rms_stats(t, 8, lambda ch, t: R[:, ch, tsl_(t)], lambda ch, t: ("R", ch, t), 1.0 / D, EPS)
                    for ch in range(8):
                        o, ko = tt()
                        stt("dve" if ch % 2 == 0 else "pool", o, R[:, ch, tsl_(t)], VEC[:, L * NV + ch:L * NV + ch + 1], rs,
                            ALU.mult, ALU.mult, [("R", ch, t), kr], [ko])
                        S.dma(yT[s, ch * 128:(ch + 1) * 128, tsl_(t)], o, [ko], [("y", s, ch, t)])
            S.barrier()
        S.barrier()
        print("instructions:", S.ninst, {k: v["n"] for k, v in S.E.items()})
    return nc


def _slabs_kn(w):
    K, N = w.shape
    a = w.reshape(K // 128, 128, N // 128, 128)
    return np.ascontiguousarray(a.transpose(2, 1, 0, 3)).reshape(N // 128, 128, K)


def pack_weights(inp):
    wpk = np.zeros((L, 128, WTOT), np.float32)
    for l in range(L):
        w_in = inp["w_in"][l]
        cols = np.concatenate([w_in[:, 0:2048], w_in[:, 2064:3088]], axis=1)
        sl = _slabs_kn(cols)
        wpk[l, :, O_WIN:O_WIN + 24 * 1024] = sl.transpose(1, 0, 2).reshape(128, 24 * 1024)
        ab = w_in[:, 2048:2064].reshape(8, 128, 16).transpose(1, 0, 2).reshape(128, 128)
        wpk[l, :, O_WAB:O_WAB + 128] = ab
        lw = np.zeros((128, 4, 2, 2, 128), np.float32)
        for d in range(2):
            for gi, nm in enumerate(("lru_wa", "lru_wx")):
                wg = inp[nm][l, d]
                for c in range(4):
                    lw[0:64, c, d, gi, 0:64] = wg[2 * c]
                    lw[64:128, c, d, gi, 64:128] = wg[2 * c + 1]
        wpk[l, :, O_LRUW:O_LRUW + 2048] = lw.reshape(128, 2048)
        wpk[l, :, O_WOUT:O_WOUT + 8192] = inp["w_out"][l].reshape(8, 128, 1024).transpose(1, 0, 2).reshape(128, 8192)
        wpk[l, :, O_WG:O_WG + 22528] = _slabs_kn(inp["ffn_wg"][l]).transpose(1, 0, 2).reshape(128, 22528)
        wpk[l, :, O_WU:O_WU + 22528] = _slabs_kn(inp["ffn_wu"][l]).transpose(1, 0, 2).reshape(128, 22528)
        wd = inp["ffn_wd"][l].reshape(NFC, 128, 8, 128)
        wpk[l, :, O_WD:O_WD + 22528] = wd.transpose(1, 2, 0, 3).reshape(128, 22528)
        wpk[l, :, O_PWG:O_PWG + 8192] = _slabs_kn(inp["ple_wg"][l]).transpose(1, 0, 2).reshape(128, 8192)
        wpk[l, :, O_PWP:O_PWP + 2048] = _slabs_kn(inp["ple_wp"][l]).transpose(1, 0, 2).reshape(128, 2048)
    return wpk


def pack_vecs(inp):
    v = np.zeros((128, L * NV + 8), np.float32)

    def cm(a, n):
        return a.reshape(n, 128).T

    for l in range(L):
        b = l * NV
        v[:, b + V_G1:b + V_G1 + 8] = cm(inp["norm1_g"][l], 8)
        v[:, b + V_G2:b + V_G2 + 8] = cm(inp["norm2_g"][l], 8)
        v[:, b + V_GP:b + V_GP + 8] = cm(inp["ple_norm_g"][l], 8)
        v[:, b + V_BGP:b + V_BGP + 8] = cm(inp["ple_bg"][l], 8)
        v[:, b + V_DNCW:b + V_DNCW + 48] = inp["dn_conv_w"][l].reshape(4, 12, 128).transpose(2, 1, 0).reshape(128, 48)
        v[:, b + V_LCW:b + V_LCW + 16] = inp["lru_conv_w"][l].reshape(4, 4, 128).transpose(2, 1, 0).reshape(128, 16)
        v[:, b + V_LCB:b + V_LCB + 4] = cm(inp["lru_conv_b"][l], 4)
        v[:, b + V_LBA:b + V_LBA + 8] = inp["lru_ba"][l].reshape(2, 4, 128).transpose(2, 0, 1).reshape(128, 8)
        v[:, b + V_LBX:b + V_LBX + 8] = inp["lru_bx"][l].reshape(2, 4, 128).transpose(2, 0, 1).reshape(128, 8)
        v[:, b + V_LLAM:b + V_LLAM + 8] = inp["lru_lambda"][l].reshape(2, 4, 128).transpose(2, 0, 1).reshape(128, 8)
        v[:, b + V_LNG:b + V_LNG + 4] = cm(inp["lru_norm_g"][l], 4)
        v[:, b + V_DNG] = inp["dn_norm_g"][l]
        v[:, b + V_FCW:b + V_FCW + 66] = inp["ffn_conv_w"][l].reshape(3, NFC, 128).transpose(2, 1, 0).reshape(128, 66)
        v[:, b + V_FCB:b + V_FCB + 22] = cm(inp["ffn_conv_b"][l], NFC)
        v[:, b + V_ALOG:b + V_ALOG + 8] = np.broadcast_to(inp["dn_a_log"][l].reshape(1, 8), (128, 8))
        v[:, b + V_DTB:b + V_DTB + 8] = np.broadcast_to(inp["dn_dt_bias"][l].reshape(1, 8), (128, 8))
    v[:, L * NV:L * NV + 8] = cm(inp["final_g"], 8)
    return v


def kernel(**inputs):
    inp = {k: np.asarray(v) for k, v in inputs.items()}
    x = inp["x"]
    p = inp["p"]
    wpk = pack_weights(inp)
    vecs = pack_vecs(inp)
    xT = np.ascontiguousarray(x.transpose(0, 2, 1))
    pT = np.ascontiguousarray(p.transpose(0, 1, 3, 2))
    nc = build_nc()
    in_maps = []
    for c in range(NCORE):
        in_maps.append({
            "xT": xT[c * SEQ_PER_CORE:(c + 1) * SEQ_PER_CORE],
            "pT": np.ascontiguousarray(pT[:, c * SEQ_PER_CORE:(c + 1) * SEQ_PER_CORE]),
            "wpk": wpk,
            "vecs": vecs,
        })
    res = run_bass_kernel_spmd(nc, in_maps, core_ids=list(range(NCORE)))
    yT = np.concatenate([r["yT"] for r in res.results], axis=0)
    return np.ascontiguousarray(yT.transpose(0, 2, 1)).astype(np.float32)
```

```python
import math
import numpy as np
from contextlib import ExitStack
import concourse.bass as bass
import concourse.mybir as mybir
from concourse.bass_utils import run_bass_kernel_spmd

F32 = mybir.dt.float32
BF16 = mybir.dt.bfloat16
AF = mybir.ActivationFunctionType
ALU = mybir.AluOpType

D = 1024
S_LEN = 2048
L = 4
NCORE = 8
SEQ_PER_CORE = 4
DFF = 2816
NFC = 22
PLE = 256
EPS = 1e-6
NT = 4
TK = 16
GC1 = math.sqrt(2.0 / math.pi)
GC2 = GC1 * 0.044715

O_WIN = 0
O_WAB = O_WIN + 24 * 1024
O_LRUW = O_WAB + 128
O_WOUT = O_LRUW + 2048
O_WG = O_WOUT + 8192
O_WU = O_WG + 22528
O_WD = O_WU + 22528
O_PWG = O_WD + 22528
O_PWP = O_PWG + 8192
WTOT = O_PWP + 2048

V_G1, V_G2, V_GP, V_BGP = 0, 8, 16, 24
V_DNCW = 32
V_LCW = 80
V_LCB = 96
V_LBA = 100
V_LBX = 108
V_LLAM = 116
V_LNG = 124
V_DNG = 128
V_FCW = 129
V_FCB = 195
V_ALOG = 217
V_DTB = 225
NV = 240


class Sched:
    NDS = 24

    def __init__(self, nc, es):
        self.nc = nc
        self.E = {}
        for name, h in (("pe", nc.tensor), ("act", nc.scalar), ("dve", nc.vector), ("pool", nc.gpsimd), ("sp", nc.sync)):
            sem = es.enter_context(nc.semaphore("s_" + name))
            self.E[name] = dict(h=h, sem=sem, n=0, seen={}, seend={})
        self.dsem = [es.enter_context(nc.semaphore(f"sd{i}")) for i in range(self.NDS)]
        self.dcnt = [0] * self.NDS
        self.dnext = 0
        self.lastw = {}
        self.readers = {}
        self.ninst = 0

    def _wait(self, eng, tok, same_ok):
        X = self.E[eng]
        if tok[0] == "e":
            _, p, c = tok
            if p == eng and (eng == "pe" or not same_ok):
                return
            if X["seen"].get(p, 0) >= c:
                return
            X["h"].wait_ge(self.E[p]["sem"], c)
            X["seen"][p] = c
        else:
            _, i, v = tok
            if X["seend"].get(i, 0) >= v:
                return
            X["h"].wait_ge(self.dsem[i], v)
            X["seend"][i] = v

    def _deps(self, eng, r, w):
        for k in r:
            t = self.lastw.get(k)
            if t is not None:
                self._wait(eng, t, True)
            if k[0] == "ps":
                for t in self.readers.get(k, {}).values():
                    self._wait(eng, t, False)
        for k in w:
            t = self.lastw.get(k)
            if t is not None:
                self._wait(eng, t, True)
            for t in self.readers.get(k, {}).values():
                self._wait(eng, t, False)

    def _record(self, tok, r, w):
        for k in r:
            self.readers.setdefault(k, {})[tok[1]] = tok
        for k in w:
            self.lastw[k] = tok
            self.readers[k] = {}

    def op(self, eng, emit, r=(), w=()):
        X = self.E[eng]
        self._deps(eng, r, w)
        inst = emit(X["h"])
        X["n"] += 1
        inst.then_inc(X["sem"], 1)
        self.ninst += 1
        self._record(("e", eng, X["n"]), r, w)

    def dma(self, out, in_, r=(), w=()):
        self._deps("sp", r, w)
        i = self.dnext
        self.dnext = (i + 1) % self.NDS
        if self.dcnt[i] > 0:
            self._wait("sp", ("d", i, 16 * self.dcnt[i]), True)
        inst = self.nc.sync.dma_start(out=out, in_=in_)
        self.dcnt[i] += 1
        inst.then_inc(self.dsem[i], 16)
        self.ninst += 1
        self._record(("d", i, 16 * self.dcnt[i]), r, w)

    def barrier(self):
        comp = ("pe", "act", "dve", "pool")
        for e in comp:
            for p in comp:
                if p != e and self.E[p]["n"] > 0:
                    self._wait(e, ("e", p, self.E[p]["n"]), True)
            for i in range(self.NDS):
                if self.dcnt[i] > 0:
                    self._wait(e, ("d", i, 16 * self.dcnt[i]), True)
        for p in comp:
            if self.E[p]["n"] > 0:
                self._wait("sp", ("e", p, self.E[p]["n"]), True)
        for i in range(self.NDS):
            if self.dcnt[i] > 0:
                self._wait("sp", ("d", i, 16 * self.dcnt[i]), True)
        self.lastw = {}
        self.readers = {}


def rev_ap(ap2d):
    a = ap2d.ap
    n = a[-1][1]
    st = a[-1][0]
    return bass.AP(ap2d.tensor, ap2d.offset + (n - 1) * st, [list(x) for x in a[:-1]] + [[-st, n]])


DBG = dict(dn_stop=99, heads=4, gdn_m=8)


def interleave(gens):
    gens = list(gens)
    while gens:
        for g in list(gens):
            try:
                next(g)
            except StopIteration:
                gens.remove(g)


def build_nc(nseq=SEQ_PER_CORE, nlayers=L, dump_r=False, do_prepass=True, phases=("lru", "dn", "ffn", "ple"), ndbg=0):
    nc = bass.Bass("TRN2", target_bir_lowering=False)
    xT = nc.dram_tensor("xT", [SEQ_PER_CORE, D, S_LEN], F32, kind="ExternalInput").ap()
    pT = nc.dram_tensor("pT", [L, SEQ_PER_CORE, PLE, S_LEN], F32, kind="ExternalInput").ap()
    wpk = nc.dram_tensor("wpk", [L, 128, WTOT], F32, kind="ExternalInput").ap()
    vecs = nc.dram_tensor("vecs", [128, L * NV + 8], F32, kind="ExternalInput").ap()
    yT = nc.dram_tensor("yT", [SEQ_PER_CORE, D, S_LEN], F32, kind="ExternalOutput").ap()
    wsc = nc.dram_tensor("wsc", [L, 128, WTOT], BF16, kind="Internal").ap()
    dbg_out = None
    if ndbg:
        dbg_out = nc.dram_tensor("dbg", [ndbg, 128, S_LEN], F32, kind="ExternalOutput").ap()

    es = ExitStack()
    with es:
        def sb(name, shape, dt):
            return es.enter_context(nc.sbuf_tensor(name, shape, dt))

        S = Sched(nc, es)
        R = sb("R", [128, 8, S_LEN], F32)
        HN = sb("HN", [128, 8, S_LEN], BF16)
        ARENA = sb("ARENA", [128, 18560], BF16)
        TT = sb("TT", [128, 8, 512], F32)
        TB = sb("TB", [128, 6, 512], BF16)
        SV = sb("SV", [128, 20, 512], BF16)
        SF = sb("SF", [128, 512], F32)
        WB = sb("WB", [128, 5, 1024], BF16)
        VEC = sb("VEC", [128, L * NV + 8], F32)
        CST = sb("CST", [128, 13, 128], F32)
        CSB = sb("CSB", [128, 7, 128], BF16)
        MISC = sb("MISC", [128, 12, 128], F32)
        LV = sb("LV", [128, 64], F32)
        SST = sb("SST", [128, 2, 128], F32)
        SBF = sb("SBF", [128, 2, 128], BF16)
        CAR = sb("CAR", [128, 4], F32)
        PS = [es.enter_context(nc.psum_tensor(f"ps{i}", [128, 512], F32)) for i in range(8)]
        cnt = dict(ps=0, tt=0, tb=0, wb=0, sv=0, ev=0)

        def psum():
            i = cnt["ps"] % 8
            cnt["ps"] += 1
            return PS[i], ("ps", i)

        def tt():
            i = cnt["tt"] % 8
            cnt["tt"] += 1
            return TT[:, i, :], ("tt", i)

        def tb():
            i = cnt["tb"] % 6
            cnt["tb"] += 1
            return TB[:, i, :], ("tb", i)

        def wslot():
            i = cnt["wb"] % 5
            cnt["wb"] += 1
            return WB[:, i, :], ("wb", i)

        def sv():
            i = 10 + cnt["sv"] % 10
            cnt["sv"] += 1
            return SV[:, i, :], ("sv", i)

        def act(out, in_, func, r, w, bias=0.0, scale=1.0):
            S.op("act", lambda h: h.activation(out=out, in_=in_, func=func, bias=bias, scale=scale), r, w)

        def tsc(eng, out, in0, s1, s2, op0, op1, r, w):
            S.op(eng, lambda h: h.tensor_scalar(out=out, in0=in0, scalar1=s1, scalar2=s2, op0=op0, op1=op1), r, w)

        def ts1(eng, out, in_, s, op, r, w):
            S.op(eng, lambda h: h.tensor_single_scalar(out=out, in_=in_, scalar=s, op=op), r, w)

        def stt(eng, out, in0, s, in1, op0, op1, r, w):
            eng = "dve"
            S.op(eng, lambda h: h.scalar_tensor_tensor(out=out, in0=in0, scalar=s, in1=in1, op0=op0, op1=op1), r, w)

        def tten(eng, out, in0, in1, op, r, w):
            S.op(eng, lambda h: h.tensor_tensor(out=out, in0=in0, in1=in1, op=op), r, w)

        def cpy(eng, out, in_, r, w):
            if eng == "act":
                act(out, in_, AF.Identity, r, w)
            else:
                S.op(eng, lambda h: h.tensor_copy(out=out, in_=in_), r, w)

        def mm(out, lhsT, rhs, start, stop, r, w):
            S.op("pe", lambda h: h.matmul(out, lhsT=lhsT, rhs=rhs, start=start, stop=stop), r, w)

        def mset(eng, ap, val, w):
            S.op(eng, lambda h: h.memset(ap, val), (), w)

        INCF, INCB, OFFD, ONESF, IDF = (CST[:, i, :] for i in range(5))
        NEG4 = CST[:, 5:9, :]
        M16, ML0, ML1, ML2, IDB, ONESB = (CSB[:, i, :] for i in range(6))
        MLV = [ML0, ML1, ML2]
        KC = ("cst",)

        def b4(ap):
            return ap.unsqueeze(1).to_broadcast([128, 4, 128])

        def v4(ap):
            return ap.rearrange("p (e i) -> p e i", e=4)

        def aff(ap, pattern, op, fill, base, cm):
            S.op("pool", lambda h: h.affine_select(out=ap, in_=ap, pattern=pattern, compare_op=op, fill=fill,
                                                   base=base, channel_multiplier=cm), [KC], [KC])

        mset("pool", CST[:, 0:5, :], 1.0, [KC])
        aff(INCF, [[1, 128]], ALU.is_ge, 0.0, 0, -1)
        aff(INCB, [[-1, 128]], ALU.is_ge, 0.0, 0, 1)
        aff(OFFD, [[1, 128]], ALU.not_equal, 0.0, 0, -1)
        aff(IDF, [[1, 128]], ALU.is_equal, 0.0, 0, -1)
        for e in range(4):
            tsc("pool", NEG4[:, e, :], INCF if e < 2 else INCB, -1.0, 1e30, ALU.add, ALU.mult, [KC], [KC])
        mset("pool", CST[:, 9:13, :], 1.0, [KC])
        for bi, b in enumerate((16, 32, 64)):
            nb = 128 // b
            v = CST[:, 9 + bi, :].rearrange("p (k c) -> p k c", c=b)
            aff(v, [[-b, nb], [0, b]], ALU.is_ge, 0.0, 0, 1)
            aff(v, [[b, nb], [0, b]], ALU.is_gt, 0.0, b, -1)
        cpy("pool", M16, CST[:, 9, :], [KC], [KC])
        for k in range(3):
            tten("pool", MLV[k], CST[:, 10 + k, :], CST[:, 9 + k, :], ALU.subtract, [KC], [KC])
        cpy("pool", IDB, IDF, [KC], [KC])
        cpy("pool", ONESB, ONESF, [KC], [KC])
        NM16 = CSB[:, 6, :]
        ts1("pool", NM16, CST[:, 9, :], -1.0, ALU.mult, [KC], [KC])
        OFFB = sb("OFFB", [128, 128], BF16)
        cpy("pool", OFFB[:], OFFD, [KC], [KC])
        S.dma(VEC[:], vecs[:, :], [], [("vec",)])
        S.barrier()

        if do_prepass:
            CH = 2048
            st_f = ARENA[:, 0:8192].bitcast(F32).rearrange("p (b c) -> p b c", b=2)
            st_b = ARENA[:, 8192:12288].rearrange("p (b c) -> p b c", b=2)
            n = 0
            for l in range(nlayers):
                c0 = 0
                while c0 < WTOT:
                    cw = min(CH, WTOT - c0)
                    fb = n % 2
                    src = st_f[:, fb, 0:cw]
                    dst = st_b[:, fb, 0:cw]
                    S.dma(src, wpk[l, :, c0:c0 + cw], [], [("pf", fb)])
                    cpy(("dve", "pool", "act")[n % 3], dst, src, [("pf", fb)], [("pb", fb)])
                    S.dma(wsc[l, :, c0:c0 + cw], dst, [("pb", fb)], [("wsc", l)])
                    c0 += cw
                    n += 1
            S.barrier()

        def wload(l, off, width):
            slot, k = wslot()
            S.dma(slot[:, 0:width], wsc[l, :, off:off + width], [], [k])
            return slot, k

        def tsl_(t):
            return slice(t * 512, (t + 1) * 512)

        def rms_stats(t, nch, src, srck, scale, eps):
            ps, kp = psum()
            for ch in range(nch):
                sq, ks = tb()
                act(sq, src(ch, t), AF.Square, [srck(ch, t)], [ks])
                mm(ps[:], ONESB, sq, ch == 0, ch == nch - 1, [ks], [kp])
            ln, kl = tt()
            act(ln, ps[:], AF.Ln, [kp], [kl], bias=eps, scale=scale)
            rs, kr = tt()
            act(rs, ln, AF.Exp, [kl], [kr], scale=-0.5)
            return rs, kr

        def rms_to_hn(gbase):
            for t in range(NT):
                rs, kr = rms_stats(t, 8, lambda ch, t: R[:, ch, tsl_(t)], lambda ch, t: ("R", ch, t), 1.0 / D, EPS)
                for ch in range(8):
                    stt("dve" if ch % 2 == 0 else "pool", HN[:, ch, tsl_(t)], R[:, ch, tsl_(t)],
                        VEC[:, gbase + ch:gbase + ch + 1], rs, ALU.mult, ALU.mult, [("R", ch, t), kr, ("vec",)], [("HN", ch, t)])

        def proj_tile(ps, kp, slab, ks, t):
            for kc in range(8):
                mm(ps[:], slab[:, kc * 128:(kc + 1) * 128], HN[:, kc, tsl_(t)], kc == 0, kc == 7, [ks, ("HN", kc, t)], [kp])

        def radd(m, t, ps, kp):
            tten("dve", R[:, m, tsl_(t)], ps[:], R[:, m, tsl_(t)], ALU.add, [kp, ("R", m, t)], [("R", m, t)])

        def dbg_store(idx, ap2d, keys):
            if dbg_out is not None and idx < ndbg:
                S.dma(dbg_out[idx, :, 0:ap2d.shape[-1]], ap2d, keys, [("dbg", idx)])

        A_QT = ARENA[:, 0:2048]
        A_KT = ARENA[:, 2048:4096]
        A_VT = ARENA[:, 4096:6144]
        A_KTOK = ARENA[:, 6144:8192]
        A_VTOK = ARENA[:, 8192:10240]
        A_OSUM = ARENA[:, 10240:14336].bitcast(F32)
        A_PC = ARENA[:, 14336:18440].bitcast(F32)
        A_XCB = ARENA[:, 0:2048]
        A_GL = ARENA[:, 2048:6144].bitcast(F32)
        A_HS = ARENA[:, 6144:10240].bitcast(F32)
        A_XC = A_OSUM
        YL = SV[:, 0:16, :].rearrange("p (c a) n -> p c (a n)", c=4)

        def pc_pads():
            mset("pool", A_PC[:, 0:2], 0.0, [("pc", "pad")])
            mset("pool", A_PC[:, 2050:2052], 0.0, [("pc", "pad")])

        def conv4(dst, dstk, wbase, bias_ap, vk):
            rk = [("pc", t) for t in range(NT)] + [("pc", "pad"), vk]
            wk = [(dstk, t) for t in range(NT)]
            if bias_ap is None:
                ts1("pool", dst, A_PC[:, 0:2048], VEC[:, wbase:wbase + 1], ALU.mult, rk, wk)
            else:
                tsc("pool", dst, A_PC[:, 0:2048], VEC[:, wbase:wbase + 1], bias_ap, ALU.mult, ALU.add, rk, wk)
            for j in range(1, 4):
                stt("pool" if j % 2 else "dve", dst, A_PC[:, j:j + 2048], VEC[:, wbase + j:wbase + j + 1], dst,
                    ALU.mult, ALU.add, rk + wk, wk)

        def wout_pass(l, kcs, src, srck):
            slabs = [wload(l, O_WOUT + kc * 1024, 1024) for kc in kcs]
            for m in range(8):
                for t in range(NT):
                    ps, kp = psum()
                    for i, (sl, ks) in enumerate(slabs):
                        mm(ps[:], sl[:, m * 128:(m + 1) * 128], src(i, t), i == 0, i == len(slabs) - 1, [ks, srck(i, t)], [kp])
                    radd(m, t, ps, kp)

        def layer_prep(l):
            vb = l * NV
            kl = ("lv",)
            vk = ("vec",)
            act(LV[:, 48:56], VEC[:, vb + V_ALOG:vb + V_ALOG + 8], AF.Exp, [vk], [kl])
            ts1("dve", LV[:, 0:8], LV[:, 48:56], -1.0, ALU.mult, [kl], [kl])
            ts1("dve", LV[:, 8:16], VEC[:, vb + V_LBA:vb + V_LBA + 8], 0.5, ALU.mult, [vk, kl], [kl])
            ts1("dve", LV[:, 16:24], VEC[:, vb + V_LBX:vb + V_LBX + 8], 0.5, ALU.mult, [vk, kl], [kl])
            act(LV[:, 48:56], VEC[:, vb + V_LLAM:vb + V_LLAM + 8], AF.Exp, [vk, kl], [kl], scale=-1.0)
            act(LV[:, 56:64], LV[:, 48:56], AF.Ln, [kl], [kl], bias=1.0)
            ts1("dve", LV[:, 24:32], LV[:, 56:64], -8.0, ALU.mult, [kl], [kl])
            ts1("dve", LV[:, 32:40], LV[:, 56:64], -4.0, ALU.mult, [kl], [kl])
            ts1("dve", LV[:, 40:48], VEC[:, vb + V_BGP:vb + V_BGP + 8], 0.5, ALU.mult, [vk, kl], [kl])

        def m3(i):
            return MISC[:, i, :].rearrange("p (t h) -> p t h", h=8)

        AB = MISC[:, 0:2, :].rearrange("p a b -> p (a b)").rearrange("p (t c) -> p t c", c=16)
        G3, NLNB, CCOL, CLAST, BIASD, NEGEC, BDEC, ECL, TM1, TM2 = (m3(i) for i in range(2, 12))
        KM = ("misc",)

        def ab_phase(l):
            vb = l * NV
            slab, ks = wload(l, O_WAB, 128)
            ps, kp = psum()
            for n in range(TK):
                for kc in range(8):
                    mm(ps[:, n * 16:(n + 1) * 16], HN[:, kc, n * 128:(n + 1) * 128], slab[:, kc * 16:(kc + 1) * 16],
                       kc == 0, kc == 7, [ks, ("HN", kc, n // 4)], [kp])
            cpy("dve", MISC[:, 0:2, :].rearrange("p a b -> p (a b)"), ps[:, 0:256], [kp], [KM])
            r = [KM, ("lv",), ("vec",)]
            act(TM1, AB[:, :, 0:8], AF.Exp, r, [KM], scale=-1.0)
            act(NLNB, TM1, AF.Ln, r, [KM], bias=1.0)
            tten("dve", TM2, AB[:, :, 8:16], VEC[:, vb + V_DTB:vb + V_DTB + 8].unsqueeze(1).to_broadcast([128, 16, 8]), ALU.add, r, [KM])
            act(TM2, TM2, AF.Exp, r, [KM])
            act(TM2, TM2, AF.Ln, r, [KM], bias=1.0)
            tten("dve", G3, TM2, LV[:, 0:8].unsqueeze(1).to_broadcast([128, 16, 8]), ALU.mult, r, [KM])
            ps2, kp2 = psum()
            mm(ps2[:, 0:64].rearrange("p (t h) -> p t h", h=4), INCF, G3[:, :, 0:4], True, True, [KM, KC], [kp2])
            mm(ps2[:, 64:128].rearrange("p (t h) -> p t h", h=4), INCB, G3[:, :, 4:8], True, True, [KM, KC], [kp2])
            mm(ps2[:, 128:256], ONESF, MISC[:, 2, :], True, True, [KM, KC], [kp2])
            cpy("dve", CCOL[:, :, 0:4], ps2[:, 0:64].rearrange("p (t h) -> p t h", h=4), [kp2], [KM])
            cpy("dve", CCOL[:, :, 4:8], ps2[:, 64:128].rearrange("p (t h) -> p t h", h=4), [kp2, KM], [KM])
            cpy("dve", MISC[:, 5, :], ps2[:, 128:256], [kp2, KM], [KM])
            tten("dve", TM1, NLNB, CCOL, ALU.add, r, [KM])
            ts1("dve", BIASD, TM1, -1.0, ALU.mult, r, [KM])
            act(TM2, CCOL, AF.Exp, r, [KM])
            ts1("dve", NEGEC, TM2, -1.0, ALU.mult, r, [KM])
            tten("dve", TM1, BIASD, CLAST, ALU.add, r, [KM])
            act(BDEC, TM1, AF.Exp, r, [KM])
            act(ECL, CLAST, AF.Exp, r, [KM])

        def lru_phase(l):
            vb = l * NV
            vk = ("vec",)
            for c in range(4):
                pc_pads()
                lw, klw = wload(l, O_LRUW + c * 512, 512)
                slab, ks = wload(l, O_WIN + (16 + c) * 1024, 1024)
                for t in range(NT):
                    ps, kp = psum()
                    proj_tile(ps, kp, slab, ks, t)
                    cpy("act", A_PC[:, 2 + t * 512:2 + (t + 1) * 512], ps[:], [kp], [("pc", t)])
                conv4(A_XC, "xc", vb + V_LCW + c * 4, VEC[:, vb + V_LCB + c:vb + V_LCB + c + 1], vk)
                cpy("act", A_XCB, A_XC, [("xc", t) for t in range(NT)], [("xcb", t) for t in range(NT)])
                slab, ks = wload(l, O_WIN + (20 + c) * 1024, 1024)
                for t in range(NT):
                    ps, kp = psum()
                    proj_tile(ps, kp, slab, ks, t)
                    x2, k2 = tt()
                    act(x2, ps[:], AF.Square, [kp], [k2])
                    tsc("dve", x2, x2, GC2, GC1, ALU.mult, ALU.add, [k2], [k2])
                    tten("dve", x2, x2, ps[:], ALU.mult, [k2, kp], [k2])
                    act(x2, x2, AF.Tanh, [k2], [k2])
                    stt("dve", A_GL[:, tsl_(t)], x2, 1.0, ps[:], ALU.add, ALU.mult, [k2, kp], [("gl", t)])
                for d in range(2):
                    order = list(range(NT)) if d == 0 else list(range(NT - 1, -1, -1))
                    lvi = d * 4 + c
                    for ti, t in enumerate(order):
                        tsl = tsl_(t)
                        psr, kpr = psum()
                        mm(psr[:], lw[:, (d * 2 + 0) * 128:(d * 2 + 1) * 128], A_XCB[:, tsl], True, True, [klw, ("xcb", t)], [kpr])
                        psi, kpi = psum()
                        mm(psi[:], lw[:, (d * 2 + 1) * 128:(d * 2 + 2) * 128], A_XCB[:, tsl], True, True, [klw, ("xcb", t)], [kpi])
                        thr, kr_ = tt()
                        act(thr, psr[:], AF.Tanh, [kpr, ("lv",)], [kr_], scale=0.5, bias=LV[:, 8 + lvi:9 + lvi])
                        thi, ki_ = tt()
                        act(thi, psi[:], AF.Tanh, [kpi, ("lv",)], [ki_], scale=0.5, bias=LV[:, 16 + lvi:17 + lvi])
                        a, ka = tt()
                        act(a, thr, AF.Exp, [kr_, ("lv",)], [ka], scale=LV[:, 32 + lvi:33 + lvi], bias=LV[:, 32 + lvi:33 + lvi])
                        a2, ka2 = tt()
                        act(a2, thr, AF.Exp, [kr_, ("lv",)], [ka2], scale=LV[:, 24 + lvi:25 + lvi], bias=LV[:, 24 + lvi:25 + lvi])
                        tl, ktl = tt()
                        act(tl, thr, AF.Tanh, [kr_, ("lv",)], [ktl], scale=LV[:, 32 + lvi:33 + lvi], bias=LV[:, 32 + lvi:33 + lvi])
                        stt("dve", a2, a2, 1.0, tl, ALU.add, ALU.mult, [ka2, ktl], [ka2])
                        act(a2, a2, AF.Ln, [ka2], [ka2], scale=-1.0)
                        act(a2, a2, AF.Exp, [ka2], [ka2], scale=0.5)
                        stt("pool", thi, thi, 1.0, A_XC[:, tsl], ALU.add, ALU.mult, [ki_, ("xc", t)], [ki_])
                        stt("pool", thi, thi, 0.5, a2, ALU.mult, ALU.mult, [ki_, ka2], [ki_])
                        if d == 0:
                            init = 0.0 if ti == 0 else A_HS[:, t * 512 - 1:t * 512]
                            rr = [ka, ki_] + ([("hs", t - 1)] if ti else [])
                            S.op("dve", lambda h, o=A_HS[:, tsl], a_=a, b_=thi, i_=init: h.tensor_tensor_scan(
                                out=o, data0=a_, data1=b_, initial=i_, op0=ALU.mult, op1=ALU.add), rr, [("hs", t)])
                        else:
                            init = 0.0 if ti == 0 else CAR[:, 0:1]
                            S.op("dve", lambda h, o=rev_ap(tl), a_=rev_ap(a), b_=rev_ap(thi), i_=init: h.tensor_tensor_scan(
                                out=o, data0=a_, data1=b_, initial=i_, op0=ALU.mult, op1=ALU.add), [ka, ki_, ktl, ("car",)], [ktl])
                            cpy("dve", CAR[:, 0:1], tl[:, 0:1], [ktl, ("car",)], [("car",)])
                            tten("pool", A_HS[:, tsl], A_HS[:, tsl], tl, ALU.add, [ktl, ("hs", t)], [("hs", t)])
                tten("pool", YL[:, c, :], A_GL, A_HS, ALU.mult, [("gl", t) for t in range(NT)] + [("hs", t) for t in range(NT)],
                     [("yl", c, t) for t in range(NT)])
            for t in range(NT):
                rs, kr = rms_stats(t, 4, lambda c, t: YL[:, c, tsl_(t)], lambda c, t: ("yl", c, t), 1.0 / 512, 4 * EPS)
                for c in range(4):
                    stt("dve", YL[:, c, tsl_(t)], YL[:, c, tsl_(t)], VEC[:, vb + V_LNG + c:vb + V_LNG + c + 1], rs,
                        ALU.mult, ALU.mult, [("yl", c, t), kr, vk], [("yl", c, t)])
            wout_pass(l, [4, 5, 6, 7], lambda i, t: YL[:, i, tsl_(t)], lambda i, t: ("yl", i, t))

        def rr(gens):
            gens = list(gens)
            while gens:
                for g in list(gens):
                    try:
                        next(g)
                    except StopIteration:
                        gens.remove(g)
                yield

        def gdn(l, h):
            KT3 = A_KT.rearrange("p (n i) -> p n i", i=128)
            QT3 = A_QT.rearrange("p (n i) -> p n i", i=128)
            KTOK3 = A_KTOK.rearrange("p (n i) -> p n i", i=128)
            VTOK3 = A_VTOK.rearrange("p (n i) -> p n i", i=128)
            OS3 = A_OSUM.rearrange("p (n i) -> p n i", i=128)
            mset("pool", SST[:], 0.0, [("sst", 0), ("sst", 1)])
            mset("pool", SBF[:], 0.0, [("sbf", 0), ("sbf", 1)])
            TTB = TT[:].rearrange("p a b -> p (a b)").bitcast(BF16).rearrange("p (s n) -> p s n", n=512)
            slots = [(SV[:, i, :], ("gsv", i)) for i in range(20)] + [(TTB[:, i, :], ("gtt", i)) for i in range(16)]
            NS, NP, PSZ = 4, 3, 7
            outsets = [slots[3 * i:3 * i + 3] for i in range(NS)]
            pools = [slots[3 * NS + PSZ * i:3 * NS + PSZ * (i + 1)] for i in range(NP)]
            MGs = [(A_PC[:, i * 512:(i + 1) * 512], ("mg", i)) for i in range(NP)]

            def elem(m, e):
                if e < 2:
                    return 2 * m + e, 0, h
                return 15 - 2 * m - (e - 2), 1, 4 + h

            def mm4(lh, klh, rh, krh):
                ps, kp = psum()
                p4 = v4(ps[:])
                l4, r4 = v4(lh), v4(rh)
                for e in range(4):
                    mm(p4[:, e, :], l4[:, e, :], r4[:, e, :], True, True, [klh, krh], [kp])
                return ps, kp

            def evac(ps, kp, dst, kd):
                cnt["ev"] += 1
                cpy("act" if cnt["ev"] % 2 else "dve", dst, ps[:], [kp], [kd])

            def transp4(src, ksrc, dst, kdst):
                pst, kpst = psum()
                pstb = pst[:].bitcast(BF16)
                for e in range(4):
                    S.op("pe", lambda hh, o=pstb[:, e * 128:(e + 1) * 128], i_=v4(src)[:, e, :]: hh.transpose(out=o, in_=i_, identity=IDB),
                         [ksrc, KC], [kpst])
                cpy("act", dst, pstb[:, 0:512], [kpst], [kdst])

            def prep(m, pipe):
                free = list(pools[pipe])

                def alloc():
                    return free.pop(0)

                def rel(*xs):
                    free.extend(xs)

                els = [elem(m, e) for e in range(4)]
                (Z, kZ), (ATT, kATT), (QD, kQD) = outsets[m % NS]
                MG, kmg = MGs[pipe]
                for e, (tile, d, dh) in enumerate(els):
                    ts1("pool", v4(MG)[:, e, :], INCF if d == 0 else INCB, G3[:, tile, dh:dh + 1], ALU.mult, [KM, KC], [kmg])
                c1, kc1 = psum()
                mm(c1[:], ONESF, MG, True, False, [kmg, KC], [kc1])
                mm(v4(c1[:]), IDF, NEG4, False, True, [KC], [kc1])
                DT = alloc()
                for e, (tile, d, dh) in enumerate(els):
                    act(v4(DT[0])[:, e, :], v4(c1[:])[:, e, :], AF.Exp, [kc1, KM], [DT[1]], bias=BIASD[:, tile, dh:dh + 1])
                yield
                c2, kc2 = psum()
                mm(c2[:], ONESF, MG, True, True, [kmg, KC], [kc2])
                ER = alloc()
                act(ER[0], c2[:], AF.Exp, [kc2], [ER[1]])
                yield
                kk, kkk = psum()
                for e, (tile, d, dh) in enumerate(els):
                    mm(v4(kk[:])[:, e, :], KT3[:, tile, :], KT3[:, tile, :], True, True, [("kt",)], [kkk])
                BP = alloc()
                tten("dve", BP[0], kk[:], DT[0], ALU.mult, [kkk, DT[1]], [BP[1]])
                yield
                qk, kqk = psum()
                for e, (tile, d, dh) in enumerate(els):
                    mm(v4(qk[:])[:, e, :], KT3[:, tile, :], QT3[:, tile, :], True, True, [("kt",), ("qt",)], [kqk])
                tten("dve", ATT, qk[:], DT[0], ALU.mult, [kqk, DT[1]], [kATT])
                rel(DT)
                yield
                tten("pool", v4(BP[0]), v4(BP[0]), b4(OFFB[:]), ALU.mult, [BP[1], KC], [BP[1]])
                for e, (tile, d, dh) in enumerate(els):
                    tten("pool", v4(QD)[:, e, :], QT3[:, tile, :], v4(ER[0])[:, e, :], ALU.mult, [("qt",), ER[1]], [kQD])
                rel(ER)
                yield
                AT = alloc()
                transp4(BP[0], BP[1], AT[0], AT[1])
                X = alloc()
                tten("pool", v4(X[0]), v4(BP[0]), b4(NM16), ALU.mult, [BP[1], KC], [X[1]])
                yield
                XT = alloc()
                tten("pool", v4(XT[0]), v4(AT[0]), b4(NM16), ALU.mult, [AT[1], KC], [XT[1]])
                rel(BP)
                yield

                def prod(lh, rh):
                    o = alloc()
                    evac(*mm4(lh[0], lh[1], rh[0], rh[1]), o[0], o[1])
                    return o

                def plus_i(x):
                    g = alloc()
                    tten("pool", v4(g[0]), v4(x[0]), b4(IDB), ALU.add, [x[1], KC], [g[1]])
                    return g

                X2 = prod(XT, X)
                yield
                X2T = prod(X, XT)
                G1T = plus_i(XT)
                rel(X, XT)
                yield
                G2 = plus_i(X2)
                Y1T = prod(G2, G1T)
                rel(G1T, G2)
                yield
                X4 = prod(X2T, X2)
                yield
                X4T = prod(X2, X2T)
                rel(X2, X2T)
                yield
                G4 = plus_i(X4)
                Y2T = prod(G4, Y1T)
                rel(Y1T, G4)
                yield
                X8 = prod(X4T, X4)
                rel(X4, X4T)
                yield
                G8 = plus_i(X8)
                rel(X8)
                Zc = prod(Y2T, G8)
                yield
                ZTc = prod(G8, Y2T)
                rel(Y2T, G8)
                yield
                for k in range(3):
                    OT = alloc()
                    tten("pool", v4(OT[0]), v4(AT[0]), b4(MLV[k]), ALU.mult, [AT[1], KC], [OT[1]])
                    w1, kw1 = mm4(OT[0], OT[1], Zc[0], Zc[1])
                    rel(OT)
                    IW = alloc()
                    tten("dve", v4(IW[0]), b4(IDB), v4(w1[:]), ALU.subtract, [kw1, KC], [IW[1]])
                    yield
                    zn, kzn = mm4(ZTc[0], ZTc[1], IW[0], IW[1])
                    rel(IW)
                    if k < 2:
                        Zn = alloc()
                        evac(zn, kzn, Zn[0], Zn[1])
                        rel(Zc, ZTc)
                        yield
                        ZTn = alloc()
                        transp4(Zn[0], Zn[1], ZTn[0], ZTn[1])
                        Zc, ZTc = Zn, ZTn
                    else:
                        evac(zn, kzn, Z, kZ)
                        rel(Zc, ZTc, AT)
                    yield

            def steps(m):
                (Z, kZ), (ATT, kATT), (QD, kQD) = outsets[m % NS]
                Z4, ATT4, QD4 = v4(Z), v4(ATT), v4(QD)
                for sidx in (2 * m, 2 * m + 1):
                    def one(d):
                        tile = sidx if d == 0 else 15 - sidx
                        e = (sidx % 2) + 2 * d
                        dh = h + 4 * d
                        vp, kvp = TB[:, 3 * d + 0, 0:128], ("tb", 3 * d + 0)
                        vraw, kvr = TB[:, 3 * d + 1, 0:128], ("tb", 3 * d + 1)
                        vdec, kvd = TB[:, 3 * d + 2, 0:128], ("tb", 3 * d + 2)
                        ksp, k1 = psum()
                        mm(ksp[:, 0:128], KT3[:, tile, :], SBF[:, d, :], True, True, [("kt",), ("sbf", d)], [k1])
                        stt("dve", vp, ksp[:, 0:128], NEGEC[:, tile, dh:dh + 1], VTOK3[:, tile, :], ALU.mult, ALU.add,
                            [k1, KM, ("vtok",)], [kvp])
                        yield
                        vrp, k2 = psum()
                        mm(vrp[:, 0:128], Z4[:, e, :], vp, True, True, [kZ, kvp], [k2])
                        cpy("act", vraw, vrp[:, 0:128], [k2], [kvr])
                        ts1("dve", vdec, vrp[:, 0:128], BDEC[:, tile, dh:dh + 1], ALU.mult, [k2, KM], [kvd])
                        yield
                        op_, k3 = psum()
                        mm(op_[:, 0:128], SBF[:, d, :], QD4[:, e, :], True, False, [("sbf", d), kQD], [k3])
                        mm(op_[:, 0:128], vraw, ATT4[:, e, :], False, True, [kvr, kATT], [k3])
                        snp, k4 = psum()
                        mm(snp[:, 0:128], KTOK3[:, tile, :], vdec, True, True, [("ktok",), kvd], [k4])
                        stt("dve", SST[:, d, :], SST[:, d, :], ECL[:, tile, dh:dh + 1], snp[:, 0:128], ALU.mult, ALU.add,
                            [k4, KM, ("sst", d)], [("sst", d)])
                        cpy("pool", SBF[:, d, :], SST[:, d, :], [("sst", d)], [("sbf", d)])
                        first = (d == 0 and tile < 8) or (d == 1 and tile >= 8)
                        if first:
                            cpy("act", OS3[:, tile, :], op_[:, 0:128], [k3], [("os", tile)])
                        else:
                            tten("dve", OS3[:, tile, :], op_[:, 0:128], OS3[:, tile, :], ALU.add, [k3, ("os", tile)], [("os", tile)])
                        yield
                    yield from rr([one(0), one(1)])

            active = {}
            prep_done = set()
            steps_done = -1
            nextp = 0
            nexts = 0
            freepipes = list(range(NP))
            cur_steps = None
            nbatch = DBG["gdn_m"]
            while nexts < nbatch or active or cur_steps is not None:
                while nextp < 8 and freepipes and (nextp - NS) <= steps_done and nextp < nbatch + 3:
                    pipe = freepipes.pop()
                    active[nextp] = (prep(nextp, pipe), pipe)
                    nextp += 1
                if cur_steps is None and nexts < nbatch and nexts in prep_done:
                    cur_steps = steps(nexts)
                for m_ in list(active.keys()):
                    g, pipe = active[m_]
                    try:
                        next(g)
                    except StopIteration:
                        del active[m_]
                        prep_done.add(m_)
                        freepipes.append(pipe)
                if cur_steps is not None:
                    try:
                        next(cur_steps)
                    except StopIteration:
                        cur_steps = None
                        steps_done = nexts
                        nexts += 1
                if nexts >= nbatch and not active and cur_steps is None:
                    break

        def dn_head(l, h):
            vb = l * NV
            vk = ("vec",)
            SVf = SV[:].rearrange("p a b -> p (a b)").bitcast(F32)
            PCs = [A_PC, SVf[:, 0:2052]]
            CYs = [A_OSUM, SVf[:, 2052:4100]]
            for par in range(2):
                mset("pool", PCs[par][:, 0:2], 0.0, [("pc", par, "pad")])
                mset("pool", PCs[par][:, 2050:2052], 0.0, [("pc", par, "pad")])
            state = dict(tt_busy=False)
            finished = {}

            def chunk_gen(ci, par):
                dst, dk = ((A_QT, "qt"), (A_KT, "kt"), (A_VT, "vt"))[ci]
                PCp, CYp = PCs[par], CYs[par]
                slab, ks = wload(l, O_WIN + (ci * 4 + h) * 1024, 1024)
                for t in range(NT):
                    ps, kp = psum()
                    proj_tile(ps, kp, slab, ks, t)
                    cpy("act", PCp[:, 2 + t * 512:2 + (t + 1) * 512], ps[:], [kp], [("pc", par, t)])
                    yield
                wbase = vb + V_DNCW + (ci * 4 + h) * 4
                rk = [("pc", par, t) for t in range(NT)] + [("pc", par, "pad"), vk]
                wk = [("cy", par, t) for t in range(NT)]
                ts1("pool", CYp, PCp[:, 0:2048], VEC[:, wbase:wbase + 1], ALU.mult, rk, wk)
                yield
                for j in range(1, 4):
                    stt("dve", CYp, PCp[:, j:j + 2048], VEC[:, wbase + j:wbase + j + 1], CYp, ALU.mult, ALU.add, rk + wk, wk)
                    yield
                while state["tt_busy"]:
                    yield
                state["tt_busy"] = True

                def tile_chain(t):
                    tsl = tsl_(t)
                    th, kth = TT[:, 2 * t, :], ("tt", 2 * t)
                    ln, kln = TT[:, 2 * t + 1, :], ("tt", 2 * t + 1)
                    sq, ksq = TB[:, t, :], ("tb", t)
                    act(th, CYp[:, tsl], AF.Tanh, [("cy", par, t)], [kth], scale=0.5)
                    yield
                    if ci == 2:
                        stt("dve", dst[:, tsl], th, 1.0, CYp[:, tsl], ALU.add, ALU.mult, [kth, ("cy", par, t)], [(dk, t)])
                        yield
                        return
                    stt("dve", th, th, 1.0, CYp[:, tsl], ALU.add, ALU.mult, [kth, ("cy", par, t)], [kth])
                    yield
                    act(sq, th, AF.Square, [kth], [ksq])
                    yield
                    ps, kp = psum()
                    mm(ps[:], ONESB, sq, True, True, [ksq, KC], [kp])
                    act(ln, ps[:], AF.Ln, [kp], [kln], bias=4 * EPS)
                    yield
                    act(ln, ln, AF.Exp, [kln], [kln], scale=-0.5, bias=(-0.5 * math.log(128.0) if ci == 0 else 0.0))
                    yield
                    tten("pool", dst[:, tsl], th, ln, ALU.mult, [kth, kln], [(dk, t)])
                    yield

                yield from rr([tile_chain(t) for t in range(NT)])
                state["tt_busy"] = False
                finished[ci] = True

            gens = {}
            started = 0
            while len(finished) < 3:
                while started < 3 and len(gens) < 2 and (started < 2 or finished.get(started - 2)):
                    gens[started] = chunk_gen(started, started % 2)
                    started += 1
                for ci in list(gens.keys()):
                    try:
                        next(gens[ci])
                    except StopIteration:
                        del gens[ci]
            if DBG["dn_stop"] <= 1:
                return
            for src, sk, dst, dk, sc in ((A_KT, "kt", A_KTOK, "ktok", 1.0), (A_VT, "vt", A_VTOK, "vtok", 0.5)):
                for half in range(2):
                    pst, kpst = psum()
                    pstb = pst[:].bitcast(BF16)
                    for n in range(8):
                        tile = half * 8 + n
                        S.op("pe", lambda hh, o=pstb[:, n * 128:(n + 1) * 128], i_=src[:, tile * 128:(tile + 1) * 128]:
                             hh.transpose(out=o, in_=i_, identity=IDB), [(sk, tile // 4), KC], [kpst])
                    act(dst[:, half * 1024:(half + 1) * 1024], pstb[:, 0:1024], AF.Identity, [kpst], [(dk,)], scale=sc)
            if DBG["dn_stop"] <= 2:
                return
            S.barrier()
            gdn(l, h)
            S.barrier()
            if DBG["dn_stop"] <= 4:
                return
            zslab, kzs = wload(l, O_WIN + (12 + h) * 1024, 1024)

            def out_chain(t):
                tsl = tsl_(t)
                osk = [("os", n) for n in range(t * 4, t * 4 + 4)]
                ln, kln = TT[:, 2 * t, :], ("tt", 2 * t)
                th, kth = TT[:, 2 * t + 1, :], ("tt", 2 * t + 1)
                sq, ksq = TB[:, t, :], ("tb", t)
                act(sq, A_OSUM[:, tsl], AF.Square, osk, [ksq])
                yield
                psz, kpz = psum()
                proj_tile(psz, kpz, zslab, kzs, t)
                act(th, psz[:], AF.Tanh, [kpz], [kth], scale=0.5)
                stt("dve", th, th, 1.0, psz[:], ALU.add, ALU.mult, [kth, kpz], [kth])
                yield
                ps, kp = psum()
                mm(ps[:], ONESB, sq, True, True, [ksq, KC], [kp])
                act(ln, ps[:], AF.Ln, [kp], [kln], bias=EPS, scale=1.0 / 128)
                yield
                act(ln, ln, AF.Exp, [kln], [kln], scale=-0.5)
                yield
                stt("dve", ln, A_OSUM[:, tsl], VEC[:, vb + V_DNG:vb + V_DNG + 1], ln, ALU.mult, ALU.mult, osk + [kln, vk], [kln])
                yield
                stt("dve", A_VT[:, tsl], ln, 0.5, th, ALU.mult, ALU.mult, [kln, kth], [("dno", t), ("vt", t)])
                yield

            interleave([out_chain(t) for t in range(NT)])
            wout_pass(l, [h], lambda i, t: A_VT[:, tsl_(t)], lambda i, t: ("dno", t))

        def ffn_phase(l):
            vb = l * NV
            vk = ("vec",)
            rms_to_hn(vb + V_G2)
            ACTB = ARENA[:, 0:11264].rearrange("p (f n) -> p f n", n=512)
            GP = ARENA[:, 11264:15376].bitcast(F32).rearrange("p (b n) -> p b n", b=4)
            NW = 3
            st = dict(down_q=0)

            def fc_chain(qt, fc, idx):
                t0 = qt * 512
                sg, ksg = wload(l, O_WG + fc * 1024, 1024)
                gb = idx % 4
                gp = GP[:, gb, :]
                kgp = ("gp", gb)
                cg, kcg = TT[:, 2 * (idx % 4), :], ("tt", 2 * (idx % 4))
                x2, k2 = TT[:, 2 * (idx % 4) + 1, :], ("tt", 2 * (idx % 4) + 1)
                psg, kpg = psum()
                proj_tile(psg, kpg, sg, ksg, qt)
                psh, kph = psum()
                sides = []
                if qt > 0:
                    sides.append((0, t0 - 1))
                if qt < NT - 1:
                    sides.append((1, t0 + 512))
                for (si, col) in sides:
                    for kc in range(8):
                        mm(psh[:, si:si + 1], sg[:, kc * 128:(kc + 1) * 128], HN[:, kc, col:col + 1], kc == 0, kc == 7,
                           [ksg, ("HN", kc, col // 512)], [kph])
                cpy("act", gp[:, 1:513], psg[:], [kpg], [kgp])
                if qt > 0:
                    cpy("dve", gp[:, 0:1], psh[:, 0:1], [kph, kgp], [kgp])
                else:
                    mset("dve", gp[:, 0:1], 0.0, [kgp])
                if qt < NT - 1:
                    cpy("dve", gp[:, 513:514], psh[:, 1:2], [kph, kgp], [kgp])
                else:
                    mset("dve", gp[:, 513:514], 0.0, [kgp])
                yield
                wb_ = vb + V_FCW + fc * 3
                tsc("pool", cg, gp[:, 0:512], VEC[:, wb_:wb_ + 1], VEC[:, vb + V_FCB + fc:vb + V_FCB + fc + 1], ALU.mult, ALU.add,
                    [kgp, vk], [kcg])
                yield
                stt("dve", cg, gp[:, 1:513], VEC[:, wb_ + 1:wb_ + 2], cg, ALU.mult, ALU.add, [kgp, vk, kcg], [kcg])
                yield
                stt("dve", cg, gp[:, 2:514], VEC[:, wb_ + 2:wb_ + 3], cg, ALU.mult, ALU.add, [kgp, vk, kcg], [kcg])
                yield
                act(x2, cg, AF.Square, [kcg], [k2])
                yield
                tsc("pool", x2, x2, GC2, GC1, ALU.mult, ALU.add, [k2], [k2])
                yield
                tten("pool", x2, x2, cg, ALU.mult, [k2, kcg], [k2])
                yield
                act(x2, x2, AF.Tanh, [k2], [k2])
                yield
                stt("dve", cg, x2, 1.0, cg, ALU.add, ALU.mult, [k2, kcg], [kcg])
                yield
                while st["down_q"] < qt:
                    yield
                su, ksu = wload(l, O_WU + fc * 1024, 1024)
                psu, kpu = psum()
                proj_tile(psu, kpu, su, ksu, qt)
                stt("dve", ACTB[:, fc, :], cg, 0.5, psu[:], ALU.mult, ALU.mult, [kcg, kpu], [("actb", fc)])
                yield

            def down_gen(qt):
                for m in range(8):
                    ps, kp = psum()
                    pieces = ((0, 8), (8, 16), (16, 22))
                    for (f0, f1) in pieces:
                        sl, ksl = wload(l, O_WD + m * 2816 + f0 * 128, (f1 - f0) * 128)
                        for fc in range(f0, f1):
                            mm(ps[:], sl[:, (fc - f0) * 128:(fc - f0 + 1) * 128], ACTB[:, fc, :], fc == 0, fc == NFC - 1,
                               [ksl, ("actb", fc)], [kp])
                    radd(m, qt, ps, kp)
                    yield
                st["down_q"] = qt + 1

            chains = [(qt, fc) for qt in range(NT) for fc in range(NFC)]
            active = []
            ci = 0
            down = None
            nfin = {qt: 0 for qt in range(NT)}
            while ci < len(chains) or active or down is not None:
                while ci < len(chains) and len(active) < NW:
                    qt, fc = chains[ci]
                    active.append((fc_chain(qt, fc, ci), qt))
                    ci += 1
                for item in list(active):
                    g, qt = item
                    try:
                        next(g)
                    except StopIteration:
                        active.remove(item)
                        nfin[qt] += 1
                        if nfin[qt] == NFC:
                            down = down_gen(qt)
                if down is not None:
                    try:
                        next(down)
                    except StopIteration:
                        down = None

        def ple_phase(l, s):
            vb = l * NV
            rms_to_hn(vb + V_GP)
            PB = ARENA[:, 0:4096].rearrange("p (k n) -> p k n", k=2)
            PF = ARENA[:, 4096:8192].bitcast(F32).rearrange("p (b k n) -> p b k n", b=2, k=2)
            psrc = pT[l, s].rearrange("(k p) n -> p k n", p=128)
            for t in range(NT):
                S.dma(PF[:, t % 2], psrc[:, :, tsl_(t)], [], [("pf", t % 2)])
                cpy("pool", PB[:, :, tsl_(t)], PF[:, t % 2], [("pf", t % 2)], [("pb", t)])
            for m in range(8):
                sg, ksg = wload(l, O_PWG + m * 1024, 1024)
                sp_, ksp_ = wload(l, O_PWP + m * 256, 256)
                for t in range(NT):
                    ps1, kp1 = psum()
                    proj_tile(ps1, kp1, sg, ksg, t)
                    th, kth = tt()
                    act(th, ps1[:], AF.Tanh, [kp1, ("lv",)], [kth], scale=0.5, bias=LV[:, 40 + m:41 + m])
                    ps2, kp2 = psum()
                    for k2 in range(2):
                        mm(ps2[:], sp_[:, k2 * 128:(k2 + 1) * 128], PB[:, k2, tsl_(t)], k2 == 0, k2 == 1, [ksp_, ("pb", t)], [kp2])
                    stt("dve", th, th, 1.0, ps2[:], ALU.add, ALU.mult, [kth, kp2], [kth])
                    stt("dve", R[:, m, tsl_(t)], th, 0.5, R[:, m, tsl_(t)], ALU.mult, ALU.add, [kth, ("R", m, t)], [("R", m, t)])


        for s in range(nseq):
            for ch in range(8):
                S.dma(R[:, ch, :], xT[s, ch * 128:(ch + 1) * 128, :], [], [("R", ch, t) for t in range(NT)])
            for l in range(nlayers):
                layer_prep(l)
                rms_to_hn(l * NV + V_G1)
                if "lru" in phases or "dn" in phases:
                    ab_phase(l)
                S.barrier()
                if "lru" in phases:
                    lru_phase(l)
                    S.barrier()
                if "dn" in phases:
                    for h in range(DBG["heads"]):
                        dn_head(l, h)
                        S.barrier()
                if "ffn" in phases:
                    ffn_phase(l)
                    S.barrier()
                if "ple" in phases:
                    ple_phase(l, s)
                    S.barrier()
            S.barrier()
            if dump_r:
                for ch in range(8):
                    S.dma(yT[s, ch * 128:(ch + 1) * 128, :], R[:, ch, :], [("R", ch, t) for t in range(NT)], [("y", s, ch)])
            else:
                for t in range(NT):
                    rs, kr = rms_stats(t, 8, lambda ch, t: R[:, ch, tsl_(t)], lambda ch, t: ("R", ch, t), 1.0 / D, EPS)
                    for ch in range(8):
                        o, ko = tt()
                        stt("dve" if ch % 2 == 0 else "pool", o, R[:, ch, tsl_(t)], VEC[:, L * NV + ch:L * NV + ch + 1], rs,
                            ALU.mult, ALU.mult, [("R", ch, t), kr], [ko])
                        S.dma(yT[s, ch * 128:(ch + 1) * 128, tsl_(t)], o, [ko], [("y", s, ch, t)])
            S.barrier()
        S.barrier()
        print("instructions:", S.ninst, {k: v["n"] for k, v in S.E.items()})
    return nc


def _slabs_kn(w):
    K, N = w.shape
    a = w.reshape(K // 128, 128, N // 128, 128)
    return np.ascontiguousarray(a.transpose(2, 1, 0, 3)).reshape(N // 128, 128, K)


def pack_weights(inp):
    wpk = np.zeros((L, 128, WTOT), np.float32)
    for l in range(L):
        w_in = inp["w_in"][l]
        cols = np.concatenate([w_in[:, 0:2048], w_in[:, 2064:3088]], axis=1)
        sl = _slabs_kn(cols)
        wpk[l, :, O_WIN:O_WIN + 24 * 1024] = sl.transpose(1, 0, 2).reshape(128, 24 * 1024)
        ab = w_in[:, 2048:2064].reshape(8, 128, 16).transpose(1, 0, 2).reshape(128, 128)
        wpk[l, :, O_WAB:O_WAB + 128] = ab
        lw = np.zeros((128, 4, 2, 2, 128), np.float32)
        for d in range(2):
            for gi, nm in enumerate(("lru_wa", "lru_wx")):
                wg = inp[nm][l, d]
                for c in range(4):
                    lw[0:64, c, d, gi, 0:64] = wg[2 * c]
                    lw[64:128, c, d, gi, 64:128] = wg[2 * c + 1]
        wpk[l, :, O_LRUW:O_LRUW + 2048] = lw.reshape(128, 2048)
        wpk[l, :, O_WOUT:O_WOUT + 8192] = inp["w_out"][l].reshape(8, 128, 1024).transpose(1, 0, 2).reshape(128, 8192)
        wpk[l, :, O_WG:O_WG + 22528] = _slabs_kn(inp["ffn_wg"][l]).transpose(1, 0, 2).reshape(128, 22528)
        wpk[l, :, O_WU:O_WU + 22528] = _slabs_kn(inp["ffn_wu"][l]).transpose(1, 0, 2).reshape(128, 22528)
        wd = inp["ffn_wd"][l].reshape(NFC, 128, 8, 128)
        wpk[l, :, O_WD:O_WD + 22528] = wd.transpose(1, 2, 0, 3).reshape(128, 22528)
        wpk[l, :, O_PWG:O_PWG + 8192] = _slabs_kn(inp["ple_wg"][l]).transpose(1, 0, 2).reshape(128, 8192)
        wpk[l, :, O_PWP:O_PWP + 2048] = _slabs_kn(inp["ple_wp"][l]).transpose(1, 0, 2).reshape(128, 2048)
    return wpk


def pack_vecs(inp):
    v = np.zeros((128, L * NV + 8), np.float32)

    def cm(a, n):
        return a.reshape(n, 128).T

    for l in range(L):
        b = l * NV
        v[:, b + V_G1:b + V_G1 + 8] = cm(inp["norm1_g"][l], 8)
        v[:, b + V_G2:b + V_G2 + 8] = cm(inp["norm2_g"][l], 8)
        v[:, b + V_GP:b + V_GP + 8] = cm(inp["ple_norm_g"][l], 8)
        v[:, b + V_BGP:b + V_BGP + 8] = cm(inp["ple_bg"][l], 8)
        v[:, b + V_DNCW:b + V_DNCW + 48] = inp["dn_conv_w"][l].reshape(4, 12, 128).transpose(2, 1, 0).reshape(128, 48)
        v[:, b + V_LCW:b + V_LCW + 16] = inp["lru_conv_w"][l].reshape(4, 4, 128).transpose(2, 1, 0).reshape(128, 16)
        v[:, b + V_LCB:b + V_LCB + 4] = cm(inp["lru_conv_b"][l], 4)
        v[:, b + V_LBA:b + V_LBA + 8] = inp["lru_ba"][l].reshape(2, 4, 128).transpose(2, 0, 1).reshape(128, 8)
        v[:, b + V_LBX:b + V_LBX + 8] = inp["lru_bx"][l].reshape(2, 4, 128).transpose(2, 0, 1).reshape(128, 8)
        v[:, b + V_LLAM:b + V_LLAM + 8] = inp["lru_lambda"][l].reshape(2, 4, 128).transpose(2, 0, 1).reshape(128, 8)
        v[:, b + V_LNG:b + V_LNG + 4] = cm(inp["lru_norm_g"][l], 4)
        v[:, b + V_DNG] = inp["dn_norm_g"][l]
        v[:, b + V_FCW:b + V_FCW + 66] = inp["ffn_conv_w"][l].reshape(3, NFC, 128).transpose(2, 1, 0).reshape(128, 66)
        v[:, b + V_FCB:b + V_FCB + 22] = cm(inp["ffn_conv_b"][l], NFC)
        v[:, b + V_ALOG:b + V_ALOG + 8] = np.broadcast_to(inp["dn_a_log"][l].reshape(1, 8), (128, 8))
        v[:, b + V_DTB:b + V_DTB + 8] = np.broadcast_to(inp["dn_dt_bias"][l].reshape(1, 8), (128, 8))
    v[:, L * NV:L * NV + 8] = cm(inp["final_g"], 8)
    return v


def kernel(**inputs):
    inp = {k: np.asarray(v) for k, v in inputs.items()}
    x = inp["x"]
    p = inp["p"]
    wpk = pack_weights(inp)
    vecs = pack_vecs(inp)
    xT = np.ascontiguousarray(x.transpose(0, 2, 1))
    pT = np.ascontiguousarray(p.transpose(0, 1, 3, 2))
    nc = build_nc()
    in_maps = []
    for c in range(NCORE):
        in_maps.append({
            "xT": xT[c * SEQ_PER_CORE:(c + 1) * SEQ_PER_CORE],
            "pT": np.ascontiguousarray(pT[:, c * SEQ_PER_CORE:(c + 1) * SEQ_PER_CORE]),
            "wpk": wpk,
            "vecs": vecs,
        })
    res = run_bass_kernel_spmd(nc, in_maps, core_ids=list(range(NCORE)))
    yT = np.concatenate([r["yT"] for r in res.results], axis=0)
    return np.ascontiguousarray(yT.transpose(0, 2, 1)).astype(np.float32)
```

```python
import math
import numpy as np
from contextlib import ExitStack
import concourse.bass as bass
import concourse.mybir as mybir
from concourse.bass_utils import run_bass_kernel_spmd

F32 = mybir.dt.float32
BF16 = mybir.dt.bfloat16
AF = mybir.ActivationFunctionType
ALU = mybir.AluOpType

D = 1024
S_LEN = 2048
L = 4
NCORE = 8
SEQ_PER_CORE = 4
DFF = 2816
NFC = 22
PLE = 256
EPS = 1e-6
NT = 4
TK = 16
GC1 = math.sqrt(2.0 / math.pi)
GC2 = GC1 * 0.044715

O_WIN = 0
O_WAB = O_WIN + 24 * 1024
O_LRUW = O_WAB + 128
O_WOUT = O_LRUW + 2048
O_WG = O_WOUT + 8192
O_WU = O_WG + 22528
O_WD = O_WU + 22528
O_PWG = O_WD + 22528
O_PWP = O_PWG + 8192
WTOT = O_PWP + 2048

V_G1, V_G2, V_GP, V_BGP = 0, 8, 16, 24
V_DNCW = 32
V_LCW = 80
V_LCB = 96
V_LBA = 100
V_LBX = 108
V_LLAM = 116
V_LNG = 124
V_DNG = 128
V_FCW = 129
V_FCB = 195
V_ALOG = 217
V_DTB = 225
NV = 240


class Sched:
    NDS = 24

    def __init__(self, nc, es):
        self.nc = nc
        self.E = {}
        for name, h in (("pe", nc.tensor), ("act", nc.scalar), ("dve", nc.vector), ("pool", nc.gpsimd), ("sp", nc.sync)):
            sem = es.enter_context(nc.semaphore("s_" + name))
            self.E[name] = dict(h=h, sem=sem, n=0, seen={}, seend={})
        self.dsem = [es.enter_context(nc.semaphore(f"sd{i}")) for i in range(self.NDS)]
        self.dcnt = [0] * self.NDS
        self.dnext = 0
        self.lastw = {}
        self.readers = {}
        self.ninst = 0

    def _wait(self, eng, tok, same_ok):
        X = self.E[eng]
        if tok[0] == "e":
            _, p, c = tok
            if p == eng and (eng == "pe" or not same_ok):
                return
            if X["seen"].get(p, 0) >= c:
                return
            X["h"].wait_ge(self.E[p]["sem"], c)
            X["seen"][p] = c
        else:
            _, i, v = tok
            if X["seend"].get(i, 0) >= v:
                return
            X["h"].wait_ge(self.dsem[i], v)
            X["seend"][i] = v

    def _deps(self, eng, r, w):
        for k in r:
            t = self.lastw.get(k)
            if t is not None:
                self._wait(eng, t, True)
            if k[0] == "ps":
                for t in self.readers.get(k, {}).values():
                    self._wait(eng, t, False)
        for k in w:
            t = self.lastw.get(k)
            if t is not None:
                self._wait(eng, t, True)
            for t in self.readers.get(k, {}).values():
                self._wait(eng, t, False)

    def _record(self, tok, r, w):
        for k in r:
            self.readers.setdefault(k, {})[tok[1]] = tok
        for k in w:
            self.lastw[k] = tok
            self.readers[k] = {}

    def op(self, eng, emit, r=(), w=()):
        X = self.E[eng]
        self._deps(eng, r, w)
        inst = emit(X["h"])
        X["n"] += 1
        inst.then_inc(X["sem"], 1)
        self.ninst += 1
        self._record(("e", eng, X["n"]), r, w)

    def dma(self, out, in_, r=(), w=()):
        self._deps("sp", r, w)
        i = self.dnext
        self.dnext = (i + 1) % self.NDS
        if self.dcnt[i] > 0:
            self._wait("sp", ("d", i, 16 * self.dcnt[i]), True)
        inst = self.nc.sync.dma_start(out=out, in_=in_)
        self.dcnt[i] += 1
        inst.then_inc(self.dsem[i], 16)
        self.ninst += 1
        self._record(("d", i, 16 * self.dcnt[i]), r, w)

    def barrier(self):
        comp = ("pe", "act", "dve", "pool")
        for e in comp:
            for p in comp:
                if p != e and self.E[p]["n"] > 0:
                    self._wait(e, ("e", p, self.E[p]["n"]), True)
            for i in range(self.NDS):
                if self.dcnt[i] > 0:
                    self._wait(e, ("d", i, 16 * self.dcnt[i]), True)
        for p in comp:
            if self.E[p]["n"] > 0:
                self._wait("sp", ("e", p, self.E[p]["n"]), True)
        for i in range(self.NDS):
            if self.dcnt[i] > 0:
                self._wait("sp", ("d", i, 16 * self.dcnt[i]), True)
        self.lastw = {}
        self.readers = {}


def rev_ap(ap2d):
    a = ap2d.ap
    n = a[-1][1]
    st = a[-1][0]
    return bass.AP(ap2d.tensor, ap2d.offset + (n - 1) * st, [list(x) for x in a[:-1]] + [[-st, n]])


DBG = dict(dn_stop=99, heads=4, gdn_m=8)


def interleave(gens):
    gens = list(gens)
    while gens:
        for g in list(gens):
            try:
                next(g)
            except StopIteration:
                gens.remove(g)


def build_nc(nseq=SEQ_PER_CORE, nlayers=L, dump_r=False, do_prepass=True, phases=("lru", "dn", "ffn", "ple"), ndbg=0):
    nc = bass.Bass("TRN2", target_bir_lowering=False)
    xT = nc.dram_tensor("xT", [SEQ_PER_CORE, D, S_LEN], F32, kind="ExternalInput").ap()
    pT = nc.dram_tensor("pT", [L, SEQ_PER_CORE, PLE, S_LEN], F32, kind="ExternalInput").ap()
    wpk = nc.dram_tensor("wpk", [L, 128, WTOT], F32, kind="ExternalInput").ap()
    vecs = nc.dram_tensor("vecs", [128, L * NV + 8], F32, kind="ExternalInput").ap()
    yT = nc.dram_tensor("yT", [SEQ_PER_CORE, D, S_LEN], F32, kind="ExternalOutput").ap()
    wsc = nc.dram_tensor("wsc", [L, 128, WTOT], BF16, kind="Internal").ap()
    dbg_out = None
    if ndbg:
        dbg_out = nc.dram_tensor("dbg", [ndbg, 128, S_LEN], F32, kind="ExternalOutput").ap()

    es = ExitStack()
    with es:
        def sb(name, shape, dt):
            return es.enter_context(nc.sbuf_tensor(name, shape, dt))

        S = Sched(nc, es)
        R = sb("R", [128, 8, S_LEN], F32)
        HN = sb("HN", [128, 8, S_LEN], BF16)
        ARENA = sb("ARENA", [128, 18560], BF16)
        TT = sb("TT", [128, 8, 512], F32)
        TB = sb("TB", [128, 6, 512], BF16)
        SV = sb("SV", [128, 20, 512], BF16)
        SF = sb("SF", [128, 512], F32)
        WB = sb("WB", [128, 5, 1024], BF16)
        VEC = sb("VEC", [128, L * NV + 8], F32)
        CST = sb("CST", [128, 13, 128], F32)
        CSB = sb("CSB", [128, 7, 128], BF16)
        MISC = sb("MISC", [128, 12, 128], F32)
        LV = sb("LV", [128, 64], F32)
        SST = sb("SST", [128, 2, 128], F32)
        SBF = sb("SBF", [128, 2, 128], BF16)
        CAR = sb("CAR", [128, 4], F32)
        PS = [es.enter_context(nc.psum_tensor(f"ps{i}", [128, 512], F32)) for i in range(8)]
        cnt = dict(ps=0, tt=0, tb=0, wb=0, sv=0, ev=0)

        def psum():
            i = cnt["ps"] % 8
            cnt["ps"] += 1
            return PS[i], ("ps", i)

        def tt():
            i = cnt["tt"] % 8
            cnt["tt"] += 1
            return TT[:, i, :], ("tt", i)

        def tb():
            i = cnt["tb"] % 6
            cnt["tb"] += 1
            return TB[:, i, :], ("tb", i)

        def wslot():
            i = cnt["wb"] % 5
            cnt["wb"] += 1
            return WB[:, i, :], ("wb", i)

        def sv():
            i = 10 + cnt["sv"] % 10
            cnt["sv"] += 1
            return SV[:, i, :], ("sv", i)

        def act(out, in_, func, r, w, bias=0.0, scale=1.0):
            S.op("act", lambda h: h.activation(out=out, in_=in_, func=func, bias=bias, scale=scale), r, w)

        def tsc(eng, out, in0, s1, s2, op0, op1, r, w):
            S.op(eng, lambda h: h.tensor_scalar(out=out, in0=in0, scalar1=s1, scalar2=s2, op0=op0, op1=op1), r, w)

        def ts1(eng, out, in_, s, op, r, w):
            S.op(eng, lambda h: h.tensor_single_scalar(out=out, in_=in_, scalar=s, op=op), r, w)

        def stt(eng, out, in0, s, in1, op0, op1, r, w):
            eng = "dve"
            S.op(eng, lambda h: h.scalar_tensor_tensor(out=out, in0=in0, scalar=s, in1=in1, op0=op0, op1=op1), r, w)

        def tten(eng, out, in0, in1, op, r, w):
            S.op(eng, lambda h: h.tensor_tensor(out=out, in0=in0, in1=in1, op=op), r, w)

        def cpy(eng, out, in_, r, w):
            if eng == "act":
                act(out, in_, AF.Identity, r, w)
            else:
                S.op(eng, lambda h: h.tensor_copy(out=out, in_=in_), r, w)

        def mm(out, lhsT, rhs, start, stop, r, w):
            S.op("pe", lambda h: h.matmul(out, lhsT=lhsT, rhs=rhs, start=start, stop=stop), r, w)

        def mset(eng, ap, val, w):
            S.op(eng, lambda h: h.memset(ap, val), (), w)

        INCF, INCB, OFFD, ONESF, IDF = (CST[:, i, :] for i in range(5))
        NEG4 = CST[:, 5:9, :]
        M16, ML0, ML1, ML2, IDB, ONESB = (CSB[:, i, :] for i in range(6))
        MLV = [ML0, ML1, ML2]
        KC = ("cst",)

        def b4(ap):
            return ap.unsqueeze(1).to_broadcast([128, 4, 128])

        def v4(ap):
            return ap.rearrange("p (e i) -> p e i", e=4)

        def aff(ap, pattern, op, fill, base, cm):
            S.op("pool", lambda h: h.affine_select(out=ap, in_=ap, pattern=pattern, compare_op=op, fill=fill,
                                                   base=base, channel_multiplier=cm), [KC], [KC])

        mset("pool", CST[:, 0:5, :], 1.0, [KC])
        aff(INCF, [[1, 128]], ALU.is_ge, 0.0, 0, -1)
        aff(INCB, [[-1, 128]], ALU.is_ge, 0.0, 0, 1)
        aff(OFFD, [[1, 128]], ALU.not_equal, 0.0, 0, -1)
        aff(IDF, [[1, 128]], ALU.is_equal, 0.0, 0, -1)
        for e in range(4):
            tsc("pool", NEG4[:, e, :], INCF if e < 2 else INCB, -1.0, 1e30, ALU.add, ALU.mult, [KC], [KC])
        mset("pool", CST[:, 9:13, :], 1.0, [KC])
        for bi, b in enumerate((16, 32, 64)):
            nb = 128 // b
            v = CST[:, 9 + bi, :].rearrange("p (k c) -> p k c", c=b)
            aff(v, [[-b, nb], [0, b]], ALU.is_ge, 0.0, 0, 1)
            aff(v, [[b, nb], [0, b]], ALU.is_gt, 0.0, b, -1)
        cpy("pool", M16, CST[:, 9, :], [KC], [KC])
        for k in range(3):
            tten("pool", MLV[k], CST[:, 10 + k, :], CST[:, 9 + k, :], ALU.subtract, [KC], [KC])
        cpy("pool", IDB, IDF, [KC], [KC])
        cpy("pool", ONESB, ONESF, [KC], [KC])
        NM16 = CSB[:, 6, :]
        ts1("pool", NM16, CST[:, 9, :], -1.0, ALU.mult, [KC], [KC])
        OFFB = sb("OFFB", [128, 128], BF16)
        cpy("pool", OFFB[:], OFFD, [KC], [KC])
        S.dma(VEC[:], vecs[:, :], [], [("vec",)])
        S.barrier()

        if do_prepass:
            CH = 2048
            st_f = R
            st_b = HN
            n = 0
            for l in range(nlayers):
                c0 = 0
                while c0 < WTOT:
                    cw = min(CH, WTOT - c0)
                    fb = n % 8
                    src = st_f[:, fb, 0:cw]
                    dst = st_b[:, fb, 0:cw]
                    S.dma(src, wpk[l, :, c0:c0 + cw], [], [("pf", fb)])
                    cpy(("dve", "pool", "act")[n % 3], dst, src, [("pf", fb)], [("pb", fb)])
                    S.dma(wsc[l, :, c0:c0 + cw], dst, [("pb", fb)], [("wsc", l)])
                    c0 += cw
                    n += 1
            S.barrier()

        def wload(l, off, width):
            slot, k = wslot()
            S.dma(slot[:, 0:width], wsc[l, :, off:off + width], [], [k])
            return slot, k

        def tsl_(t):
            return slice(t * 512, (t + 1) * 512)

        def rms_stats(t, nch, src, srck, scale, eps):
            ps, kp = psum()
            for ch in range(nch):
                sq, ks = tb()
                act(sq, src(ch, t), AF.Square, [srck(ch, t)], [ks])
                mm(ps[:], ONESB, sq, ch == 0, ch == nch - 1, [ks], [kp])
            ln, kl = tt()
            act(ln, ps[:], AF.Ln, [kp], [kl], bias=eps, scale=scale)
            rs, kr = tt()
            act(rs, ln, AF.Exp, [kl], [kr], scale=-0.5)
            return rs, kr

        def rms_to_hn(gbase):
            for t in range(NT):
                rs, kr = rms_stats(t, 8, lambda ch, t: R[:, ch, tsl_(t)], lambda ch, t: ("R", ch, t), 1.0 / D, EPS)
                for ch in range(8):
                    stt("dve" if ch % 2 == 0 else "pool", HN[:, ch, tsl_(t)], R[:, ch, tsl_(t)],
                        VEC[:, gbase + ch:gbase + ch + 1], rs, ALU.mult, ALU.mult, [("R", ch, t), kr, ("vec",)], [("HN", ch, t)])

        def proj_tile(ps, kp, slab, ks, t):
            for kc in range(8):
                mm(ps[:], slab[:, kc * 128:(kc + 1) * 128], HN[:, kc, tsl_(t)], kc == 0, kc == 7, [ks, ("HN", kc, t)], [kp])

        def radd(m, t, ps, kp):
            tten("dve", R[:, m, tsl_(t)], ps[:], R[:, m, tsl_(t)], ALU.add, [kp, ("R", m, t)], [("R", m, t)])

        def dbg_store(idx, ap2d, keys):
            if dbg_out is not None and idx < ndbg:
                S.dma(dbg_out[idx, :, 0:ap2d.shape[-1]], ap2d, keys, [("dbg", idx)])

        A_QT = ARENA[:, 0:2048]
        A_KT = ARENA[:, 2048:4096]
        A_VT = ARENA[:, 4096:6144]
        A_KTOK = ARENA[:, 6144:8192]
        A_VTOK = ARENA[:, 8192:10240]
        A_OSUM = ARENA[:, 10240:14336].bitcast(F32)
        A_PC = ARENA[:, 14336:18440].bitcast(F32)
        A_XCB = ARENA[:, 0:2048]
        A_GL = ARENA[:, 2048:6144].bitcast(F32)
        A_HS = ARENA[:, 6144:10240].bitcast(F32)
        A_XC = A_OSUM
        YL = SV[:, 0:16, :].rearrange("p (c a) n -> p c (a n)", c=4)

        def pc_pads():
            mset("pool", A_PC[:, 0:2], 0.0, [("pc", "pad")])
            mset("pool", A_PC[:, 2050:2052], 0.0, [("pc", "pad")])

        def conv4(dst, dstk, wbase, bias_ap, vk):
            rk = [("pc", t) for t in range(NT)] + [("pc", "pad"), vk]
            wk = [(dstk, t) for t in range(NT)]
            if bias_ap is None:
                ts1("pool", dst, A_PC[:, 0:2048], VEC[:, wbase:wbase + 1], ALU.mult, rk, wk)
            else:
                tsc("pool", dst, A_PC[:, 0:2048], VEC[:, wbase:wbase + 1], bias_ap, ALU.mult, ALU.add, rk, wk)
            for j in range(1, 4):
                stt("pool" if j % 2 else "dve", dst, A_PC[:, j:j + 2048], VEC[:, wbase + j:wbase + j + 1], dst,
                    ALU.mult, ALU.add, rk + wk, wk)

        def wout_pass(l, kcs, src, srck):
            slabs = [wload(l, O_WOUT + kc * 1024, 1024) for kc in kcs]
            for m in range(8):
                for t in range(NT):
                    ps, kp = psum()
                    for i, (sl, ks) in enumerate(slabs):
                        mm(ps[:], sl[:, m * 128:(m + 1) * 128], src(i, t), i == 0, i == len(slabs) - 1, [ks, srck(i, t)], [kp])
                    radd(m, t, ps, kp)

        def layer_prep(l):
            vb = l * NV
            kl = ("lv",)
            vk = ("vec",)
            act(LV[:, 48:56], VEC[:, vb + V_ALOG:vb + V_ALOG + 8], AF.Exp, [vk], [kl])
            ts1("dve", LV[:, 0:8], LV[:, 48:56], -1.0, ALU.mult, [kl], [kl])
            ts1("dve", LV[:, 8:16], VEC[:, vb + V_LBA:vb + V_LBA + 8], 0.5, ALU.mult, [vk, kl], [kl])
            ts1("dve", LV[:, 16:24], VEC[:, vb + V_LBX:vb + V_LBX + 8], 0.5, ALU.mult, [vk, kl], [kl])
            act(LV[:, 48:56], VEC[:, vb + V_LLAM:vb + V_LLAM + 8], AF.Exp, [vk, kl], [kl], scale=-1.0)
            act(LV[:, 56:64], LV[:, 48:56], AF.Ln, [kl], [kl], bias=1.0)
            ts1("dve", LV[:, 24:32], LV[:, 56:64], -8.0, ALU.mult, [kl], [kl])
            ts1("dve", LV[:, 32:40], LV[:, 56:64], -4.0, ALU.mult, [kl], [kl])
            ts1("dve", LV[:, 40:48], VEC[:, vb + V_BGP:vb + V_BGP + 8], 0.5, ALU.mult, [vk, kl], [kl])

        def m3(i):
            return MISC[:, i, :].rearrange("p (t h) -> p t h", h=8)

        AB = MISC[:, 0:2, :].rearrange("p a b -> p (a b)").rearrange("p (t c) -> p t c", c=16)
        G3, NLNB, CCOL, CLAST, BIASD, NEGEC, BDEC, ECL, TM1, TM2 = (m3(i) for i in range(2, 12))
        KM = ("misc",)

        def ab_phase(l):
            vb = l * NV
            slab, ks = wload(l, O_WAB, 128)
            ps, kp = psum()
            for n in range(TK):
                for kc in range(8):
                    mm(ps[:, n * 16:(n + 1) * 16], HN[:, kc, n * 128:(n + 1) * 128], slab[:, kc * 16:(kc + 1) * 16],
                       kc == 0, kc == 7, [ks, ("HN", kc, n // 4)], [kp])
            cpy("dve", MISC[:, 0:2, :].rearrange("p a b -> p (a b)"), ps[:, 0:256], [kp], [KM])
            r = [KM, ("lv",), ("vec",)]
            act(TM1, AB[:, :, 0:8], AF.Exp, r, [KM], scale=-1.0)
            act(NLNB, TM1, AF.Ln, r, [KM], bias=1.0)
            tten("dve", TM2, AB[:, :, 8:16], VEC[:, vb + V_DTB:vb + V_DTB + 8].unsqueeze(1).to_broadcast([128, 16, 8]), ALU.add, r, [KM])
            act(TM2, TM2, AF.Exp, r, [KM])
            act(TM2, TM2, AF.Ln, r, [KM], bias=1.0)
            tten("dve", G3, TM2, LV[:, 0:8].unsqueeze(1).to_broadcast([128, 16, 8]), ALU.mult, r, [KM])
            ps2, kp2 = psum()
            mm(ps2[:, 0:64].rearrange("p (t h) -> p t h", h=4), INCF, G3[:, :, 0:4], True, True, [KM, KC], [kp2])
            mm(ps2[:, 64:128].rearrange("p (t h) -> p t h", h=4), INCB, G3[:, :, 4:8], True, True, [KM, KC], [kp2])
            mm(ps2[:, 128:256], ONESF, MISC[:, 2, :], True, True, [KM, KC], [kp2])
            cpy("dve", CCOL[:, :, 0:4], ps2[:, 0:64].rearrange("p (t h) -> p t h", h=4), [kp2], [KM])
            cpy("dve", CCOL[:, :, 4:8], ps2[:, 64:128].rearrange("p (t h) -> p t h", h=4), [kp2, KM], [KM])
            cpy("dve", MISC[:, 5, :], ps2[:, 128:256], [kp2, KM], [KM])
            tten("dve", TM1, NLNB, CCOL, ALU.add, r, [KM])
            ts1("dve", BIASD, TM1, -1.0, ALU.mult, r, [KM])
            act(TM2, CCOL, AF.Exp, r, [KM])
            ts1("dve", NEGEC, TM2, -1.0, ALU.mult, r, [KM])
            tten("dve", TM1, BIASD, CLAST, ALU.add, r, [KM])
            act(BDEC, TM1, AF.Exp, r, [KM])
            act(ECL, CLAST, AF.Exp, r, [KM])

        def lru_phase(l):
            vb = l * NV
            vk = ("vec",)
            done = {-1: True}
            conv_done = {-1: True}

            def seq(*gs):
                for g in gs:
                    yield from g

            LWALL = SV[:, 16:20, :].rearrange("p a b -> p (a b)")
            S.dma(LWALL, wsc[l, :, O_LRUW:O_LRUW + 2048], [], [("lwall",)])

            def chunk_gen(c):
                lw, klw = LWALL[:, c * 512:(c + 1) * 512], ("lwall",)
                slab, ks = wload(l, O_WIN + (16 + c) * 1024, 1024)
                while not conv_done.get(c - 1):
                    yield
                if c == 0:
                    pc_pads()
                for t in range(NT):
                    ps, kp = psum()
                    proj_tile(ps, kp, slab, ks, t)
                    cpy("act", A_PC[:, 2 + t * 512:2 + (t + 1) * 512], ps[:], [kp], [("pc", t)])
                    yield
                while not done.get(c - 1):
                    yield
                wbase = vb + V_LCW + c * 4
                rk = [("pc", t) for t in range(NT)] + [("pc", "pad"), vk]
                wk = [("xc", t) for t in range(NT)]
                tsc("dve", A_XC, A_PC[:, 0:2048], VEC[:, wbase:wbase + 1], VEC[:, vb + V_LCB + c:vb + V_LCB + c + 1], ALU.mult, ALU.add, rk, wk)
                yield
                for j in range(1, 4):
                    stt("dve", A_XC, A_PC[:, j:j + 2048], VEC[:, wbase + j:wbase + j + 1], A_XC, ALU.mult, ALU.add, rk + wk, wk)
                    yield
                conv_done[c] = True
                cpy("act", A_XCB, A_XC, wk, [("xcb", t) for t in range(NT)])
                yield
                gslab, kgs = wload(l, O_WIN + (20 + c) * 1024, 1024)

                def gelu_gen():
                    for t in range(NT):
                        x2 = TB[:, 2 * (t % 2):2 * (t % 2) + 2, :].rearrange("p a b -> p (a b)").bitcast(F32)
                        k2 = ("tbf", t % 2)
                        ps, kp = psum()
                        proj_tile(ps, kp, gslab, kgs, t)
                        act(x2, ps[:], AF.Square, [kp], [k2])
                        tsc("pool", x2, x2, GC2, GC1, ALU.mult, ALU.add, [k2], [k2])
                        tten("dve", x2, x2, ps[:], ALU.mult, [k2, kp], [k2])
                        act(x2, x2, AF.Tanh, [k2], [k2])
                        stt("dve", A_GL[:, tsl_(t)], x2, 1.0, ps[:], ALU.add, ALU.mult, [k2, kp], [("gl", t)])
                        yield

                def dir_gen(d):
                    order = list(range(NT)) if d == 0 else list(range(NT - 1, -1, -1))
                    lvi = d * 4 + c
                    for ti, t in enumerate(order):
                        tsl = tsl_(t)
                        base = 4 * (ti % 2)
                        thr, kr_ = TT[:, base + 0, :], ("tt", base + 0)
                        thi, ki_ = TT[:, base + 1, :], ("tt", base + 1)
                        a, ka = TT[:, base + 2, :], ("tt", base + 2)
                        a2, ka2 = TT[:, base + 3, :], ("tt", base + 3)
                        psr, kpr = psum()
                        mm(psr[:], lw[:, (d * 2 + 0) * 128:(d * 2 + 1) * 128], A_XCB[:, tsl], True, True, [klw, ("xcb", t)], [kpr])
                        act(thr, psr[:], AF.Tanh, [kpr, ("lv",)], [kr_], scale=0.5, bias=LV[:, 8 + lvi:9 + lvi])
                        psi, kpi = psum()
                        mm(psi[:], lw[:, (d * 2 + 1) * 128:(d * 2 + 2) * 128], A_XCB[:, tsl], True, True, [klw, ("xcb", t)], [kpi])
                        act(thi, psi[:], AF.Tanh, [kpi, ("lv",)], [ki_], scale=0.5, bias=LV[:, 16 + lvi:17 + lvi])
                        yield
                        act(a, thr, AF.Exp, [kr_, ("lv",)], [ka], scale=LV[:, 32 + lvi:33 + lvi], bias=LV[:, 32 + lvi:33 + lvi])
                        act(a2, thr, AF.Exp, [kr_, ("lv",)], [ka2], scale=LV[:, 24 + lvi:25 + lvi], bias=LV[:, 24 + lvi:25 + lvi])
                        yield
                        act(thr, thr, AF.Tanh, [kr_, ("lv",)], [kr_], scale=LV[:, 32 + lvi:33 + lvi], bias=LV[:, 32 + lvi:33 + lvi])
                        stt("dve", thi, thi, 1.0, A_XC[:, tsl], ALU.add, ALU.mult, [ki_, ("xc", t)], [ki_])
                        yield
                        stt("dve", a2, a2, 1.0, thr, ALU.add, ALU.mult, [ka2, kr_], [ka2])
                        yield
                        act(a2, a2, AF.Ln, [ka2], [ka2], scale=-1.0)
                        yield
                        act(a2, a2, AF.Exp, [ka2], [ka2], scale=0.5)
                        yield
                        stt("dve", thi, thi, 0.5, a2, ALU.mult, ALU.mult, [ki_, ka2], [ki_])
                        yield
                        if d == 0:
                            init = 0.0 if ti == 0 else A_HS[:, t * 512 - 1:t * 512]
                            rr_ = [ka, ki_] + ([("hs", t - 1)] if ti else [])
                            S.op("dve", lambda h, o=A_HS[:, tsl], a_=a, b_=thi, i_=init: h.tensor_tensor_scan(
                                out=o, data0=a_, data1=b_, initial=i_, op0=ALU.mult, op1=ALU.add), rr_, [("hs", t)])
                        else:
                            init = 0.0 if ti == 0 else CAR[:, 0:1]
                            S.op("dve", lambda h, o=rev_ap(thr), a_=rev_ap(a), b_=rev_ap(thi), i_=init: h.tensor_tensor_scan(
                                out=o, data0=a_, data1=b_, initial=i_, op0=ALU.mult, op1=ALU.add), [ka, ki_, kr_, ("car",)], [kr_])
                            cpy("dve", CAR[:, 0:1], thr[:, 0:1], [kr_, ("car",)], [("car",)])
                            tten("pool", A_HS[:, tsl], A_HS[:, tsl], thr, ALU.add, [kr_, ("hs", t)], [("hs", t)])
                        yield

                yield from rr([gelu_gen(), seq(dir_gen(0), dir_gen(1))])
                tten("pool", YL[:, c, :], A_GL, A_HS, ALU.mult, [("gl", t) for t in range(NT)] + [("hs", t) for t in range(NT)],
                     [("yl", c, t) for t in range(NT)])
                done[c] = True
                yield

            gens = {}
            started = 0
            while started < 4 or gens:
                while started < 4 and len(gens) < 2:
                    gens[started] = chunk_gen(started)
                    started += 1
                for c_ in list(gens.keys()):
                    try:
                        next(gens[c_])
                    except StopIteration:
                        del gens[c_]
            for t in range(NT):
                rs, kr = rms_stats(t, 4, lambda c, t: YL[:, c, tsl_(t)], lambda c, t: ("yl", c, t), 1.0 / 512, 4 * EPS)
                for c in range(4):
                    stt("dve", YL[:, c, tsl_(t)], YL[:, c, tsl_(t)], VEC[:, vb + V_LNG + c:vb + V_LNG + c + 1], rs,
                        ALU.mult, ALU.mult, [("yl", c, t), kr, vk], [("yl", c, t)])
            wout_pass(l, [4, 5, 6, 7], lambda i, t: YL[:, i, tsl_(t)], lambda i, t: ("yl", i, t))

        def rr(gens):
            gens = list(gens)
            while gens:
                for g in list(gens):
                    try:
                        next(g)
                    except StopIteration:
                        gens.remove(g)
                yield

        def gdn(l, h):
            KT3 = A_KT.rearrange("p (n i) -> p n i", i=128)
            QT3 = A_QT.rearrange("p (n i) -> p n i", i=128)
            KTOK3 = A_KTOK.rearrange("p (n i) -> p n i", i=128)
            VTOK3 = A_VTOK.rearrange("p (n i) -> p n i", i=128)
            OS3 = A_OSUM.rearrange("p (n i) -> p n i", i=128)
            mset("pool", SST[:], 0.0, [("sst", 0), ("sst", 1)])
            mset("pool", SBF[:], 0.0, [("sbf", 0), ("sbf", 1)])
            TTB = TT[:].rearrange("p a b -> p (a b)").bitcast(BF16).rearrange("p (s n) -> p s n", n=512)
            slots = [(SV[:, i, :], ("gsv", i)) for i in range(20)] + [(TTB[:, i, :], ("gtt", i)) for i in range(16)]
            NS, NP, PSZ = DBG.get("NS", 4), DBG.get("NP", 3), DBG.get("PSZ", 7)
            outsets = [slots[3 * i:3 * i + 3] for i in range(NS)]
            pools = [slots[3 * NS + PSZ * i:3 * NS + PSZ * (i + 1)] for i in range(NP)]
            MGs = [(A_PC[:, i * 512:(i + 1) * 512], ("mg", i)) for i in range(NP)]

            def elem(m, e):
                if e < 2:
                    return 2 * m + e, 0, h
                return 15 - 2 * m - (e - 2), 1, 4 + h

            def mm4(lh, klh, rh, krh):
                ps, kp = psum()
                p4 = v4(ps[:])
                l4, r4 = v4(lh), v4(rh)
                for e in range(4):
                    mm(p4[:, e, :], l4[:, e, :], r4[:, e, :], True, True, [klh, krh], [kp])
                return ps, kp

            def evac(ps, kp, dst, kd):
                cnt["ev"] += 1
                cpy("act" if cnt["ev"] % 2 else "dve", dst, ps[:], [kp], [kd])

            def transp4(src, ksrc, dst, kdst):
                pst, kpst = psum()
                pstb = pst[:].bitcast(BF16)
                for e in range(4):
                    S.op("pe", lambda hh, o=pstb[:, e * 128:(e + 1) * 128], i_=v4(src)[:, e, :]: hh.transpose(out=o, in_=i_, identity=IDB),
                         [ksrc, KC], [kpst])
                cpy("act", dst, pstb[:, 0:512], [kpst], [kdst])

            def prep(m, pipe):
                free = list(pools[pipe])

                def alloc():
                    return free.pop(0)

                def rel(*xs):
                    free.extend(xs)

                els = [elem(m, e) for e in range(4)]
                (Z, kZ), (ATT, kATT), (QD, kQD) = outsets[m % NS]
                MG, kmg = MGs[pipe]
                for e, (tile, d, dh) in enumerate(els):
                    ts1("pool", v4(MG)[:, e, :], INCF if d == 0 else INCB, G3[:, tile, dh:dh + 1], ALU.mult, [KM, KC], [kmg])
                c1, kc1 = psum()
                mm(c1[:], ONESF, MG, True, False, [kmg, KC], [kc1])
                mm(v4(c1[:]), IDF, NEG4, False, True, [KC], [kc1])
                DT = alloc()
                for e, (tile, d, dh) in enumerate(els):
                    act(v4(DT[0])[:, e, :], v4(c1[:])[:, e, :], AF.Exp, [kc1, KM], [DT[1]], bias=BIASD[:, tile, dh:dh + 1])
                yield
                c2, kc2 = psum()
                mm(c2[:], ONESF, MG, True, True, [kmg, KC], [kc2])
                ER = alloc()
                act(ER[0], c2[:], AF.Exp, [kc2], [ER[1]])
                yield
                kk, kkk = psum()
                for e, (tile, d, dh) in enumerate(els):
                    mm(v4(kk[:])[:, e, :], KT3[:, tile, :], KT3[:, tile, :], True, True, [("kt",)], [kkk])
                BP = alloc()
                tten("dve", BP[0], kk[:], DT[0], ALU.mult, [kkk, DT[1]], [BP[1]])
                yield
                qk, kqk = psum()
                for e, (tile, d, dh) in enumerate(els):
                    mm(v4(qk[:])[:, e, :], KT3[:, tile, :], QT3[:, tile, :], True, True, [("kt",), ("qt",)], [kqk])
                tten("dve", ATT, qk[:], DT[0], ALU.mult, [kqk, DT[1]], [kATT])
                rel(DT)
                yield
                tten("pool", v4(BP[0]), v4(BP[0]), b4(OFFB[:]), ALU.mult, [BP[1], KC], [BP[1]])
                for e, (tile, d, dh) in enumerate(els):
                    tten("pool", v4(QD)[:, e, :], QT3[:, tile, :], v4(ER[0])[:, e, :], ALU.mult, [("qt",), ER[1]], [kQD])
                rel(ER)
                yield
                AT = alloc()
                transp4(BP[0], BP[1], AT[0], AT[1])
                X = alloc()
                tten("pool", v4(X[0]), v4(BP[0]), b4(NM16), ALU.mult, [BP[1], KC], [X[1]])
                yield
                XT = alloc()
                tten("pool", v4(XT[0]), v4(AT[0]), b4(NM16), ALU.mult, [AT[1], KC], [XT[1]])
                rel(BP)
                yield

                def prod(lh, rh):
                    o = alloc()
                    evac(*mm4(lh[0], lh[1], rh[0], rh[1]), o[0], o[1])
                    return o

                def plus_i(x):
                    g = alloc()
                    tten("pool", v4(g[0]), v4(x[0]), b4(IDB), ALU.add, [x[1], KC], [g[1]])
                    return g

                X2 = prod(XT, X)
                yield
                X2T = prod(X, XT)
                G1T = plus_i(XT)
                rel(X, XT)
                yield
                G2 = plus_i(X2)
                Y1T = prod(G2, G1T)
                rel(G1T, G2)
                yield
                X4 = prod(X2T, X2)
                yield
                X4T = prod(X2, X2T)
                rel(X2, X2T)
                yield
                G4 = plus_i(X4)
                Y2T = prod(G4, Y1T)
                rel(Y1T, G4)
                yield
                X8 = prod(X4T, X4)
                rel(X4, X4T)
                yield
                G8 = plus_i(X8)
                rel(X8)
                Zc = prod(Y2T, G8)
                yield
                ZTc = prod(G8, Y2T)
                rel(Y2T, G8)
                yield
                for k in range(3):
                    OT = alloc()
                    tten("pool", v4(OT[0]), v4(AT[0]), b4(MLV[k]), ALU.mult, [AT[1], KC], [OT[1]])
                    w1, kw1 = mm4(OT[0], OT[1], Zc[0], Zc[1])
                    rel(OT)
                    IW = alloc()
                    tten("dve", v4(IW[0]), b4(IDB), v4(w1[:]), ALU.subtract, [kw1, KC], [IW[1]])
                    yield
                    zn, kzn = mm4(ZTc[0], ZTc[1], IW[0], IW[1])
                    rel(IW)
                    if k < 2:
                        Zn = alloc()
                        evac(zn, kzn, Zn[0], Zn[1])
                        rel(Zc, ZTc)
                        yield
                        ZTn = alloc()
                        transp4(Zn[0], Zn[1], ZTn[0], ZTn[1])
                        Zc, ZTc = Zn, ZTn
                    else:
                        evac(zn, kzn, Z, kZ)
                        rel(Zc, ZTc, AT)
                    yield

            def steps(m):
                (Z, kZ), (ATT, kATT), (QD, kQD) = outsets[m % NS]
                Z4, ATT4, QD4 = v4(Z), v4(ATT), v4(QD)
                for sidx in (2 * m, 2 * m + 1):
                    def one(d):
                        tile = sidx if d == 0 else 15 - sidx
                        e = (sidx % 2) + 2 * d
                        dh = h + 4 * d
                        vp, kvp = TB[:, 3 * d + 0, 0:128], ("tb", 3 * d + 0)
                        vraw, kvr = TB[:, 3 * d + 1, 0:128], ("tb", 3 * d + 1)
                        vdec, kvd = TB[:, 3 * d + 2, 0:128], ("tb", 3 * d + 2)
                        ksp, k1 = psum()
                        mm(ksp[:, 0:128], KT3[:, tile, :], SBF[:, d, :], True, True, [("kt",), ("sbf", d)], [k1])
                        stt("dve", vp, ksp[:, 0:128], NEGEC[:, tile, dh:dh + 1], VTOK3[:, tile, :], ALU.mult, ALU.add,
                            [k1, KM, ("vtok",)], [kvp])
                        yield
                        vrp, k2 = psum()
                        mm(vrp[:, 0:128], Z4[:, e, :], vp, True, True, [kZ, kvp], [k2])
                        cpy("act", vraw, vrp[:, 0:128], [k2], [kvr])
                        ts1("dve", vdec, vrp[:, 0:128], BDEC[:, tile, dh:dh + 1], ALU.mult, [k2, KM], [kvd])
                        yield
                        op_, k3 = psum()
                        mm(op_[:, 0:128], SBF[:, d, :], QD4[:, e, :], True, False, [("sbf", d), kQD], [k3])
                        mm(op_[:, 0:128], vraw, ATT4[:, e, :], False, True, [kvr, kATT], [k3])
                        snp, k4 = psum()
                        mm(snp[:, 0:128], KTOK3[:, tile, :], vdec, True, True, [("ktok",), kvd], [k4])
                        stt("dve", SST[:, d, :], SST[:, d, :], ECL[:, tile, dh:dh + 1], snp[:, 0:128], ALU.mult, ALU.add,
                            [k4, KM, ("sst", d)], [("sst", d)])
                        cpy("pool", SBF[:, d, :], SST[:, d, :], [("sst", d)], [("sbf", d)])
                        first = (d == 0 and tile < 8) or (d == 1 and tile >= 8)
                        if first:
                            cpy("act", OS3[:, tile, :], op_[:, 0:128], [k3], [("os", tile)])
                        else:
                            tten("dve", OS3[:, tile, :], op_[:, 0:128], OS3[:, tile, :], ALU.add, [k3, ("os", tile)], [("os", tile)])
                        yield
                    yield from rr([one(0), one(1)])

            active = {}
            prep_done = set()
            steps_done = -1
            nextp = 0
            nexts = 0
            freepipes = list(range(NP))
            cur_steps = None
            nbatch = DBG["gdn_m"]
            while nexts < nbatch or active or cur_steps is not None:
                while nextp < 8 and freepipes and (nextp - NS) <= steps_done and nextp < nbatch + 3:
                    pipe = freepipes.pop()
                    active[nextp] = (prep(nextp, pipe), pipe)
                    nextp += 1
                if cur_steps is None and nexts < nbatch and nexts in prep_done:
                    cur_steps = steps(nexts)
                for m_ in list(active.keys()):
                    g, pipe = active[m_]
                    try:
                        next(g)
                    except StopIteration:
                        del active[m_]
                        prep_done.add(m_)
                        freepipes.append(pipe)
                if cur_steps is not None:
                    try:
                        next(cur_steps)
                    except StopIteration:
                        cur_steps = None
                        steps_done = nexts
                        nexts += 1
                if nexts >= nbatch and not active and cur_steps is None:
                    break

        def dn_head(l, h):
            vb = l * NV
            vk = ("vec",)
            SVf = SV[:].rearrange("p a b -> p (a b)").bitcast(F32)
            PCs = [A_PC, SVf[:, 0:2052]]
            CYs = [A_OSUM, SVf[:, 2052:4100]]
            for par in range(2):
                mset("pool", PCs[par][:, 0:2], 0.0, [("pc", par, "pad")])
                mset("pool", PCs[par][:, 2050:2052], 0.0, [("pc", par, "pad")])
            state = dict(tt_busy=False)
            finished = {}

            def chunk_gen(ci, par):
                dst, dk = ((A_QT, "qt"), (A_KT, "kt"), (A_VT, "vt"))[ci]
                PCp, CYp = PCs[par], CYs[par]
                slab, ks = wload(l, O_WIN + (ci * 4 + h) * 1024, 1024)
                for t in range(NT):
                    ps, kp = psum()
                    proj_tile(ps, kp, slab, ks, t)
                    cpy("act", PCp[:, 2 + t * 512:2 + (t + 1) * 512], ps[:], [kp], [("pc", par, t)])
                    yield
                wbase = vb + V_DNCW + (ci * 4 + h) * 4
                rk = [("pc", par, t) for t in range(NT)] + [("pc", par, "pad"), vk]
                wk = [("os", n) for n in range(16)] if par == 0 else [("cy", par, t) for t in range(NT)]
                act(CYp, PCp[:, 0:2048], AF.Identity, rk, wk, scale=VEC[:, wbase:wbase + 1])
                yield
                for j in range(1, 4):
                    stt("dve", CYp, PCp[:, j:j + 2048], VEC[:, wbase + j:wbase + j + 1], CYp, ALU.mult, ALU.add, rk + wk, wk)
                    yield
                while state["tt_busy"]:
                    yield
                state["tt_busy"] = True

                def tile_chain(t):
                    tsl = tsl_(t)
                    kcy = [("os", n) for n in range(4 * t, 4 * t + 4)] if par == 0 else [("cy", par, t)]
                    th, kth = TT[:, 2 * t, :], ("tt", 2 * t)
                    ln, kln = TT[:, 2 * t + 1, :], ("tt", 2 * t + 1)
                    sq, ksq = TB[:, t, :], ("tb", t)
                    act(th, CYp[:, tsl], AF.Tanh, kcy, [kth], scale=0.5)
                    yield
                    if ci == 2:
                        stt("dve", dst[:, tsl], th, 1.0, CYp[:, tsl], ALU.add, ALU.mult, [kth] + kcy, [(dk, t)])
                        yield
                        return
                    stt("dve", th, th, 1.0, CYp[:, tsl], ALU.add, ALU.mult, [kth] + kcy, [kth])
                    yield
                    act(sq, th, AF.Square, [kth], [ksq])
                    yield
                    ps, kp = psum()
                    mm(ps[:], ONESB, sq, True, True, [ksq, KC], [kp])
                    act(ln, ps[:], AF.Ln, [kp], [kln], bias=4 * EPS)
                    yield
                    act(ln, ln, AF.Exp, [kln], [kln], scale=-0.5, bias=(-0.5 * math.log(128.0) if ci == 0 else 0.0))
                    yield
                    tten("pool", dst[:, tsl], th, ln, ALU.mult, [kth, kln], [(dk, t)])
                    yield

                yield from rr([tile_chain(t) for t in range(NT)])
                state["tt_busy"] = False
                finished[ci] = True

            gens = {}
            started = 0
            while len(finished) < 3:
                while started < 3 and len(gens) < 2 and (started < 2 or finished.get(started - 2)):
                    gens[started] = chunk_gen(started, started % 2)
                    started += 1
                for ci in list(gens.keys()):
                    try:
                        next(gens[ci])
                    except StopIteration:
                        del gens[ci]
            if DBG["dn_stop"] <= 1:
                return
            for src, sk, dst, dk, sc in ((A_KT, "kt", A_KTOK, "ktok", 1.0), (A_VT, "vt", A_VTOK, "vtok", 0.5)):
                for half in range(2):
                    pst, kpst = psum()
                    pstb = pst[:].bitcast(BF16)
                    for n in range(8):
                        tile = half * 8 + n
                        S.op("pe", lambda hh, o=pstb[:, n * 128:(n + 1) * 128], i_=src[:, tile * 128:(tile + 1) * 128]:
                             hh.transpose(out=o, in_=i_, identity=IDB), [(sk, tile // 4), KC], [kpst])
                    act(dst[:, half * 1024:(half + 1) * 1024], pstb[:, 0:1024], AF.Identity, [kpst], [(dk,)], scale=sc)
            if DBG["dn_stop"] <= 2:
                return
            S.barrier()
            gdn(l, h)
            S.barrier()
            if DBG["dn_stop"] <= 4:
                return
            zslab, kzs = wload(l, O_WIN + (12 + h) * 1024, 1024)

            def out_chain(t):
                tsl = tsl_(t)
                osk = [("os", n) for n in range(t * 4, t * 4 + 4)]
                ln, kln = TT[:, 2 * t, :], ("tt", 2 * t)
                th, kth = TT[:, 2 * t + 1, :], ("tt", 2 * t + 1)
                sq, ksq = TB[:, t, :], ("tb", t)
                act(sq, A_OSUM[:, tsl], AF.Square, osk, [ksq])
                yield
                psz, kpz = psum()
                proj_tile(psz, kpz, zslab, kzs, t)
                act(th, psz[:], AF.Tanh, [kpz], [kth], scale=0.5)
                stt("dve", th, th, 1.0, psz[:], ALU.add, ALU.mult, [kth, kpz], [kth])
                yield
                ps, kp = psum()
                mm(ps[:], ONESB, sq, True, True, [ksq, KC], [kp])
                act(ln, ps[:], AF.Ln, [kp], [kln], bias=EPS, scale=1.0 / 128)
                yield
                act(ln, ln, AF.Exp, [kln], [kln], scale=-0.5)
                yield
                stt("dve", ln, A_OSUM[:, tsl], VEC[:, vb + V_DNG:vb + V_DNG + 1], ln, ALU.mult, ALU.mult, osk + [kln, vk], [kln])
                yield
                stt("dve", A_VT[:, tsl], ln, 0.5, th, ALU.mult, ALU.mult, [kln, kth], [("vt", t)])
                yield

            interleave([out_chain(t) for t in range(NT)])
            wout_pass(l, [h], lambda i, t: A_VT[:, tsl_(t)], lambda i, t: ("vt", t))

        def ffn_phase(l):
            vb = l * NV
            vk = ("vec",)
            rms_to_hn(vb + V_G2)
            ACTB = ARENA[:, 0:11264].rearrange("p (f n) -> p f n", n=512)
            GP = ARENA[:, 11264:15376].bitcast(F32).rearrange("p (b n) -> p b n", b=4)
            NW = 3
            st = dict(down_q=0)

            def fc_chain(qt, fc, idx):
                t0 = qt * 512
                sg, ksg = wload(l, O_WG + fc * 1024, 1024)
                gb = idx % 4
                gp = GP[:, gb, :]
                kgp = ("gp", gb)
                cg, kcg = TT[:, 2 * (idx % 4), :], ("tt", 2 * (idx % 4))
                x2, k2 = TT[:, 2 * (idx % 4) + 1, :], ("tt", 2 * (idx % 4) + 1)
                psg, kpg = psum()
                proj_tile(psg, kpg, sg, ksg, qt)
                psh, kph = psum()
                sides = []
                if qt > 0:
                    sides.append((0, t0 - 1))
                if qt < NT - 1:
                    sides.append((1, t0 + 512))
                for (si, col) in sides:
                    for kc in range(8):
                        mm(psh[:, si:si + 1], sg[:, kc * 128:(kc + 1) * 128], HN[:, kc, col:col + 1], kc == 0, kc == 7,
                           [ksg, ("HN", kc, col // 512)], [kph])
                cpy("act", gp[:, 1:513], psg[:], [kpg], [kgp])
                if qt > 0:
                    cpy("dve", gp[:, 0:1], psh[:, 0:1], [kph, kgp], [kgp])
                else:
                    mset("dve", gp[:, 0:1], 0.0, [kgp])
                if qt < NT - 1:
                    cpy("dve", gp[:, 513:514], psh[:, 1:2], [kph, kgp], [kgp])
                else:
                    mset("dve", gp[:, 513:514], 0.0, [kgp])
                yield
                wb_ = vb + V_FCW + fc * 3
                tsc("pool", cg, gp[:, 0:512], VEC[:, wb_:wb_ + 1], VEC[:, vb + V_FCB + fc:vb + V_FCB + fc + 1], ALU.mult, ALU.add,
                    [kgp, vk], [kcg])
                yield
                stt("dve", cg, gp[:, 1:513], VEC[:, wb_ + 1:wb_ + 2], cg, ALU.mult, ALU.add, [kgp, vk, kcg], [kcg])
                yield
                stt("dve", cg, gp[:, 2:514], VEC[:, wb_ + 2:wb_ + 3], cg, ALU.mult, ALU.add, [kgp, vk, kcg], [kcg])
                yield
                act(x2, cg, AF.Square, [kcg], [k2])
                yield
                tsc("pool", x2, x2, GC2, GC1, ALU.mult, ALU.add, [k2], [k2])
                yield
                tten("pool", x2, x2, cg, ALU.mult, [k2, kcg], [k2])
                yield
                act(x2, x2, AF.Tanh, [k2], [k2])
                yield
                stt("dve", cg, x2, 1.0, cg, ALU.add, ALU.mult, [k2, kcg], [kcg])
                yield
                while st["down_q"] < qt:
                    yield
                su, ksu = wload(l, O_WU + fc * 1024, 1024)
                psu, kpu = psum()
                proj_tile(psu, kpu, su, ksu, qt)
                stt("dve", ACTB[:, fc, :], cg, 0.5, psu[:], ALU.mult, ALU.mult, [kcg, kpu], [("actb", fc)])
                yield

            def down_gen(qt):
                for m in range(8):
                    ps, kp = psum()
                    pieces = ((0, 8), (8, 16), (16, 22))
                    for (f0, f1) in pieces:
                        sl, ksl = wload(l, O_WD + m * 2816 + f0 * 128, (f1 - f0) * 128)
                        for fc in range(f0, f1):
                            mm(ps[:], sl[:, (fc - f0) * 128:(fc - f0 + 1) * 128], ACTB[:, fc, :], fc == 0, fc == NFC - 1,
                               [ksl, ("actb", fc)], [kp])
                    radd(m, qt, ps, kp)
                    yield
                st["down_q"] = qt + 1

            chains = [(qt, fc) for qt in range(NT) for fc in range(NFC)]
            active = []
            ci = 0
            down = None
            nfin = {qt: 0 for qt in range(NT)}
            while ci < len(chains) or active or down is not None:
                while ci < len(chains) and len(active) < NW:
                    qt, fc = chains[ci]
                    active.append((fc_chain(qt, fc, ci), qt))
                    ci += 1
                for item in list(active):
                    g, qt = item
                    try:
                        next(g)
                    except StopIteration:
                        active.remove(item)
                        nfin[qt] += 1
                        if nfin[qt] == NFC:
                            down = down_gen(qt)
                if down is not None:
                    try:
                        next(down)
                    except StopIteration:
                        down = None

        def ple_phase(l, s):
            vb = l * NV
            rms_to_hn(vb + V_GP)
            PB = ARENA[:, 0:4096].rearrange("p (k n) -> p k n", k=2)
            PF = ARENA[:, 4096:8192].bitcast(F32).rearrange("p (b k n) -> p b k n", b=2, k=2)
            psrc = pT[l, s].rearrange("(k p) n -> p k n", p=128)
            for t in range(NT):
                S.dma(PF[:, t % 2], psrc[:, :, tsl_(t)], [], [("pf", t % 2)])
                cpy("pool", PB[:, :, tsl_(t)], PF[:, t % 2], [("pf", t % 2)], [("pb", t)])
            for m in range(8):
                sg, ksg = wload(l, O_PWG + m * 1024, 1024)
                sp_, ksp_ = wload(l, O_PWP + m * 256, 256)
                for t in range(NT):
                    ps1, kp1 = psum()
                    proj_tile(ps1, kp1, sg, ksg, t)
                    th, kth = tt()
                    act(th, ps1[:], AF.Tanh, [kp1, ("lv",)], [kth], scale=0.5, bias=LV[:, 40 + m:41 + m])
                    ps2, kp2 = psum()
                    for k2 in range(2):
                        mm(ps2[:], sp_[:, k2 * 128:(k2 + 1) * 128], PB[:, k2, tsl_(t)], k2 == 0, k2 == 1, [ksp_, ("pb", t)], [kp2])
                    stt("dve", th, th, 1.0, ps2[:], ALU.add, ALU.mult, [kth, kp2], [kth])
                    stt("dve", R[:, m, tsl_(t)], th, 0.5, R[:, m, tsl_(t)], ALU.mult, ALU.add, [kth, ("R", m, t)], [("R", m, t)])


        for s in range(nseq):
            for ch in range(8):
                S.dma(R[:, ch, :], xT[s, ch * 128:(ch + 1) * 128, :], [], [("R", ch, t) for t in range(NT)])
            for l in range(nlayers):
                layer_prep(l)
                rms_to_hn(l * NV + V_G1)
                if "lru" in phases or "dn" in phases:
                    ab_phase(l)
                S.barrier()
                if "lru" in phases:
                    lru_phase(l)
                    S.barrier()
                if "dn" in phases:
                    for h in range(DBG["heads"]):
                        dn_head(l, h)
                    S.barrier()
                if "ffn" in phases:
                    ffn_phase(l)
                    S.barrier()
                if "ple" in phases:
                    ple_phase(l, s)
                    S.barrier()
            S.barrier()
            if dump_r:
                for ch in range(8):
                    S.dma(yT[s, ch * 128:(ch + 1) * 128, :], R[:, ch, :], [("R", ch, t) for t in range(NT)], [("y", s, ch)])
            else:
                for t in range(NT):
                    rs, kr = rms_stats(t, 8, lambda ch, t: R[:, ch, tsl_(t)], lambda ch, t: ("R", ch, t), 1.0 / D, EPS)
                    for ch in range(8):
                        o, ko = tt()
                        stt("dve" if ch % 2 == 0 else "pool", o, R[:, ch, tsl_(t)], VEC[:, L * NV + ch:L * NV + ch + 1], rs,
                            ALU.mult, ALU.mult, [("R", ch, t), kr], [ko])
                        S.dma(yT[s, ch * 128:(ch + 1) * 128, tsl_(t)], o, [ko], [("y", s, ch, t)])
            S.barrier()
        S.barrier()
        print("instructions:", S.ninst, {k: v["n"] for k, v in S.E.items()})
    return nc


def _slabs_kn(w):
    K, N = w.shape
    a = w.reshape(K // 128, 128, N // 128, 128)
    return np.ascontiguousarray(a.transpose(2, 1, 0, 3)).reshape(N // 128, 128, K)


def pack_weights(inp):
    wpk = np.zeros((L, 128, WTOT), np.float32)
    for l in range(L):
        w_in = inp["w_in"][l]
        cols = np.concatenate([w_in[:, 0:2048], w_in[:, 2064:3088]], axis=1)
        sl = _slabs_kn(cols)
        wpk[l, :, O_WIN:O_WIN + 24 * 1024] = sl.transpose(1, 0, 2).reshape(128, 24 * 1024)
        ab = w_in[:, 2048:2064].reshape(8, 128, 16).transpose(1, 0, 2).reshape(128, 128)
        wpk[l, :, O_WAB:O_WAB + 128] = ab
        lw = np.zeros((128, 4, 2, 2, 128), np.float32)
        for d in range(2):
            for gi, nm in enumerate(("lru_wa", "lru_wx")):
                wg = inp[nm][l, d]
                for c in range(4):
                    lw[0:64, c, d, gi, 0:64] = wg[2 * c]
                    lw[64:128, c, d, gi, 64:128] = wg[2 * c + 1]
        wpk[l, :, O_LRUW:O_LRUW + 2048] = lw.reshape(128, 2048)
        wpk[l, :, O_WOUT:O_WOUT + 8192] = inp["w_out"][l].reshape(8, 128, 1024).transpose(1, 0, 2).reshape(128, 8192)
        wpk[l, :, O_WG:O_WG + 22528] = _slabs_kn(inp["ffn_wg"][l]).transpose(1, 0, 2).reshape(128, 22528)
        wpk[l, :, O_WU:O_WU + 22528] = _slabs_kn(inp["ffn_wu"][l]).transpose(1, 0, 2).reshape(128, 22528)
        wd = inp["ffn_wd"][l].reshape(NFC, 128, 8, 128)
        wpk[l, :, O_WD:O_WD + 22528] = wd.transpose(1, 2, 0, 3).reshape(128, 22528)
        wpk[l, :, O_PWG:O_PWG + 8192] = _slabs_kn(inp["ple_wg"][l]).transpose(1, 0, 2).reshape(128, 8192)
        wpk[l, :, O_PWP:O_PWP + 2048] = _slabs_kn(inp["ple_wp"][l]).transpose(1, 0, 2).reshape(128, 2048)
    return wpk


def pack_vecs(inp):
    v = np.zeros((128, L * NV + 8), np.float32)

    def cm(a, n):
        return a.reshape(n, 128).T

    for l in range(L):
        b = l * NV
        v[:, b + V_G1:b + V_G1 + 8] = cm(inp["norm1_g"][l], 8)
        v[:, b + V_G2:b + V_G2 + 8] = cm(inp["norm2_g"][l], 8)
        v[:, b + V_GP:b + V_GP + 8] = cm(inp["ple_norm_g"][l], 8)
        v[:, b + V_BGP:b + V_BGP + 8] = cm(inp["ple_bg"][l], 8)
        v[:, b + V_DNCW:b + V_DNCW + 48] = inp["dn_conv_w"][l].reshape(4, 12, 128).transpose(2, 1, 0).reshape(128, 48)
        v[:, b + V_LCW:b + V_LCW + 16] = inp["lru_conv_w"][l].reshape(4, 4, 128).transpose(2, 1, 0).reshape(128, 16)
        v[:, b + V_LCB:b + V_LCB + 4] = cm(inp["lru_conv_b"][l], 4)
        v[:, b + V_LBA:b + V_LBA + 8] = inp["lru_ba"][l].reshape(2, 4, 128).transpose(2, 0, 1).reshape(128, 8)
        v[:, b + V_LBX:b + V_LBX + 8] = inp["lru_bx"][l].reshape(2, 4, 128).transpose(2, 0, 1).reshape(128, 8)
        v[:, b + V_LLAM:b + V_LLAM + 8] = inp["lru_lambda"][l].reshape(2, 4, 128).transpose(2, 0, 1).reshape(128, 8)
        v[:, b + V_LNG:b + V_LNG + 4] = cm(inp["lru_norm_g"][l], 4)
        v[:, b + V_DNG] = inp["dn_norm_g"][l]
        v[:, b + V_FCW:b + V_FCW + 66] = inp["ffn_conv_w"][l].reshape(3, NFC, 128).transpose(2, 1, 0).reshape(128, 66)
        v[:, b + V_FCB:b + V_FCB + 22] = cm(inp["ffn_conv_b"][l], NFC)
        v[:, b + V_ALOG:b + V_ALOG + 8] = np.broadcast_to(inp["dn_a_log"][l].reshape(1, 8), (128, 8))
        v[:, b + V_DTB:b + V_DTB + 8] = np.broadcast_to(inp["dn_dt_bias"][l].reshape(1, 8), (128, 8))
    v[:, L * NV:L * NV + 8] = cm(inp["final_g"], 8)
    return v


def kernel(**inputs):
    inp = {k: np.asarray(v) for k, v in inputs.items()}
    x = inp["x"]
    p = inp["p"]
    wpk = pack_weights(inp)
    vecs = pack_vecs(inp)
    xT = np.ascontiguousarray(x.transpose(0, 2, 1))
    pT = np.ascontiguousarray(p.transpose(0, 1, 3, 2))
    nc = build_nc()
    in_maps = []
    for c in range(NCORE):
        in_maps.append({
            "xT": xT[c * SEQ_PER_CORE:(c + 1) * SEQ_PER_CORE],
            "pT": np.ascontiguousarray(pT[:, c * SEQ_PER_CORE:(c + 1) * SEQ_PER_CORE]),
            "wpk": wpk,
            "vecs": vecs,
        })
    res = run_bass_kernel_spmd(nc, in_maps, core_ids=list(range(NCORE)))
    yT = np.concatenate([r["yT"] for r in res.results], axis=0)
    return np.ascontiguousarray(yT.transpose(0, 2, 1)).astype(np.float32)
```

```python
import math
import numpy as np
from contextlib import ExitStack
import concourse.bass as bass
import concourse.mybir as mybir
from concourse.bass_utils import run_bass_kernel_spmd

F32 = mybir.dt.float32
BF16 = mybir.dt.bfloat16
AF = mybir.ActivationFunctionType
ALU = mybir.AluOpType

D = 1024
S_LEN = 2048
L = 4
NCORE = 8
SEQ_PER_CORE = 4
DFF = 2816
NFC = 22
PLE = 256
EPS = 1e-6
NT = 4
TK = 16
GC1 = math.sqrt(2.0 / math.pi)
GC2 = GC1 * 0.044715

O_WIN = 0
O_WAB = O_WIN + 24 * 1024
O_LRUW = O_WAB + 128
O_WOUT = O_LRUW + 2048
O_WG = O_WOUT + 8192
O_WU = O_WG + 22528
O_WD = O_WU + 22528
O_PWG = O_WD + 22528
O_PWP = O_PWG + 8192
WTOT = O_PWP + 2048

V_G1, V_G2, V_GP, V_BGP = 0, 8, 16, 24
V_DNCW = 32
V_LCW = 80
V_LCB = 96
V_LBA = 100
V_LBX = 108
V_LLAM = 116
V_LNG = 124
V_DNG = 128
V_FCW = 129
V_FCB = 195
V_ALOG = 217
V_DTB = 225
NV = 240


class Sched:
    NDS = 24

    def __init__(self, nc, es):
        self.nc = nc
        self.E = {}
        for name, h in (("pe", nc.tensor), ("act", nc.scalar), ("dve", nc.vector), ("pool", nc.gpsimd), ("sp", nc.sync)):
            sem = es.enter_context(nc.semaphore("s_" + name))
            self.E[name] = dict(h=h, sem=sem, n=0, seen={}, seend={})
        self.dsem = [es.enter_context(nc.semaphore(f"sd{i}")) for i in range(self.NDS)]
        self.dcnt = [0] * self.NDS
        self.dnext = 0
        self.lastw = {}
        self.readers = {}
        self.ninst = 0

    def _wait(self, eng, tok, same_ok):
        X = self.E[eng]
        if tok[0] == "e":
            _, p, c = tok
            if p == eng and (eng == "pe" or not same_ok):
                return
            if X["seen"].get(p, 0) >= c:
                return
            X["h"].wait_ge(self.E[p]["sem"], c)
            X["seen"][p] = c
        else:
            _, i, v = tok
            if X["seend"].get(i, 0) >= v:
                return
            X["h"].wait_ge(self.dsem[i], v)
            X["seend"][i] = v

    def _deps(self, eng, r, w):
        for k in r:
            t = self.lastw.get(k)
            if t is not None:
                self._wait(eng, t, True)
            if k[0] == "ps":
                for t in self.readers.get(k, {}).values():
                    self._wait(eng, t, False)
        for k in w:
            t = self.lastw.get(k)
            if t is not None:
                self._wait(eng, t, True)
            for t in self.readers.get(k, {}).values():
                self._wait(eng, t, False)

    def _record(self, tok, r, w):
        for k in r:
            self.readers.setdefault(k, {})[tok[1]] = tok
        for k in w:
            self.lastw[k] = tok
            self.readers[k] = {}

    def op(self, eng, emit, r=(), w=()):
        X = self.E[eng]
        self._deps(eng, r, w)
        inst = emit(X["h"])
        X["n"] += 1
        inst.then_inc(X["sem"], 1)
        self.ninst += 1
        self._record(("e", eng, X["n"]), r, w)

    def dma(self, out, in_, r=(), w=(), eng="sp"):
        self._deps(eng, r, w)
        i = self.dnext
        self.dnext = (i + 1) % self.NDS
        if self.dcnt[i] > 0:
            self._wait(eng, ("d", i, 16 * self.dcnt[i]), True)
        inst = self.E[eng]["h"].dma_start(out=out, in_=in_)
        self.dcnt[i] += 1
        inst.then_inc(self.dsem[i], 16)
        self.ninst += 1
        self._record(("d", i, 16 * self.dcnt[i]), r, w)

    def barrier(self):
        comp = ("pe", "act", "dve", "pool")
        for e in comp:
            for p in comp:
                if p != e and self.E[p]["n"] > 0:
                    self._wait(e, ("e", p, self.E[p]["n"]), True)
            for i in range(self.NDS):
                if self.dcnt[i] > 0:
                    self._wait(e, ("d", i, 16 * self.dcnt[i]), True)
        for p in comp:
            if self.E[p]["n"] > 0:
                self._wait("sp", ("e", p, self.E[p]["n"]), True)
        for i in range(self.NDS):
            if self.dcnt[i] > 0:
                self._wait("sp", ("d", i, 16 * self.dcnt[i]), True)
        self.lastw = {}
        self.readers = {}


def rev_ap(ap2d):
    a = ap2d.ap
    n = a[-1][1]
    st = a[-1][0]
    return bass.AP(ap2d.tensor, ap2d.offset + (n - 1) * st, [list(x) for x in a[:-1]] + [[-st, n]])


DBG = dict(dn_stop=99, heads=4, gdn_m=8)


def interleave(gens):
    gens = list(gens)
    while gens:
        for g in list(gens):
            try:
                next(g)
            except StopIteration:
                gens.remove(g)


def build_nc(nseq=SEQ_PER_CORE, nlayers=L, dump_r=False, do_prepass=True, phases=("lru", "dn", "ffn", "ple"), ndbg=0):
    nc = bass.Bass("TRN2", target_bir_lowering=False)
    xT = nc.dram_tensor("xT", [SEQ_PER_CORE, D, S_LEN], F32, kind="ExternalInput").ap()
    pT = nc.dram_tensor("pT", [L, SEQ_PER_CORE, PLE, S_LEN], F32, kind="ExternalInput").ap()
    wpk = nc.dram_tensor("wpk", [L, 128, WTOT], F32, kind="ExternalInput").ap()
    vecs = nc.dram_tensor("vecs", [128, L * NV + 8], F32, kind="ExternalInput").ap()
    yT = nc.dram_tensor("yT", [SEQ_PER_CORE, D, S_LEN], F32, kind="ExternalOutput").ap()
    wsc = nc.dram_tensor("wsc", [L, 128, WTOT], BF16, kind="Internal").ap()
    dbg_out = None
    if ndbg:
        dbg_out = nc.dram_tensor("dbg", [ndbg, 128, S_LEN], F32, kind="ExternalOutput").ap()

    es = ExitStack()
    with es:
        def sb(name, shape, dt):
            return es.enter_context(nc.sbuf_tensor(name, shape, dt))

        S = Sched(nc, es)
        R = sb("R", [128, 8, S_LEN], F32)
        HN = sb("HN", [128, 8, S_LEN], BF16)
        ARENA = sb("ARENA", [128, 18560], BF16)
        TT = sb("TT", [128, 8, 512], F32)
        TB = sb("TB", [128, 6, 512], BF16)
        SV = sb("SV", [128, 20, 512], BF16)
        SF = sb("SF", [128, 512], F32)
        WB = sb("WB", [128, 5, 1024], BF16)
        VEC = sb("VEC", [128, L * NV + 8], F32)
        CST = sb("CST", [128, 13, 128], F32)
        CSB = sb("CSB", [128, 7, 128], BF16)
        MISC = sb("MISC", [128, 12, 128], F32)
        LV = sb("LV", [128, 64], F32)
        SST = sb("SST", [128, 2, 128], F32)
        SBF = sb("SBF", [128, 2, 128], BF16)
        CAR = sb("CAR", [128, 4], F32)
        PS = [es.enter_context(nc.psum_tensor(f"ps{i}", [128, 512], F32)) for i in range(8)]
        cnt = dict(ps=0, tt=0, tb=0, wb=0, sv=0, ev=0)

        def psum():
            i = cnt["ps"] % 8
            cnt["ps"] += 1
            return PS[i], ("ps", i)

        def tt():
            i = cnt["tt"] % 8
            cnt["tt"] += 1
            return TT[:, i, :], ("tt", i)

        def tb():
            i = cnt["tb"] % 6
            cnt["tb"] += 1
            return TB[:, i, :], ("tb", i)

        def wslot():
            i = cnt["wb"] % 5
            cnt["wb"] += 1
            return WB[:, i, :], ("wb", i)

        def sv():
            i = 10 + cnt["sv"] % 10
            cnt["sv"] += 1
            return SV[:, i, :], ("sv", i)

        def act(out, in_, func, r, w, bias=0.0, scale=1.0):
            S.op("act", lambda h: h.activation(out=out, in_=in_, func=func, bias=bias, scale=scale), r, w)

        def tsc(eng, out, in0, s1, s2, op0, op1, r, w):
            S.op(eng, lambda h: h.tensor_scalar(out=out, in0=in0, scalar1=s1, scalar2=s2, op0=op0, op1=op1), r, w)

        def ts1(eng, out, in_, s, op, r, w):
            S.op(eng, lambda h: h.tensor_single_scalar(out=out, in_=in_, scalar=s, op=op), r, w)

        def stt(eng, out, in0, s, in1, op0, op1, r, w):
            eng = "dve"
            S.op(eng, lambda h: h.scalar_tensor_tensor(out=out, in0=in0, scalar=s, in1=in1, op0=op0, op1=op1), r, w)

        def tten(eng, out, in0, in1, op, r, w):
            S.op(eng, lambda h: h.tensor_tensor(out=out, in0=in0, in1=in1, op=op), r, w)

        def cpy(eng, out, in_, r, w):
            if eng == "act":
                act(out, in_, AF.Identity, r, w)
            else:
                S.op(eng, lambda h: h.tensor_copy(out=out, in_=in_), r, w)

        def mm(out, lhsT, rhs, start, stop, r, w):
            S.op("pe", lambda h: h.matmul(out, lhsT=lhsT, rhs=rhs, start=start, stop=stop), r, w)

        def mset(eng, ap, val, w):
            S.op(eng, lambda h: h.memset(ap, val), (), w)

        INCF, INCB, OFFD, ONESF, IDF = (CST[:, i, :] for i in range(5))
        NEG4 = CST[:, 5:9, :]
        M16, ML0, ML1, ML2, IDB, ONESB = (CSB[:, i, :] for i in range(6))
        MLV = [ML0, ML1, ML2]
        KC = ("cst",)

        def b4(ap):
            return ap.unsqueeze(1).to_broadcast([128, 4, 128])

        def v4(ap):
            return ap.rearrange("p (e i) -> p e i", e=4)

        def aff(ap, pattern, op, fill, base, cm):
            S.op("pool", lambda h: h.affine_select(out=ap, in_=ap, pattern=pattern, compare_op=op, fill=fill,
                                                   base=base, channel_multiplier=cm), [KC], [KC])

        mset("pool", CST[:, 0:5, :], 1.0, [KC])
        aff(INCF, [[1, 128]], ALU.is_ge, 0.0, 0, -1)
        aff(INCB, [[-1, 128]], ALU.is_ge, 0.0, 0, 1)
        aff(OFFD, [[1, 128]], ALU.not_equal, 0.0, 0, -1)
        aff(IDF, [[1, 128]], ALU.is_equal, 0.0, 0, -1)
        for e in range(4):
            tsc("pool", NEG4[:, e, :], INCF if e < 2 else INCB, -1.0, 1e30, ALU.add, ALU.mult, [KC], [KC])
        mset("pool", CST[:, 9:13, :], 1.0, [KC])
        for bi, b in enumerate((16, 32, 64)):
            nb = 128 // b
            v = CST[:, 9 + bi, :].rearrange("p (k c) -> p k c", c=b)
            aff(v, [[-b, nb], [0, b]], ALU.is_ge, 0.0, 0, 1)
            aff(v, [[b, nb], [0, b]], ALU.is_gt, 0.0, b, -1)
        cpy("pool", M16, CST[:, 9, :], [KC], [KC])
        for k in range(3):
            tten("pool", MLV[k], CST[:, 10 + k, :], CST[:, 9 + k, :], ALU.subtract, [KC], [KC])
        cpy("pool", IDB, IDF, [KC], [KC])
        cpy("pool", ONESB, ONESF, [KC], [KC])
        NM16 = CSB[:, 6, :]
        ts1("pool", NM16, CST[:, 9, :], -1.0, ALU.mult, [KC], [KC])
        OFFB = sb("OFFB", [128, 128], BF16)
        cpy("pool", OFFB[:], OFFD, [KC], [KC])
        S.dma(VEC[:], vecs[:, :], [], [("vec",)])
        S.barrier()

        if do_prepass:
            CH = 8192
            for l in range(nlayers):
                c0 = 0
                while c0 < WTOT:
                    cw = min(CH, WTOT - c0)
                    S.dma(wsc[l, :, c0:c0 + cw], wpk[l, :, c0:c0 + cw], [], [("wsc", l, c0)], eng="pool")
                    c0 += cw
            S.barrier()

        def wload(l, off, width):
            slot, k = wslot()
            S.dma(slot[:, 0:width], wsc[l, :, off:off + width], [], [k])
            return slot, k

        def tsl_(t):
            return slice(t * 512, (t + 1) * 512)

        def rms_stats(t, nch, src, srck, scale, eps):
            ps, kp = psum()
            for ch in range(nch):
                sq, ks = tb()
                act(sq, src(ch, t), AF.Square, [srck(ch, t)], [ks])
                mm(ps[:], ONESB, sq, ch == 0, ch == nch - 1, [ks], [kp])
            ln, kl = tt()
            act(ln, ps[:], AF.Ln, [kp], [kl], bias=eps, scale=scale)
            rs, kr = tt()
            act(rs, ln, AF.Exp, [kl], [kr], scale=-0.5)
            return rs, kr

        def rms_to_hn(gbase):
            pss = []
            for t in range(NT):
                ps, kp = psum()
                for ch in range(8):
                    sq, ks = tb()
                    act(sq, R[:, ch, tsl_(t)], AF.Square, [("R", ch, t)], [ks])
                    mm(ps[:], ONESB, sq, ch == 0, ch == 7, [ks], [kp])
                pss.append((ps, kp))
            rss = []
            for t in range(NT):
                ps, kp = pss[t]
                rs, kr = TT[:, 4 + t, :], ("tt", 4 + t)
                act(rs, ps[:], AF.Ln, [kp], [kr], bias=EPS, scale=1.0 / D)
                rss.append((rs, kr))
            for t in range(NT):
                rs, kr = rss[t]
                act(rs, rs, AF.Exp, [kr], [kr], scale=-0.5)
            for t in range(NT):
                rs, kr = rss[t]
                for ch in range(8):
                    stt("dve", HN[:, ch, tsl_(t)], R[:, ch, tsl_(t)],
                        VEC[:, gbase + ch:gbase + ch + 1], rs, ALU.mult, ALU.mult, [("R", ch, t), kr, ("vec",)], [("HN", ch, t)])

        def proj_tile(ps, kp, slab, ks, t):
            for kc in range(8):
                mm(ps[:], slab[:, kc * 128:(kc + 1) * 128], HN[:, kc, tsl_(t)], kc == 0, kc == 7, [ks, ("HN", kc, t)], [kp])

        def radd(m, t, ps, kp):
            tten("dve", R[:, m, tsl_(t)], ps[:], R[:, m, tsl_(t)], ALU.add, [kp, ("R", m, t)], [("R", m, t)])

        def dbg_store(idx, ap2d, keys):
            if dbg_out is not None and idx < ndbg:
                S.dma(dbg_out[idx, :, 0:ap2d.shape[-1]], ap2d, keys, [("dbg", idx)])

        A_QT = ARENA[:, 0:2048]
        A_KT = ARENA[:, 2048:4096]
        A_VT = ARENA[:, 4096:6144]
        A_KTOK = ARENA[:, 6144:8192]
        A_VTOK = ARENA[:, 8192:10240]
        A_OSUM = ARENA[:, 10240:14336].bitcast(F32)
        A_PC = ARENA[:, 14336:18440].bitcast(F32)
        A_XCB = ARENA[:, 0:2048]
        A_GL = ARENA[:, 2048:6144].bitcast(F32)
        A_HS = ARENA[:, 6144:10240].bitcast(F32)
        A_XC = A_OSUM
        YL = SV[:, 0:16, :].rearrange("p (c a) n -> p c (a n)", c=4)

        def pc_pads():
            mset("pool", A_PC[:, 0:2], 0.0, [("pc", "pad")])
            mset("pool", A_PC[:, 2050:2052], 0.0, [("pc", "pad")])

        def conv4(dst, dstk, wbase, bias_ap, vk):
            rk = [("pc", t) for t in range(NT)] + [("pc", "pad"), vk]
            wk = [(dstk, t) for t in range(NT)]
            if bias_ap is None:
                ts1("pool", dst, A_PC[:, 0:2048], VEC[:, wbase:wbase + 1], ALU.mult, rk, wk)
            else:
                tsc("pool", dst, A_PC[:, 0:2048], VEC[:, wbase:wbase + 1], bias_ap, ALU.mult, ALU.add, rk, wk)
            for j in range(1, 4):
                stt("pool" if j % 2 else "dve", dst, A_PC[:, j:j + 2048], VEC[:, wbase + j:wbase + j + 1], dst,
                    ALU.mult, ALU.add, rk + wk, wk)

        def wout_pass(l, kcs, src, srck):
            slabs = [wload(l, O_WOUT + kc * 1024, 1024) for kc in kcs]
            for m in range(8):
                for t in range(NT):
                    ps, kp = psum()
                    for i, (sl, ks) in enumerate(slabs):
                        mm(ps[:], sl[:, m * 128:(m + 1) * 128], src(i, t), i == 0, i == len(slabs) - 1, [ks, srck(i, t)], [kp])
                    radd(m, t, ps, kp)

        def layer_prep(l):
            vb = l * NV
            kl = ("lv",)
            vk = ("vec",)
            act(LV[:, 48:56], VEC[:, vb + V_ALOG:vb + V_ALOG + 8], AF.Exp, [vk], [kl])
            ts1("dve", LV[:, 0:8], LV[:, 48:56], -1.0, ALU.mult, [kl], [kl])
            ts1("dve", LV[:, 8:16], VEC[:, vb + V_LBA:vb + V_LBA + 8], 0.5, ALU.mult, [vk, kl], [kl])
            ts1("dve", LV[:, 16:24], VEC[:, vb + V_LBX:vb + V_LBX + 8], 0.5, ALU.mult, [vk, kl], [kl])
            act(LV[:, 48:56], VEC[:, vb + V_LLAM:vb + V_LLAM + 8], AF.Exp, [vk, kl], [kl], scale=-1.0)
            act(LV[:, 56:64], LV[:, 48:56], AF.Ln, [kl], [kl], bias=1.0)
            ts1("dve", LV[:, 24:32], LV[:, 56:64], -8.0, ALU.mult, [kl], [kl])
            ts1("dve", LV[:, 32:40], LV[:, 56:64], -4.0, ALU.mult, [kl], [kl])
            ts1("dve", LV[:, 40:48], VEC[:, vb + V_BGP:vb + V_BGP + 8], 0.5, ALU.mult, [vk, kl], [kl])

        def m3(i):
            return MISC[:, i, :].rearrange("p (t h) -> p t h", h=8)

        AB = MISC[:, 0:2, :].rearrange("p a b -> p (a b)").rearrange("p (t c) -> p t c", c=16)
        G3, NLNB, CCOL, CLAST, BIASD, NEGEC, BDEC, ECL, TM1, TM2 = (m3(i) for i in range(2, 12))
        KM = ("misc",)

        def ab_phase(l):
            vb = l * NV
            slab, ks = wload(l, O_WAB, 128)
            ps, kp = psum()
            for n in range(TK):
                for kc in range(8):
                    mm(ps[:, n * 16:(n + 1) * 16], HN[:, kc, n * 128:(n + 1) * 128], slab[:, kc * 16:(kc + 1) * 16],
                       kc == 0, kc == 7, [ks, ("HN", kc, n // 4)], [kp])
            cpy("dve", MISC[:, 0:2, :].rearrange("p a b -> p (a b)"), ps[:, 0:256], [kp], [KM])
            r = [KM, ("lv",), ("vec",)]
            act(TM1, AB[:, :, 0:8], AF.Exp, r, [KM], scale=-1.0)
            act(NLNB, TM1, AF.Ln, r, [KM], bias=1.0)
            tten("dve", TM2, AB[:, :, 8:16], VEC[:, vb + V_DTB:vb + V_DTB + 8].unsqueeze(1).to_broadcast([128, 16, 8]), ALU.add, r, [KM])
            act(TM2, TM2, AF.Exp, r, [KM])
            act(TM2, TM2, AF.Ln, r, [KM], bias=1.0)
            tten("dve", G3, TM2, LV[:, 0:8].unsqueeze(1).to_broadcast([128, 16, 8]), ALU.mult, r, [KM])
            ps2, kp2 = psum()
            mm(ps2[:, 0:64].rearrange("p (t h) -> p t h", h=4), INCF, G3[:, :, 0:4], True, True, [KM, KC], [kp2])
            mm(ps2[:, 64:128].rearrange("p (t h) -> p t h", h=4), INCB, G3[:, :, 4:8], True, True, [KM, KC], [kp2])
            mm(ps2[:, 128:256], ONESF, MISC[:, 2, :], True, True, [KM, KC], [kp2])
            cpy("dve", CCOL[:, :, 0:4], ps2[:, 0:64].rearrange("p (t h) -> p t h", h=4), [kp2], [KM])
            cpy("dve", CCOL[:, :, 4:8], ps2[:, 64:128].rearrange("p (t h) -> p t h", h=4), [kp2, KM], [KM])
            cpy("dve", MISC[:, 5, :], ps2[:, 128:256], [kp2, KM], [KM])
            tten("dve", TM1, NLNB, CCOL, ALU.add, r, [KM])
            ts1("dve", BIASD, TM1, -1.0, ALU.mult, r, [KM])
            act(TM2, CCOL, AF.Exp, r, [KM])
            ts1("dve", NEGEC, TM2, -1.0, ALU.mult, r, [KM])
            tten("dve", TM1, BIASD, CLAST, ALU.add, r, [KM])
            act(BDEC, TM1, AF.Exp, r, [KM])
            act(ECL, CLAST, AF.Exp, r, [KM])

        def lru_phase(l):
            vb = l * NV
            vk = ("vec",)
            done = {-1: True}
            conv_done = {-1: True}

            def seq(*gs):
                for g in gs:
                    yield from g

            LWALL = SV[:, 16:20, :].rearrange("p a b -> p (a b)")
            S.dma(LWALL, wsc[l, :, O_LRUW:O_LRUW + 2048], [], [("lwall",)])

            def chunk_gen(c):
                lw, klw = LWALL[:, c * 512:(c + 1) * 512], ("lwall",)
                slab, ks = wload(l, O_WIN + (16 + c) * 1024, 1024)
                while not conv_done.get(c - 1):
                    yield
                if c == 0:
                    pc_pads()
                for t in range(NT):
                    ps, kp = psum()
                    proj_tile(ps, kp, slab, ks, t)
                    cpy("act", A_PC[:, 2 + t * 512:2 + (t + 1) * 512], ps[:], [kp], [("pc", t)])
                    yield
                while not done.get(c - 1):
                    yield
                wbase = vb + V_LCW + c * 4
                rk = [("pc", t) for t in range(NT)] + [("pc", "pad"), vk]
                wk = [("xc", t) for t in range(NT)]
                tsc("dve", A_XC, A_PC[:, 0:2048], VEC[:, wbase:wbase + 1], VEC[:, vb + V_LCB + c:vb + V_LCB + c + 1], ALU.mult, ALU.add, rk, wk)
                yield
                for j in range(1, 4):
                    stt("dve", A_XC, A_PC[:, j:j + 2048], VEC[:, wbase + j:wbase + j + 1], A_XC, ALU.mult, ALU.add, rk + wk, wk)
                    yield
                conv_done[c] = True
                cpy("act", A_XCB, A_XC, wk, [("xcb", t) for t in range(NT)])
                yield
                gslab, kgs = wload(l, O_WIN + (20 + c) * 1024, 1024)

                def gelu_gen():
                    for t in range(NT):
                        x2 = TB[:, 2 * (t % 2):2 * (t % 2) + 2, :].rearrange("p a b -> p (a b)").bitcast(F32)
                        k2 = ("tbf", t % 2)
                        ps, kp = psum()
                        proj_tile(ps, kp, gslab, kgs, t)
                        act(x2, ps[:], AF.Square, [kp], [k2])
                        tsc("pool", x2, x2, GC2, GC1, ALU.mult, ALU.add, [k2], [k2])
                        tten("dve", x2, x2, ps[:], ALU.mult, [k2, kp], [k2])
                        act(x2, x2, AF.Tanh, [k2], [k2])
                        stt("dve", A_GL[:, tsl_(t)], x2, 1.0, ps[:], ALU.add, ALU.mult, [k2, kp], [("gl", t)])
                        yield

                def dir_gen(d):
                    order = list(range(NT)) if d == 0 else list(range(NT - 1, -1, -1))
                    lvi = d * 4 + c
                    for ti, t in enumerate(order):
                        tsl = tsl_(t)
                        base = 4 * (ti % 2)
                        thr, kr_ = TT[:, base + 0, :], ("tt", base + 0)
                        thi, ki_ = TT[:, base + 1, :], ("tt", base + 1)
                        a, ka = TT[:, base + 2, :], ("tt", base + 2)
                        a2, ka2 = TT[:, base + 3, :], ("tt", base + 3)
                        psr, kpr = psum()
                        mm(psr[:], lw[:, (d * 2 + 0) * 128:(d * 2 + 1) * 128], A_XCB[:, tsl], True, True, [klw, ("xcb", t)], [kpr])
                        act(thr, psr[:], AF.Tanh, [kpr, ("lv",)], [kr_], scale=0.5, bias=LV[:, 8 + lvi:9 + lvi])
                        psi, kpi = psum()
                        mm(psi[:], lw[:, (d * 2 + 1) * 128:(d * 2 + 2) * 128], A_XCB[:, tsl], True, True, [klw, ("xcb", t)], [kpi])
                        act(thi, psi[:], AF.Tanh, [kpi, ("lv",)], [ki_], scale=0.5, bias=LV[:, 16 + lvi:17 + lvi])
                        yield
                        act(a, thr, AF.Exp, [kr_, ("lv",)], [ka], scale=LV[:, 32 + lvi:33 + lvi], bias=LV[:, 32 + lvi:33 + lvi])
                        act(a2, thr, AF.Exp, [kr_, ("lv",)], [ka2], scale=LV[:, 24 + lvi:25 + lvi], bias=LV[:, 24 + lvi:25 + lvi])
                        yield
                        act(thr, thr, AF.Tanh, [kr_, ("lv",)], [kr_], scale=LV[:, 32 + lvi:33 + lvi], bias=LV[:, 32 + lvi:33 + lvi])
                        stt("dve", thi, thi, 1.0, A_XC[:, tsl], ALU.add, ALU.mult, [ki_, ("xc", t)], [ki_])
                        yield
                        stt("dve", a2, a2, 1.0, thr, ALU.add, ALU.mult, [ka2, kr_], [ka2])
                        yield
                        act(a2, a2, AF.Ln, [ka2], [ka2], scale=-1.0)
                        yield
                        act(a2, a2, AF.Exp, [ka2], [ka2], scale=0.5)
                        yield
                        stt("dve", thi, thi, 0.5, a2, ALU.mult, ALU.mult, [ki_, ka2], [ki_])
                        yield
                        if d == 0:
                            init = 0.0 if ti == 0 else A_HS[:, t * 512 - 1:t * 512]
                            rr_ = [ka, ki_] + ([("hs", t - 1)] if ti else [])
                            S.op("dve", lambda h, o=A_HS[:, tsl], a_=a, b_=thi, i_=init: h.tensor_tensor_scan(
                                out=o, data0=a_, data1=b_, initial=i_, op0=ALU.mult, op1=ALU.add), rr_, [("hs", t)])
                        else:
                            init = 0.0 if ti == 0 else CAR[:, 0:1]
                            S.op("dve", lambda h, o=rev_ap(thr), a_=rev_ap(a), b_=rev_ap(thi), i_=init: h.tensor_tensor_scan(
                                out=o, data0=a_, data1=b_, initial=i_, op0=ALU.mult, op1=ALU.add), [ka, ki_, kr_, ("car",)], [kr_])
                            cpy("dve", CAR[:, 0:1], thr[:, 0:1], [kr_, ("car",)], [("car",)])
                            tten("pool", A_HS[:, tsl], A_HS[:, tsl], thr, ALU.add, [kr_, ("hs", t)], [("hs", t)])
                        yield

                yield from rr([gelu_gen(), seq(dir_gen(0), dir_gen(1))])
                tten("pool", YL[:, c, :], A_GL, A_HS, ALU.mult, [("gl", t) for t in range(NT)] + [("hs", t) for t in range(NT)],
                     [("yl", c, t) for t in range(NT)])
                done[c] = True
                yield

            gens = {}
            started = 0
            while started < 4 or gens:
                while started < 4 and len(gens) < 2:
                    gens[started] = chunk_gen(started)
                    started += 1
                for c_ in list(gens.keys()):
                    try:
                        next(gens[c_])
                    except StopIteration:
                        del gens[c_]
            for t in range(NT):
                rs, kr = rms_stats(t, 4, lambda c, t: YL[:, c, tsl_(t)], lambda c, t: ("yl", c, t), 1.0 / 512, 4 * EPS)
                for c in range(4):
                    stt("dve", YL[:, c, tsl_(t)], YL[:, c, tsl_(t)], VEC[:, vb + V_LNG + c:vb + V_LNG + c + 1], rs,
                        ALU.mult, ALU.mult, [("yl", c, t), kr, vk], [("yl", c, t)])
            wout_pass(l, [4, 5, 6, 7], lambda i, t: YL[:, i, tsl_(t)], lambda i, t: ("yl", i, t))

        def rr(gens):
            gens = list(gens)
            while gens:
                for g in list(gens):
                    try:
                        next(g)
                    except StopIteration:
                        gens.remove(g)
                yield

        def gdn(l, h):
            KT3 = A_KT.rearrange("p (n i) -> p n i", i=128)
            QT3 = A_QT.rearrange("p (n i) -> p n i", i=128)
            KTOK3 = A_KTOK.rearrange("p (n i) -> p n i", i=128)
            VTOK3 = A_VTOK.rearrange("p (n i) -> p n i", i=128)
            OS3 = A_OSUM.rearrange("p (n i) -> p n i", i=128)
            mset("pool", SST[:], 0.0, [("sst", 0), ("sst", 1)])
            mset("pool", SBF[:], 0.0, [("sbf", 0), ("sbf", 1)])
            TTB = TT[:].rearrange("p a b -> p (a b)").bitcast(BF16).rearrange("p (s n) -> p s n", n=512)
            slots = [(SV[:, i, :], ("gsv", i)) for i in range(20)] + [(TTB[:, i, :], ("gtt", i)) for i in range(16)]
            NS, NP, PSZ = DBG.get("NS", 4), DBG.get("NP", 3), DBG.get("PSZ", 7)
            outsets = [slots[3 * i:3 * i + 3] for i in range(NS)]
            pools = [slots[3 * NS + PSZ * i:3 * NS + PSZ * (i + 1)] for i in range(NP)]
            MGs = [(A_PC[:, i * 512:(i + 1) * 512], ("mg", i)) for i in range(NP)]

            def elem(m, e):
                if e < 2:
                    return 2 * m + e, 0, h
                return 15 - 2 * m - (e - 2), 1, 4 + h

            def mm4(lh, klh, rh, krh):
                ps, kp = psum()
                p4 = v4(ps[:])
                l4, r4 = v4(lh), v4(rh)
                for e in range(4):
                    mm(p4[:, e, :], l4[:, e, :], r4[:, e, :], True, True, [klh, krh], [kp])
                return ps, kp

            def evac(ps, kp, dst, kd):
                cnt["ev"] += 1
                cpy("act" if cnt["ev"] % DBG.get("evmod", 4) else "dve", dst, ps[:], [kp], [kd])

            def transp4(src, ksrc, dst, kdst):
                pst, kpst = psum()
                pstb = pst[:].bitcast(BF16)
                for e in range(4):
                    S.op("pe", lambda hh, o=pstb[:, e * 128:(e + 1) * 128], i_=v4(src)[:, e, :]: hh.transpose(out=o, in_=i_, identity=IDB),
                         [ksrc, KC], [kpst])
                cpy("act", dst, pstb[:, 0:512], [kpst], [kdst])

            def prep(m, pipe):
                free = list(pools[pipe])

                def alloc():
                    return free.pop(0)

                def rel(*xs):
                    free.extend(xs)

                els = [elem(m, e) for e in range(4)]
                (Z, kZ), (ATT, kATT), (QD, kQD) = outsets[m % NS]
                MG, kmg = MGs[pipe]
                for e, (tile, d, dh) in enumerate(els):
                    ts1("pool", v4(MG)[:, e, :], INCF if d == 0 else INCB, G3[:, tile, dh:dh + 1], ALU.mult, [KM, KC], [kmg])
                c1, kc1 = psum()
                mm(c1[:], ONESF, MG, True, False, [kmg, KC], [kc1])
                mm(v4(c1[:]), IDF, NEG4, False, True, [KC], [kc1])
                DT = alloc()
                for e, (tile, d, dh) in enumerate(els):
                    act(v4(DT[0])[:, e, :], v4(c1[:])[:, e, :], AF.Exp, [kc1, KM], [DT[1]], bias=BIASD[:, tile, dh:dh + 1])
                yield
                c2, kc2 = psum()
                mm(c2[:], ONESF, MG, True, True, [kmg, KC], [kc2])
                ER = alloc()
                act(ER[0], c2[:], AF.Exp, [kc2], [ER[1]])
                yield
                kk, kkk = psum()
                for e, (tile, d, dh) in enumerate(els):
                    mm(v4(kk[:])[:, e, :], KT3[:, tile, :], KT3[:, tile, :], True, True, [("kt",)], [kkk])
                BP = alloc()
                tten("dve", BP[0], kk[:], DT[0], ALU.mult, [kkk, DT[1]], [BP[1]])
                yield
                qk, kqk = psum()
                for e, (tile, d, dh) in enumerate(els):
                    mm(v4(qk[:])[:, e, :], KT3[:, tile, :], QT3[:, tile, :], True, True, [("kt",), ("qt",)], [kqk])
                tten("dve", ATT, qk[:], DT[0], ALU.mult, [kqk, DT[1]], [kATT])
                rel(DT)
                yield
                tten("pool", v4(BP[0]), v4(BP[0]), b4(OFFB[:]), ALU.mult, [BP[1], KC], [BP[1]])
                for e, (tile, d, dh) in enumerate(els):
                    tten("pool", v4(QD)[:, e, :], QT3[:, tile, :], v4(ER[0])[:, e, :], ALU.mult, [("qt",), ER[1]], [kQD])
                rel(ER)
                yield
                AT = alloc()
                transp4(BP[0], BP[1], AT[0], AT[1])
                X = alloc()
                tten("pool", v4(X[0]), v4(BP[0]), b4(NM16), ALU.mult, [BP[1], KC], [X[1]])
                yield
                XT = alloc()
                tten("pool", v4(XT[0]), v4(AT[0]), b4(NM16), ALU.mult, [AT[1], KC], [XT[1]])
                rel(BP)
                yield

                def prod(lh, rh):
                    o = alloc()
                    evac(*mm4(lh[0], lh[1], rh[0], rh[1]), o[0], o[1])
                    return o

                def plus_i(x):
                    g = alloc()
                    tten(DBG.get("pi_eng", "dve"), v4(g[0]), v4(x[0]), b4(IDB), ALU.add, [x[1], KC], [g[1]])
                    return g

                X2 = prod(XT, X)
                yield
                X2T = prod(X, XT)
                G1T = plus_i(XT)
                rel(X, XT)
                yield
                G2 = plus_i(X2)
                Y1T = prod(G2, G1T)
                rel(G1T, G2)
                yield
                X4 = prod(X2T, X2)
                yield
                X4T = prod(X2, X2T)
                rel(X2, X2T)
                yield
                G4 = plus_i(X4)
                Y2T = prod(G4, Y1T)
                rel(Y1T, G4)
                yield
                X8 = prod(X4T, X4)
                rel(X4, X4T)
                yield
                G8 = plus_i(X8)
                rel(X8)
                Zc = prod(Y2T, G8)
                yield
                ZTc = prod(G8, Y2T)
                rel(Y2T, G8)
                yield
                for k in range(3):
                    OT = alloc()
                    tten(DBG.get("ot_eng", "dve"), v4(OT[0]), v4(AT[0]), b4(MLV[k]), ALU.mult, [AT[1], KC], [OT[1]])
                    w1, kw1 = mm4(OT[0], OT[1], Zc[0], Zc[1])
                    rel(OT)
                    IW = alloc()
                    tten("dve", v4(IW[0]), b4(IDB), v4(w1[:]), ALU.subtract, [kw1, KC], [IW[1]])
                    yield
                    zn, kzn = mm4(ZTc[0], ZTc[1], IW[0], IW[1])
                    rel(IW)
                    if k < 2:
                        Zn = alloc()
                        evac(zn, kzn, Zn[0], Zn[1])
                        rel(Zc, ZTc)
                        yield
                        ZTn = alloc()
                        transp4(Zn[0], Zn[1], ZTn[0], ZTn[1])
                        Zc, ZTc = Zn, ZTn
                    else:
                        evac(zn, kzn, Z, kZ)
                        rel(Zc, ZTc, AT)
                    yield

            def steps(m):
                (Z, kZ), (ATT, kATT), (QD, kQD) = outsets[m % NS]
                Z4, ATT4, QD4 = v4(Z), v4(ATT), v4(QD)
                for sidx in (2 * m, 2 * m + 1):
                    def one(d):
                        tile = sidx if d == 0 else 15 - sidx
                        e = (sidx % 2) + 2 * d
                        dh = h + 4 * d
                        vp, kvp = TB[:, 3 * d + 0, 0:128], ("tb", 3 * d + 0)
                        vraw, kvr = TB[:, 3 * d + 1, 0:128], ("tb", 3 * d + 1)
                        vdec, kvd = TB[:, 3 * d + 2, 0:128], ("tb", 3 * d + 2)
                        ksp, k1 = psum()
                        mm(ksp[:, 0:128], KT3[:, tile, :], SBF[:, d, :], True, True, [("kt",), ("sbf", d)], [k1])
                        stt("dve", vp, ksp[:, 0:128], NEGEC[:, tile, dh:dh + 1], VTOK3[:, tile, :], ALU.mult, ALU.add,
                            [k1, KM, ("vtok",)], [kvp])
                        yield
                        vrp, k2 = psum()
                        mm(vrp[:, 0:128], Z4[:, e, :], vp, True, True, [kZ, kvp], [k2])
                        cpy("act", vraw, vrp[:, 0:128], [k2], [kvr])
                        ts1("dve", vdec, vrp[:, 0:128], BDEC[:, tile, dh:dh + 1], ALU.mult, [k2, KM], [kvd])
                        yield
                        op_, k3 = psum()
                        mm(op_[:, 0:128], SBF[:, d, :], QD4[:, e, :], True, False, [("sbf", d), kQD], [k3])
                        mm(op_[:, 0:128], vraw, ATT4[:, e, :], False, True, [kvr, kATT], [k3])
                        snp, k4 = psum()
                        mm(snp[:, 0:128], KTOK3[:, tile, :], vdec, True, True, [("ktok",), kvd], [k4])
                        stt("dve", SST[:, d, :], SST[:, d, :], ECL[:, tile, dh:dh + 1], snp[:, 0:128], ALU.mult, ALU.add,
                            [k4, KM, ("sst", d)], [("sst", d)])
                        cpy("pool", SBF[:, d, :], SST[:, d, :], [("sst", d)], [("sbf", d)])
                        first = (d == 0 and tile < 8) or (d == 1 and tile >= 8)
                        if first:
                            cpy("act", OS3[:, tile, :], op_[:, 0:128], [k3], [("os", tile)])
                        else:
                            tten("dve", OS3[:, tile, :], op_[:, 0:128], OS3[:, tile, :], ALU.add, [k3, ("os", tile)], [("os", tile)])
                        yield
                    yield from rr([one(0), one(1)])

            active = {}
            prep_done = set()
            steps_done = -1
            nextp = 0
            nexts = 0
            freepipes = list(range(NP))
            cur_steps = None
            nbatch = DBG["gdn_m"]
            while nexts < nbatch or active or cur_steps is not None:
                while nextp < 8 and freepipes and (nextp - NS) <= steps_done and nextp < nbatch + 3:
                    pipe = freepipes.pop()
                    active[nextp] = (prep(nextp, pipe), pipe)
                    nextp += 1
                if cur_steps is None and nexts < nbatch and nexts in prep_done:
                    cur_steps = steps(nexts)
                for m_ in list(active.keys()):
                    g, pipe = active[m_]
                    try:
                        next(g)
                    except StopIteration:
                        del active[m_]
                        prep_done.add(m_)
                        freepipes.append(pipe)
                if cur_steps is not None:
                    try:
                        next(cur_steps)
                    except StopIteration:
                        cur_steps = None
                        steps_done = nexts
                        nexts += 1
                if nexts >= nbatch and not active and cur_steps is None:
                    break

        def dn_head(l, h):
            vb = l * NV
            vk = ("vec",)
            SVf = SV[:].rearrange("p a b -> p (a b)").bitcast(F32)
            PCs = [A_PC, SVf[:, 0:2052]]
            CYs = [A_OSUM, SVf[:, 2052:4100]]
            for par in range(2):
                mset("pool", PCs[par][:, 0:2], 0.0, [("pc", par, "pad")])
                mset("pool", PCs[par][:, 2050:2052], 0.0, [("pc", par, "pad")])
            state = dict(tt_busy=False)
            finished = {}

            def chunk_gen(ci, par):
                dst, dk = ((A_QT, "qt"), (A_KT, "kt"), (A_VT, "vt"))[ci]
                PCp, CYp = PCs[par], CYs[par]
                slab, ks = wload(l, O_WIN + (ci * 4 + h) * 1024, 1024)
                for t in range(NT):
                    ps, kp = psum()
                    proj_tile(ps, kp, slab, ks, t)
                    cpy("act", PCp[:, 2 + t * 512:2 + (t + 1) * 512], ps[:], [kp], [("pc", par, t)])
                    yield
                wbase = vb + V_DNCW + (ci * 4 + h) * 4
                rk = [("pc", par, t) for t in range(NT)] + [("pc", par, "pad"), vk]
                wk = [("os", n) for n in range(16)] if par == 0 else [("cy", par, t) for t in range(NT)]
                act(CYp, PCp[:, 0:2048], AF.Identity, rk, wk, scale=VEC[:, wbase:wbase + 1])
                yield
                for j in range(1, 4):
                    stt("dve", CYp, PCp[:, j:j + 2048], VEC[:, wbase + j:wbase + j + 1], CYp, ALU.mult, ALU.add, rk + wk, wk)
                    yield
                while state["tt_busy"]:
                    yield
                state["tt_busy"] = True

                def tile_chain(t):
                    tsl = tsl_(t)
                    kcy = [("os", n) for n in range(4 * t, 4 * t + 4)] if par == 0 else [("cy", par, t)]
                    th, kth = TT[:, 2 * t, :], ("tt", 2 * t)
                    ln, kln = TT[:, 2 * t + 1, :], ("tt", 2 * t + 1)
                    sq, ksq = TB[:, t, :], ("tb", t)
                    act(th, CYp[:, tsl], AF.Tanh, kcy, [kth], scale=0.5)
                    yield
                    if ci == 2:
                        stt("dve", dst[:, tsl], th, 1.0, CYp[:, tsl], ALU.add, ALU.mult, [kth] + kcy, [(dk, t)])
                        yield
                        return
                    stt("dve", th, th, 1.0, CYp[:, tsl], ALU.add, ALU.mult, [kth] + kcy, [kth])
                    yield
                    act(sq, th, AF.Square, [kth], [ksq])
                    yield
                    ps, kp = psum()
                    mm(ps[:], ONESB, sq, True, True, [ksq, KC], [kp])
                    act(ln, ps[:], AF.Ln, [kp], [kln], bias=4 * EPS)
                    yield
                    act(ln, ln, AF.Exp, [kln], [kln], scale=-0.5, bias=(-0.5 * math.log(128.0) if ci == 0 else 0.0))
                    yield
                    tten("pool", dst[:, tsl], th, ln, ALU.mult, [kth, kln], [(dk, t)])
                    yield

                yield from rr([tile_chain(t) for t in range(NT)])
                state["tt_busy"] = False
                finished[ci] = True

            gens = {}
            started = 0
            while len(finished) < 3:
                while started < 3 and len(gens) < 2 and (started < 2 or finished.get(started - 2)):
                    gens[started] = chunk_gen(started, started % 2)
                    started += 1
                for ci in list(gens.keys()):
                    try:
                        next(gens[ci])
                    except StopIteration:
                        del gens[ci]
            if DBG["dn_stop"] <= 1:
                return
            for src, sk, dst, dk, sc in ((A_KT, "kt", A_KTOK, "ktok", 1.0), (A_VT, "vt", A_VTOK, "vtok", 0.5)):
                for half in range(2):
                    pst, kpst = psum()
                    pstb = pst[:].bitcast(BF16)
                    for n in range(8):
                        tile = half * 8 + n
                        S.op("pe", lambda hh, o=pstb[:, n * 128:(n + 1) * 128], i_=src[:, tile * 128:(tile + 1) * 128]:
                             hh.transpose(out=o, in_=i_, identity=IDB), [(sk, tile // 4), KC], [kpst])
                    act(dst[:, half * 1024:(half + 1) * 1024], pstb[:, 0:1024], AF.Identity, [kpst], [(dk,)], scale=sc)
            if DBG["dn_stop"] <= 2:
                return
            S.barrier()
            gdn(l, h)
            S.barrier()
            if DBG["dn_stop"] <= 4:
                return
            zslab, kzs = wload(l, O_WIN + (12 + h) * 1024, 1024)

            def out_chain(t):
                tsl = tsl_(t)
                osk = [("os", n) for n in range(t * 4, t * 4 + 4)]
                ln, kln = TT[:, 2 * t, :], ("tt", 2 * t)
                th, kth = TT[:, 2 * t + 1, :], ("tt", 2 * t + 1)
                sq, ksq = TB[:, t, :], ("tb", t)
                act(sq, A_OSUM[:, tsl], AF.Square, osk, [ksq])
                yield
                psz, kpz = psum()
                proj_tile(psz, kpz, zslab, kzs, t)
                act(th, psz[:], AF.Tanh, [kpz], [kth], scale=0.5)
                stt("dve", th, th, 1.0, psz[:], ALU.add, ALU.mult, [kth, kpz], [kth])
                yield
                ps, kp = psum()
                mm(ps[:], ONESB, sq, True, True, [ksq, KC], [kp])
                act(ln, ps[:], AF.Ln, [kp], [kln], bias=EPS, scale=1.0 / 128)
                yield
                act(ln, ln, AF.Exp, [kln], [kln], scale=-0.5)
                yield
                stt("dve", ln, A_OSUM[:, tsl], VEC[:, vb + V_DNG:vb + V_DNG + 1], ln, ALU.mult, ALU.mult, osk + [kln, vk], [kln])
                yield
                stt("dve", A_VT[:, tsl], ln, 0.5, th, ALU.mult, ALU.mult, [kln, kth], [("vt", t)])
                yield

            interleave([out_chain(t) for t in range(NT)])
            wout_pass(l, [h], lambda i, t: A_VT[:, tsl_(t)], lambda i, t: ("vt", t))

        def ffn_phase(l):
            vb = l * NV
            vk = ("vec",)
            rms_to_hn(vb + V_G2)
            ACTB = ARENA[:, 0:11264].rearrange("p (f n) -> p f n", n=512)
            GP = ARENA[:, 11264:15376].bitcast(F32).rearrange("p (b n) -> p b n", b=4)
            NW = 3
            st = dict(down_q=0)

            def fc_chain(qt, fc, idx):
                t0 = qt * 512
                sg, ksg = wload(l, O_WG + fc * 1024, 1024)
                gb = idx % 4
                gp = GP[:, gb, :]
                kgp = ("gp", gb)
                cg, kcg = TT[:, 2 * (idx % 4), :], ("tt", 2 * (idx % 4))
                x2, k2 = TT[:, 2 * (idx % 4) + 1, :], ("tt", 2 * (idx % 4) + 1)
                psg, kpg = psum()
                proj_tile(psg, kpg, sg, ksg, qt)
                psh, kph = psum()
                sides = []
                if qt > 0:
                    sides.append((0, t0 - 1))
                if qt < NT - 1:
                    sides.append((1, t0 + 512))
                for (si, col) in sides:
                    for kc in range(8):
                        mm(psh[:, si:si + 1], sg[:, kc * 128:(kc + 1) * 128], HN[:, kc, col:col + 1], kc == 0, kc == 7,
                           [ksg, ("HN", kc, col // 512)], [kph])
                cpy("act", gp[:, 1:513], psg[:], [kpg], [kgp])
                if qt > 0:
                    cpy("dve", gp[:, 0:1], psh[:, 0:1], [kph, kgp], [kgp])
                else:
                    mset("dve", gp[:, 0:1], 0.0, [kgp])
                if qt < NT - 1:
                    cpy("dve", gp[:, 513:514], psh[:, 1:2], [kph, kgp], [kgp])
                else:
                    mset("dve", gp[:, 513:514], 0.0, [kgp])
                yield
                wb_ = vb + V_FCW + fc * 3
                tsc("pool", cg, gp[:, 0:512], VEC[:, wb_:wb_ + 1], VEC[:, vb + V_FCB + fc:vb + V_FCB + fc + 1], ALU.mult, ALU.add,
                    [kgp, vk], [kcg])
                yield
                stt("dve", cg, gp[:, 1:513], VEC[:, wb_ + 1:wb_ + 2], cg, ALU.mult, ALU.add, [kgp, vk, kcg], [kcg])
                yield
                stt("dve", cg, gp[:, 2:514], VEC[:, wb_ + 2:wb_ + 3], cg, ALU.mult, ALU.add, [kgp, vk, kcg], [kcg])
                yield
                act(x2, cg, AF.Square, [kcg], [k2])
                yield
                tsc("pool", x2, x2, GC2, GC1, ALU.mult, ALU.add, [k2], [k2])
                yield
                tten("pool", x2, x2, cg, ALU.mult, [k2, kcg], [k2])
                yield
                act(x2, x2, AF.Tanh, [k2], [k2])
                yield
                stt("dve", cg, x2, 1.0, cg, ALU.add, ALU.mult, [k2, kcg], [kcg])
                yield
                while st["down_q"] < qt:
                    yield
                su, ksu = wload(l, O_WU + fc * 1024, 1024)
                psu, kpu = psum()
                proj_tile(psu, kpu, su, ksu, qt)
                stt("dve", ACTB[:, fc, :], cg, 0.5, psu[:], ALU.mult, ALU.mult, [kcg, kpu], [("actb", fc)])
                yield

            def down_gen(qt):
                for m in range(8):
                    ps, kp = psum()
                    pieces = ((0, 8), (8, 16), (16, 22))
                    for (f0, f1) in pieces:
                        sl, ksl = wload(l, O_WD + m * 2816 + f0 * 128, (f1 - f0) * 128)
                        for fc in range(f0, f1):
                            mm(ps[:], sl[:, (fc - f0) * 128:(fc - f0 + 1) * 128], ACTB[:, fc, :], fc == 0, fc == NFC - 1,
                               [ksl, ("actb", fc)], [kp])
                    radd(m, qt, ps, kp)
                    yield
                st["down_q"] = qt + 1

            chains = [(qt, fc) for qt in range(NT) for fc in range(NFC)]
            active = []
            ci = 0
            down = None
            nfin = {qt: 0 for qt in range(NT)}
            while ci < len(chains) or active or down is not None:
                while ci < len(chains) and len(active) < NW:
                    qt, fc = chains[ci]
                    active.append((fc_chain(qt, fc, ci), qt))
                    ci += 1
                for item in list(active):
                    g, qt = item
                    try:
                        next(g)
                    except StopIteration:
                        active.remove(item)
                        nfin[qt] += 1
                        if nfin[qt] == NFC:
                            down = down_gen(qt)
                if down is not None:
                    try:
                        next(down)
                    except StopIteration:
                        down = None

        def ple_phase(l, s):
            vb = l * NV
            rms_to_hn(vb + V_GP)
            PB = ARENA[:, 0:4096].rearrange("p (k n) -> p k n", k=2)
            PF = ARENA[:, 4096:8192].bitcast(F32).rearrange("p (b k n) -> p b k n", b=2, k=2)
            psrc = pT[l, s].rearrange("(k p) n -> p k n", p=128)
            for t in range(NT):
                S.dma(PF[:, t % 2], psrc[:, :, tsl_(t)], [], [("pf", t % 2)])
                cpy("pool", PB[:, :, tsl_(t)], PF[:, t % 2], [("pf", t % 2)], [("pb", t)])
            for m in range(8):
                sg, ksg = wload(l, O_PWG + m * 1024, 1024)
                sp_, ksp_ = wload(l, O_PWP + m * 256, 256)
                for t in range(NT):
                    ps1, kp1 = psum()
                    proj_tile(ps1, kp1, sg, ksg, t)
                    th, kth = tt()
                    act(th, ps1[:], AF.Tanh, [kp1, ("lv",)], [kth], scale=0.5, bias=LV[:, 40 + m:41 + m])
                    ps2, kp2 = psum()
                    for k2 in range(2):
                        mm(ps2[:], sp_[:, k2 * 128:(k2 + 1) * 128], PB[:, k2, tsl_(t)], k2 == 0, k2 == 1, [ksp_, ("pb", t)], [kp2])
                    stt("dve", th, th, 1.0, ps2[:], ALU.add, ALU.mult, [kth, kp2], [kth])
                    stt("dve", R[:, m, tsl_(t)], th, 0.5, R[:, m, tsl_(t)], ALU.mult, ALU.add, [kth, ("R", m, t)], [("R", m, t)])


        for s in range(nseq):
            for ch in range(8):
                S.dma(R[:, ch, :], xT[s, ch * 128:(ch + 1) * 128, :], [], [("R", ch, t) for t in range(NT)])
            for l in range(nlayers):
                layer_prep(l)
                rms_to_hn(l * NV + V_G1)
                if "lru" in phases or "dn" in phases:
                    ab_phase(l)
                S.barrier()
                if "lru" in phases:
                    lru_phase(l)
                    S.barrier()
                if "dn" in phases:
                    for h in range(DBG["heads"]):
                        dn_head(l, h)
                    S.barrier()
                if "ffn" in phases:
                    ffn_phase(l)
                    S.barrier()
                if "ple" in phases:
                    ple_phase(l, s)
                    S.barrier()
            S.barrier()
            if dump_r:
                for ch in range(8):
                    S.dma(yT[s, ch * 128:(ch + 1) * 128, :], R[:, ch, :], [("R", ch, t) for t in range(NT)], [("y", s, ch)])
            else:
                for t in range(NT):
                    rs, kr = rms_stats(t, 8, lambda ch, t: R[:, ch, tsl_(t)], lambda ch, t: ("R", ch, t), 1.0 / D, EPS)
                    for ch in range(8):
                        o, ko = tt()
                        stt("dve" if ch % 2 == 0 else "pool", o, R[:, ch, tsl_(t)], VEC[:, L * NV + ch:L * NV + ch + 1], rs,
                            ALU.mult, ALU.mult, [("R", ch, t), kr], [ko])
                        S.dma(yT[s, ch * 128:(ch + 1) * 128, tsl_(t)], o, [ko], [("y", s, ch, t)])
            S.barrier()
        S.barrier()
        print("instructions:", S.ninst, {k: v["n"] for k, v in S.E.items()})
    return nc


def _slabs_kn(w):
    K, N = w.shape
    a = w.reshape(K // 128, 128, N // 128, 128)
    return np.ascontiguousarray(a.transpose(2, 1, 0, 3)).reshape(N // 128, 128, K)


def pack_weights(inp):
    wpk = np.zeros((L, 128, WTOT), np.float32)
    for l in range(L):
        w_in = inp["w_in"][l]
        cols = np.concatenate([w_in[:, 0:2048], w_in[:, 2064:3088]], axis=1)
        sl = _slabs_kn(cols)
        wpk[l, :, O_WIN:O_WIN + 24 * 1024] = sl.transpose(1, 0, 2).reshape(128, 24 * 1024)
        ab = w_in[:, 2048:2064].reshape(8, 128, 16).transpose(1, 0, 2).reshape(128, 128)
        wpk[l, :, O_WAB:O_WAB + 128] = ab
        lw = np.zeros((128, 4, 2, 2, 128), np.float32)
        for d in range(2):
            for gi, nm in enumerate(("lru_wa", "lru_wx")):
                wg = inp[nm][l, d]
                for c in range(4):
                    lw[0:64, c, d, gi, 0:64] = wg[2 * c]
                    lw[64:128, c, d, gi, 64:128] = wg[2 * c + 1]
        wpk[l, :, O_LRUW:O_LRUW + 2048] = lw.reshape(128, 2048)
        wpk[l, :, O_WOUT:O_WOUT + 8192] = inp["w_out"][l].reshape(8, 128, 1024).transpose(1, 0, 2).reshape(128, 8192)
        wpk[l, :, O_WG:O_WG + 22528] = _slabs_kn(inp["ffn_wg"][l]).transpose(1, 0, 2).reshape(128, 22528)
        wpk[l, :, O_WU:O_WU + 22528] = _slabs_kn(inp["ffn_wu"][l]).transpose(1, 0, 2).reshape(128, 22528)
        wd = inp["ffn_wd"][l].reshape(NFC, 128, 8, 128)
        wpk[l, :, O_WD:O_WD + 22528] = wd.transpose(1, 2, 0, 3).reshape(128, 22528)
        wpk[l, :, O_PWG:O_PWG + 8192] = _slabs_kn(inp["ple_wg"][l]).transpose(1, 0, 2).reshape(128, 8192)
        wpk[l, :, O_PWP:O_PWP + 2048] = _slabs_kn(inp["ple_wp"][l]).transpose(1, 0, 2).reshape(128, 2048)
    return wpk


def pack_vecs(inp):
    v = np.zeros((128, L * NV + 8), np.float32)

    def cm(a, n):
        return a.reshape(n, 128).T

    for l in range(L):
        b = l * NV
        v[:, b + V_G1:b + V_G1 + 8] = cm(inp["norm1_g"][l], 8)
        v[:, b + V_G2:b + V_G2 + 8] = cm(inp["norm2_g"][l], 8)
        v[:, b + V_GP:b + V_GP + 8] = cm(inp["ple_norm_g"][l], 8)
        v[:, b + V_BGP:b + V_BGP + 8] = cm(inp["ple_bg"][l], 8)
        v[:, b + V_DNCW:b + V_DNCW + 48] = inp["dn_conv_w"][l].reshape(4, 12, 128).transpose(2, 1, 0).reshape(128, 48)
        v[:, b + V_LCW:b + V_LCW + 16] = inp["lru_conv_w"][l].reshape(4, 4, 128).transpose(2, 1, 0).reshape(128, 16)
        v[:, b + V_LCB:b + V_LCB + 4] = cm(inp["lru_conv_b"][l], 4)
        v[:, b + V_LBA:b + V_LBA + 8] = inp["lru_ba"][l].reshape(2, 4, 128).transpose(2, 0, 1).reshape(128, 8)
        v[:, b + V_LBX:b + V_LBX + 8] = inp["lru_bx"][l].reshape(2, 4, 128).transpose(2, 0, 1).reshape(128, 8)
        v[:, b + V_LLAM:b + V_LLAM + 8] = inp["lru_lambda"][l].reshape(2, 4, 128).transpose(2, 0, 1).reshape(128, 8)
        v[:, b + V_LNG:b + V_LNG + 4] = cm(inp["lru_norm_g"][l], 4)
        v[:, b + V_DNG] = inp["dn_norm_g"][l]
        v[:, b + V_FCW:b + V_FCW + 66] = inp["ffn_conv_w"][l].reshape(3, NFC, 128).transpose(2, 1, 0).reshape(128, 66)
        v[:, b + V_FCB:b + V_FCB + 22] = cm(inp["ffn_conv_b"][l], NFC)
        v[:, b + V_ALOG:b + V_ALOG + 8] = np.broadcast_to(inp["dn_a_log"][l].reshape(1, 8), (128, 8))
        v[:, b + V_DTB:b + V_DTB + 8] = np.broadcast_to(inp["dn_dt_bias"][l].reshape(1, 8), (128, 8))
    v[:, L * NV:L * NV + 8] = cm(inp["final_g"], 8)
    return v


def kernel(**inputs):
    inp = {k: np.asarray(v) for k, v in inputs.items()}
    x = inp["x"]
    p = inp["p"]
    wpk = pack_weights(inp)
    vecs = pack_vecs(inp)
    xT = np.ascontiguousarray(x.transpose(0, 2, 1))
    pT = np.ascontiguousarray(p.transpose(0, 1, 3, 2))
    nc = build_nc()
    in_maps = []
    for c in range(NCORE):
        in_maps.append({
            "xT": xT[c * SEQ_PER_CORE:(c + 1) * SEQ_PER_CORE],
            "pT": np.ascontiguousarray(pT[:, c * SEQ_PER_CORE:(c + 1) * SEQ_PER_CORE]),
            "wpk": wpk,
            "vecs": vecs,
        })
    res = run_bass_kernel_spmd(nc, in_maps, core_ids=list(range(NCORE)))
    yT = np.concatenate([r["yT"] for r in res.results], axis=0)
    return np.ascontiguousarray(yT.transpose(0, 2, 1)).astype(np.float32)
```

```python
import math
import numpy as np
from contextlib import ExitStack
import concourse.bass as bass
import concourse.mybir as mybir
from concourse.bass_utils import run_bass_kernel_spmd

F32 = mybir.dt.float32
BF16 = mybir.dt.bfloat16
AF = mybir.ActivationFunctionType
ALU = mybir.AluOpType

D = 1024
S_LEN = 2048
L = 4
NCORE = 8
SEQ_PER_CORE = 4
DFF = 2816
NFC = 22
PLE = 256
EPS = 1e-6
NT = 4
TK = 16
GC1 = math.sqrt(2.0 / math.pi)
GC2 = GC1 * 0.044715

O_WIN = 0
O_WAB = O_WIN + 24 * 1024
O_LRUW = O_WAB + 128
O_WOUT = O_LRUW + 2048
O_WG = O_WOUT + 8192
O_WU = O_WG + 22528
O_WD = O_WU + 22528
O_PWG = O_WD + 22528
O_PWP = O_PWG + 8192
WTOT = O_PWP + 2048

V_G1, V_G2, V_GP, V_BGP = 0, 8, 16, 24
V_DNCW = 32
V_LCW = 80
V_LCB = 96
V_LBA = 100
V_LBX = 108
V_LLAM = 116
V_LNG = 124
V_DNG = 128
V_FCW = 129
V_FCB = 195
V_ALOG = 217
V_DTB = 225
NV = 240


class Sched:
    NDS = 24

    def __init__(self, nc, es):
        self.nc = nc
        self.E = {}
        for name, h in (("pe", nc.tensor), ("act", nc.scalar), ("dve", nc.vector), ("pool", nc.gpsimd), ("sp", nc.sync)):
            sem = es.enter_context(nc.semaphore("s_" + name))
            self.E[name] = dict(h=h, sem=sem, n=0, seen={}, seend={})
        self.dsem = [es.enter_context(nc.semaphore(f"sd{i}")) for i in range(self.NDS)]
        self.dcnt = [0] * self.NDS
        self.dnext = 0
        self.lastw = {}
        self.readers = {}
        self.ninst = 0

    def _wait(self, eng, tok, same_ok):
        X = self.E[eng]
        if tok[0] == "e":
            _, p, c = tok
            if p == eng and (eng == "pe" or not same_ok):
                return
            if X["seen"].get(p, 0) >= c:
                return
            X["h"].wait_ge(self.E[p]["sem"], c)
            X["seen"][p] = c
        else:
            _, i, v = tok
            if X["seend"].get(i, 0) >= v:
                return
            X["h"].wait_ge(self.dsem[i], v)
            X["seend"][i] = v

    def _deps(self, eng, r, w):
        for k in r:
            t = self.lastw.get(k)
            if t is not None:
                self._wait(eng, t, True)
            if k[0] == "ps":
                for t in self.readers.get(k, {}).values():
                    self._wait(eng, t, False)
        for k in w:
            t = self.lastw.get(k)
            if t is not None:
                self._wait(eng, t, True)
            for t in self.readers.get(k, {}).values():
                self._wait(eng, t, False)

    def _record(self, tok, r, w):
        for k in r:
            self.readers.setdefault(k, {})[tok[1]] = tok
        for k in w:
            self.lastw[k] = tok
            self.readers[k] = {}

    def op(self, eng, emit, r=(), w=()):
        X = self.E[eng]
        self._deps(eng, r, w)
        inst = emit(X["h"])
        X["n"] += 1
        inst.then_inc(X["sem"], 1)
        self.ninst += 1
        self._record(("e", eng, X["n"]), r, w)

    def dma(self, out, in_, r=(), w=(), eng="sp"):
        self._deps(eng, r, w)
        i = self.dnext
        self.dnext = (i + 1) % self.NDS
        if self.dcnt[i] > 0:
            self._wait(eng, ("d", i, 16 * self.dcnt[i]), True)
        inst = self.E[eng]["h"].dma_start(out=out, in_=in_)
        self.dcnt[i] += 1
        inst.then_inc(self.dsem[i], 16)
        self.ninst += 1
        self._record(("d", i, 16 * self.dcnt[i]), r, w)

    def barrier(self):
        comp = ("pe", "act", "dve", "pool")
        for e in comp:
            for p in comp:
                if p != e and self.E[p]["n"] > 0:
                    self._wait(e, ("e", p, self.E[p]["n"]), True)
            for i in range(self.NDS):
                if self.dcnt[i] > 0:
                    self._wait(e, ("d", i, 16 * self.dcnt[i]), True)
        for p in comp:
            if self.E[p]["n"] > 0:
                self._wait("sp", ("e", p, self.E[p]["n"]), True)
        for i in range(self.NDS):
            if self.dcnt[i] > 0:
                self._wait("sp", ("d", i, 16 * self.dcnt[i]), True)
        self.lastw = {}
        self.readers = {}


def rev_ap(ap2d):
    a = ap2d.ap
    n = a[-1][1]
    st = a[-1][0]
    return bass.AP(ap2d.tensor, ap2d.offset + (n - 1) * st, [list(x) for x in a[:-1]] + [[-st, n]])


DBG = dict(dn_stop=99, heads=4, gdn_m=8)


def interleave(gens):
    gens = list(gens)
    while gens:
        for g in list(gens):
            try:
                next(g)
            except StopIteration:
                gens.remove(g)


def build_nc(nseq=SEQ_PER_CORE, nlayers=L, dump_r=False, do_prepass=True, phases=("lru", "dn", "ffn", "ple"), ndbg=0):
    nc = bass.Bass("TRN2", target_bir_lowering=False)
    xT = nc.dram_tensor("xT", [SEQ_PER_CORE, D, S_LEN], F32, kind="ExternalInput").ap()
    pT = nc.dram_tensor("pT", [L, SEQ_PER_CORE, PLE, S_LEN], F32, kind="ExternalInput").ap()
    wpk = nc.dram_tensor("wpk", [L, 128, WTOT], F32, kind="ExternalInput").ap()
    vecs = nc.dram_tensor("vecs", [128, L * NV + 8], F32, kind="ExternalInput").ap()
    yT = nc.dram_tensor("yT", [SEQ_PER_CORE, D, S_LEN], F32, kind="ExternalOutput").ap()
    wsc = nc.dram_tensor("wsc", [L, 128, WTOT], BF16, kind="Internal").ap()
    dbg_out = None
    if ndbg:
        dbg_out = nc.dram_tensor("dbg", [ndbg, 128, S_LEN], F32, kind="ExternalOutput").ap()

    es = ExitStack()
    with es:
        def sb(name, shape, dt):
            return es.enter_context(nc.sbuf_tensor(name, shape, dt))

        S = Sched(nc, es)
        R = sb("R", [128, 8, S_LEN], F32)
        HN = sb("HN", [128, 8, S_LEN], BF16)
        ARENA = sb("ARENA", [128, 18560], BF16)
        TT = sb("TT", [128, 8, 512], F32)
        TB = sb("TB", [128, 6, 512], BF16)
        SV = sb("SV", [128, 20, 512], BF16)
        SF = sb("SF", [128, 512], F32)
        WB = sb("WB", [128, 5, 1024], BF16)
        VEC = sb("VEC", [128, L * NV + 8], F32)
        CST = sb("CST", [128, 13, 128], F32)
        CSB = sb("CSB", [128, 7, 128], BF16)
        MISC = sb("MISC", [128, 12, 128], F32)
        LV = sb("LV", [128, 64], F32)
        SST = sb("SST", [128, 2, 128], F32)
        SBF = sb("SBF", [128, 2, 128], BF16)
        CAR = sb("CAR", [128, 4], F32)
        PS = [es.enter_context(nc.psum_tensor(f"ps{i}", [128, 512], F32)) for i in range(8)]
        cnt = dict(ps=0, tt=0, tb=0, wb=0, sv=0, ev=0)

        def psum():
            i = cnt["ps"] % 8
            cnt["ps"] += 1
            return PS[i], ("ps", i)

        def tt():
            i = cnt["tt"] % 8
            cnt["tt"] += 1
            return TT[:, i, :], ("tt", i)

        def tb():
            i = cnt["tb"] % 6
            cnt["tb"] += 1
            return TB[:, i, :], ("tb", i)

        def wslot():
            i = cnt["wb"] % 5
            cnt["wb"] += 1
            return WB[:, i, :], ("wb", i)

        def sv():
            i = 10 + cnt["sv"] % 10
            cnt["sv"] += 1
            return SV[:, i, :], ("sv", i)

        def act(out, in_, func, r, w, bias=0.0, scale=1.0):
            S.op("act", lambda h: h.activation(out=out, in_=in_, func=func, bias=bias, scale=scale), r, w)

        def tsc(eng, out, in0, s1, s2, op0, op1, r, w):
            S.op(eng, lambda h: h.tensor_scalar(out=out, in0=in0, scalar1=s1, scalar2=s2, op0=op0, op1=op1), r, w)

        def ts1(eng, out, in_, s, op, r, w):
            S.op(eng, lambda h: h.tensor_single_scalar(out=out, in_=in_, scalar=s, op=op), r, w)

        def stt(eng, out, in0, s, in1, op0, op1, r, w):
            eng = "dve"
            S.op(eng, lambda h: h.scalar_tensor_tensor(out=out, in0=in0, scalar=s, in1=in1, op0=op0, op1=op1), r, w)

        def tten(eng, out, in0, in1, op, r, w):
            S.op(eng, lambda h: h.tensor_tensor(out=out, in0=in0, in1=in1, op=op), r, w)

        def cpy(eng, out, in_, r, w):
            if eng == "act":
                act(out, in_, AF.Identity, r, w)
            else:
                S.op(eng, lambda h: h.tensor_copy(out=out, in_=in_), r, w)

        def mm(out, lhsT, rhs, start, stop, r, w):
            S.op("pe", lambda h: h.matmul(out, lhsT=lhsT, rhs=rhs, start=start, stop=stop), r, w)

        def mset(eng, ap, val, w):
            S.op(eng, lambda h: h.memset(ap, val), (), w)

        INCF, INCB, OFFD, ONESF, IDF = (CST[:, i, :] for i in range(5))
        NEG4 = CST[:, 5:9, :]
        M16, ML0, ML1, ML2, IDB, ONESB = (CSB[:, i, :] for i in range(6))
        MLV = [ML0, ML1, ML2]
        KC = ("cst",)

        def b4(ap):
            return ap.unsqueeze(1).to_broadcast([128, 4, 128])

        def v4(ap):
            return ap.rearrange("p (e i) -> p e i", e=4)

        def aff(ap, pattern, op, fill, base, cm):
            S.op("pool", lambda h: h.affine_select(out=ap, in_=ap, pattern=pattern, compare_op=op, fill=fill,
                                                   base=base, channel_multiplier=cm), [KC], [KC])

        mset("pool", CST[:, 0:5, :], 1.0, [KC])
        aff(INCF, [[1, 128]], ALU.is_ge, 0.0, 0, -1)
        aff(INCB, [[-1, 128]], ALU.is_ge, 0.0, 0, 1)
        aff(OFFD, [[1, 128]], ALU.not_equal, 0.0, 0, -1)
        aff(IDF, [[1, 128]], ALU.is_equal, 0.0, 0, -1)
        for e in range(4):
            tsc("pool", NEG4[:, e, :], INCF if e < 2 else INCB, -1.0, 1e30, ALU.add, ALU.mult, [KC], [KC])
        mset("pool", CST[:, 9:13, :], 1.0, [KC])
        for bi, b in enumerate((16, 32, 64)):
            nb = 128 // b
            v = CST[:, 9 + bi, :].rearrange("p (k c) -> p k c", c=b)
            aff(v, [[-b, nb], [0, b]], ALU.is_ge, 0.0, 0, 1)
            aff(v, [[b, nb], [0, b]], ALU.is_gt, 0.0, b, -1)
        cpy("pool", M16, CST[:, 9, :], [KC], [KC])
        for k in range(3):
            tten("pool", MLV[k], CST[:, 10 + k, :], CST[:, 9 + k, :], ALU.subtract, [KC], [KC])
        cpy("pool", IDB, IDF, [KC], [KC])
        cpy("pool", ONESB, ONESF, [KC], [KC])
        NM16 = CSB[:, 6, :]
        ts1("pool", NM16, CST[:, 9, :], -1.0, ALU.mult, [KC], [KC])
        OFFB = sb("OFFB", [128, 128], BF16)
        cpy("pool", OFFB[:], OFFD, [KC], [KC])
        NEG4B = CST[:, 9:11, :].rearrange("p a b -> p (a b)").bitcast(BF16).rearrange("p (e i) -> p e i", e=4)
        cpy("pool", NEG4B, NEG4, [KC], [KC])
        S.dma(VEC[:], vecs[:, :], [], [("vec",)])
        S.barrier()

        if do_prepass:
            CH = 8192
            for l in range(nlayers):
                c0 = 0
                while c0 < WTOT:
                    cw = min(CH, WTOT - c0)
                    S.dma(wsc[l, :, c0:c0 + cw], wpk[l, :, c0:c0 + cw], [], [("wsc", l, c0)], eng="pool")
                    c0 += cw
            S.barrier()

        def wload(l, off, width):
            slot, k = wslot()
            S.dma(slot[:, 0:width], wsc[l, :, off:off + width], [], [k])
            return slot, k

        def tsl_(t):
            return slice(t * 512, (t + 1) * 512)

        def rms_stats(t, nch, src, srck, scale, eps):
            ps, kp = psum()
            for ch in range(nch):
                sq, ks = tb()
                act(sq, src(ch, t), AF.Square, [srck(ch, t)], [ks])
                mm(ps[:], ONESB, sq, ch == 0, ch == nch - 1, [ks], [kp])
            ln, kl = tt()
            act(ln, ps[:], AF.Ln, [kp], [kl], bias=eps, scale=scale)
            rs, kr = tt()
            act(rs, ln, AF.Exp, [kl], [kr], scale=-0.5)
            return rs, kr

        def rms_to_hn(gbase):
            pss = []
            for t in range(NT):
                ps, kp = psum()
                for ch in range(8):
                    sq, ks = tb()
                    act(sq, R[:, ch, tsl_(t)], AF.Square, [("R", ch, t)], [ks])
                    mm(ps[:], ONESB, sq, ch == 0, ch == 7, [ks], [kp])
                pss.append((ps, kp))
            rss = []
            for t in range(NT):
                ps, kp = pss[t]
                rs, kr = TT[:, 4 + t, :], ("tt", 4 + t)
                act(rs, ps[:], AF.Ln, [kp], [kr], bias=EPS, scale=1.0 / D)
                rss.append((rs, kr))
            for t in range(NT):
                rs, kr = rss[t]
                act(rs, rs, AF.Exp, [kr], [kr], scale=-0.5)
            for t in range(NT):
                rs, kr = rss[t]
                for ch in range(8):
                    stt("dve", HN[:, ch, tsl_(t)], R[:, ch, tsl_(t)],
                        VEC[:, gbase + ch:gbase + ch + 1], rs, ALU.mult, ALU.mult, [("R", ch, t), kr, ("vec",)], [("HN", ch, t)])

        def proj_tile(ps, kp, slab, ks, t):
            for kc in range(8):
                mm(ps[:], slab[:, kc * 128:(kc + 1) * 128], HN[:, kc, tsl_(t)], kc == 0, kc == 7, [ks, ("HN", kc, t)], [kp])

        def radd(m, t, ps, kp):
            tten("dve", R[:, m, tsl_(t)], ps[:], R[:, m, tsl_(t)], ALU.add, [kp, ("R", m, t)], [("R", m, t)])

        def dbg_store(idx, ap2d, keys):
            if dbg_out is not None and idx < ndbg:
                S.dma(dbg_out[idx, :, 0:ap2d.shape[-1]], ap2d, keys, [("dbg", idx)])

        A_QT = ARENA[:, 0:2048]
        A_KT = ARENA[:, 2048:4096]
        A_VT = ARENA[:, 4096:6144]
        A_KTOK = ARENA[:, 6144:8192]
        A_VTOK = ARENA[:, 8192:10240]
        A_OSUM = ARENA[:, 10240:14336].bitcast(F32)
        A_PC = ARENA[:, 14336:18440].bitcast(F32)
        A_XCB = ARENA[:, 0:2048]
        A_GL = ARENA[:, 2048:6144].bitcast(F32)
        A_HS = ARENA[:, 6144:10240].bitcast(F32)
        A_XC = A_OSUM
        YL = SV[:, 0:16, :].rearrange("p (c a) n -> p c (a n)", c=4)

        def pc_pads():
            mset("pool", A_PC[:, 0:2], 0.0, [("pc", "pad")])
            mset("pool", A_PC[:, 2050:2052], 0.0, [("pc", "pad")])

        def conv4(dst, dstk, wbase, bias_ap, vk):
            rk = [("pc", t) for t in range(NT)] + [("pc", "pad"), vk]
            wk = [(dstk, t) for t in range(NT)]
            if bias_ap is None:
                ts1("pool", dst, A_PC[:, 0:2048], VEC[:, wbase:wbase + 1], ALU.mult, rk, wk)
            else:
                tsc("pool", dst, A_PC[:, 0:2048], VEC[:, wbase:wbase + 1], bias_ap, ALU.mult, ALU.add, rk, wk)
            for j in range(1, 4):
                stt("pool" if j % 2 else "dve", dst, A_PC[:, j:j + 2048], VEC[:, wbase + j:wbase + j + 1], dst,
                    ALU.mult, ALU.add, rk + wk, wk)

        def wout_pass(l, kcs, src, srck):
            slabs = [wload(l, O_WOUT + kc * 1024, 1024) for kc in kcs]
            for m in range(8):
                for t in range(NT):
                    ps, kp = psum()
                    for i, (sl, ks) in enumerate(slabs):
                        mm(ps[:], sl[:, m * 128:(m + 1) * 128], src(i, t), i == 0, i == len(slabs) - 1, [ks, srck(i, t)], [kp])
                    radd(m, t, ps, kp)

        def layer_prep(l):
            vb = l * NV
            kl = ("lv",)
            vk = ("vec",)
            act(LV[:, 48:56], VEC[:, vb + V_ALOG:vb + V_ALOG + 8], AF.Exp, [vk], [kl])
            ts1("dve", LV[:, 0:8], LV[:, 48:56], -1.0, ALU.mult, [kl], [kl])
            ts1("dve", LV[:, 8:16], VEC[:, vb + V_LBA:vb + V_LBA + 8], 0.5, ALU.mult, [vk, kl], [kl])
            ts1("dve", LV[:, 16:24], VEC[:, vb + V_LBX:vb + V_LBX + 8], 0.5, ALU.mult, [vk, kl], [kl])
            act(LV[:, 48:56], VEC[:, vb + V_LLAM:vb + V_LLAM + 8], AF.Exp, [vk, kl], [kl], scale=-1.0)
            act(LV[:, 56:64], LV[:, 48:56], AF.Ln, [kl], [kl], bias=1.0)
            ts1("dve", LV[:, 24:32], LV[:, 56:64], -8.0, ALU.mult, [kl], [kl])
            ts1("dve", LV[:, 32:40], LV[:, 56:64], -4.0, ALU.mult, [kl], [kl])
            ts1("dve", LV[:, 40:48], VEC[:, vb + V_BGP:vb + V_BGP + 8], 0.5, ALU.mult, [vk, kl], [kl])

        def m3(i):
            return MISC[:, i, :].rearrange("p (t h) -> p t h", h=8)

        AB = MISC[:, 0:2, :].rearrange("p a b -> p (a b)").rearrange("p (t c) -> p t c", c=16)
        G3, NLNB, CCOL, CLAST, BIASD, NEGEC, BDEC, ECL, TM1, TM2 = (m3(i) for i in range(2, 12))
        KM = ("misc",)
        GHL = MISC[:, 10, :].bitcast(BF16).rearrange("p (a t h) -> p a t h", a=2, h=8)
        GHI, GLO = GHL[:, 0], GHL[:, 1]
        ECOL = TM2

        def ab_phase(l):
            vb = l * NV
            slab, ks = wload(l, O_WAB, 128)
            ps, kp = psum()
            for n in range(TK):
                for kc in range(8):
                    mm(ps[:, n * 16:(n + 1) * 16], HN[:, kc, n * 128:(n + 1) * 128], slab[:, kc * 16:(kc + 1) * 16],
                       kc == 0, kc == 7, [ks, ("HN", kc, n // 4)], [kp])
            cpy("dve", MISC[:, 0:2, :].rearrange("p a b -> p (a b)"), ps[:, 0:256], [kp], [KM])
            r = [KM, ("lv",), ("vec",)]
            act(TM1, AB[:, :, 0:8], AF.Exp, r, [KM], scale=-1.0)
            act(NLNB, TM1, AF.Ln, r, [KM], bias=1.0)
            tten("dve", TM2, AB[:, :, 8:16], VEC[:, vb + V_DTB:vb + V_DTB + 8].unsqueeze(1).to_broadcast([128, 16, 8]), ALU.add, r, [KM])
            act(TM2, TM2, AF.Exp, r, [KM])
            act(TM2, TM2, AF.Ln, r, [KM], bias=1.0)
            tten("dve", G3, TM2, LV[:, 0:8].unsqueeze(1).to_broadcast([128, 16, 8]), ALU.mult, r, [KM])
            ps2, kp2 = psum()
            mm(ps2[:, 0:64].rearrange("p (t h) -> p t h", h=4), INCF, G3[:, :, 0:4], True, True, [KM, KC], [kp2])
            mm(ps2[:, 64:128].rearrange("p (t h) -> p t h", h=4), INCB, G3[:, :, 4:8], True, True, [KM, KC], [kp2])
            mm(ps2[:, 128:256], ONESF, MISC[:, 2, :], True, True, [KM, KC], [kp2])
            cpy("dve", CCOL[:, :, 0:4], ps2[:, 0:64].rearrange("p (t h) -> p t h", h=4), [kp2], [KM])
            cpy("dve", CCOL[:, :, 4:8], ps2[:, 64:128].rearrange("p (t h) -> p t h", h=4), [kp2, KM], [KM])
            cpy("dve", MISC[:, 5, :], ps2[:, 128:256], [kp2, KM], [KM])
            tten("dve", TM1, NLNB, CCOL, ALU.add, r, [KM])
            ts1("dve", BIASD, TM1, -1.0, ALU.mult, r, [KM])
            act(TM2, CCOL, AF.Exp, r, [KM])
            ts1("dve", NEGEC, TM2, -1.0, ALU.mult, r, [KM])
            tten("dve", TM1, BIASD, CLAST, ALU.add, r, [KM])
            act(BDEC, TM1, AF.Exp, r, [KM])
            act(ECL, CLAST, AF.Exp, r, [KM])
            cpy("dve", GHI, G3, r, [KM])
            tten("dve", GLO, G3, GHI, ALU.subtract, r, [KM])

        def lru_phase(l):
            vb = l * NV
            vk = ("vec",)
            done = {-1: True}
            conv_done = {-1: True}

            def seq(*gs):
                for g in gs:
                    yield from g

            LWALL = SV[:, 16:20, :].rearrange("p a b -> p (a b)")
            S.dma(LWALL, wsc[l, :, O_LRUW:O_LRUW + 2048], [], [("lwall",)])

            def chunk_gen(c):
                lw, klw = LWALL[:, c * 512:(c + 1) * 512], ("lwall",)
                slab, ks = wload(l, O_WIN + (16 + c) * 1024, 1024)
                while not conv_done.get(c - 1):
                    yield
                if c == 0:
                    pc_pads()
                for t in range(NT):
                    ps, kp = psum()
                    proj_tile(ps, kp, slab, ks, t)
                    cpy("act", A_PC[:, 2 + t * 512:2 + (t + 1) * 512], ps[:], [kp], [("pc", t)])
                    yield
                while not done.get(c - 1):
                    yield
                wbase = vb + V_LCW + c * 4
                rk = [("pc", t) for t in range(NT)] + [("pc", "pad"), vk]
                wk = [("xc", t) for t in range(NT)]
                tsc("dve", A_XC, A_PC[:, 0:2048], VEC[:, wbase:wbase + 1], VEC[:, vb + V_LCB + c:vb + V_LCB + c + 1], ALU.mult, ALU.add, rk, wk)
                yield
                for j in range(1, 4):
                    stt("dve", A_XC, A_PC[:, j:j + 2048], VEC[:, wbase + j:wbase + j + 1], A_XC, ALU.mult, ALU.add, rk + wk, wk)
                    yield
                conv_done[c] = True
                cpy(DBG.get("xcb_eng", "act"), A_XCB, A_XC, wk, [("xcb", t) for t in range(NT)])
                yield
                gslab, kgs = wload(l, O_WIN + (20 + c) * 1024, 1024)

                def gelu_gen():
                    for t in range(NT):
                        x2 = TB[:, 2 * (t % 2):2 * (t % 2) + 2, :].rearrange("p a b -> p (a b)").bitcast(F32)
                        k2 = ("tbf", t % 2)
                        ps, kp = psum()
                        proj_tile(ps, kp, gslab, kgs, t)
                        act(x2, ps[:], AF.Square, [kp], [k2])
                        tsc("pool", x2, x2, GC2, GC1, ALU.mult, ALU.add, [k2], [k2])
                        tten("dve", x2, x2, ps[:], ALU.mult, [k2, kp], [k2])
                        act(x2, x2, AF.Tanh, [k2], [k2])
                        stt("dve", A_GL[:, tsl_(t)], x2, 1.0, ps[:], ALU.add, ALU.mult, [k2, kp], [("gl", t)])
                        yield

                def dir_gen(d):
                    order = list(range(NT)) if d == 0 else list(range(NT - 1, -1, -1))
                    lvi = d * 4 + c
                    for ti, t in enumerate(order):
                        tsl = tsl_(t)
                        base = 4 * (ti % 2)
                        thr, kr_ = TT[:, base + 0, :], ("tt", base + 0)
                        thi, ki_ = TT[:, base + 1, :], ("tt", base + 1)
                        a, ka = TT[:, base + 2, :], ("tt", base + 2)
                        a2, ka2 = TT[:, base + 3, :], ("tt", base + 3)
                        psr, kpr = psum()
                        mm(psr[:], lw[:, (d * 2 + 0) * 128:(d * 2 + 1) * 128], A_XCB[:, tsl], True, True, [klw, ("xcb", t)], [kpr])
                        act(thr, psr[:], AF.Tanh, [kpr, ("lv",)], [kr_], scale=0.5, bias=LV[:, 8 + lvi:9 + lvi])
                        psi, kpi = psum()
                        mm(psi[:], lw[:, (d * 2 + 1) * 128:(d * 2 + 2) * 128], A_XCB[:, tsl], True, True, [klw, ("xcb", t)], [kpi])
                        act(thi, psi[:], AF.Tanh, [kpi, ("lv",)], [ki_], scale=0.5, bias=LV[:, 16 + lvi:17 + lvi])
                        yield
                        act(a, thr, AF.Exp, [kr_, ("lv",)], [ka], scale=LV[:, 32 + lvi:33 + lvi], bias=LV[:, 32 + lvi:33 + lvi])
                        if DBG.get("a2_eng", "act") == "act":
                            act(a2, thr, AF.Exp, [kr_, ("lv",)], [ka2], scale=LV[:, 24 + lvi:25 + lvi], bias=LV[:, 24 + lvi:25 + lvi])
                        else:
                            tten(DBG["a2_eng"], a2, a, a, ALU.mult, [ka], [ka2])
                        yield
                        act(thr, thr, AF.Tanh, [kr_, ("lv",)], [kr_], scale=LV[:, 32 + lvi:33 + lvi], bias=LV[:, 32 + lvi:33 + lvi])
                        stt("dve", thi, thi, 1.0, A_XC[:, tsl], ALU.add, ALU.mult, [ki_, ("xc", t)], [ki_])
                        yield
                        stt("dve", a2, a2, 1.0, thr, ALU.add, ALU.mult, [ka2, kr_], [ka2])
                        yield
                        act(a2, a2, AF.Ln, [ka2], [ka2], scale=-1.0)
                        yield
                        act(a2, a2, AF.Exp, [ka2], [ka2], scale=0.5)
                        yield
                        stt("dve", thi, thi, 0.5, a2, ALU.mult, ALU.mult, [ki_, ka2], [ki_])
                        yield
                        if d == 0:
                            init = 0.0 if ti == 0 else A_HS[:, t * 512 - 1:t * 512]
                            rr_ = [ka, ki_] + ([("hs", t - 1)] if ti else [])
                            S.op("dve", lambda h, o=A_HS[:, tsl], a_=a, b_=thi, i_=init: h.tensor_tensor_scan(
                                out=o, data0=a_, data1=b_, initial=i_, op0=ALU.mult, op1=ALU.add), rr_, [("hs", t)])
                        else:
                            init = 0.0 if ti == 0 else CAR[:, 0:1]
                            S.op("dve", lambda h, o=rev_ap(thr), a_=rev_ap(a), b_=rev_ap(thi), i_=init: h.tensor_tensor_scan(
                                out=o, data0=a_, data1=b_, initial=i_, op0=ALU.mult, op1=ALU.add), [ka, ki_, kr_, ("car",)], [kr_])
                            cpy("dve", CAR[:, 0:1], thr[:, 0:1], [kr_, ("car",)], [("car",)])
                            tten("pool", A_HS[:, tsl], A_HS[:, tsl], thr, ALU.add, [kr_, ("hs", t)], [("hs", t)])
                        yield

                yield from rr([gelu_gen(), seq(dir_gen(0), dir_gen(1))])
                tten(DBG.get("yl_eng", "pool"), YL[:, c, :], A_GL, A_HS, ALU.mult, [("gl", t) for t in range(NT)] + [("hs", t) for t in range(NT)],
                     [("yl", c, t) for t in range(NT)])
                done[c] = True
                yield

            gens = {}
            started = 0
            while started < 4 or gens:
                while started < 4 and len(gens) < 2:
                    gens[started] = chunk_gen(started)
                    started += 1
                for c_ in list(gens.keys()):
                    try:
                        next(gens[c_])
                    except StopIteration:
                        del gens[c_]
            for t in range(NT):
                rs, kr = rms_stats(t, 4, lambda c, t: YL[:, c, tsl_(t)], lambda c, t: ("yl", c, t), 1.0 / 512, 4 * EPS)
                for c in range(4):
                    stt("dve", YL[:, c, tsl_(t)], YL[:, c, tsl_(t)], VEC[:, vb + V_LNG + c:vb + V_LNG + c + 1], rs,
                        ALU.mult, ALU.mult, [("yl", c, t), kr, vk], [("yl", c, t)])
            wout_pass(l, [4, 5, 6, 7], lambda i, t: YL[:, i, tsl_(t)], lambda i, t: ("yl", i, t))

        def rr(gens):
            gens = list(gens)
            while gens:
                for g in list(gens):
                    try:
                        next(g)
                    except StopIteration:
                        gens.remove(g)
                yield

        def gdn(l, h):
            KT3 = A_KT.rearrange("p (n i) -> p n i", i=128)
            QT3 = A_QT.rearrange("p (n i) -> p n i", i=128)
            KTOK3 = A_KTOK.rearrange("p (n i) -> p n i", i=128)
            VTOK3 = A_VTOK.rearrange("p (n i) -> p n i", i=128)
            OS3 = A_OSUM.rearrange("p (n i) -> p n i", i=128)
            mset("pool", SST[:], 0.0, [("sst", 0), ("sst", 1)])
            mset("pool", SBF[:], 0.0, [("sbf", 0), ("sbf", 1)])
            TTB = TT[:].rearrange("p a b -> p (a b)").bitcast(BF16).rearrange("p (s n) -> p s n", n=512)
            slots = [(SV[:, i, :], ("gsv", i)) for i in range(20)] + [(TTB[:, i, :], ("gtt", i)) for i in range(16)]
            NS, NP, PSZ = DBG.get("NS", 4), DBG.get("NP", 3), DBG.get("PSZ", 7)
            outsets = [slots[3 * i:3 * i + 3] for i in range(NS)]
            pools = [slots[3 * NS + PSZ * i:3 * NS + PSZ * (i + 1)] for i in range(NP)]
            MGs = [(A_PC[:, i * 512:(i + 1) * 512], ("mg", i)) for i in range(NP)]

            def elem(m, e):
                if e < 2:
                    return 2 * m + e, 0, h
                return 15 - 2 * m - (e - 2), 1, 4 + h

            def mm4(lh, klh, rh, krh):
                ps, kp = psum()
                p4 = v4(ps[:])
                l4, r4 = v4(lh), v4(rh)
                for e in range(4):
                    mm(p4[:, e, :], l4[:, e, :], r4[:, e, :], True, True, [klh, krh], [kp])
                return ps, kp

            def evac(ps, kp, dst, kd):
                cnt["ev"] += 1
                cpy("act" if cnt["ev"] % DBG.get("evmod", 4) else "dve", dst, ps[:], [kp], [kd])

            def transp4(src, ksrc, dst, kdst):
                pst, kpst = psum()
                pstb = pst[:].bitcast(BF16)
                for e in range(4):
                    S.op("pe", lambda hh, o=pstb[:, e * 128:(e + 1) * 128], i_=v4(src)[:, e, :]: hh.transpose(out=o, in_=i_, identity=IDB),
                         [ksrc, KC], [kpst])
                cpy("act", dst, pstb[:, 0:512], [kpst], [kdst])

            def prep(m, pipe):
                free = list(pools[pipe])

                def alloc():
                    return free.pop(0)

                def rel(*xs):
                    free.extend(xs)

                els = [elem(m, e) for e in range(4)]
                (Z, kZ), (ATT, kATT), (QD, kQD) = outsets[m % NS]
                MH = alloc()
                ML = alloc()
                for e, (tile, d, dh) in enumerate(els):
                    ts1(DBG.get("mg_eng", "dve"), v4(MH[0])[:, e, :], INCF if d == 0 else INCB, GHI[:, tile, dh:dh + 1], ALU.mult, [KM, KC], [MH[1]])
                    ts1(DBG.get("mg_eng", "dve"), v4(ML[0])[:, e, :], INCF if d == 0 else INCB, GLO[:, tile, dh:dh + 1], ALU.mult, [KM, KC], [ML[1]])
                c1, kc1 = psum()
                mm(c1[:], ONESB, MH[0], True, False, [MH[1], KC], [kc1])
                mm(c1[:], ONESB, ML[0], False, False, [ML[1], KC], [kc1])
                mm(v4(c1[:]), IDB, NEG4B, False, True, [KC], [kc1])
                rel(MH, ML)
                DT = alloc()
                for e, (tile, d, dh) in enumerate(els):
                    act(v4(DT[0])[:, e, :], v4(c1[:])[:, e, :], AF.Exp, [kc1, KM], [DT[1]], bias=BIASD[:, tile, dh:dh + 1])
                yield
                DG = alloc()
                for e, (tile, d, dh) in enumerate(els):
                    ts1(DBG.get("dg_eng", "dve"), v4(DG[0])[:, e, :], IDB, ECOL[:, tile, dh:dh + 1], ALU.mult, [KM, KC], [DG[1]])
                c2, kc2 = psum()
                mm(c2[:], ONESB, DG[0], True, True, [DG[1], KC], [kc2])
                rel(DG)
                ER = alloc()
                cpy("act", ER[0], c2[:], [kc2], [ER[1]])
                yield
                kk, kkk = psum()
                for e, (tile, d, dh) in enumerate(els):
                    mm(v4(kk[:])[:, e, :], KT3[:, tile, :], KT3[:, tile, :], True, True, [("kt",)], [kkk])
                BP = alloc()
                tten("dve", BP[0], kk[:], DT[0], ALU.mult, [kkk, DT[1]], [BP[1]])
                yield
                qk, kqk = psum()
                for e, (tile, d, dh) in enumerate(els):
                    mm(v4(qk[:])[:, e, :], KT3[:, tile, :], QT3[:, tile, :], True, True, [("kt",), ("qt",)], [kqk])
                tten("dve", ATT, qk[:], DT[0], ALU.mult, [kqk, DT[1]], [kATT])
                rel(DT)
                yield
                tten(DBG.get("off_eng", "pool"), v4(BP[0]), v4(BP[0]), b4(OFFB[:]), ALU.mult, [BP[1], KC], [BP[1]])
                for e, (tile, d, dh) in enumerate(els):
                    tten(DBG.get("qd_eng", "dve"), v4(QD)[:, e, :], QT3[:, tile, :], v4(ER[0])[:, e, :], ALU.mult, [("qt",), ER[1]], [kQD])
                rel(ER)
                yield
                AT = alloc()
                transp4(BP[0], BP[1], AT[0], AT[1])
                X = alloc()
                tten(DBG.get("x_eng", "pool"), v4(X[0]), v4(BP[0]), b4(NM16), ALU.mult, [BP[1], KC], [X[1]])
                yield
                XT = alloc()
                tten(DBG.get("x_eng", "pool"), v4(XT[0]), v4(AT[0]), b4(NM16), ALU.mult, [AT[1], KC], [XT[1]])
                rel(BP)
                yield

                def prod(lh, rh):
                    o = alloc()
                    evac(*mm4(lh[0], lh[1], rh[0], rh[1]), o[0], o[1])
                    return o

                def plus_i(x):
                    g = alloc()
                    tten(DBG.get("pi_eng", "dve"), v4(g[0]), v4(x[0]), b4(IDB), ALU.add, [x[1], KC], [g[1]])
                    return g

                X2 = prod(XT, X)
                yield
                X2T = prod(X, XT)
                G1T = plus_i(XT)
                rel(X, XT)
                yield
                G2 = plus_i(X2)
                Y1T = prod(G2, G1T)
                rel(G1T, G2)
                yield
                X4 = prod(X2T, X2)
                yield
                X4T = prod(X2, X2T)
                rel(X2, X2T)
                yield
                G4 = plus_i(X4)
                Y2T = prod(G4, Y1T)
                rel(Y1T, G4)
                yield
                X8 = prod(X4T, X4)
                rel(X4, X4T)
                yield
                G8 = plus_i(X8)
                rel(X8)
                Zc = prod(Y2T, G8)
                yield
                ZTc = prod(G8, Y2T)
                rel(Y2T, G8)
                yield
                for k in range(3):
                    OT = alloc()
                    tten(DBG.get("ot_eng", "dve"), v4(OT[0]), v4(AT[0]), b4(MLV[k]), ALU.mult, [AT[1], KC], [OT[1]])
                    w1, kw1 = mm4(OT[0], OT[1], Zc[0], Zc[1])
                    rel(OT)
                    IW = alloc()
                    tten("dve", v4(IW[0]), b4(IDB), v4(w1[:]), ALU.subtract, [kw1, KC], [IW[1]])
                    yield
                    zn, kzn = mm4(ZTc[0], ZTc[1], IW[0], IW[1])
                    rel(IW)
                    if k < 2:
                        Zn = alloc()
                        evac(zn, kzn, Zn[0], Zn[1])
                        rel(Zc, ZTc)
                        yield
                        ZTn = alloc()
                        transp4(Zn[0], Zn[1], ZTn[0], ZTn[1])
                        Zc, ZTc = Zn, ZTn
                    else:
                        evac(zn, kzn, Z, kZ)
                        rel(Zc, ZTc, AT)
                    yield

            def steps(m):
                (Z, kZ), (ATT, kATT), (QD, kQD) = outsets[m % NS]
                Z4, ATT4, QD4 = v4(Z), v4(ATT), v4(QD)
                for sidx in (2 * m, 2 * m + 1):
                    def one(d):
                        tile = sidx if d == 0 else 15 - sidx
                        e = (sidx % 2) + 2 * d
                        dh = h + 4 * d
                        vp, kvp = TB[:, 3 * d + 0, 0:128], ("tb", 3 * d + 0)
                        vraw, kvr = TB[:, 3 * d + 1, 0:128], ("tb", 3 * d + 1)
                        vdec, kvd = TB[:, 3 * d + 2, 0:128], ("tb", 3 * d + 2)
                        ksp, k1 = psum()
                        mm(ksp[:, 0:128], KT3[:, tile, :], SBF[:, d, :], True, True, [("kt",), ("sbf", d)], [k1])
                        stt("dve", vp, ksp[:, 0:128], NEGEC[:, tile, dh:dh + 1], VTOK3[:, tile, :], ALU.mult, ALU.add,
                            [k1, KM, ("vtok",)], [kvp])
                        yield
                        vrp, k2 = psum()
                        mm(vrp[:, 0:128], Z4[:, e, :], vp, True, True, [kZ, kvp], [k2])
                        cpy("act", vraw, vrp[:, 0:128], [k2], [kvr])
                        ts1("dve", vdec, vrp[:, 0:128], BDEC[:, tile, dh:dh + 1], ALU.mult, [k2, KM], [kvd])
                        yield
                        op_, k3 = psum()
                        mm(op_[:, 0:128], SBF[:, d, :], QD4[:, e, :], True, False, [("sbf", d), kQD], [k3])
                        mm(op_[:, 0:128], vraw, ATT4[:, e, :], False, True, [kvr, kATT], [k3])
                        snp, k4 = psum()
                        mm(snp[:, 0:128], KTOK3[:, tile, :], vdec, True, True, [("ktok",), kvd], [k4])
                        stt("dve", SST[:, d, :], SST[:, d, :], ECL[:, tile, dh:dh + 1], snp[:, 0:128], ALU.mult, ALU.add,
                            [k4, KM, ("sst", d)], [("sst", d)])
                        cpy(DBG.get("sbf_eng", "act"), SBF[:, d, :], SST[:, d, :], [("sst", d)], [("sbf", d)])
                        first = (d == 0 and tile < 8) or (d == 1 and tile >= 8)
                        if first:
                            cpy("act", OS3[:, tile, :], op_[:, 0:128], [k3], [("os", tile)])
                        else:
                            tten("dve", OS3[:, tile, :], op_[:, 0:128], OS3[:, tile, :], ALU.add, [k3, ("os", tile)], [("os", tile)])
                        yield
                    yield from rr([one(0), one(1)])

            active = {}
            prep_done = set()
            steps_done = -1
            nextp = 0
            nexts = 0
            freepipes = list(range(NP))
            cur_steps = None
            nbatch = DBG["gdn_m"]
            while nexts < nbatch or active or cur_steps is not None:
                while nextp < 8 and freepipes and (nextp - NS) <= steps_done and nextp < nbatch + 3:
                    pipe = freepipes.pop()
                    active[nextp] = (prep(nextp, pipe), pipe)
                    nextp += 1
                if cur_steps is None and nexts < nbatch and nexts in prep_done:
                    cur_steps = steps(nexts)
                for m_ in list(active.keys()):
                    g, pipe = active[m_]
                    try:
                        next(g)
                    except StopIteration:
                        del active[m_]
                        prep_done.add(m_)
                        freepipes.append(pipe)
                if cur_steps is not None:
                    try:
                        next(cur_steps)
                    except StopIteration:
                        cur_steps = None
                        steps_done = nexts
                        nexts += 1
                if nexts >= nbatch and not active and cur_steps is None:
                    break

        def dn_head(l, h):
            vb = l * NV
            vk = ("vec",)
            SVf = SV[:].rearrange("p a b -> p (a b)").bitcast(F32)
            PCs = [A_PC, SVf[:, 0:2052]]
            CYs = [A_OSUM, SVf[:, 2052:4100]]
            for par in range(2):
                mset("pool", PCs[par][:, 0:2], 0.0, [("pc", par, "pad")])
                mset("pool", PCs[par][:, 2050:2052], 0.0, [("pc", par, "pad")])
            state = dict(tt_busy=False)
            finished = {}

            def chunk_gen(ci, par):
                dst, dk = ((A_QT, "qt"), (A_KT, "kt"), (A_VT, "vt"))[ci]
                PCp, CYp = PCs[par], CYs[par]
                slab, ks = wload(l, O_WIN + (ci * 4 + h) * 1024, 1024)
                for t in range(NT):
                    ps, kp = psum()
                    proj_tile(ps, kp, slab, ks, t)
                    cpy("act", PCp[:, 2 + t * 512:2 + (t + 1) * 512], ps[:], [kp], [("pc", par, t)])
                    yield
                wbase = vb + V_DNCW + (ci * 4 + h) * 4
                rk = [("pc", par, t) for t in range(NT)] + [("pc", par, "pad"), vk]
                wk = [("os", n) for n in range(16)] if par == 0 else [("cy", par, t) for t in range(NT)]
                act(CYp, PCp[:, 0:2048], AF.Identity, rk, wk, scale=VEC[:, wbase:wbase + 1])
                yield
                for j in range(1, 4):
                    stt("dve", CYp, PCp[:, j:j + 2048], VEC[:, wbase + j:wbase + j + 1], CYp, ALU.mult, ALU.add, rk + wk, wk)
                    yield
                while state["tt_busy"]:
                    yield
                state["tt_busy"] = True

                def tile_chain(t):
                    tsl = tsl_(t)
                    kcy = [("os", n) for n in range(4 * t, 4 * t + 4)] if par == 0 else [("cy", par, t)]
                    th, kth = TT[:, 2 * t, :], ("tt", 2 * t)
                    ln, kln = TT[:, 2 * t + 1, :], ("tt", 2 * t + 1)
                    sq, ksq = TB[:, t, :], ("tb", t)
                    act(th, CYp[:, tsl], AF.Tanh, kcy, [kth], scale=0.5)
                    yield
                    if ci == 2:
                        stt("dve", dst[:, tsl], th, 1.0, CYp[:, tsl], ALU.add, ALU.mult, [kth] + kcy, [(dk, t)])
                        yield
                        return
                    stt("dve", th, th, 1.0, CYp[:, tsl], ALU.add, ALU.mult, [kth] + kcy, [kth])
                    yield
                    act(sq, th, AF.Square, [kth], [ksq])
                    yield
                    ps, kp = psum()
                    mm(ps[:], ONESB, sq, True, True, [ksq, KC], [kp])
                    act(ln, ps[:], AF.Ln, [kp], [kln], bias=4 * EPS)
                    yield
                    act(ln, ln, AF.Exp, [kln], [kln], scale=-0.5, bias=(-0.5 * math.log(128.0) if ci == 0 else 0.0))
                    yield
                    tten("pool", dst[:, tsl], th, ln, ALU.mult, [kth, kln], [(dk, t)])
                    yield

                yield from rr([tile_chain(t) for t in range(NT)])
                state["tt_busy"] = False
                finished[ci] = True

            gens = {}
            started = 0
            while len(finished) < 3:
                while started < 3 and len(gens) < 2 and (started < 2 or finished.get(started - 2)):
                    gens[started] = chunk_gen(started, started % 2)
                    started += 1
                for ci in list(gens.keys()):
                    try:
                        next(gens[ci])
                    except StopIteration:
                        del gens[ci]
            if DBG["dn_stop"] <= 1:
                return
            for src, sk, dst, dk, sc in ((A_KT, "kt", A_KTOK, "ktok", 1.0), (A_VT, "vt", A_VTOK, "vtok", 0.5)):
                for half in range(2):
                    pst, kpst = psum()
                    pstb = pst[:].bitcast(BF16)
                    for n in range(8):
                        tile = half * 8 + n
                        S.op("pe", lambda hh, o=pstb[:, n * 128:(n + 1) * 128], i_=src[:, tile * 128:(tile + 1) * 128]:
                             hh.transpose(out=o, in_=i_, identity=IDB), [(sk, tile // 4), KC], [kpst])
                    act(dst[:, half * 1024:(half + 1) * 1024], pstb[:, 0:1024], AF.Identity, [kpst], [(dk,)], scale=sc)
            if DBG["dn_stop"] <= 2:
                return
            S.barrier()
            gdn(l, h)
            S.barrier()
            if DBG["dn_stop"] <= 4:
                return
            zslab, kzs = wload(l, O_WIN + (12 + h) * 1024, 1024)

            def out_chain(t):
                tsl = tsl_(t)
                osk = [("os", n) for n in range(t * 4, t * 4 + 4)]
                ln, kln = TT[:, 2 * t, :], ("tt", 2 * t)
                th, kth = TT[:, 2 * t + 1, :], ("tt", 2 * t + 1)
                sq, ksq = TB[:, t, :], ("tb", t)
                act(sq, A_OSUM[:, tsl], AF.Square, osk, [ksq])
                yield
                psz, kpz = psum()
                proj_tile(psz, kpz, zslab, kzs, t)
                act(th, psz[:], AF.Tanh, [kpz], [kth], scale=0.5)
                stt("dve", th, th, 1.0, psz[:], ALU.add, ALU.mult, [kth, kpz], [kth])
                yield
                ps, kp = psum()
                mm(ps[:], ONESB, sq, True, True, [ksq, KC], [kp])
                act(ln, ps[:], AF.Ln, [kp], [kln], bias=EPS, scale=1.0 / 128)
                yield
                act(ln, ln, AF.Exp, [kln], [kln], scale=-0.5)
                yield
                stt("dve", ln, A_OSUM[:, tsl], VEC[:, vb + V_DNG:vb + V_DNG + 1], ln, ALU.mult, ALU.mult, osk + [kln, vk], [kln])
                yield
                stt("dve", A_VT[:, tsl], ln, 0.5, th, ALU.mult, ALU.mult, [kln, kth], [("vt", t)])
                yield

            interleave([out_chain(t) for t in range(NT)])
            wout_pass(l, [h], lambda i, t: A_VT[:, tsl_(t)], lambda i, t: ("vt", t))

        def ffn_phase(l):
            vb = l * NV
            vk = ("vec",)
            rms_to_hn(vb + V_G2)
            ACTB = ARENA[:, 0:11264].rearrange("p (f n) -> p f n", n=512)
            GP = ARENA[:, 11264:15376].bitcast(F32).rearrange("p (b n) -> p b n", b=4)
            NW = 3
            st = dict(down_q=0)

            def fc_chain(qt, fc, idx):
                t0 = qt * 512
                sg, ksg = wload(l, O_WG + fc * 1024, 1024)
                gb = idx % 4
                gp = GP[:, gb, :]
                kgp = ("gp", gb)
                cg, kcg = TT[:, 2 * (idx % 4), :], ("tt", 2 * (idx % 4))
                x2, k2 = TT[:, 2 * (idx % 4) + 1, :], ("tt", 2 * (idx % 4) + 1)
                psg, kpg = psum()
                proj_tile(psg, kpg, sg, ksg, qt)
                psh, kph = psum()
                sides = []
                if qt > 0:
                    sides.append((0, t0 - 1))
                if qt < NT - 1:
                    sides.append((1, t0 + 512))
                for (si, col) in sides:
                    for kc in range(8):
                        mm(psh[:, si:si + 1], sg[:, kc * 128:(kc + 1) * 128], HN[:, kc, col:col + 1], kc == 0, kc == 7,
                           [ksg, ("HN", kc, col // 512)], [kph])
                cpy("act", gp[:, 1:513], psg[:], [kpg], [kgp])
                if qt > 0:
                    cpy("dve", gp[:, 0:1], psh[:, 0:1], [kph, kgp], [kgp])
                else:
                    mset("dve", gp[:, 0:1], 0.0, [kgp])
                if qt < NT - 1:
                    cpy("dve", gp[:, 513:514], psh[:, 1:2], [kph, kgp], [kgp])
                else:
                    mset("dve", gp[:, 513:514], 0.0, [kgp])
                yield
                wb_ = vb + V_FCW + fc * 3
                tsc("pool", cg, gp[:, 0:512], VEC[:, wb_:wb_ + 1], VEC[:, vb + V_FCB + fc:vb + V_FCB + fc + 1], ALU.mult, ALU.add,
                    [kgp, vk], [kcg])
                yield
                stt("dve", cg, gp[:, 1:513], VEC[:, wb_ + 1:wb_ + 2], cg, ALU.mult, ALU.add, [kgp, vk, kcg], [kcg])
                yield
                stt("dve", cg, gp[:, 2:514], VEC[:, wb_ + 2:wb_ + 3], cg, ALU.mult, ALU.add, [kgp, vk, kcg], [kcg])
                yield
                act(x2, cg, AF.Square, [kcg], [k2])
                yield
                tsc("pool", x2, x2, GC2, GC1, ALU.mult, ALU.add, [k2], [k2])
                yield
                tten("pool", x2, x2, cg, ALU.mult, [k2, kcg], [k2])
                yield
                act(x2, x2, AF.Tanh, [k2], [k2])
                yield
                stt("dve", cg, x2, 1.0, cg, ALU.add, ALU.mult, [k2, kcg], [kcg])
                yield
                while st["down_q"] < qt:
                    yield
                su, ksu = wload(l, O_WU + fc * 1024, 1024)
                psu, kpu = psum()
                proj_tile(psu, kpu, su, ksu, qt)
                stt("dve", ACTB[:, fc, :], cg, 0.5, psu[:], ALU.mult, ALU.mult, [kcg, kpu], [("actb", fc)])
                yield

            def down_gen(qt):
                for m in range(8):
                    ps, kp = psum()
                    pieces = ((0, 8), (8, 16), (16, 22))
                    for (f0, f1) in pieces:
                        sl, ksl = wload(l, O_WD + m * 2816 + f0 * 128, (f1 - f0) * 128)
                        for fc in range(f0, f1):
                            mm(ps[:], sl[:, (fc - f0) * 128:(fc - f0 + 1) * 128], ACTB[:, fc, :], fc == 0, fc == NFC - 1,
                               [ksl, ("actb", fc)], [kp])
                    radd(m, qt, ps, kp)
                    yield
                st["down_q"] = qt + 1

            chains = [(qt, fc) for qt in range(NT) for fc in range(NFC)]
            active = []
            ci = 0
            down = None
            nfin = {qt: 0 for qt in range(NT)}
            while ci < len(chains) or active or down is not None:
                while ci < len(chains) and len(active) < NW:
                    qt, fc = chains[ci]
                    active.append((fc_chain(qt, fc, ci), qt))
                    ci += 1
                for item in list(active):
                    g, qt = item
                    try:
                        next(g)
                    except StopIteration:
                        active.remove(item)
                        nfin[qt] += 1
                        if nfin[qt] == NFC:
                            down = down_gen(qt)
                if down is not None:
                    try:
                        next(down)
                    except StopIteration:
                        down = None

        def ple_phase(l, s):
            vb = l * NV
            rms_to_hn(vb + V_GP)
            PB = ARENA[:, 0:4096].rearrange("p (k n) -> p k n", k=2)
            PF = ARENA[:, 4096:8192].bitcast(F32).rearrange("p (b k n) -> p b k n", b=2, k=2)
            psrc = pT[l, s].rearrange("(k p) n -> p k n", p=128)
            for t in range(NT):
                S.dma(PF[:, t % 2], psrc[:, :, tsl_(t)], [], [("pf", t % 2)])
                cpy("pool", PB[:, :, tsl_(t)], PF[:, t % 2], [("pf", t % 2)], [("pb", t)])
            for m in range(8):
                sg, ksg = wload(l, O_PWG + m * 1024, 1024)
                sp_, ksp_ = wload(l, O_PWP + m * 256, 256)
                for t in range(NT):
                    ps1, kp1 = psum()
                    proj_tile(ps1, kp1, sg, ksg, t)
                    th, kth = tt()
                    act(th, ps1[:], AF.Tanh, [kp1, ("lv",)], [kth], scale=0.5, bias=LV[:, 40 + m:41 + m])
                    ps2, kp2 = psum()
                    for k2 in range(2):
                        mm(ps2[:], sp_[:, k2 * 128:(k2 + 1) * 128], PB[:, k2, tsl_(t)], k2 == 0, k2 == 1, [ksp_, ("pb", t)], [kp2])
                    stt("dve", th, th, 1.0, ps2[:], ALU.add, ALU.mult, [kth, kp2], [kth])
                    stt("dve", R[:, m, tsl_(t)], th, 0.5, R[:, m, tsl_(t)], ALU.mult, ALU.add, [kth, ("R", m, t)], [("R", m, t)])


        for s in range(nseq):
            for ch in range(8):
                S.dma(R[:, ch, :], xT[s, ch * 128:(ch + 1) * 128, :], [], [("R", ch, t) for t in range(NT)])
            for l in range(nlayers):
                layer_prep(l)
                rms_to_hn(l * NV + V_G1)
                if "lru" in phases or "dn" in phases:
                    ab_phase(l)
                S.barrier()
                if "lru" in phases:
                    lru_phase(l)
                    S.barrier()
                if "dn" in phases:
                    for h in range(DBG["heads"]):
                        dn_head(l, h)
                    S.barrier()
                if "ffn" in phases:
                    ffn_phase(l)
                    S.barrier()
                if "ple" in phases:
                    ple_phase(l, s)
                    S.barrier()
            S.barrier()
            if dump_r:
                for ch in range(8):
                    S.dma(yT[s, ch * 128:(ch + 1) * 128, :], R[:, ch, :], [("R", ch, t) for t in range(NT)], [("y", s, ch)])
            else:
                for t in range(NT):
                    rs, kr = rms_stats(t, 8, lambda ch, t: R[:, ch, tsl_(t)], lambda ch, t: ("R", ch, t), 1.0 / D, EPS)
                    for ch in range(8):
                        o, ko = tt()
                        stt("dve" if ch % 2 == 0 else "pool", o, R[:, ch, tsl_(t)], VEC[:, L * NV + ch:L * NV + ch + 1], rs,
                            ALU.mult, ALU.mult, [("R", ch, t), kr], [ko])
                        S.dma(yT[s, ch * 128:(ch + 1) * 128, tsl_(t)], o, [ko], [("y", s, ch, t)])
        S.barrier()
        print("instructions:", S.ninst, {k: v["n"] for k, v in S.E.items()})
    return nc


def _slabs_kn(w):
    K, N = w.shape
    a = w.reshape(K // 128, 128, N // 128, 128)
    return np.ascontiguousarray(a.transpose(2, 1, 0, 3)).reshape(N // 128, 128, K)


def pack_weights(inp):
    wpk = np.zeros((L, 128, WTOT), np.float32)
    for l in range(L):
        w_in = inp["w_in"][l]
        cols = np.concatenate([w_in[:, 0:2048], w_in[:, 2064:3088]], axis=1)
        sl = _slabs_kn(cols)
        wpk[l, :, O_WIN:O_WIN + 24 * 1024] = sl.transpose(1, 0, 2).reshape(128, 24 * 1024)
        ab = w_in[:, 2048:2064].reshape(8, 128, 16).transpose(1, 0, 2).reshape(128, 128)
        wpk[l, :, O_WAB:O_WAB + 128] = ab
        lw = np.zeros((128, 4, 2, 2, 128), np.float32)
        for d in range(2):
            for gi, nm in enumerate(("lru_wa", "lru_wx")):
                wg = inp[nm][l, d]
                for c in range(4):
                    lw[0:64, c, d, gi, 0:64] = wg[2 * c]
                    lw[64:128, c, d, gi, 64:128] = wg[2 * c + 1]
        wpk[l, :, O_LRUW:O_LRUW + 2048] = lw.reshape(128, 2048)
        wpk[l, :, O_WOUT:O_WOUT + 8192] = inp["w_out"][l].reshape(8, 128, 1024).transpose(1, 0, 2).reshape(128, 8192)
        wpk[l, :, O_WG:O_WG + 22528] = _slabs_kn(inp["ffn_wg"][l]).transpose(1, 0, 2).reshape(128, 22528)
        wpk[l, :, O_WU:O_WU + 22528] = _slabs_kn(inp["ffn_wu"][l]).transpose(1, 0, 2).reshape(128, 22528)
        wd = inp["ffn_wd"][l].reshape(NFC, 128, 8, 128)
        wpk[l, :, O_WD:O_WD + 22528] = wd.transpose(1, 2, 0, 3).reshape(128, 22528)
        wpk[l, :, O_PWG:O_PWG + 8192] = _slabs_kn(inp["ple_wg"][l]).transpose(1, 0, 2).reshape(128, 8192)
        wpk[l, :, O_PWP:O_PWP + 2048] = _slabs_kn(inp["ple_wp"][l]).transpose(1, 0, 2).reshape(128, 2048)
    return wpk


def pack_vecs(inp):
    v = np.zeros((128, L * NV + 8), np.float32)

    def cm(a, n):
        return a.reshape(n, 128).T

    for l in range(L):
        b = l * NV
        v[:, b + V_G1:b + V_G1 + 8] = cm(inp["norm1_g"][l], 8)
        v[:, b + V_G2:b + V_G2 + 8] = cm(inp["norm2_g"][l], 8)
        v[:, b + V_GP:b + V_GP + 8] = cm(inp["ple_norm_g"][l], 8)
        v[:, b + V_BGP:b + V_BGP + 8] = cm(inp["ple_bg"][l], 8)
        v[:, b + V_DNCW:b + V_DNCW + 48] = inp["dn_conv_w"][l].reshape(4, 12, 128).transpose(2, 1, 0).reshape(128, 48)
        v[:, b + V_LCW:b + V_LCW + 16] = inp["lru_conv_w"][l].reshape(4, 4, 128).transpose(2, 1, 0).reshape(128, 16)
        v[:, b + V_LCB:b + V_LCB + 4] = cm(inp["lru_conv_b"][l], 4)
        v[:, b + V_LBA:b + V_LBA + 8] = inp["lru_ba"][l].reshape(2, 4, 128).transpose(2, 0, 1).reshape(128, 8)
        v[:, b + V_LBX:b + V_LBX + 8] = inp["lru_bx"][l].reshape(2, 4, 128).transpose(2, 0, 1).reshape(128, 8)
        v[:, b + V_LLAM:b + V_LLAM + 8] = inp["lru_lambda"][l].reshape(2, 4, 128).transpose(2, 0, 1).reshape(128, 8)
        v[:, b + V_LNG:b + V_LNG + 4] = cm(inp["lru_norm_g"][l], 4)
        v[:, b + V_DNG] = inp["dn_norm_g"][l]
        v[:, b + V_FCW:b + V_FCW + 66] = inp["ffn_conv_w"][l].reshape(3, NFC, 128).transpose(2, 1, 0).reshape(128, 66)
        v[:, b + V_FCB:b + V_FCB + 22] = cm(inp["ffn_conv_b"][l], NFC)
        v[:, b + V_ALOG:b + V_ALOG + 8] = np.broadcast_to(inp["dn_a_log"][l].reshape(1, 8), (128, 8))
        v[:, b + V_DTB:b + V_DTB + 8] = np.broadcast_to(inp["dn_dt_bias"][l].reshape(1, 8), (128, 8))
    v[:, L * NV:L * NV + 8] = cm(inp["final_g"], 8)
    return v


def kernel(**inputs):
    inp = {k: np.asarray(v) for k, v in inputs.items()}
    x = inp["x"]
    p = inp["p"]
    wpk = pack_weights(inp)
    vecs = pack_vecs(inp)
    xT = np.ascontiguousarray(x.transpose(0, 2, 1))
    pT = np.ascontiguousarray(p.transpose(0, 1, 3, 2))
    nc = build_nc()
    in_maps = []
    for c in range(NCORE):
        in_maps.append({
            "xT": xT[c * SEQ_PER_CORE:(c + 1) * SEQ_PER_CORE],
            "pT": np.ascontiguousarray(pT[:, c * SEQ_PER_CORE:(c + 1) * SEQ_PER_CORE]),
            "wpk": wpk,
            "vecs": vecs,
        })
    res = run_bass_kernel_spmd(nc, in_maps, core_ids=list(range(NCORE)))
    yT = np.concatenate([r["yT"] for r in res.results], axis=0)
    return np.ascontiguousarray(yT.transpose(0, 2, 1)).astype(np.float32)
```

```python
import math
import numpy as np
from contextlib import ExitStack
import concourse.bass as bass
import concourse.mybir as mybir
from concourse.bass_utils import run_bass_kernel_spmd

F32 = mybir.dt.float32
BF16 = mybir.dt.bfloat16
AF = mybir.ActivationFunctionType
ALU = mybir.AluOpType

D = 1024
S_LEN = 2048
L = 4
NCORE = 8
SEQ_PER_CORE = 4
DFF = 2816
NFC = 22
PLE = 256
EPS = 1e-6
NT = 4
TK = 16
GC1 = math.sqrt(2.0 / math.pi)
GC2 = GC1 * 0.044715

O_WIN = 0
O_WAB = O_WIN + 24 * 1024
O_LRUW = O_WAB + 128
O_WOUT = O_LRUW + 2048
O_WG = O_WOUT + 8192
O_WU = O_WG + 22528
O_WD = O_WU + 22528
O_PWG = O_WD + 22528
O_PWP = O_PWG + 8192
WTOT = O_PWP + 2048

V_G1, V_G2, V_GP, V_BGP = 0, 8, 16, 24
V_DNCW = 32
V_LCW = 80
V_LCB = 96
V_LBA = 100
V_LBX = 108
V_LLAM = 116
V_LNG = 124
V_DNG = 128
V_FCW = 129
V_FCB = 195
V_ALOG = 217
V_DTB = 225
NV = 240


class Sched:
    NDS = 24

    def __init__(self, nc, es):
        self.nc = nc
        self.E = {}
        for name, h in (("pe", nc.tensor), ("act", nc.scalar), ("dve", nc.vector), ("pool", nc.gpsimd), ("sp", nc.sync)):
            sem = es.enter_context(nc.semaphore("s_" + name))
            self.E[name] = dict(h=h, sem=sem, n=0, seen={}, seend={})
        self.dsem = [es.enter_context(nc.semaphore(f"sd{i}")) for i in range(self.NDS)]
        self.dcnt = [0] * self.NDS
        self.dnext = 0
        self.lastw = {}
        self.readers = {}
        self.ninst = 0

    def _wait(self, eng, tok, same_ok):
        X = self.E[eng]
        if tok[0] == "e":
            _, p, c = tok
            if p == eng and (eng == "pe" or not same_ok):
                return
            if X["seen"].get(p, 0) >= c:
                return
            X["h"].wait_ge(self.E[p]["sem"], c)
            X["seen"][p] = c
        else:
            _, i, v = tok
            if X["seend"].get(i, 0) >= v:
                return
            X["h"].wait_ge(self.dsem[i], v)
            X["seend"][i] = v

    def _deps(self, eng, r, w):
        for k in r:
            t = self.lastw.get(k)
            if t is not None:
                self._wait(eng, t, True)
            if k[0] == "ps":
                for t in self.readers.get(k, {}).values():
                    self._wait(eng, t, False)
        for k in w:
            t = self.lastw.get(k)
            if t is not None:
                self._wait(eng, t, True)
            for t in self.readers.get(k, {}).values():
                self._wait(eng, t, False)

    def _record(self, tok, r, w):
        for k in r:
            self.readers.setdefault(k, {})[tok[1]] = tok
        for k in w:
            self.lastw[k] = tok
            self.readers[k] = {}

    def op(self, eng, emit, r=(), w=()):
        X = self.E[eng]
        self._deps(eng, r, w)
        inst = emit(X["h"])
        X["n"] += 1
        inst.then_inc(X["sem"], 1)
        self.ninst += 1
        self._record(("e", eng, X["n"]), r, w)

    def dma(self, out, in_, r=(), w=(), eng="sp"):
        self._deps(eng, r, w)
        i = self.dnext
        self.dnext = (i + 1) % self.NDS
        if self.dcnt[i] > 0:
            self._wait(eng, ("d", i, 16 * self.dcnt[i]), True)
        inst = self.E[eng]["h"].dma_start(out=out, in_=in_)
        self.dcnt[i] += 1
        inst.then_inc(self.dsem[i], 16)
        self.ninst += 1
        self._record(("d", i, 16 * self.dcnt[i]), r, w)

    def barrier(self):
        comp = ("pe", "act", "dve", "pool")
        for e in comp:
            for p in comp:
                if p != e and self.E[p]["n"] > 0:
                    self._wait(e, ("e", p, self.E[p]["n"]), True)
            for i in range(self.NDS):
                if self.dcnt[i] > 0:
                    self._wait(e, ("d", i, 16 * self.dcnt[i]), True)
        for p in comp:
            if self.E[p]["n"] > 0:
                self._wait("sp", ("e", p, self.E[p]["n"]), True)
        for i in range(self.NDS):
            if self.dcnt[i] > 0:
                self._wait("sp", ("d", i, 16 * self.dcnt[i]), True)
        self.lastw = {}
        self.readers = {}


def rev_ap(ap2d):
    a = ap2d.ap
    n = a[-1][1]
    st = a[-1][0]
    return bass.AP(ap2d.tensor, ap2d.offset + (n - 1) * st, [list(x) for x in a[:-1]] + [[-st, n]])


DBG = dict(dn_stop=99, heads=4, gdn_m=8)


def interleave(gens):
    gens = list(gens)
    while gens:
        for g in list(gens):
            try:
                next(g)
            except StopIteration:
                gens.remove(g)


def build_nc(nseq=SEQ_PER_CORE, nlayers=L, dump_r=False, do_prepass=True, phases=("lru", "dn", "ffn", "ple"), ndbg=0):
    nc = bass.Bass("TRN2", target_bir_lowering=False)
    xT = nc.dram_tensor("xT", [SEQ_PER_CORE, D, S_LEN], F32, kind="ExternalInput").ap()
    pT = nc.dram_tensor("pT", [L, SEQ_PER_CORE, PLE, S_LEN], F32, kind="ExternalInput").ap()
    wpk = nc.dram_tensor("wpk", [L, 128, WTOT], F32, kind="ExternalInput").ap()
    vecs = nc.dram_tensor("vecs", [128, L * NV + 8], F32, kind="ExternalInput").ap()
    yT = nc.dram_tensor("yT", [SEQ_PER_CORE, D, S_LEN], F32, kind="ExternalOutput").ap()
    wsc = nc.dram_tensor("wsc", [L, 128, WTOT], BF16, kind="Internal").ap()
    dbg_out = None
    if ndbg:
        dbg_out = nc.dram_tensor("dbg", [ndbg, 128, S_LEN], F32, kind="ExternalOutput").ap()

    es = ExitStack()
    with es:
        def sb(name, shape, dt):
            return es.enter_context(nc.sbuf_tensor(name, shape, dt))

        S = Sched(nc, es)
        R = sb("R", [128, 8, S_LEN], F32)
        HN = sb("HN", [128, 8, S_LEN], BF16)
        ARENA = sb("ARENA", [128, 18560], BF16)
        TT = sb("TT", [128, 8, 512], F32)
        TB = sb("TB", [128, 6, 512], BF16)
        SV = sb("SV", [128, 20, 512], BF16)
        SF = sb("SF", [128, 512], F32)
        WB = sb("WB", [128, 5, 1024], BF16)
        VEC = sb("VEC", [128, L * NV + 8], F32)
        CST = sb("CST", [128, 13, 128], F32)
        CSB = sb("CSB", [128, 7, 128], BF16)
        MISC = sb("MISC", [128, 12, 128], F32)
        LV = sb("LV", [128, 64], F32)
        SST = sb("SST", [128, 2, 128], F32)
        SBF = sb("SBF", [128, 2, 128], BF16)
        CAR = sb("CAR", [128, 4], F32)
        PS = [es.enter_context(nc.psum_tensor(f"ps{i}", [128, 512], F32)) for i in range(8)]
        cnt = dict(ps=0, tt=0, tb=0, wb=0, sv=0, ev=0)

        def psum():
            i = cnt["ps"] % 8
            cnt["ps"] += 1
            return PS[i], ("ps", i)

        def tt():
            i = cnt["tt"] % 8
            cnt["tt"] += 1
            return TT[:, i, :], ("tt", i)

        def tb():
            i = cnt["tb"] % 6
            cnt["tb"] += 1
            return TB[:, i, :], ("tb", i)

        def wslot():
            i = cnt["wb"] % 5
            cnt["wb"] += 1
            return WB[:, i, :], ("wb", i)

        def sv():
            i = 10 + cnt["sv"] % 10
            cnt["sv"] += 1
            return SV[:, i, :], ("sv", i)

        def act(out, in_, func, r, w, bias=0.0, scale=1.0):
            S.op("act", lambda h: h.activation(out=out, in_=in_, func=func, bias=bias, scale=scale), r, w)

        def tsc(eng, out, in0, s1, s2, op0, op1, r, w):
            S.op(eng, lambda h: h.tensor_scalar(out=out, in0=in0, scalar1=s1, scalar2=s2, op0=op0, op1=op1), r, w)

        def ts1(eng, out, in_, s, op, r, w):
            S.op(eng, lambda h: h.tensor_single_scalar(out=out, in_=in_, scalar=s, op=op), r, w)

        def stt(eng, out, in0, s, in1, op0, op1, r, w):
            eng = "dve"
            S.op(eng, lambda h: h.scalar_tensor_tensor(out=out, in0=in0, scalar=s, in1=in1, op0=op0, op1=op1), r, w)

        def tten(eng, out, in0, in1, op, r, w):
            S.op(eng, lambda h: h.tensor_tensor(out=out, in0=in0, in1=in1, op=op), r, w)

        def cpy(eng, out, in_, r, w):
            if eng == "act":
                act(out, in_, AF.Identity, r, w)
            else:
                S.op(eng, lambda h: h.tensor_copy(out=out, in_=in_), r, w)

        def mm(out, lhsT, rhs, start, stop, r, w):
            S.op("pe", lambda h: h.matmul(out, lhsT=lhsT, rhs=rhs, start=start, stop=stop), r, w)

        def mset(eng, ap, val, w):
            S.op(eng, lambda h: h.memset(ap, val), (), w)

        INCF, INCB, OFFD, ONESF, IDF = (CST[:, i, :] for i in range(5))
        NEG4 = CST[:, 5:9, :]
        M16, ML0, ML1, ML2, IDB, ONESB = (CSB[:, i, :] for i in range(6))
        MLV = [ML0, ML1, ML2]
        KC = ("cst",)

        def b4(ap):
            return ap.unsqueeze(1).to_broadcast([128, 4, 128])

        def v4(ap):
            return ap.rearrange("p (e i) -> p e i", e=4)

        def aff(ap, pattern, op, fill, base, cm):
            S.op("pool", lambda h: h.affine_select(out=ap, in_=ap, pattern=pattern, compare_op=op, fill=fill,
                                                   base=base, channel_multiplier=cm), [KC], [KC])

        mset("pool", CST[:, 0:5, :], 1.0, [KC])
        aff(INCF, [[1, 128]], ALU.is_ge, 0.0, 0, -1)
        aff(INCB, [[-1, 128]], ALU.is_ge, 0.0, 0, 1)
        aff(OFFD, [[1, 128]], ALU.not_equal, 0.0, 0, -1)
        aff(IDF, [[1, 128]], ALU.is_equal, 0.0, 0, -1)
        for e in range(4):
            tsc("pool", NEG4[:, e, :], INCF if e < 2 else INCB, -1.0, 1e30, ALU.add, ALU.mult, [KC], [KC])
        mset("pool", CST[:, 9:13, :], 1.0, [KC])
        for bi, b in enumerate((16, 32, 64)):
            nb = 128 // b
            v = CST[:, 9 + bi, :].rearrange("p (k c) -> p k c", c=b)
            aff(v, [[-b, nb], [0, b]], ALU.is_ge, 0.0, 0, 1)
            aff(v, [[b, nb], [0, b]], ALU.is_gt, 0.0, b, -1)
        cpy("pool", M16, CST[:, 9, :], [KC], [KC])
        for k in range(3):
            tten("pool", MLV[k], CST[:, 10 + k, :], CST[:, 9 + k, :], ALU.subtract, [KC], [KC])
        cpy("pool", IDB, IDF, [KC], [KC])
        cpy("pool", ONESB, ONESF, [KC], [KC])
        NM16 = CSB[:, 6, :]
        ts1("pool", NM16, CST[:, 9, :], -1.0, ALU.mult, [KC], [KC])
        OFFB = sb("OFFB", [128, 128], BF16)
        cpy("pool", OFFB[:], OFFD, [KC], [KC])
        NEG4B = CST[:, 9:11, :].rearrange("p a b -> p (a b)").bitcast(BF16).rearrange("p (e i) -> p e i", e=4)
        cpy("pool", NEG4B, NEG4, [KC], [KC])
        S.dma(VEC[:], vecs[:, :], [], [("vec",)])
        S.barrier()

        if do_prepass:
            CH = 8192
            for l in range(nlayers):
                c0 = 0
                while c0 < WTOT:
                    cw = min(CH, WTOT - c0)
                    S.dma(wsc[l, :, c0:c0 + cw], wpk[l, :, c0:c0 + cw], [], [("wsc", l, c0)], eng="pool")
                    c0 += cw
            S.barrier()

        def wload(l, off, width):
            slot, k = wslot()
            S.dma(slot[:, 0:width], wsc[l, :, off:off + width], [], [k])
            return slot, k

        def tsl_(t):
            return slice(t * 512, (t + 1) * 512)

        def rms_stats(t, nch, src, srck, scale, eps):
            ps, kp = psum()
            for ch in range(nch):
                sq, ks = tb()
                act(sq, src(ch, t), AF.Square, [srck(ch, t)], [ks])
                mm(ps[:], ONESB, sq, ch == 0, ch == nch - 1, [ks], [kp])
            ln, kl = tt()
            act(ln, ps[:], AF.Ln, [kp], [kl], bias=eps, scale=scale)
            rs, kr = tt()
            act(rs, ln, AF.Exp, [kl], [kr], scale=-0.5)
            return rs, kr

        def rms_to_hn(gbase):
            pss = []
            for t in range(NT):
                ps, kp = psum()
                for ch in range(8):
                    sq, ks = tb()
                    act(sq, R[:, ch, tsl_(t)], AF.Square, [("R", ch, t)], [ks])
                    mm(ps[:], ONESB, sq, ch == 0, ch == 7, [ks], [kp])
                pss.append((ps, kp))
            rss = []
            for t in range(NT):
                ps, kp = pss[t]
                rs, kr = TT[:, 4 + t, :], ("tt", 4 + t)
                act(rs, ps[:], AF.Ln, [kp], [kr], bias=EPS, scale=1.0 / D)
                rss.append((rs, kr))
            for t in range(NT):
                rs, kr = rss[t]
                act(rs, rs, AF.Exp, [kr], [kr], scale=-0.5)
            for t in range(NT):
                rs, kr = rss[t]
                for ch in range(8):
                    stt("dve", HN[:, ch, tsl_(t)], R[:, ch, tsl_(t)],
                        VEC[:, gbase + ch:gbase + ch + 1], rs, ALU.mult, ALU.mult, [("R", ch, t), kr, ("vec",)], [("HN", ch, t)])

        def proj_tile(ps, kp, slab, ks, t):
            for kc in range(8):
                mm(ps[:], slab[:, kc * 128:(kc + 1) * 128], HN[:, kc, tsl_(t)], kc == 0, kc == 7, [ks, ("HN", kc, t)], [kp])

        def radd(m, t, ps, kp):
            tten("dve", R[:, m, tsl_(t)], ps[:], R[:, m, tsl_(t)], ALU.add, [kp, ("R", m, t)], [("R", m, t)])

        def dbg_store(idx, ap2d, keys):
            if dbg_out is not None and idx < ndbg:
                S.dma(dbg_out[idx, :, 0:ap2d.shape[-1]], ap2d, keys, [("dbg", idx)])

        A_QT = ARENA[:, 0:2048]
        A_KT = ARENA[:, 2048:4096]
        A_VT = ARENA[:, 4096:6144]
        A_KTOK = ARENA[:, 6144:8192]
        A_VTOK = ARENA[:, 8192:10240]
        A_OSUM = ARENA[:, 10240:14336].bitcast(F32)
        A_PC = ARENA[:, 14336:18440].bitcast(F32)
        A_XCB = ARENA[:, 0:2048]
        A_GL = ARENA[:, 2048:6144].bitcast(F32)
        A_HS = ARENA[:, 6144:10240].bitcast(F32)
        A_XC = A_OSUM
        YL = SV[:, 0:16, :].rearrange("p (c a) n -> p c (a n)", c=4)

        def pc_pads():
            mset("pool", A_PC[:, 0:2], 0.0, [("pc", "pad")])
            mset("pool", A_PC[:, 2050:2052], 0.0, [("pc", "pad")])

        def conv4(dst, dstk, wbase, bias_ap, vk):
            rk = [("pc", t) for t in range(NT)] + [("pc", "pad"), vk]
            wk = [(dstk, t) for t in range(NT)]
            if bias_ap is None:
                ts1("pool", dst, A_PC[:, 0:2048], VEC[:, wbase:wbase + 1], ALU.mult, rk, wk)
            else:
                tsc("pool", dst, A_PC[:, 0:2048], VEC[:, wbase:wbase + 1], bias_ap, ALU.mult, ALU.add, rk, wk)
            for j in range(1, 4):
                stt("pool" if j % 2 else "dve", dst, A_PC[:, j:j + 2048], VEC[:, wbase + j:wbase + j + 1], dst,
                    ALU.mult, ALU.add, rk + wk, wk)

        def wout_pass(l, kcs, src, srck):
            slabs = [wload(l, O_WOUT + kc * 1024, 1024) for kc in kcs]
            for m in range(8):
                for t in range(NT):
                    ps, kp = psum()
                    for i, (sl, ks) in enumerate(slabs):
                        mm(ps[:], sl[:, m * 128:(m + 1) * 128], src(i, t), i == 0, i == len(slabs) - 1, [ks, srck(i, t)], [kp])
                    radd(m, t, ps, kp)

        def layer_prep(l):
            vb = l * NV
            kl = ("lv",)
            vk = ("vec",)
            act(LV[:, 48:56], VEC[:, vb + V_ALOG:vb + V_ALOG + 8], AF.Exp, [vk], [kl])
            ts1("dve", LV[:, 0:8], LV[:, 48:56], -1.0, ALU.mult, [kl], [kl])
            ts1("dve", LV[:, 8:16], VEC[:, vb + V_LBA:vb + V_LBA + 8], 0.5, ALU.mult, [vk, kl], [kl])
            ts1("dve", LV[:, 16:24], VEC[:, vb + V_LBX:vb + V_LBX + 8], 0.5, ALU.mult, [vk, kl], [kl])
            act(LV[:, 48:56], VEC[:, vb + V_LLAM:vb + V_LLAM + 8], AF.Exp, [vk, kl], [kl], scale=-1.0)
            act(LV[:, 56:64], LV[:, 48:56], AF.Ln, [kl], [kl], bias=1.0)
            ts1("dve", LV[:, 24:32], LV[:, 56:64], -8.0, ALU.mult, [kl], [kl])
            ts1("dve", LV[:, 32:40], LV[:, 56:64], -4.0, ALU.mult, [kl], [kl])
            ts1("dve", LV[:, 40:48], VEC[:, vb + V_BGP:vb + V_BGP + 8], 0.5, ALU.mult, [vk, kl], [kl])

        def m3(i):
            return MISC[:, i, :].rearrange("p (t h) -> p t h", h=8)

        AB = MISC[:, 0:2, :].rearrange("p a b -> p (a b)").rearrange("p (t c) -> p t c", c=16)
        G3, NLNB, CCOL, CLAST, BIASD, NEGEC, BDEC, ECL, TM1, TM2 = (m3(i) for i in range(2, 12))
        KM = ("misc",)
        GHL = MISC[:, 10, :].bitcast(BF16).rearrange("p (a t h) -> p a t h", a=2, h=8)
        GHI, GLO = GHL[:, 0], GHL[:, 1]
        ECOL = TM2

        def ab_phase(l):
            vb = l * NV
            slab, ks = wload(l, O_WAB, 128)
            ps, kp = psum()
            for n in range(TK):
                for kc in range(8):
                    mm(ps[:, n * 16:(n + 1) * 16], HN[:, kc, n * 128:(n + 1) * 128], slab[:, kc * 16:(kc + 1) * 16],
                       kc == 0, kc == 7, [ks, ("HN", kc, n // 4)], [kp])
            cpy("dve", MISC[:, 0:2, :].rearrange("p a b -> p (a b)"), ps[:, 0:256], [kp], [KM])
            r = [KM, ("lv",), ("vec",)]
            act(TM1, AB[:, :, 0:8], AF.Exp, r, [KM], scale=-1.0)
            act(NLNB, TM1, AF.Ln, r, [KM], bias=1.0)
            tten("dve", TM2, AB[:, :, 8:16], VEC[:, vb + V_DTB:vb + V_DTB + 8].unsqueeze(1).to_broadcast([128, 16, 8]), ALU.add, r, [KM])
            act(TM2, TM2, AF.Exp, r, [KM])
            act(TM2, TM2, AF.Ln, r, [KM], bias=1.0)
            tten("dve", G3, TM2, LV[:, 0:8].unsqueeze(1).to_broadcast([128, 16, 8]), ALU.mult, r, [KM])
            ps2, kp2 = psum()
            mm(ps2[:, 0:64].rearrange("p (t h) -> p t h", h=4), INCF, G3[:, :, 0:4], True, True, [KM, KC], [kp2])
            mm(ps2[:, 64:128].rearrange("p (t h) -> p t h", h=4), INCB, G3[:, :, 4:8], True, True, [KM, KC], [kp2])
            mm(ps2[:, 128:256], ONESF, MISC[:, 2, :], True, True, [KM, KC], [kp2])
            cpy("dve", CCOL[:, :, 0:4], ps2[:, 0:64].rearrange("p (t h) -> p t h", h=4), [kp2], [KM])
            cpy("dve", CCOL[:, :, 4:8], ps2[:, 64:128].rearrange("p (t h) -> p t h", h=4), [kp2, KM], [KM])
            cpy("dve", MISC[:, 5, :], ps2[:, 128:256], [kp2, KM], [KM])
            tten("dve", TM1, NLNB, CCOL, ALU.add, r, [KM])
            ts1("dve", BIASD, TM1, -1.0, ALU.mult, r, [KM])
            act(TM2, CCOL, AF.Exp, r, [KM])
            ts1("dve", NEGEC, TM2, -1.0, ALU.mult, r, [KM])
            tten("dve", TM1, BIASD, CLAST, ALU.add, r, [KM])
            act(BDEC, TM1, AF.Exp, r, [KM])
            act(ECL, CLAST, AF.Exp, r, [KM])
            cpy("dve", GHI, G3, r, [KM])
            tten("dve", GLO, G3, GHI, ALU.subtract, r, [KM])

        def lru_phase(l):
            vb = l * NV
            vk = ("vec",)
            done = {-1: True}
            conv_done = {-1: True}

            def seq(*gs):
                for g in gs:
                    yield from g

            LWALL = SV[:, 16:20, :].rearrange("p a b -> p (a b)")
            S.dma(LWALL, wsc[l, :, O_LRUW:O_LRUW + 2048], [], [("lwall",)])

            def chunk_gen(c):
                lw, klw = LWALL[:, c * 512:(c + 1) * 512], ("lwall",)
                slab, ks = wload(l, O_WIN + (16 + c) * 1024, 1024)
                while not conv_done.get(c - 1):
                    yield
                if c == 0:
                    pc_pads()
                for t in range(NT):
                    ps, kp = psum()
                    proj_tile(ps, kp, slab, ks, t)
                    cpy("act", A_PC[:, 2 + t * 512:2 + (t + 1) * 512], ps[:], [kp], [("pc", t)])
                    yield
                while not done.get(c - 1):
                    yield
                wbase = vb + V_LCW + c * 4
                rk = [("pc", t) for t in range(NT)] + [("pc", "pad"), vk]
                wk = [("xc", t) for t in range(NT)]
                tsc("dve", A_XC, A_PC[:, 0:2048], VEC[:, wbase:wbase + 1], VEC[:, vb + V_LCB + c:vb + V_LCB + c + 1], ALU.mult, ALU.add, rk, wk)
                yield
                for j in range(1, 4):
                    stt("dve", A_XC, A_PC[:, j:j + 2048], VEC[:, wbase + j:wbase + j + 1], A_XC, ALU.mult, ALU.add, rk + wk, wk)
                    yield
                conv_done[c] = True
                cpy(DBG.get("xcb_eng", "act"), A_XCB, A_XC, wk, [("xcb", t) for t in range(NT)])
                yield
                gslab, kgs = wload(l, O_WIN + (20 + c) * 1024, 1024)

                def gelu_gen():
                    for t in range(NT):
                        x2 = TB[:, 2 * (t % 2):2 * (t % 2) + 2, :].rearrange("p a b -> p (a b)").bitcast(F32)
                        k2 = ("tbf", t % 2)
                        ps, kp = psum()
                        proj_tile(ps, kp, gslab, kgs, t)
                        act(x2, ps[:], AF.Square, [kp], [k2])
                        tsc("pool", x2, x2, GC2, GC1, ALU.mult, ALU.add, [k2], [k2])
                        tten("dve", x2, x2, ps[:], ALU.mult, [k2, kp], [k2])
                        act(x2, x2, AF.Tanh, [k2], [k2])
                        stt("dve", A_GL[:, tsl_(t)], x2, 1.0, ps[:], ALU.add, ALU.mult, [k2, kp], [("gl", t)])
                        yield

                def dir_gen(d):
                    order = list(range(NT)) if d == 0 else list(range(NT - 1, -1, -1))
                    yield from rr([tile_gen(d, 0, order[0]), tile_gen(d, 1, order[1])])
                    yield from rr([tile_gen(d, 2, order[2]), tile_gen(d, 3, order[3])])

                def tile_gen(d, ti, t):
                    lvi = d * 4 + c
                    if True:
                        tsl = tsl_(t)
                        base = 4 * (ti % 2)
                        thr, kr_ = TT[:, base + 0, :], ("tt", base + 0)
                        thi, ki_ = TT[:, base + 1, :], ("tt", base + 1)
                        a, ka = TT[:, base + 2, :], ("tt", base + 2)
                        a2, ka2 = TT[:, base + 3, :], ("tt", base + 3)
                        psr, kpr = psum()
                        mm(psr[:], lw[:, (d * 2 + 0) * 128:(d * 2 + 1) * 128], A_XCB[:, tsl], True, True, [klw, ("xcb", t)], [kpr])
                        act(thr, psr[:], AF.Tanh, [kpr, ("lv",)], [kr_], scale=0.5, bias=LV[:, 8 + lvi:9 + lvi])
                        psi, kpi = psum()
                        mm(psi[:], lw[:, (d * 2 + 1) * 128:(d * 2 + 2) * 128], A_XCB[:, tsl], True, True, [klw, ("xcb", t)], [kpi])
                        act(thi, psi[:], AF.Tanh, [kpi, ("lv",)], [ki_], scale=0.5, bias=LV[:, 16 + lvi:17 + lvi])
                        yield
                        act(a, thr, AF.Exp, [kr_, ("lv",)], [ka], scale=LV[:, 32 + lvi:33 + lvi], bias=LV[:, 32 + lvi:33 + lvi])
                        if DBG.get("a2_eng", "act") == "act":
                            act(a2, thr, AF.Exp, [kr_, ("lv",)], [ka2], scale=LV[:, 24 + lvi:25 + lvi], bias=LV[:, 24 + lvi:25 + lvi])
                        else:
                            tten(DBG["a2_eng"], a2, a, a, ALU.mult, [ka], [ka2])
                        yield
                        act(thr, thr, AF.Tanh, [kr_, ("lv",)], [kr_], scale=LV[:, 32 + lvi:33 + lvi], bias=LV[:, 32 + lvi:33 + lvi])
                        stt("dve", thi, thi, 1.0, A_XC[:, tsl], ALU.add, ALU.mult, [ki_, ("xc", t)], [ki_])
                        yield
                        stt("dve", a2, a2, 1.0, thr, ALU.add, ALU.mult, [ka2, kr_], [ka2])
                        yield
                        act(a2, a2, AF.Ln, [ka2], [ka2], scale=-1.0)
                        yield
                        act(a2, a2, AF.Exp, [ka2], [ka2], scale=0.5)
                        yield
                        stt("dve", thi, thi, 0.5, a2, ALU.mult, ALU.mult, [ki_, ka2], [ki_])
                        yield
                        if d == 0:
                            init = 0.0 if ti == 0 else A_HS[:, t * 512 - 1:t * 512]
                            rr_ = [ka, ki_] + ([("hs", t - 1)] if ti else [])
                            S.op("dve", lambda h, o=A_HS[:, tsl], a_=a, b_=thi, i_=init: h.tensor_tensor_scan(
                                out=o, data0=a_, data1=b_, initial=i_, op0=ALU.mult, op1=ALU.add), rr_, [("hs", t)])
                        else:
                            init = 0.0 if ti == 0 else CAR[:, 0:1]
                            S.op("dve", lambda h, o=rev_ap(thr), a_=rev_ap(a), b_=rev_ap(thi), i_=init: h.tensor_tensor_scan(
                                out=o, data0=a_, data1=b_, initial=i_, op0=ALU.mult, op1=ALU.add), [ka, ki_, kr_, ("car",)], [kr_])
                            cpy("dve", CAR[:, 0:1], thr[:, 0:1], [kr_, ("car",)], [("car",)])
                            tten("pool", A_HS[:, tsl], A_HS[:, tsl], thr, ALU.add, [kr_, ("hs", t)], [("hs", t)])
                        yield

                yield from rr([gelu_gen(), seq(dir_gen(0), dir_gen(1))])
                tten(DBG.get("yl_eng", "pool"), YL[:, c, :], A_GL, A_HS, ALU.mult, [("gl", t) for t in range(NT)] + [("hs", t) for t in range(NT)],
                     [("yl", c, t) for t in range(NT)])
                done[c] = True
                yield

            gens = {}
            started = 0
            while started < 4 or gens:
                while started < 4 and len(gens) < 2:
                    gens[started] = chunk_gen(started)
                    started += 1
                for c_ in list(gens.keys()):
                    try:
                        next(gens[c_])
                    except StopIteration:
                        del gens[c_]
            for t in range(NT):
                rs, kr = rms_stats(t, 4, lambda c, t: YL[:, c, tsl_(t)], lambda c, t: ("yl", c, t), 1.0 / 512, 4 * EPS)
                for c in range(4):
                    stt("dve", YL[:, c, tsl_(t)], YL[:, c, tsl_(t)], VEC[:, vb + V_LNG + c:vb + V_LNG + c + 1], rs,
                        ALU.mult, ALU.mult, [("yl", c, t), kr, vk], [("yl", c, t)])
            wout_pass(l, [4, 5, 6, 7], lambda i, t: YL[:, i, tsl_(t)], lambda i, t: ("yl", i, t))

        def rr(gens):
            gens = list(gens)
            while gens:
                for g in list(gens):
                    try:
                        next(g)
                    except StopIteration:
                        gens.remove(g)
                yield

        def gdn(l, h):
            KT3 = A_KT.rearrange("p (n i) -> p n i", i=128)
            QT3 = A_QT.rearrange("p (n i) -> p n i", i=128)
            KTOK3 = A_KTOK.rearrange("p (n i) -> p n i", i=128)
            VTOK3 = A_VTOK.rearrange("p (n i) -> p n i", i=128)
            OS3 = A_OSUM.rearrange("p (n i) -> p n i", i=128)
            mset("pool", SST[:], 0.0, [("sst", 0), ("sst", 1)])
            mset("pool", SBF[:], 0.0, [("sbf", 0), ("sbf", 1)])
            TTB = TT[:].rearrange("p a b -> p (a b)").bitcast(BF16).rearrange("p (s n) -> p s n", n=512)
            slots = [(SV[:, i, :], ("gsv", i)) for i in range(20)] + [(TTB[:, i, :], ("gtt", i)) for i in range(16)]
            NS, NP, PSZ = DBG.get("NS", 4), DBG.get("NP", 3), DBG.get("PSZ", 7)
            outsets = [slots[3 * i:3 * i + 3] for i in range(NS)]
            pools = [slots[3 * NS + PSZ * i:3 * NS + PSZ * (i + 1)] for i in range(NP)]
            MGs = [(A_PC[:, i * 512:(i + 1) * 512], ("mg", i)) for i in range(NP)]

            def elem(m, e):
                if e < 2:
                    return 2 * m + e, 0, h
                return 15 - 2 * m - (e - 2), 1, 4 + h

            def mm4(lh, klh, rh, krh):
                ps, kp = psum()
                p4 = v4(ps[:])
                l4, r4 = v4(lh), v4(rh)
                for e in range(4):
                    mm(p4[:, e, :], l4[:, e, :], r4[:, e, :], True, True, [klh, krh], [kp])
                return ps, kp

            def evac(ps, kp, dst, kd):
                cnt["ev"] += 1
                cpy("act" if cnt["ev"] % DBG.get("evmod", 4) else "dve", dst, ps[:], [kp], [kd])

            def transp4(src, ksrc, dst, kdst):
                pst, kpst = psum()
                pstb = pst[:].bitcast(BF16)
                for e in range(4):
                    S.op("pe", lambda hh, o=pstb[:, e * 128:(e + 1) * 128], i_=v4(src)[:, e, :]: hh.transpose(out=o, in_=i_, identity=IDB),
                         [ksrc, KC], [kpst])
                cpy("act", dst, pstb[:, 0:512], [kpst], [kdst])

            def prep(m, pipe):
                free = list(pools[pipe])

                def alloc():
                    return free.pop(0)

                def rel(*xs):
                    free.extend(xs)

                els = [elem(m, e) for e in range(4)]
                (Z, kZ), (ATT, kATT), (QD, kQD) = outsets[m % NS]
                MH = alloc()
                ML = alloc()
                for e, (tile, d, dh) in enumerate(els):
                    ts1(DBG.get("mg_eng", "dve"), v4(MH[0])[:, e, :], INCF if d == 0 else INCB, GHI[:, tile, dh:dh + 1], ALU.mult, [KM, KC], [MH[1]])
                    ts1(DBG.get("mg_eng", "dve"), v4(ML[0])[:, e, :], INCF if d == 0 else INCB, GLO[:, tile, dh:dh + 1], ALU.mult, [KM, KC], [ML[1]])
                c1, kc1 = psum()
                mm(c1[:], ONESB, MH[0], True, False, [MH[1], KC], [kc1])
                mm(c1[:], ONESB, ML[0], False, False, [ML[1], KC], [kc1])
                mm(v4(c1[:]), IDB, NEG4B, False, True, [KC], [kc1])
                rel(MH, ML)
                DT = alloc()
                for e, (tile, d, dh) in enumerate(els):
                    act(v4(DT[0])[:, e, :], v4(c1[:])[:, e, :], AF.Exp, [kc1, KM], [DT[1]], bias=BIASD[:, tile, dh:dh + 1])
                yield
                DG = alloc()
                for e, (tile, d, dh) in enumerate(els):
                    ts1(DBG.get("dg_eng", "dve"), v4(DG[0])[:, e, :], IDB, ECOL[:, tile, dh:dh + 1], ALU.mult, [KM, KC], [DG[1]])
                c2, kc2 = psum()
                mm(c2[:], ONESB, DG[0], True, True, [DG[1], KC], [kc2])
                rel(DG)
                ER = alloc()
                cpy("act", ER[0], c2[:], [kc2], [ER[1]])
                yield
                kk, kkk = psum()
                for e, (tile, d, dh) in enumerate(els):
                    mm(v4(kk[:])[:, e, :], KT3[:, tile, :], KT3[:, tile, :], True, True, [("kt",)], [kkk])
                BP = alloc()
                tten("dve", BP[0], kk[:], DT[0], ALU.mult, [kkk, DT[1]], [BP[1]])
                yield
                qk, kqk = psum()
                for e, (tile, d, dh) in enumerate(els):
                    mm(v4(qk[:])[:, e, :], KT3[:, tile, :], QT3[:, tile, :], True, True, [("kt",), ("qt",)], [kqk])
                tten("dve", ATT, qk[:], DT[0], ALU.mult, [kqk, DT[1]], [kATT])
                rel(DT)
                yield
                tten(DBG.get("off_eng", "pool"), v4(BP[0]), v4(BP[0]), b4(OFFB[:]), ALU.mult, [BP[1], KC], [BP[1]])
                for e, (tile, d, dh) in enumerate(els):
                    tten(DBG.get("qd_eng", "dve"), v4(QD)[:, e, :], QT3[:, tile, :], v4(ER[0])[:, e, :], ALU.mult, [("qt",), ER[1]], [kQD])
                rel(ER)
                yield
                AT = alloc()
                transp4(BP[0], BP[1], AT[0], AT[1])
                X = alloc()
                tten(DBG.get("x_eng", "pool"), v4(X[0]), v4(BP[0]), b4(NM16), ALU.mult, [BP[1], KC], [X[1]])
                yield
                XT = alloc()
                tten(DBG.get("x_eng", "pool"), v4(XT[0]), v4(AT[0]), b4(NM16), ALU.mult, [AT[1], KC], [XT[1]])
                rel(BP)
                yield

                def prod(lh, rh):
                    o = alloc()
                    evac(*mm4(lh[0], lh[1], rh[0], rh[1]), o[0], o[1])
                    return o

                def plus_i(x):
                    g = alloc()
                    tten(DBG.get("pi_eng", "dve"), v4(g[0]), v4(x[0]), b4(IDB), ALU.add, [x[1], KC], [g[1]])
                    return g

                X2 = prod(XT, X)
                yield
                X2T = prod(X, XT)
                G1T = plus_i(XT)
                rel(X, XT)
                yield
                G2 = plus_i(X2)
                Y1T = prod(G2, G1T)
                rel(G1T, G2)
                yield
                X4 = prod(X2T, X2)
                yield
                X4T = prod(X2, X2T)
                rel(X2, X2T)
                yield
                G4 = plus_i(X4)
                Y2T = prod(G4, Y1T)
                rel(Y1T, G4)
                yield
                X8 = prod(X4T, X4)
                rel(X4, X4T)
                yield
                G8 = plus_i(X8)
                rel(X8)
                Zc = prod(Y2T, G8)
                yield
                ZTc = prod(G8, Y2T)
                rel(Y2T, G8)
                yield
                for k in range(3):
                    OT = alloc()
                    tten(DBG.get("ot_eng", "dve"), v4(OT[0]), v4(AT[0]), b4(MLV[k]), ALU.mult, [AT[1], KC], [OT[1]])
                    w1, kw1 = mm4(OT[0], OT[1], Zc[0], Zc[1])
                    rel(OT)
                    IW = alloc()
                    tten("dve", v4(IW[0]), b4(IDB), v4(w1[:]), ALU.subtract, [kw1, KC], [IW[1]])
                    yield
                    zn, kzn = mm4(ZTc[0], ZTc[1], IW[0], IW[1])
                    rel(IW)
                    if k < 2:
                        Zn = alloc()
                        evac(zn, kzn, Zn[0], Zn[1])
                        rel(Zc, ZTc)
                        yield
                        ZTn = alloc()
                        transp4(Zn[0], Zn[1], ZTn[0], ZTn[1])
                        Zc, ZTc = Zn, ZTn
                    else:
                        evac(zn, kzn, Z, kZ)
                        rel(Zc, ZTc, AT)
                    yield

            def steps(m):
                (Z, kZ), (ATT, kATT), (QD, kQD) = outsets[m % NS]
                Z4, ATT4, QD4 = v4(Z), v4(ATT), v4(QD)
                for sidx in (2 * m, 2 * m + 1):
                    def one(d):
                        tile = sidx if d == 0 else 15 - sidx
                        e = (sidx % 2) + 2 * d
                        dh = h + 4 * d
                        vp, kvp = TB[:, 3 * d + 0, 0:128], ("tb", 3 * d + 0)
                        vraw, kvr = TB[:, 3 * d + 1, 0:128], ("tb", 3 * d + 1)
                        vdec, kvd = TB[:, 3 * d + 2, 0:128], ("tb", 3 * d + 2)
                        ksp, k1 = psum()
                        mm(ksp[:, 0:128], KT3[:, tile, :], SBF[:, d, :], True, True, [("kt",), ("sbf", d)], [k1])
                        stt("dve", vp, ksp[:, 0:128], NEGEC[:, tile, dh:dh + 1], VTOK3[:, tile, :], ALU.mult, ALU.add,
                            [k1, KM, ("vtok",)], [kvp])
                        yield
                        vrp, k2 = psum()
                        mm(vrp[:, 0:128], Z4[:, e, :], vp, True, True, [kZ, kvp], [k2])
                        cpy("act", vraw, vrp[:, 0:128], [k2], [kvr])
                        ts1("dve", vdec, vrp[:, 0:128], BDEC[:, tile, dh:dh + 1], ALU.mult, [k2, KM], [kvd])
                        yield
                        op_, k3 = psum()
                        mm(op_[:, 0:128], SBF[:, d, :], QD4[:, e, :], True, False, [("sbf", d), kQD], [k3])
                        mm(op_[:, 0:128], vraw, ATT4[:, e, :], False, True, [kvr, kATT], [k3])
                        snp, k4 = psum()
                        mm(snp[:, 0:128], KTOK3[:, tile, :], vdec, True, True, [("ktok",), kvd], [k4])
                        stt("dve", SST[:, d, :], SST[:, d, :], ECL[:, tile, dh:dh + 1], snp[:, 0:128], ALU.mult, ALU.add,
                            [k4, KM, ("sst", d)], [("sst", d)])
                        cpy(DBG.get("sbf_eng", "act"), SBF[:, d, :], SST[:, d, :], [("sst", d)], [("sbf", d)])
                        first = (d == 0 and tile < 8) or (d == 1 and tile >= 8)
                        if first:
                            cpy("act", OS3[:, tile, :], op_[:, 0:128], [k3], [("os", tile)])
                        else:
                            tten("dve", OS3[:, tile, :], op_[:, 0:128], OS3[:, tile, :], ALU.add, [k3, ("os", tile)], [("os", tile)])
                        yield
                    yield from rr([one(0), one(1)])

            active = {}
            prep_done = set()
            steps_done = -1
            nextp = 0
            nexts = 0
            freepipes = list(range(NP))
            cur_steps = None
            nbatch = DBG["gdn_m"]
            while nexts < nbatch or active or cur_steps is not None:
                while nextp < 8 and freepipes and (nextp - NS) <= steps_done and nextp < nbatch + 3:
                    pipe = freepipes.pop()
                    active[nextp] = (prep(nextp, pipe), pipe)
                    nextp += 1
                if cur_steps is None and nexts < nbatch and nexts in prep_done:
                    cur_steps = steps(nexts)
                for m_ in list(active.keys()):
                    g, pipe = active[m_]
                    try:
                        next(g)
                    except StopIteration:
                        del active[m_]
                        prep_done.add(m_)
                        freepipes.append(pipe)
                if cur_steps is not None:
                    try:
                        next(cur_steps)
                    except StopIteration:
                        cur_steps = None
                        steps_done = nexts
                        nexts += 1
                if nexts >= nbatch and not active and cur_steps is None:
                    break

        def dn_head(l, h):
            vb = l * NV
            vk = ("vec",)
            SVf = SV[:].rearrange("p a b -> p (a b)").bitcast(F32)
            PCs = [A_PC, SVf[:, 0:2052]]
            CYs = [A_OSUM, SVf[:, 2052:4100]]
            for par in range(2):
                mset("pool", PCs[par][:, 0:2], 0.0, [("pc", par, "pad")])
                mset("pool", PCs[par][:, 2050:2052], 0.0, [("pc", par, "pad")])
            state = dict(tt_busy=False)
            finished = {}

            def chunk_gen(ci, par):
                dst, dk = ((A_QT, "qt"), (A_KT, "kt"), (A_VT, "vt"))[ci]
                PCp, CYp = PCs[par], CYs[par]
                slab, ks = wload(l, O_WIN + (ci * 4 + h) * 1024, 1024)
                for t in range(NT):
                    ps, kp = psum()
                    proj_tile(ps, kp, slab, ks, t)
                    cpy("act", PCp[:, 2 + t * 512:2 + (t + 1) * 512], ps[:], [kp], [("pc", par, t)])
                    yield
                wbase = vb + V_DNCW + (ci * 4 + h) * 4
                rk = [("pc", par, t) for t in range(NT)] + [("pc", par, "pad"), vk]
                wk = [("os", n) for n in range(16)] if par == 0 else [("cy", par, t) for t in range(NT)]
                act(CYp, PCp[:, 0:2048], AF.Identity, rk, wk, scale=VEC[:, wbase:wbase + 1])
                yield
                for j in range(1, 4):
                    stt("dve", CYp, PCp[:, j:j + 2048], VEC[:, wbase + j:wbase + j + 1], CYp, ALU.mult, ALU.add, rk + wk, wk)
                    yield
                while state["tt_busy"]:
                    yield
                state["tt_busy"] = True

                def tile_chain(t):
                    tsl = tsl_(t)
                    kcy = [("os", n) for n in range(4 * t, 4 * t + 4)] if par == 0 else [("cy", par, t)]
                    th, kth = TT[:, 2 * t, :], ("tt", 2 * t)
                    ln, kln = TT[:, 2 * t + 1, :], ("tt", 2 * t + 1)
                    sq, ksq = TB[:, t, :], ("tb", t)
                    act(th, CYp[:, tsl], AF.Tanh, kcy, [kth], scale=0.5)
                    yield
                    if ci == 2:
                        stt("dve", dst[:, tsl], th, 1.0, CYp[:, tsl], ALU.add, ALU.mult, [kth] + kcy, [(dk, t)])
                        yield
                        return
                    stt("dve", th, th, 1.0, CYp[:, tsl], ALU.add, ALU.mult, [kth] + kcy, [kth])
                    yield
                    act(sq, th, AF.Square, [kth], [ksq])
                    yield
                    ps, kp = psum()
                    mm(ps[:], ONESB, sq, True, True, [ksq, KC], [kp])
                    act(ln, ps[:], AF.Ln, [kp], [kln], bias=4 * EPS)
                    yield
                    act(ln, ln, AF.Exp, [kln], [kln], scale=-0.5, bias=(-0.5 * math.log(128.0) if ci == 0 else 0.0))
                    yield
                    tten("pool", dst[:, tsl], th, ln, ALU.mult, [kth, kln], [(dk, t)])
                    yield

                yield from rr([tile_chain(t) for t in range(NT)])
                state["tt_busy"] = False
                finished[ci] = True

            gens = {}
            started = 0
            while len(finished) < 3:
                while started < 3 and len(gens) < 2 and (started < 2 or finished.get(started - 2)):
                    gens[started] = chunk_gen(started, started % 2)
                    started += 1
                for ci in list(gens.keys()):
                    try:
                        next(gens[ci])
                    except StopIteration:
                        del gens[ci]
            if DBG["dn_stop"] <= 1:
                return
            for src, sk, dst, dk, sc in ((A_KT, "kt", A_KTOK, "ktok", 1.0), (A_VT, "vt", A_VTOK, "vtok", 0.5)):
                for half in range(2):
                    pst, kpst = psum()
                    pstb = pst[:].bitcast(BF16)
                    for n in range(8):
                        tile = half * 8 + n
                        S.op("pe", lambda hh, o=pstb[:, n * 128:(n + 1) * 128], i_=src[:, tile * 128:(tile + 1) * 128]:
                             hh.transpose(out=o, in_=i_, identity=IDB), [(sk, tile // 4), KC], [kpst])
                    act(dst[:, half * 1024:(half + 1) * 1024], pstb[:, 0:1024], AF.Identity, [kpst], [(dk,)], scale=sc)
            if DBG["dn_stop"] <= 2:
                return
            S.barrier()
            gdn(l, h)
            S.barrier()
            if DBG["dn_stop"] <= 4:
                return
            zslab, kzs = wload(l, O_WIN + (12 + h) * 1024, 1024)

            def out_chain(t):
                tsl = tsl_(t)
                osk = [("os", n) for n in range(t * 4, t * 4 + 4)]
                ln, kln = TT[:, 2 * t, :], ("tt", 2 * t)
                th, kth = TT[:, 2 * t + 1, :], ("tt", 2 * t + 1)
                sq, ksq = TB[:, t, :], ("tb", t)
                act(sq, A_OSUM[:, tsl], AF.Square, osk, [ksq])
                yield
                psz, kpz = psum()
                proj_tile(psz, kpz, zslab, kzs, t)
                act(th, psz[:], AF.Tanh, [kpz], [kth], scale=0.5)
                stt("dve", th, th, 1.0, psz[:], ALU.add, ALU.mult, [kth, kpz], [kth])
                yield
                ps, kp = psum()
                mm(ps[:], ONESB, sq, True, True, [ksq, KC], [kp])
                act(ln, ps[:], AF.Ln, [kp], [kln], bias=EPS, scale=1.0 / 128)
                yield
                act(ln, ln, AF.Exp, [kln], [kln], scale=-0.5)
                yield
                stt("dve", ln, A_OSUM[:, tsl], VEC[:, vb + V_DNG:vb + V_DNG + 1], ln, ALU.mult, ALU.mult, osk + [kln, vk], [kln])
                yield
                stt("dve", A_VT[:, tsl], ln, 0.5, th, ALU.mult, ALU.mult, [kln, kth], [("vt", t)])
                yield

            interleave([out_chain(t) for t in range(NT)])
            wout_pass(l, [h], lambda i, t: A_VT[:, tsl_(t)], lambda i, t: ("vt", t))

        def ffn_phase(l):
            vb = l * NV
            vk = ("vec",)
            rms_to_hn(vb + V_G2)
            ACTB = ARENA[:, 0:11264].rearrange("p (f n) -> p f n", n=512)
            GP = ARENA[:, 11264:15376].bitcast(F32).rearrange("p (b n) -> p b n", b=4)
            NW = DBG.get("NW", 3)
            st = dict(down_q=0)

            def fc_chain(qt, fc, idx):
                t0 = qt * 512
                sg, ksg = wload(l, O_WG + fc * 1024, 1024)
                gb = idx % 4
                gp = GP[:, gb, :]
                kgp = ("gp", gb)
                cg, kcg = TT[:, 2 * (idx % 4), :], ("tt", 2 * (idx % 4))
                x2, k2 = TT[:, 2 * (idx % 4) + 1, :], ("tt", 2 * (idx % 4) + 1)
                psg, kpg = psum()
                proj_tile(psg, kpg, sg, ksg, qt)
                psh, kph = psum()
                sides = []
                if qt > 0:
                    sides.append((0, t0 - 1))
                if qt < NT - 1:
                    sides.append((1, t0 + 512))
                for (si, col) in sides:
                    for kc in range(8):
                        mm(psh[:, si:si + 1], sg[:, kc * 128:(kc + 1) * 128], HN[:, kc, col:col + 1], kc == 0, kc == 7,
                           [ksg, ("HN", kc, col // 512)], [kph])
                cpy("act", gp[:, 1:513], psg[:], [kpg], [kgp])
                if qt > 0:
                    cpy("dve", gp[:, 0:1], psh[:, 0:1], [kph, kgp], [kgp])
                else:
                    mset("dve", gp[:, 0:1], 0.0, [kgp])
                if qt < NT - 1:
                    cpy("dve", gp[:, 513:514], psh[:, 1:2], [kph, kgp], [kgp])
                else:
                    mset("dve", gp[:, 513:514], 0.0, [kgp])
                yield
                wb_ = vb + V_FCW + fc * 3
                tsc("pool", cg, gp[:, 0:512], VEC[:, wb_:wb_ + 1], VEC[:, vb + V_FCB + fc:vb + V_FCB + fc + 1], ALU.mult, ALU.add,
                    [kgp, vk], [kcg])
                yield
                stt("dve", cg, gp[:, 1:513], VEC[:, wb_ + 1:wb_ + 2], cg, ALU.mult, ALU.add, [kgp, vk, kcg], [kcg])
                yield
                stt("dve", cg, gp[:, 2:514], VEC[:, wb_ + 2:wb_ + 3], cg, ALU.mult, ALU.add, [kgp, vk, kcg], [kcg])
                yield
                act(x2, cg, AF.Square, [kcg], [k2])
                yield
                tsc("pool", x2, x2, GC2, GC1, ALU.mult, ALU.add, [k2], [k2])
                yield
                tten("pool", x2, x2, cg, ALU.mult, [k2, kcg], [k2])
                yield
                act(x2, x2, AF.Tanh, [k2], [k2])
                yield
                stt("dve", cg, x2, 1.0, cg, ALU.add, ALU.mult, [k2, kcg], [kcg])
                yield
                while st["down_q"] < qt:
                    yield
                su, ksu = wload(l, O_WU + fc * 1024, 1024)
                psu, kpu = psum()
                proj_tile(psu, kpu, su, ksu, qt)
                stt("dve", ACTB[:, fc, :], cg, 0.5, psu[:], ALU.mult, ALU.mult, [kcg, kpu], [("actb", fc)])
                yield

            def down_gen(qt):
                for m in range(8):
                    ps, kp = psum()
                    pieces = ((0, 8), (8, 16), (16, 22))
                    for (f0, f1) in pieces:
                        sl, ksl = wload(l, O_WD + m * 2816 + f0 * 128, (f1 - f0) * 128)
                        for fc in range(f0, f1):
                            mm(ps[:], sl[:, (fc - f0) * 128:(fc - f0 + 1) * 128], ACTB[:, fc, :], fc == 0, fc == NFC - 1,
                               [ksl, ("actb", fc)], [kp])
                    radd(m, qt, ps, kp)
                    yield
                st["down_q"] = qt + 1

            chains = [(qt, fc) for qt in range(NT) for fc in range(NFC)]
            active = []
            ci = 0
            down = None
            nfin = {qt: 0 for qt in range(NT)}
            while ci < len(chains) or active or down is not None:
                while ci < len(chains) and len(active) < NW:
                    qt, fc = chains[ci]
                    active.append((fc_chain(qt, fc, ci), qt))
                    ci += 1
                for item in list(active):
                    g, qt = item
                    try:
                        next(g)
                    except StopIteration:
                        active.remove(item)
                        nfin[qt] += 1
                        if nfin[qt] == NFC:
                            down = down_gen(qt)
                if down is not None:
                    try:
                        next(down)
                    except StopIteration:
                        down = None

        def ple_phase(l, s):
            vb = l * NV
            rms_to_hn(vb + V_GP)
            PB = ARENA[:, 0:4096].rearrange("p (k n) -> p k n", k=2)
            PF = ARENA[:, 4096:8192].bitcast(F32).rearrange("p (b k n) -> p b k n", b=2, k=2)
            psrc = pT[l, s].rearrange("(k p) n -> p k n", p=128)
            for t in range(NT):
                S.dma(PF[:, t % 2], psrc[:, :, tsl_(t)], [], [("pf", t % 2)])
                cpy("pool", PB[:, :, tsl_(t)], PF[:, t % 2], [("pf", t % 2)], [("pb", t)])
            for m in range(8):
                sg, ksg = wload(l, O_PWG + m * 1024, 1024)
                sp_, ksp_ = wload(l, O_PWP + m * 256, 256)
                for t in range(NT):
                    ps1, kp1 = psum()
                    proj_tile(ps1, kp1, sg, ksg, t)
                    th, kth = tt()
                    act(th, ps1[:], AF.Tanh, [kp1, ("lv",)], [kth], scale=0.5, bias=LV[:, 40 + m:41 + m])
                    ps2, kp2 = psum()
                    for k2 in range(2):
                        mm(ps2[:], sp_[:, k2 * 128:(k2 + 1) * 128], PB[:, k2, tsl_(t)], k2 == 0, k2 == 1, [ksp_, ("pb", t)], [kp2])
                    stt("dve", th, th, 1.0, ps2[:], ALU.add, ALU.mult, [kth, kp2], [kth])
                    stt("dve", R[:, m, tsl_(t)], th, 0.5, R[:, m, tsl_(t)], ALU.mult, ALU.add, [kth, ("R", m, t)], [("R", m, t)])


        for s in range(nseq):
            for ch in range(8):
                S.dma(R[:, ch, :], xT[s, ch * 128:(ch + 1) * 128, :], [], [("R", ch, t) for t in range(NT)])
            for l in range(nlayers):
                layer_prep(l)
                rms_to_hn(l * NV + V_G1)
                if "lru" in phases or "dn" in phases:
                    ab_phase(l)
                S.barrier()
                if "lru" in phases:
                    lru_phase(l)
                    S.barrier()
                if "dn" in phases:
                    for h in range(DBG["heads"]):
                        dn_head(l, h)
                    S.barrier()
                if "ffn" in phases:
                    ffn_phase(l)
                    S.barrier()
                if "ple" in phases:
                    ple_phase(l, s)
                    S.barrier()
            S.barrier()
            if dump_r:
                for ch in range(8):
                    S.dma(yT[s, ch * 128:(ch + 1) * 128, :], R[:, ch, :], [("R", ch, t) for t in range(NT)], [("y", s, ch)])
            else:
                for t in range(NT):
                    rs, kr = rms_stats(t, 8, lambda ch, t: R[:, ch, tsl_(t)], lambda ch, t: ("R", ch, t), 1.0 / D, EPS)
                    for ch in range(8):
                        o, ko = tt()
                        stt("dve" if ch % 2 == 0 else "pool", o, R[:, ch, tsl_(t)], VEC[:, L * NV + ch:L * NV + ch + 1], rs,
                            ALU.mult, ALU.mult, [("R", ch, t), kr], [ko])
                        S.dma(yT[s, ch * 128:(ch + 1) * 128, tsl_(t)], o, [ko], [("y", s, ch, t)])
        S.barrier()
        print("instructions:", S.ninst, {k: v["n"] for k, v in S.E.items()})
    return nc


def _slabs_kn(w):
    K, N = w.shape
    a = w.reshape(K // 128, 128, N // 128, 128)
    return np.ascontiguousarray(a.transpose(2, 1, 0, 3)).reshape(N // 128, 128, K)


def pack_weights(inp):
    wpk = np.zeros((L, 128, WTOT), np.float32)
    for l in range(L):
        w_in = inp["w_in"][l]
        cols = np.concatenate([w_in[:, 0:2048], w_in[:, 2064:3088]], axis=1)
        sl = _slabs_kn(cols)
        wpk[l, :, O_WIN:O_WIN + 24 * 1024] = sl.transpose(1, 0, 2).reshape(128, 24 * 1024)
        ab = w_in[:, 2048:2064].reshape(8, 128, 16).transpose(1, 0, 2).reshape(128, 128)
        wpk[l, :, O_WAB:O_WAB + 128] = ab
        lw = np.zeros((128, 4, 2, 2, 128), np.float32)
        for d in range(2):
            for gi, nm in enumerate(("lru_wa", "lru_wx")):
                wg = inp[nm][l, d]
                for c in range(4):
                    lw[0:64, c, d, gi, 0:64] = wg[2 * c]
                    lw[64:128, c, d, gi, 64:128] = wg[2 * c + 1]
        wpk[l, :, O_LRUW:O_LRUW + 2048] = lw.reshape(128, 2048)
        wpk[l, :, O_WOUT:O_WOUT + 8192] = inp["w_out"][l].reshape(8, 128, 1024).transpose(1, 0, 2).reshape(128, 8192)
        wpk[l, :, O_WG:O_WG + 22528] = _slabs_kn(inp["ffn_wg"][l]).transpose(1, 0, 2).reshape(128, 22528)
        wpk[l, :, O_WU:O_WU + 22528] = _slabs_kn(inp["ffn_wu"][l]).transpose(1, 0, 2).reshape(128, 22528)
        wd = inp["ffn_wd"][l].reshape(NFC, 128, 8, 128)
        wpk[l, :, O_WD:O_WD + 22528] = wd.transpose(1, 2, 0, 3).reshape(128, 22528)
        wpk[l, :, O_PWG:O_PWG + 8192] = _slabs_kn(inp["ple_wg"][l]).transpose(1, 0, 2).reshape(128, 8192)
        wpk[l, :, O_PWP:O_PWP + 2048] = _slabs_kn(inp["ple_wp"][l]).transpose(1, 0, 2).reshape(128, 2048)
    return wpk


def pack_vecs(inp):
    v = np.zeros((128, L * NV + 8), np.float32)

    def cm(a, n):
        return a.reshape(n, 128).T

    for l in range(L):
        b = l * NV
        v[:, b + V_G1:b + V_G1 + 8] = cm(inp["norm1_g"][l], 8)
        v[:, b + V_G2:b + V_G2 + 8] = cm(inp["norm2_g"][l], 8)
        v[:, b + V_GP:b + V_GP + 8] = cm(inp["ple_norm_g"][l], 8)
        v[:, b + V_BGP:b + V_BGP + 8] = cm(inp["ple_bg"][l], 8)
        v[:, b + V_DNCW:b + V_DNCW + 48] = inp["dn_conv_w"][l].reshape(4, 12, 128).transpose(2, 1, 0).reshape(128, 48)
        v[:, b + V_LCW:b + V_LCW + 16] = inp["lru_conv_w"][l].reshape(4, 4, 128).transpose(2, 1, 0).reshape(128, 16)
        v[:, b + V_LCB:b + V_LCB + 4] = cm(inp["lru_conv_b"][l], 4)
        v[:, b + V_LBA:b + V_LBA + 8] = inp["lru_ba"][l].reshape(2, 4, 128).transpose(2, 0, 1).reshape(128, 8)
        v[:, b + V_LBX:b + V_LBX + 8] = inp["lru_bx"][l].reshape(2, 4, 128).transpose(2, 0, 1).reshape(128, 8)
        v[:, b + V_LLAM:b + V_LLAM + 8] = inp["lru_lambda"][l].reshape(2, 4, 128).transpose(2, 0, 1).reshape(128, 8)
        v[:, b + V_LNG:b + V_LNG + 4] = cm(inp["lru_norm_g"][l], 4)
        v[:, b + V_DNG] = inp["dn_norm_g"][l]
        v[:, b + V_FCW:b + V_FCW + 66] = inp["ffn_conv_w"][l].reshape(3, NFC, 128).transpose(2, 1, 0).reshape(128, 66)
        v[:, b + V_FCB:b + V_FCB + 22] = cm(inp["ffn_conv_b"][l], NFC)
        v[:, b + V_ALOG:b + V_ALOG + 8] = np.broadcast_to(inp["dn_a_log"][l].reshape(1, 8), (128, 8))
        v[:, b + V_DTB:b + V_DTB + 8] = np.broadcast_to(inp["dn_dt_bias"][l].reshape(1, 8), (128, 8))
    v[:, L * NV:L * NV + 8] = cm(inp["final_g"], 8)
    return v


def kernel(**inputs):
    inp = {k: np.asarray(v) for k, v in inputs.items()}
    x = inp["x"]
    p = inp["p"]
    wpk = pack_weights(inp)
    vecs = pack_vecs(inp)
    xT = np.ascontiguousarray(x.transpose(0, 2, 1))
    pT = np.ascontiguousarray(p.transpose(0, 1, 3, 2))
    nc = build_nc()
    in_maps = []
    for c in range(NCORE):
        in_maps.append({
            "xT": xT[c * SEQ_PER_CORE:(c + 1) * SEQ_PER_CORE],
            "pT": np.ascontiguousarray(pT[:, c * SEQ_PER_CORE:(c + 1) * SEQ_PER_CORE]),
            "wpk": wpk,
            "vecs": vecs,
        })
    res = run_bass_kernel_spmd(nc, in_maps, core_ids=list(range(NCORE)))
    yT = np.concatenate([r["yT"] for r in res.results], axis=0)
    return np.ascontiguousarray(yT.transpose(0, 2, 1)).astype(np.float32)
```

```python
import math
import numpy as np
from contextlib import ExitStack
import concourse.bass as bass
import concourse.mybir as mybir
from concourse.bass_utils import run_bass_kernel_spmd

F32 = mybir.dt.float32
BF16 = mybir.dt.bfloat16
AF = mybir.ActivationFunctionType
ALU = mybir.AluOpType

D = 1024
S_LEN = 2048
L = 4
NCORE = 8
SEQ_PER_CORE = 4
DFF = 2816
NFC = 22
PLE = 256
EPS = 1e-6
NT = 4
TK = 16
GC1 = math.sqrt(2.0 / math.pi)
GC2 = GC1 * 0.044715

O_WIN = 0
O_WAB = O_WIN + 24 * 1024
O_LRUW = O_WAB + 128
O_WOUT = O_LRUW + 2048
O_WG = O_WOUT + 8192
O_WU = O_WG + 22528
O_WD = O_WU + 22528
O_PWG = O_WD + 22528
O_PWP = O_PWG + 8192
WTOT = O_PWP + 2048

V_G1, V_G2, V_GP, V_BGP = 0, 8, 16, 24
V_DNCW = 32
V_LCW = 80
V_LCB = 96
V_LBA = 100
V_LBX = 108
V_LLAM = 116
V_LNG = 124
V_DNG = 128
V_FCW = 129
V_FCB = 195
V_ALOG = 217
V_DTB = 225
NV = 240


class Sched:
    NDS = 24

    def __init__(self, nc, es):
        self.nc = nc
        self.E = {}
        for name, h in (("pe", nc.tensor), ("act", nc.scalar), ("dve", nc.vector), ("pool", nc.gpsimd), ("sp", nc.sync)):
            sem = es.enter_context(nc.semaphore("s_" + name))
            self.E[name] = dict(h=h, sem=sem, n=0, seen={}, seend={})
        self.dsem = [es.enter_context(nc.semaphore(f"sd{i}")) for i in range(self.NDS)]
        self.dcnt = [0] * self.NDS
        self.dnext = 0
        self.lastw = {}
        self.readers = {}
        self.ninst = 0

    def _wait(self, eng, tok, same_ok):
        X = self.E[eng]
        if tok[0] == "e":
            _, p, c = tok
            if p == eng and (eng == "pe" or not same_ok):
                return
            if X["seen"].get(p, 0) >= c:
                return
            X["h"].wait_ge(self.E[p]["sem"], c)
            X["seen"][p] = c
        else:
            _, i, v = tok
            if X["seend"].get(i, 0) >= v:
                return
            X["h"].wait_ge(self.dsem[i], v)
            X["seend"][i] = v

    def _deps(self, eng, r, w):
        for k in r:
            t = self.lastw.get(k)
            if t is not None:
                self._wait(eng, t, True)
            if k[0] == "ps":
                for t in self.readers.get(k, {}).values():
                    self._wait(eng, t, False)
        for k in w:
            t = self.lastw.get(k)
            if t is not None:
                self._wait(eng, t, True)
            for t in self.readers.get(k, {}).values():
                self._wait(eng, t, False)

    def _record(self, tok, r, w):
        for k in r:
            self.readers.setdefault(k, {})[tok[1]] = tok
        for k in w:
            self.lastw[k] = tok
            self.readers[k] = {}

    def op(self, eng, emit, r=(), w=()):
        X = self.E[eng]
        self._deps(eng, r, w)
        inst = emit(X["h"])
        X["n"] += 1
        inst.then_inc(X["sem"], 1)
        self.ninst += 1
        self._record(("e", eng, X["n"]), r, w)

    def dma(self, out, in_, r=(), w=(), eng="sp"):
        self._deps(eng, r, w)
        i = self.dnext
        self.dnext = (i + 1) % self.NDS
        if self.dcnt[i] > 0:
            self._wait(eng, ("d", i, 16 * self.dcnt[i]), True)
        inst = self.E[eng]["h"].dma_start(out=out, in_=in_)
        self.dcnt[i] += 1
        inst.then_inc(self.dsem[i], 16)
        self.ninst += 1
        self._record(("d", i, 16 * self.dcnt[i]), r, w)

    def barrier(self):
        comp = ("pe", "act", "dve", "pool")
        for e in comp:
            for p in comp:
                if p != e and self.E[p]["n"] > 0:
                    self._wait(e, ("e", p, self.E[p]["n"]), True)
            for i in range(self.NDS):
                if self.dcnt[i] > 0:
                    self._wait(e, ("d", i, 16 * self.dcnt[i]), True)
        for p in comp:
            if self.E[p]["n"] > 0:
                self._wait("sp", ("e", p, self.E[p]["n"]), True)
        for i in range(self.NDS):
            if self.dcnt[i] > 0:
                self._wait("sp", ("d", i, 16 * self.dcnt[i]), True)
        self.lastw = {}
        self.readers = {}


def rev_ap(ap2d):
    a = ap2d.ap
    n = a[-1][1]
    st = a[-1][0]
    return bass.AP(ap2d.tensor, ap2d.offset + (n - 1) * st, [list(x) for x in a[:-1]] + [[-st, n]])


DBG = dict(dn_stop=99, heads=4, gdn_m=8)


def interleave(gens):
    gens = list(gens)
    while gens:
        for g in list(gens):
            try:
                next(g)
            except StopIteration:
                gens.remove(g)


def build_nc(nseq=SEQ_PER_CORE, nlayers=L, dump_r=False, do_prepass=True, phases=("lru", "dn", "ffn", "ple"), ndbg=0):
    nc = bass.Bass("TRN2", target_bir_lowering=False)
    xT = nc.dram_tensor("xT", [SEQ_PER_CORE, D, S_LEN], F32, kind="ExternalInput").ap()
    pT = nc.dram_tensor("pT", [L, SEQ_PER_CORE, PLE, S_LEN], F32, kind="ExternalInput").ap()
    wpk = nc.dram_tensor("wpk", [L, 128, WTOT], F32, kind="ExternalInput").ap()
    vecs = nc.dram_tensor("vecs", [128, L * NV + 8], F32, kind="ExternalInput").ap()
    yT = nc.dram_tensor("yT", [SEQ_PER_CORE, D, S_LEN], F32, kind="ExternalOutput").ap()
    wsc = nc.dram_tensor("wsc", [L, 128, WTOT], BF16, kind="Internal").ap()
    dbg_out = None
    if ndbg:
        dbg_out = nc.dram_tensor("dbg", [ndbg, 128, S_LEN], F32, kind="ExternalOutput").ap()

    es = ExitStack()
    with es:
        def sb(name, shape, dt):
            return es.enter_context(nc.sbuf_tensor(name, shape, dt))

        S = Sched(nc, es)
        R = sb("R", [128, 8, S_LEN], F32)
        HN = sb("HN", [128, 8, S_LEN], BF16)
        ARENA = sb("ARENA", [128, 18560], BF16)
        TT = sb("TT", [128, 8, 512], F32)
        TB = sb("TB", [128, 6, 512], BF16)
        SV = sb("SV", [128, 20, 512], BF16)
        SF = sb("SF", [128, 512], F32)
        WB = sb("WB", [128, 5, 1024], BF16)
        VEC = sb("VEC", [128, L * NV + 8], F32)
        CST = sb("CST", [128, 13, 128], F32)
        CSB = sb("CSB", [128, 7, 128], BF16)
        MISC = sb("MISC", [128, 12, 128], F32)
        LV = sb("LV", [128, 64], F32)
        SST = sb("SST", [128, 2, 128], F32)
        SBF = sb("SBF", [128, 2, 128], BF16)
        CAR = sb("CAR", [128, 4], F32)
        PS = [es.enter_context(nc.psum_tensor(f"ps{i}", [128, 512], F32)) for i in range(8)]
        cnt = dict(ps=0, tt=0, tb=0, wb=0, sv=0, ev=0)

        def psum():
            i = cnt["ps"] % 8
            cnt["ps"] += 1
            return PS[i], ("ps", i)

        def tt():
            i = cnt["tt"] % 8
            cnt["tt"] += 1
            return TT[:, i, :], ("tt", i)

        def tb():
            i = cnt["tb"] % 6
            cnt["tb"] += 1
            return TB[:, i, :], ("tb", i)

        def wslot():
            i = cnt["wb"] % 5
            cnt["wb"] += 1
            return WB[:, i, :], ("wb", i)

        def sv():
            i = 10 + cnt["sv"] % 10
            cnt["sv"] += 1
            return SV[:, i, :], ("sv", i)

        def act(out, in_, func, r, w, bias=0.0, scale=1.0):
            S.op("act", lambda h: h.activation(out=out, in_=in_, func=func, bias=bias, scale=scale), r, w)

        def tsc(eng, out, in0, s1, s2, op0, op1, r, w):
            S.op(eng, lambda h: h.tensor_scalar(out=out, in0=in0, scalar1=s1, scalar2=s2, op0=op0, op1=op1), r, w)

        def ts1(eng, out, in_, s, op, r, w):
            S.op(eng, lambda h: h.tensor_single_scalar(out=out, in_=in_, scalar=s, op=op), r, w)

        def stt(eng, out, in0, s, in1, op0, op1, r, w):
            eng = "dve"
            S.op(eng, lambda h: h.scalar_tensor_tensor(out=out, in0=in0, scalar=s, in1=in1, op0=op0, op1=op1), r, w)

        def tten(eng, out, in0, in1, op, r, w):
            S.op(eng, lambda h: h.tensor_tensor(out=out, in0=in0, in1=in1, op=op), r, w)

        def cpy(eng, out, in_, r, w):
            if eng == "act":
                act(out, in_, AF.Identity, r, w)
            else:
                S.op(eng, lambda h: h.tensor_copy(out=out, in_=in_), r, w)

        def mm(out, lhsT, rhs, start, stop, r, w):
            S.op("pe", lambda h: h.matmul(out, lhsT=lhsT, rhs=rhs, start=start, stop=stop), r, w)

        def mset(eng, ap, val, w):
            S.op(eng, lambda h: h.memset(ap, val), (), w)

        INCF, INCB, OFFD, ONESF, IDF = (CST[:, i, :] for i in range(5))
        NEG4 = CST[:, 5:9, :]
        M16, ML0, ML1, ML2, IDB, ONESB = (CSB[:, i, :] for i in range(6))
        MLV = [ML0, ML1, ML2]
        KC = ("cst",)

        def b4(ap):
            return ap.unsqueeze(1).to_broadcast([128, 4, 128])

        def v4(ap):
            return ap.rearrange("p (e i) -> p e i", e=4)

        def aff(ap, pattern, op, fill, base, cm):
            S.op("pool", lambda h: h.affine_select(out=ap, in_=ap, pattern=pattern, compare_op=op, fill=fill,
                                                   base=base, channel_multiplier=cm), [KC], [KC])

        mset("pool", CST[:, 0:5, :], 1.0, [KC])
        aff(INCF, [[1, 128]], ALU.is_ge, 0.0, 0, -1)
        aff(INCB, [[-1, 128]], ALU.is_ge, 0.0, 0, 1)
        aff(OFFD, [[1, 128]], ALU.not_equal, 0.0, 0, -1)
        aff(IDF, [[1, 128]], ALU.is_equal, 0.0, 0, -1)
        for e in range(4):
            tsc("pool", NEG4[:, e, :], INCF if e < 2 else INCB, -1.0, 1e30, ALU.add, ALU.mult, [KC], [KC])
        mset("pool", CST[:, 9:13, :], 1.0, [KC])
        for bi, b in enumerate((16, 32, 64)):
            nb = 128 // b
            v = CST[:, 9 + bi, :].rearrange("p (k c) -> p k c", c=b)
            aff(v, [[-b, nb], [0, b]], ALU.is_ge, 0.0, 0, 1)
            aff(v, [[b, nb], [0, b]], ALU.is_gt, 0.0, b, -1)
        cpy("pool", M16, CST[:, 9, :], [KC], [KC])
        for k in range(3):
            tten("pool", MLV[k], CST[:, 10 + k, :], CST[:, 9 + k, :], ALU.subtract, [KC], [KC])
        cpy("pool", IDB, IDF, [KC], [KC])
        cpy("pool", ONESB, ONESF, [KC], [KC])
        NM16 = CSB[:, 6, :]
        ts1("pool", NM16, CST[:, 9, :], -1.0, ALU.mult, [KC], [KC])
        OFFB = sb("OFFB", [128, 128], BF16)
        cpy("pool", OFFB[:], OFFD, [KC], [KC])
        NEG4B = CST[:, 9:11, :].rearrange("p a b -> p (a b)").bitcast(BF16).rearrange("p (e i) -> p e i", e=4)
        cpy("pool", NEG4B, NEG4, [KC], [KC])
        S.dma(VEC[:], vecs[:, :], [], [("vec",)])
        S.barrier()

        if do_prepass:
            CH = 8192
            for l in range(nlayers):
                c0 = 0
                while c0 < WTOT:
                    cw = min(CH, WTOT - c0)
                    S.dma(wsc[l, :, c0:c0 + cw], wpk[l, :, c0:c0 + cw], [], [("wsc", l, c0)], eng="pool")
                    c0 += cw
            S.barrier()

        def wload(l, off, width):
            slot, k = wslot()
            S.dma(slot[:, 0:width], wsc[l, :, off:off + width], [], [k])
            return slot, k

        def tsl_(t):
            return slice(t * 512, (t + 1) * 512)

        def rms_stats(t, nch, src, srck, scale, eps):
            ps, kp = psum()
            for ch in range(nch):
                sq, ks = tb()
                act(sq, src(ch, t), AF.Square, [srck(ch, t)], [ks])
                mm(ps[:], ONESB, sq, ch == 0, ch == nch - 1, [ks], [kp])
            ln, kl = tt()
            act(ln, ps[:], AF.Ln, [kp], [kl], bias=eps, scale=scale)
            rs, kr = tt()
            act(rs, ln, AF.Exp, [kl], [kr], scale=-0.5)
            return rs, kr

        def rms_to_hn(gbase):
            pss = []
            for t in range(NT):
                ps, kp = psum()
                for ch in range(8):
                    sq, ks = tb()
                    act(sq, R[:, ch, tsl_(t)], AF.Square, [("R", ch, t)], [ks])
                    mm(ps[:], ONESB, sq, ch == 0, ch == 7, [ks], [kp])
                pss.append((ps, kp))
            rss = []
            for t in range(NT):
                ps, kp = pss[t]
                rs, kr = TT[:, 4 + t, :], ("tt", 4 + t)
                act(rs, ps[:], AF.Ln, [kp], [kr], bias=EPS, scale=1.0 / D)
                rss.append((rs, kr))
            for t in range(NT):
                rs, kr = rss[t]
                act(rs, rs, AF.Exp, [kr], [kr], scale=-0.5)
            for t in range(NT):
                rs, kr = rss[t]
                for ch in range(8):
                    stt("dve", HN[:, ch, tsl_(t)], R[:, ch, tsl_(t)],
                        VEC[:, gbase + ch:gbase + ch + 1], rs, ALU.mult, ALU.mult, [("R", ch, t), kr, ("vec",)], [("HN", ch, t)])

        def proj_tile(ps, kp, slab, ks, t):
            for kc in range(8):
                mm(ps[:], slab[:, kc * 128:(kc + 1) * 128], HN[:, kc, tsl_(t)], kc == 0, kc == 7, [ks, ("HN", kc, t)], [kp])

        def radd(m, t, ps, kp):
            tten("dve", R[:, m, tsl_(t)], ps[:], R[:, m, tsl_(t)], ALU.add, [kp, ("R", m, t)], [("R", m, t)])

        def dbg_store(idx, ap2d, keys):
            if dbg_out is not None and idx < ndbg:
                S.dma(dbg_out[idx, :, 0:ap2d.shape[-1]], ap2d, keys, [("dbg", idx)])

        A_QT = ARENA[:, 0:2048]
        A_KT = ARENA[:, 2048:4096]
        A_VT = ARENA[:, 4096:6144]
        A_KTOK = ARENA[:, 6144:8192]
        A_VTOK = ARENA[:, 8192:10240]
        A_OSUM = ARENA[:, 10240:14336].bitcast(F32)
        A_PC = ARENA[:, 14336:18440].bitcast(F32)
        A_XCB = ARENA[:, 0:2048]
        A_GL = ARENA[:, 2048:6144].bitcast(F32)
        A_HS = ARENA[:, 6144:10240].bitcast(F32)
        A_XC = A_OSUM
        YL = SV[:, 0:16, :].rearrange("p (c a) n -> p c (a n)", c=4)

        def pc_pads():
            mset("pool", A_PC[:, 0:2], 0.0, [("pc", "pad")])
            mset("pool", A_PC[:, 2050:2052], 0.0, [("pc", "pad")])

        def conv4(dst, dstk, wbase, bias_ap, vk):
            rk = [("pc", t) for t in range(NT)] + [("pc", "pad"), vk]
            wk = [(dstk, t) for t in range(NT)]
            if bias_ap is None:
                ts1("pool", dst, A_PC[:, 0:2048], VEC[:, wbase:wbase + 1], ALU.mult, rk, wk)
            else:
                tsc("pool", dst, A_PC[:, 0:2048], VEC[:, wbase:wbase + 1], bias_ap, ALU.mult, ALU.add, rk, wk)
            for j in range(1, 4):
                stt("pool" if j % 2 else "dve", dst, A_PC[:, j:j + 2048], VEC[:, wbase + j:wbase + j + 1], dst,
                    ALU.mult, ALU.add, rk + wk, wk)

        def wout_pass(l, kcs, src, srck):
            slabs = [wload(l, O_WOUT + kc * 1024, 1024) for kc in kcs]
            for m in range(8):
                for t in range(NT):
                    ps, kp = psum()
                    for i, (sl, ks) in enumerate(slabs):
                        mm(ps[:], sl[:, m * 128:(m + 1) * 128], src(i, t), i == 0, i == len(slabs) - 1, [ks, srck(i, t)], [kp])
                    radd(m, t, ps, kp)

        def layer_prep(l):
            vb = l * NV
            kl = ("lv",)
            vk = ("vec",)
            act(LV[:, 48:56], VEC[:, vb + V_ALOG:vb + V_ALOG + 8], AF.Exp, [vk], [kl])
            ts1("dve", LV[:, 0:8], LV[:, 48:56], -1.0, ALU.mult, [kl], [kl])
            ts1("dve", LV[:, 8:16], VEC[:, vb + V_LBA:vb + V_LBA + 8], 0.5, ALU.mult, [vk, kl], [kl])
            ts1("dve", LV[:, 16:24], VEC[:, vb + V_LBX:vb + V_LBX + 8], 0.5, ALU.mult, [vk, kl], [kl])
            act(LV[:, 48:56], VEC[:, vb + V_LLAM:vb + V_LLAM + 8], AF.Exp, [vk, kl], [kl], scale=-1.0)
            act(LV[:, 56:64], LV[:, 48:56], AF.Ln, [kl], [kl], bias=1.0)
            ts1("dve", LV[:, 24:32], LV[:, 56:64], -8.0, ALU.mult, [kl], [kl])
            ts1("dve", LV[:, 32:40], LV[:, 56:64], -4.0, ALU.mult, [kl], [kl])
            ts1("dve", LV[:, 40:48], VEC[:, vb + V_BGP:vb + V_BGP + 8], 0.5, ALU.mult, [vk, kl], [kl])

        def m3(i):
            return MISC[:, i, :].rearrange("p (t h) -> p t h", h=8)

        AB = MISC[:, 0:2, :].rearrange("p a b -> p (a b)").rearrange("p (t c) -> p t c", c=16)
        G3, NLNB, CCOL, CLAST, BIASD, NEGEC, BDEC, ECL, TM1, TM2 = (m3(i) for i in range(2, 12))
        KM = ("misc",)
        GHL = MISC[:, 10, :].bitcast(BF16).rearrange("p (a t h) -> p a t h", a=2, h=8)
        GHI, GLO = GHL[:, 0], GHL[:, 1]
        ECOL = TM2

        def ab_phase(l):
            vb = l * NV
            slab, ks = wload(l, O_WAB, 128)
            ps, kp = psum()
            for n in range(TK):
                for kc in range(8):
                    mm(ps[:, n * 16:(n + 1) * 16], HN[:, kc, n * 128:(n + 1) * 128], slab[:, kc * 16:(kc + 1) * 16],
                       kc == 0, kc == 7, [ks, ("HN", kc, n // 4)], [kp])
            cpy("dve", MISC[:, 0:2, :].rearrange("p a b -> p (a b)"), ps[:, 0:256], [kp], [KM])
            yield
            r = [KM, ("lv",), ("vec",)]
            act(TM1, AB[:, :, 0:8], AF.Exp, r, [KM], scale=-1.0)
            act(NLNB, TM1, AF.Ln, r, [KM], bias=1.0)
            yield
            tten("dve", TM2, AB[:, :, 8:16], VEC[:, vb + V_DTB:vb + V_DTB + 8].unsqueeze(1).to_broadcast([128, 16, 8]), ALU.add, r, [KM])
            act(TM2, TM2, AF.Exp, r, [KM])
            act(TM2, TM2, AF.Ln, r, [KM], bias=1.0)
            tten("dve", G3, TM2, LV[:, 0:8].unsqueeze(1).to_broadcast([128, 16, 8]), ALU.mult, r, [KM])
            yield
            ps2, kp2 = psum()
            mm(ps2[:, 0:64].rearrange("p (t h) -> p t h", h=4), INCF, G3[:, :, 0:4], True, True, [KM, KC], [kp2])
            mm(ps2[:, 64:128].rearrange("p (t h) -> p t h", h=4), INCB, G3[:, :, 4:8], True, True, [KM, KC], [kp2])
            mm(ps2[:, 128:256], ONESF, MISC[:, 2, :], True, True, [KM, KC], [kp2])
            cpy("dve", CCOL[:, :, 0:4], ps2[:, 0:64].rearrange("p (t h) -> p t h", h=4), [kp2], [KM])
            cpy("dve", CCOL[:, :, 4:8], ps2[:, 64:128].rearrange("p (t h) -> p t h", h=4), [kp2, KM], [KM])
            cpy("dve", MISC[:, 5, :], ps2[:, 128:256], [kp2, KM], [KM])
            yield
            tten("dve", TM1, NLNB, CCOL, ALU.add, r, [KM])
            ts1("dve", BIASD, TM1, -1.0, ALU.mult, r, [KM])
            act(TM2, CCOL, AF.Exp, r, [KM])
            ts1("dve", NEGEC, TM2, -1.0, ALU.mult, r, [KM])
            yield
            tten("dve", TM1, BIASD, CLAST, ALU.add, r, [KM])
            act(BDEC, TM1, AF.Exp, r, [KM])
            act(ECL, CLAST, AF.Exp, r, [KM])
            cpy("dve", GHI, G3, r, [KM])
            tten("dve", GLO, G3, GHI, ALU.subtract, r, [KM])

        def lru_phase(l, extra=()):
            vb = l * NV
            vk = ("vec",)
            done = {-1: True}
            conv_done = {-1: True}

            def seq(*gs):
                for g in gs:
                    yield from g

            LWALL = SV[:, 16:20, :].rearrange("p a b -> p (a b)")
            S.dma(LWALL, wsc[l, :, O_LRUW:O_LRUW + 2048], [], [("lwall",)])

            def chunk_gen(c):
                lw, klw = LWALL[:, c * 512:(c + 1) * 512], ("lwall",)
                slab, ks = wload(l, O_WIN + (16 + c) * 1024, 1024)
                while not conv_done.get(c - 1):
                    yield
                if c == 0:
                    pc_pads()
                for t in range(NT):
                    ps, kp = psum()
                    proj_tile(ps, kp, slab, ks, t)
                    cpy("act", A_PC[:, 2 + t * 512:2 + (t + 1) * 512], ps[:], [kp], [("pc", t)])
                    yield
                while not done.get(c - 1):
                    yield
                wbase = vb + V_LCW + c * 4
                rk = [("pc", t) for t in range(NT)] + [("pc", "pad"), vk]
                wk = [("xc", t) for t in range(NT)]
                tsc("dve", A_XC, A_PC[:, 0:2048], VEC[:, wbase:wbase + 1], VEC[:, vb + V_LCB + c:vb + V_LCB + c + 1], ALU.mult, ALU.add, rk, wk)
                yield
                for j in range(1, 4):
                    stt("dve", A_XC, A_PC[:, j:j + 2048], VEC[:, wbase + j:wbase + j + 1], A_XC, ALU.mult, ALU.add, rk + wk, wk)
                    yield
                conv_done[c] = True
                cpy(DBG.get("xcb_eng", "act"), A_XCB, A_XC, wk, [("xcb", t) for t in range(NT)])
                yield
                gslab, kgs = wload(l, O_WIN + (20 + c) * 1024, 1024)

                def gelu_gen():
                    for t in range(NT):
                        x2 = TB[:, 2 * (t % 2):2 * (t % 2) + 2, :].rearrange("p a b -> p (a b)").bitcast(F32)
                        k2 = ("tbf", t % 2)
                        ps, kp = psum()
                        proj_tile(ps, kp, gslab, kgs, t)
                        act(x2, ps[:], AF.Square, [kp], [k2])
                        tsc("pool", x2, x2, GC2, GC1, ALU.mult, ALU.add, [k2], [k2])
                        tten("dve", x2, x2, ps[:], ALU.mult, [k2, kp], [k2])
                        act(x2, x2, AF.Tanh, [k2], [k2])
                        stt("dve", A_GL[:, tsl_(t)], x2, 1.0, ps[:], ALU.add, ALU.mult, [k2, kp], [("gl", t)])
                        yield

                def dir_gen(d):
                    order = list(range(NT)) if d == 0 else list(range(NT - 1, -1, -1))
                    yield from rr([tile_gen(d, 0, order[0]), tile_gen(d, 1, order[1])])
                    yield from rr([tile_gen(d, 2, order[2]), tile_gen(d, 3, order[3])])

                def tile_gen(d, ti, t):
                    lvi = d * 4 + c
                    if True:
                        tsl = tsl_(t)
                        base = 4 * (ti % 2)
                        thr, kr_ = TT[:, base + 0, :], ("tt", base + 0)
                        thi, ki_ = TT[:, base + 1, :], ("tt", base + 1)
                        a, ka = TT[:, base + 2, :], ("tt", base + 2)
                        a2, ka2 = TT[:, base + 3, :], ("tt", base + 3)
                        psr, kpr = psum()
                        mm(psr[:], lw[:, (d * 2 + 0) * 128:(d * 2 + 1) * 128], A_XCB[:, tsl], True, True, [klw, ("xcb", t)], [kpr])
                        act(thr, psr[:], AF.Tanh, [kpr, ("lv",)], [kr_], scale=0.5, bias=LV[:, 8 + lvi:9 + lvi])
                        psi, kpi = psum()
                        mm(psi[:], lw[:, (d * 2 + 1) * 128:(d * 2 + 2) * 128], A_XCB[:, tsl], True, True, [klw, ("xcb", t)], [kpi])
                        act(thi, psi[:], AF.Tanh, [kpi, ("lv",)], [ki_], scale=0.5, bias=LV[:, 16 + lvi:17 + lvi])
                        yield
                        act(a, thr, AF.Exp, [kr_, ("lv",)], [ka], scale=LV[:, 32 + lvi:33 + lvi], bias=LV[:, 32 + lvi:33 + lvi])
                        if DBG.get("a2_eng", "act") == "act":
                            act(a2, thr, AF.Exp, [kr_, ("lv",)], [ka2], scale=LV[:, 24 + lvi:25 + lvi], bias=LV[:, 24 + lvi:25 + lvi])
                        else:
                            tten(DBG["a2_eng"], a2, a, a, ALU.mult, [ka], [ka2])
                        yield
                        act(thr, thr, AF.Tanh, [kr_, ("lv",)], [kr_], scale=LV[:, 32 + lvi:33 + lvi], bias=LV[:, 32 + lvi:33 + lvi])
                        stt("dve", thi, thi, 1.0, A_XC[:, tsl], ALU.add, ALU.mult, [ki_, ("xc", t)], [ki_])
                        yield
                        stt("dve", a2, a2, 1.0, thr, ALU.add, ALU.mult, [ka2, kr_], [ka2])
                        yield
                        act(a2, a2, AF.Ln, [ka2], [ka2], scale=-1.0)
                        yield
                        act(a2, a2, AF.Exp, [ka2], [ka2], scale=0.5)
                        yield
                        stt("dve", thi, thi, 0.5, a2, ALU.mult, ALU.mult, [ki_, ka2], [ki_])
                        yield
                        if d == 0:
                            init = 0.0 if ti == 0 else A_HS[:, t * 512 - 1:t * 512]
                            rr_ = [ka, ki_] + ([("hs", t - 1)] if ti else [])
                            S.op("dve", lambda h, o=A_HS[:, tsl], a_=a, b_=thi, i_=init: h.tensor_tensor_scan(
                                out=o, data0=a_, data1=b_, initial=i_, op0=ALU.mult, op1=ALU.add), rr_, [("hs", t)])
                        else:
                            init = 0.0 if ti == 0 else CAR[:, 0:1]
                            S.op("dve", lambda h, o=rev_ap(thr), a_=rev_ap(a), b_=rev_ap(thi), i_=init: h.tensor_tensor_scan(
                                out=o, data0=a_, data1=b_, initial=i_, op0=ALU.mult, op1=ALU.add), [ka, ki_, kr_, ("car",)], [kr_])
                            cpy("dve", CAR[:, 0:1], thr[:, 0:1], [kr_, ("car",)], [("car",)])
                            tten("pool", A_HS[:, tsl], A_HS[:, tsl], thr, ALU.add, [kr_, ("hs", t)], [("hs", t)])
                        yield

                yield from rr([gelu_gen(), seq(dir_gen(0), dir_gen(1))])
                tten(DBG.get("yl_eng", "pool"), YL[:, c, :], A_GL, A_HS, ALU.mult, [("gl", t) for t in range(NT)] + [("hs", t) for t in range(NT)],
                     [("yl", c, t) for t in range(NT)])
                done[c] = True
                yield

            gens = {}
            for xi, xg in enumerate(extra):
                gens[100 + xi] = xg
            started = 0
            while started < 4 or gens:
                while started < 4 and len([k_ for k_ in gens if k_ < 100]) < 2:
                    gens[started] = chunk_gen(started)
                    started += 1
                for c_ in list(gens.keys()):
                    try:
                        next(gens[c_])
                    except StopIteration:
                        del gens[c_]
            for t in range(NT):
                rs, kr = rms_stats(t, 4, lambda c, t: YL[:, c, tsl_(t)], lambda c, t: ("yl", c, t), 1.0 / 512, 4 * EPS)
                for c in range(4):
                    stt("dve", YL[:, c, tsl_(t)], YL[:, c, tsl_(t)], VEC[:, vb + V_LNG + c:vb + V_LNG + c + 1], rs,
                        ALU.mult, ALU.mult, [("yl", c, t), kr, vk], [("yl", c, t)])
            wout_pass(l, [4, 5, 6, 7], lambda i, t: YL[:, i, tsl_(t)], lambda i, t: ("yl", i, t))

        def rr(gens):
            gens = list(gens)
            while gens:
                for g in list(gens):
                    try:
                        next(g)
                    except StopIteration:
                        gens.remove(g)
                yield

        def gdn(l, h):
            KT3 = A_KT.rearrange("p (n i) -> p n i", i=128)
            QT3 = A_QT.rearrange("p (n i) -> p n i", i=128)
            KTOK3 = A_KTOK.rearrange("p (n i) -> p n i", i=128)
            VTOK3 = A_VTOK.rearrange("p (n i) -> p n i", i=128)
            OS3 = A_OSUM.rearrange("p (n i) -> p n i", i=128)
            mset("pool", SST[:], 0.0, [("sst", 0), ("sst", 1)])
            mset("pool", SBF[:], 0.0, [("sbf", 0), ("sbf", 1)])
            TTB = TT[:].rearrange("p a b -> p (a b)").bitcast(BF16).rearrange("p (s n) -> p s n", n=512)
            slots = [(SV[:, i, :], ("gsv", i)) for i in range(20)] + [(TTB[:, i, :], ("gtt", i)) for i in range(16)]
            NS, NP, PSZ = DBG.get("NS", 4), DBG.get("NP", 3), DBG.get("PSZ", 7)
            outsets = [slots[3 * i:3 * i + 3] for i in range(NS)]
            pools = [slots[3 * NS + PSZ * i:3 * NS + PSZ * (i + 1)] for i in range(NP)]
            MGs = [(A_PC[:, i * 512:(i + 1) * 512], ("mg", i)) for i in range(NP)]

            def elem(m, e):
                if e < 2:
                    return 2 * m + e, 0, h
                return 15 - 2 * m - (e - 2), 1, 4 + h

            def mm4(lh, klh, rh, krh):
                ps, kp = psum()
                p4 = v4(ps[:])
                l4, r4 = v4(lh), v4(rh)
                for e in range(4):
                    mm(p4[:, e, :], l4[:, e, :], r4[:, e, :], True, True, [klh, krh], [kp])
                return ps, kp

            def evac(ps, kp, dst, kd):
                cnt["ev"] += 1
                cpy("act" if cnt["ev"] % DBG.get("evmod", 1000) else "dve", dst, ps[:], [kp], [kd])

            def transp4(src, ksrc, dst, kdst):
                pst, kpst = psum()
                pstb = pst[:].bitcast(BF16)
                for e in range(4):
                    S.op("pe", lambda hh, o=pstb[:, e * 128:(e + 1) * 128], i_=v4(src)[:, e, :]: hh.transpose(out=o, in_=i_, identity=IDB),
                         [ksrc, KC], [kpst])
                cpy("act", dst, pstb[:, 0:512], [kpst], [kdst])

            def prep(m, pipe):
                free = list(pools[pipe])

                def alloc():
                    return free.pop(0)

                def rel(*xs):
                    free.extend(xs)

                els = [elem(m, e) for e in range(4)]
                (Z, kZ), (ATT, kATT), (QD, kQD) = outsets[m % NS]
                MH = alloc()
                ML = alloc()
                for e, (tile, d, dh) in enumerate(els):
                    ts1(DBG.get("mg_eng", "dve"), v4(MH[0])[:, e, :], INCF if d == 0 else INCB, GHI[:, tile, dh:dh + 1], ALU.mult, [KM, KC], [MH[1]])
                    ts1(DBG.get("mg_eng", "dve"), v4(ML[0])[:, e, :], INCF if d == 0 else INCB, GLO[:, tile, dh:dh + 1], ALU.mult, [KM, KC], [ML[1]])
                c1, kc1 = psum()
                mm(c1[:], ONESB, MH[0], True, False, [MH[1], KC], [kc1])
                mm(c1[:], ONESB, ML[0], False, False, [ML[1], KC], [kc1])
                mm(v4(c1[:]), IDB, NEG4B, False, True, [KC], [kc1])
                rel(MH, ML)
                DT = alloc()
                for e, (tile, d, dh) in enumerate(els):
                    act(v4(DT[0])[:, e, :], v4(c1[:])[:, e, :], AF.Exp, [kc1, KM], [DT[1]], bias=BIASD[:, tile, dh:dh + 1])
                yield
                DG = alloc()
                for e, (tile, d, dh) in enumerate(els):
                    ts1(DBG.get("dg_eng", "dve"), v4(DG[0])[:, e, :], IDB, ECOL[:, tile, dh:dh + 1], ALU.mult, [KM, KC], [DG[1]])
                c2, kc2 = psum()
                mm(c2[:], ONESB, DG[0], True, True, [DG[1], KC], [kc2])
                rel(DG)
                ER = alloc()
                cpy("act", ER[0], c2[:], [kc2], [ER[1]])
                yield
                kk, kkk = psum()
                for e, (tile, d, dh) in enumerate(els):
                    mm(v4(kk[:])[:, e, :], KT3[:, tile, :], KT3[:, tile, :], True, True, [("kt",)], [kkk])
                BP = alloc()
                tten("dve", BP[0], kk[:], DT[0], ALU.mult, [kkk, DT[1]], [BP[1]])
                yield
                qk, kqk = psum()
                for e, (tile, d, dh) in enumerate(els):
                    mm(v4(qk[:])[:, e, :], KT3[:, tile, :], QT3[:, tile, :], True, True, [("kt",), ("qt",)], [kqk])
                tten("dve", ATT, qk[:], DT[0], ALU.mult, [kqk, DT[1]], [kATT])
                rel(DT)
                yield
                tten(DBG.get("off_eng", "pool"), v4(BP[0]), v4(BP[0]), b4(OFFB[:]), ALU.mult, [BP[1], KC], [BP[1]])
                for e, (tile, d, dh) in enumerate(els):
                    tten(DBG.get("qd_eng", "dve"), v4(QD)[:, e, :], QT3[:, tile, :], v4(ER[0])[:, e, :], ALU.mult, [("qt",), ER[1]], [kQD])
                rel(ER)
                yield
                AT = alloc()
                transp4(BP[0], BP[1], AT[0], AT[1])
                X = alloc()
                tten(DBG.get("x_eng", "pool"), v4(X[0]), v4(BP[0]), b4(NM16), ALU.mult, [BP[1], KC], [X[1]])
                yield
                XT = alloc()
                tten(DBG.get("x_eng", "pool"), v4(XT[0]), v4(AT[0]), b4(NM16), ALU.mult, [AT[1], KC], [XT[1]])
                rel(BP)
                yield

                def prod(lh, rh):
                    o = alloc()
                    evac(*mm4(lh[0], lh[1], rh[0], rh[1]), o[0], o[1])
                    return o

                def plus_i(x):
                    g = alloc()
                    tten(DBG.get("pi_eng", "dve"), v4(g[0]), v4(x[0]), b4(IDB), ALU.add, [x[1], KC], [g[1]])
                    return g

                X2 = prod(XT, X)
                yield
                X2T = prod(X, XT)
                G1T = plus_i(XT)
                rel(X, XT)
                yield
                G2 = plus_i(X2)
                Y1T = prod(G2, G1T)
                rel(G1T, G2)
                yield
                X4 = prod(X2T, X2)
                yield
                X4T = prod(X2, X2T)
                rel(X2, X2T)
                yield
                G4 = plus_i(X4)
                Y2T = prod(G4, Y1T)
                rel(Y1T, G4)
                yield
                X8 = prod(X4T, X4)
                rel(X4, X4T)
                yield
                G8 = plus_i(X8)
                rel(X8)
                Zc = prod(Y2T, G8)
                yield
                ZTc = prod(G8, Y2T)
                rel(Y2T, G8)
                yield
                for k in range(3):
                    OT = alloc()
                    tten(DBG.get("ot_eng", "pool"), v4(OT[0]), v4(AT[0]), b4(MLV[k]), ALU.mult, [AT[1], KC], [OT[1]])
                    w1, kw1 = mm4(OT[0], OT[1], Zc[0], Zc[1])
                    rel(OT)
                    IW = alloc()
                    tten("dve", v4(IW[0]), b4(IDB), v4(w1[:]), ALU.subtract, [kw1, KC], [IW[1]])
                    yield
                    zn, kzn = mm4(ZTc[0], ZTc[1], IW[0], IW[1])
                    rel(IW)
                    if k < 2:
                        Zn = alloc()
                        evac(zn, kzn, Zn[0], Zn[1])
                        rel(Zc, ZTc)
                        yield
                        ZTn = alloc()
                        transp4(Zn[0], Zn[1], ZTn[0], ZTn[1])
                        Zc, ZTc = Zn, ZTn
                    else:
                        evac(zn, kzn, Z, kZ)
                        rel(Zc, ZTc, AT)
                    yield

            def steps(m):
                (Z, kZ), (ATT, kATT), (QD, kQD) = outsets[m % NS]
                Z4, ATT4, QD4 = v4(Z), v4(ATT), v4(QD)
                for sidx in (2 * m, 2 * m + 1):
                    def one(d):
                        tile = sidx if d == 0 else 15 - sidx
                        e = (sidx % 2) + 2 * d
                        dh = h + 4 * d
                        vp, kvp = TB[:, 3 * d + 0, 0:128], ("tb", 3 * d + 0)
                        vraw, kvr = TB[:, 3 * d + 1, 0:128], ("tb", 3 * d + 1)
                        vdec, kvd = TB[:, 3 * d + 2, 0:128], ("tb", 3 * d + 2)
                        ksp, k1 = psum()
                        mm(ksp[:, 0:128], KT3[:, tile, :], SBF[:, d, :], True, True, [("kt",), ("sbf", d)], [k1])
                        stt("dve", vp, ksp[:, 0:128], NEGEC[:, tile, dh:dh + 1], VTOK3[:, tile, :], ALU.mult, ALU.add,
                            [k1, KM, ("vtok",)], [kvp])
                        yield
                        vrp, k2 = psum()
                        mm(vrp[:, 0:128], Z4[:, e, :], vp, True, True, [kZ, kvp], [k2])
                        cpy("act", vraw, vrp[:, 0:128], [k2], [kvr])
                        ts1("dve", vdec, vrp[:, 0:128], BDEC[:, tile, dh:dh + 1], ALU.mult, [k2, KM], [kvd])
                        yield
                        op_, k3 = psum()
                        mm(op_[:, 0:128], SBF[:, d, :], QD4[:, e, :], True, False, [("sbf", d), kQD], [k3])
                        mm(op_[:, 0:128], vraw, ATT4[:, e, :], False, True, [kvr, kATT], [k3])
                        snp, k4 = psum()
                        mm(snp[:, 0:128], KTOK3[:, tile, :], vdec, True, True, [("ktok",), kvd], [k4])
                        stt("dve", SST[:, d, :], SST[:, d, :], ECL[:, tile, dh:dh + 1], snp[:, 0:128], ALU.mult, ALU.add,
                            [k4, KM, ("sst", d)], [("sst", d)])
                        cpy(DBG.get("sbf_eng", "act"), SBF[:, d, :], SST[:, d, :], [("sst", d)], [("sbf", d)])
                        first = (d == 0 and tile < 8) or (d == 1 and tile >= 8)
                        if first:
                            cpy("act", OS3[:, tile, :], op_[:, 0:128], [k3], [("os", tile)])
                        else:
                            tten("dve", OS3[:, tile, :], op_[:, 0:128], OS3[:, tile, :], ALU.add, [k3, ("os", tile)], [("os", tile)])
                        yield
                    yield from rr([one(0), one(1)])

            active = {}
            prep_done = set()
            steps_done = -1
            nextp = 0
            nexts = 0
            freepipes = list(range(NP))
            cur_steps = None
            nbatch = DBG["gdn_m"]
            while nexts < nbatch or active or cur_steps is not None:
                while nextp < 8 and freepipes and (nextp - NS) <= steps_done and nextp < nbatch + 3:
                    pipe = freepipes.pop()
                    active[nextp] = (prep(nextp, pipe), pipe)
                    nextp += 1
                if cur_steps is None and nexts < nbatch and nexts in prep_done:
                    cur_steps = steps(nexts)
                for m_ in list(active.keys()):
                    g, pipe = active[m_]
                    try:
                        next(g)
                    except StopIteration:
                        del active[m_]
                        prep_done.add(m_)
                        freepipes.append(pipe)
                if cur_steps is not None:
                    try:
                        next(cur_steps)
                    except StopIteration:
                        cur_steps = None
                        steps_done = nexts
                        nexts += 1
                if nexts >= nbatch and not active and cur_steps is None:
                    break

        def dn_head(l, h):
            vb = l * NV
            vk = ("vec",)
            SVf = SV[:].rearrange("p a b -> p (a b)").bitcast(F32)
            PCs = [A_PC, SVf[:, 0:2052]]
            CYs = [A_OSUM, SVf[:, 2052:4100]]
            for par in range(2):
                mset("pool", PCs[par][:, 0:2], 0.0, [("pc", par, "pad")])
                mset("pool", PCs[par][:, 2050:2052], 0.0, [("pc", par, "pad")])
            state = dict(tt_busy=False)
            finished = {}

            def chunk_gen(ci, par):
                dst, dk = ((A_QT, "qt"), (A_KT, "kt"), (A_VT, "vt"))[ci]
                PCp, CYp = PCs[par], CYs[par]
                slab, ks = wload(l, O_WIN + (ci * 4 + h) * 1024, 1024)
                for t in range(NT):
                    ps, kp = psum()
                    proj_tile(ps, kp, slab, ks, t)
                    cpy("act", PCp[:, 2 + t * 512:2 + (t + 1) * 512], ps[:], [kp], [("pc", par, t)])
                    yield
                wbase = vb + V_DNCW + (ci * 4 + h) * 4
                rk = [("pc", par, t) for t in range(NT)] + [("pc", par, "pad"), vk]
                wk = [("os", n) for n in range(16)] if par == 0 else [("cy", par, t) for t in range(NT)]
                act(CYp, PCp[:, 0:2048], AF.Identity, rk, wk, scale=VEC[:, wbase:wbase + 1])
                yield
                for j in range(1, 4):
                    stt("dve", CYp, PCp[:, j:j + 2048], VEC[:, wbase + j:wbase + j + 1], CYp, ALU.mult, ALU.add, rk + wk, wk)
                    yield
                while state["tt_busy"]:
                    yield
                state["tt_busy"] = True

                def tile_chain(t):
                    tsl = tsl_(t)
                    kcy = [("os", n) for n in range(4 * t, 4 * t + 4)] if par == 0 else [("cy", par, t)]
                    th, kth = TT[:, 2 * t, :], ("tt", 2 * t)
                    ln, kln = TT[:, 2 * t + 1, :], ("tt", 2 * t + 1)
                    sq, ksq = TB[:, t, :], ("tb", t)
                    act(th, CYp[:, tsl], AF.Tanh, kcy, [kth], scale=0.5)
                    yield
                    if ci == 2:
                        stt("dve", dst[:, tsl], th, 1.0, CYp[:, tsl], ALU.add, ALU.mult, [kth] + kcy, [(dk, t)])
                        yield
                        return
                    stt("dve", th, th, 1.0, CYp[:, tsl], ALU.add, ALU.mult, [kth] + kcy, [kth])
                    yield
                    act(sq, th, AF.Square, [kth], [ksq])
                    yield
                    ps, kp = psum()
                    mm(ps[:], ONESB, sq, True, True, [ksq, KC], [kp])
                    act(ln, ps[:], AF.Ln, [kp], [kln], bias=4 * EPS)
                    yield
                    act(ln, ln, AF.Exp, [kln], [kln], scale=-0.5, bias=(-0.5 * math.log(128.0) if ci == 0 else 0.0))
                    yield
                    tten("pool", dst[:, tsl], th, ln, ALU.mult, [kth, kln], [(dk, t)])
                    yield

                yield from rr([tile_chain(t) for t in range(NT)])
                state["tt_busy"] = False
                finished[ci] = True

            gens = {}
            started = 0
            while len(finished) < 3:
                while started < 3 and len(gens) < 2 and (started < 2 or finished.get(started - 2)):
                    gens[started] = chunk_gen(started, started % 2)
                    started += 1
                for ci in list(gens.keys()):
                    try:
                        next(gens[ci])
                    except StopIteration:
                        del gens[ci]
            if DBG["dn_stop"] <= 1:
                return
            for src, sk, dst, dk, sc in ((A_KT, "kt", A_KTOK, "ktok", 1.0), (A_VT, "vt", A_VTOK, "vtok", 0.5)):
                for half in range(2):
                    pst, kpst = psum()
                    pstb = pst[:].bitcast(BF16)
                    for n in range(8):
                        tile = half * 8 + n
                        S.op("pe", lambda hh, o=pstb[:, n * 128:(n + 1) * 128], i_=src[:, tile * 128:(tile + 1) * 128]:
                             hh.transpose(out=o, in_=i_, identity=IDB), [(sk, tile // 4), KC], [kpst])
                    act(dst[:, half * 1024:(half + 1) * 1024], pstb[:, 0:1024], AF.Identity, [kpst], [(dk,)], scale=sc)
            if DBG["dn_stop"] <= 2:
                return
            S.barrier()
            gdn(l, h)
            S.barrier()
            if DBG["dn_stop"] <= 4:
                return
            zslab, kzs = wload(l, O_WIN + (12 + h) * 1024, 1024)

            def out_chain(t):
                tsl = tsl_(t)
                osk = [("os", n) for n in range(t * 4, t * 4 + 4)]
                ln, kln = TT[:, 2 * t, :], ("tt", 2 * t)
                th, kth = TT[:, 2 * t + 1, :], ("tt", 2 * t + 1)
                sq, ksq = TB[:, t, :], ("tb", t)
                act(sq, A_OSUM[:, tsl], AF.Square, osk, [ksq])
                yield
                psz, kpz = psum()
                proj_tile(psz, kpz, zslab, kzs, t)
                act(th, psz[:], AF.Tanh, [kpz], [kth], scale=0.5)
                stt("dve", th, th, 1.0, psz[:], ALU.add, ALU.mult, [kth, kpz], [kth])
                yield
                ps, kp = psum()
                mm(ps[:], ONESB, sq, True, True, [ksq, KC], [kp])
                act(ln, ps[:], AF.Ln, [kp], [kln], bias=EPS, scale=1.0 / 128)
                yield
                act(ln, ln, AF.Exp, [kln], [kln], scale=-0.5)
                yield
                stt("dve", ln, A_OSUM[:, tsl], VEC[:, vb + V_DNG:vb + V_DNG + 1], ln, ALU.mult, ALU.mult, osk + [kln, vk], [kln])
                yield
                stt("dve", A_VT[:, tsl], ln, 0.5, th, ALU.mult, ALU.mult, [kln, kth], [("vt", t)])
                yield

            interleave([out_chain(t) for t in range(NT)])
            wout_pass(l, [h], lambda i, t: A_VT[:, tsl_(t)], lambda i, t: ("vt", t))

        def ffn_phase(l):
            vb = l * NV
            vk = ("vec",)
            rms_to_hn(vb + V_G2)
            ACTB = ARENA[:, 0:11264].rearrange("p (f n) -> p f n", n=512)
            GP = ARENA[:, 11264:15376].bitcast(F32).rearrange("p (b n) -> p b n", b=4)
            NW = DBG.get("NW", 3)
            st = dict(down_q=0)

            def fc_chain(qt, fc, idx):
                t0 = qt * 512
                sg, ksg = wload(l, O_WG + fc * 1024, 1024)
                gb = idx % 4
                gp = GP[:, gb, :]
                kgp = ("gp", gb)
                cg, kcg = TT[:, 2 * (idx % 4), :], ("tt", 2 * (idx % 4))
                x2, k2 = TT[:, 2 * (idx % 4) + 1, :], ("tt", 2 * (idx % 4) + 1)
                psg, kpg = psum()
                proj_tile(psg, kpg, sg, ksg, qt)
                psh, kph = psum()
                sides = []
                if qt > 0:
                    sides.append((0, t0 - 1))
                if qt < NT - 1:
                    sides.append((1, t0 + 512))
                for (si, col) in sides:
                    for kc in range(8):
                        mm(psh[:, si:si + 1], sg[:, kc * 128:(kc + 1) * 128], HN[:, kc, col:col + 1], kc == 0, kc == 7,
                           [ksg, ("HN", kc, col // 512)], [kph])
                cpy("act", gp[:, 1:513], psg[:], [kpg], [kgp])
                if qt > 0:
                    cpy("dve", gp[:, 0:1], psh[:, 0:1], [kph, kgp], [kgp])
                else:
                    mset("dve", gp[:, 0:1], 0.0, [kgp])
                if qt < NT - 1:
                    cpy("dve", gp[:, 513:514], psh[:, 1:2], [kph, kgp], [kgp])
                else:
                    mset("dve", gp[:, 513:514], 0.0, [kgp])
                yield
                wb_ = vb + V_FCW + fc * 3
                tsc("pool", cg, gp[:, 0:512], VEC[:, wb_:wb_ + 1], VEC[:, vb + V_FCB + fc:vb + V_FCB + fc + 1], ALU.mult, ALU.add,
                    [kgp, vk], [kcg])
                yield
                stt("dve", cg, gp[:, 1:513], VEC[:, wb_ + 1:wb_ + 2], cg, ALU.mult, ALU.add, [kgp, vk, kcg], [kcg])
                yield
                stt("dve", cg, gp[:, 2:514], VEC[:, wb_ + 2:wb_ + 3], cg, ALU.mult, ALU.add, [kgp, vk, kcg], [kcg])
                yield
                act(x2, cg, AF.Square, [kcg], [k2])
                yield
                tsc("pool", x2, x2, GC2, GC1, ALU.mult, ALU.add, [k2], [k2])
                yield
                tten("pool", x2, x2, cg, ALU.mult, [k2, kcg], [k2])
                yield
                act(x2, x2, AF.Tanh, [k2], [k2])
                yield
                stt("dve", cg, x2, 1.0, cg, ALU.add, ALU.mult, [k2, kcg], [kcg])
                yield
                while st["down_q"] < qt:
                    yield
                su, ksu = wload(l, O_WU + fc * 1024, 1024)
                psu, kpu = psum()
                proj_tile(psu, kpu, su, ksu, qt)
                stt("dve", ACTB[:, fc, :], cg, 0.5, psu[:], ALU.mult, ALU.mult, [kcg, kpu], [("actb", fc)])
                yield

            def down_gen(qt):
                for m in range(8):
                    ps, kp = psum()
                    pieces = ((0, 8), (8, 16), (16, 22))
                    for (f0, f1) in pieces:
                        sl, ksl = wload(l, O_WD + m * 2816 + f0 * 128, (f1 - f0) * 128)
                        for fc in range(f0, f1):
                            mm(ps[:], sl[:, (fc - f0) * 128:(fc - f0 + 1) * 128], ACTB[:, fc, :], fc == 0, fc == NFC - 1,
                               [ksl, ("actb", fc)], [kp])
                    radd(m, qt, ps, kp)
                    yield
                st["down_q"] = qt + 1

            chains = [(qt, fc) for qt in range(NT) for fc in range(NFC)]
            active = []
            ci = 0
            down = None
            nfin = {qt: 0 for qt in range(NT)}
            while ci < len(chains) or active or down is not None:
                while ci < len(chains) and len(active) < NW:
                    qt, fc = chains[ci]
                    active.append((fc_chain(qt, fc, ci), qt))
                    ci += 1
                for item in list(active):
                    g, qt = item
                    try:
                        next(g)
                    except StopIteration:
                        active.remove(item)
                        nfin[qt] += 1
                        if nfin[qt] == NFC:
                            down = down_gen(qt)
                if down is not None:
                    try:
                        next(down)
                    except StopIteration:
                        down = None

        def ple_phase(l, s):
            vb = l * NV
            rms_to_hn(vb + V_GP)
            PB = ARENA[:, 0:4096].rearrange("p (k n) -> p k n", k=2)
            PF = ARENA[:, 4096:8192].bitcast(F32).rearrange("p (b k n) -> p b k n", b=2, k=2)
            psrc = pT[l, s].rearrange("(k p) n -> p k n", p=128)
            for t in range(NT):
                S.dma(PF[:, t % 2], psrc[:, :, tsl_(t)], [], [("pf", t % 2)])
                cpy("pool", PB[:, :, tsl_(t)], PF[:, t % 2], [("pf", t % 2)], [("pb", t)])
            for m in range(8):
                sg, ksg = wload(l, O_PWG + m * 1024, 1024)
                sp_, ksp_ = wload(l, O_PWP + m * 256, 256)
                for t in range(NT):
                    ps1, kp1 = psum()
                    proj_tile(ps1, kp1, sg, ksg, t)
                    th, kth = tt()
                    act(th, ps1[:], AF.Tanh, [kp1, ("lv",)], [kth], scale=0.5, bias=LV[:, 40 + m:41 + m])
                    ps2, kp2 = psum()
                    for k2 in range(2):
                        mm(ps2[:], sp_[:, k2 * 128:(k2 + 1) * 128], PB[:, k2, tsl_(t)], k2 == 0, k2 == 1, [ksp_, ("pb", t)], [kp2])
                    stt("dve", th, th, 1.0, ps2[:], ALU.add, ALU.mult, [kth, kp2], [kth])
                    stt("dve", R[:, m, tsl_(t)], th, 0.5, R[:, m, tsl_(t)], ALU.mult, ALU.add, [kth, ("R", m, t)], [("R", m, t)])


        for s in range(nseq):
            for ch in range(8):
                S.dma(R[:, ch, :], xT[s, ch * 128:(ch + 1) * 128, :], [], [("R", ch, t) for t in range(NT)])
            for l in range(nlayers):
                layer_prep(l)
                rms_to_hn(l * NV + V_G1)
                S.barrier()
                if "lru" in phases:
                    lru_phase(l, extra=[ab_phase(l)] if "dn" in phases else [])
                    S.barrier()
                elif "dn" in phases:
                    interleave([ab_phase(l)])
                    S.barrier()
                if "dn" in phases:
                    for h in range(DBG["heads"]):
                        dn_head(l, h)
                    S.barrier()
                if "ffn" in phases:
                    ffn_phase(l)
                    S.barrier()
                if "ple" in phases:
                    ple_phase(l, s)
                    S.barrier()
            S.barrier()
            if dump_r:
                for ch in range(8):
                    S.dma(yT[s, ch * 128:(ch + 1) * 128, :], R[:, ch, :], [("R", ch, t) for t in range(NT)], [("y", s, ch)])
            else:
                for t in range(NT):
                    rs, kr = rms_stats(t, 8, lambda ch, t: R[:, ch, tsl_(t)], lambda ch, t: ("R", ch, t), 1.0 / D, EPS)
                    for ch in range(8):
                        o, ko = tt()
                        stt("dve" if ch % 2 == 0 else "pool", o, R[:, ch, tsl_(t)], VEC[:, L * NV + ch:L * NV + ch + 1], rs,
                            ALU.mult, ALU.mult, [("R", ch, t), kr], [ko])
                        S.dma(yT[s, ch * 128:(ch + 1) * 128, tsl_(t)], o, [ko], [("y", s, ch, t)])
        S.barrier()
        print("instructions:", S.ninst, {k: v["n"] for k, v in S.E.items()})
    return nc


def _slabs_kn(w):
    K, N = w.shape
    a = w.reshape(K // 128, 128, N // 128, 128)
    return np.ascontiguousarray(a.transpose(2, 1, 0, 3)).reshape(N // 128, 128, K)


def pack_weights(inp):
    wpk = np.zeros((L, 128, WTOT), np.float32)
    for l in range(L):
        w_in = inp["w_in"][l]
        cols = np.concatenate([w_in[:, 0:2048], w_in[:, 2064:3088]], axis=1)
        sl = _slabs_kn(cols)
        wpk[l, :, O_WIN:O_WIN + 24 * 1024] = sl.transpose(1, 0, 2).reshape(128, 24 * 1024)
        ab = w_in[:, 2048:2064].reshape(8, 128, 16).transpose(1, 0, 2).reshape(128, 128)
        wpk[l, :, O_WAB:O_WAB + 128] = ab
        lw = np.zeros((128, 4, 2, 2, 128), np.float32)
        for d in range(2):
            for gi, nm in enumerate(("lru_wa", "lru_wx")):
                wg = inp[nm][l, d]
                for c in range(4):
                    lw[0:64, c, d, gi, 0:64] = wg[2 * c]
                    lw[64:128, c, d, gi, 64:128] = wg[2 * c + 1]
        wpk[l, :, O_LRUW:O_LRUW + 2048] = lw.reshape(128, 2048)
        wpk[l, :, O_WOUT:O_WOUT + 8192] = inp["w_out"][l].reshape(8, 128, 1024).transpose(1, 0, 2).reshape(128, 8192)
        wpk[l, :, O_WG:O_WG + 22528] = _slabs_kn(inp["ffn_wg"][l]).transpose(1, 0, 2).reshape(128, 22528)
        wpk[l, :, O_WU:O_WU + 22528] = _slabs_kn(inp["ffn_wu"][l]).transpose(1, 0, 2).reshape(128, 22528)
        wd = inp["ffn_wd"][l].reshape(NFC, 128, 8, 128)
        wpk[l, :, O_WD:O_WD + 22528] = wd.transpose(1, 2, 0, 3).reshape(128, 22528)
        wpk[l, :, O_PWG:O_PWG + 8192] = _slabs_kn(inp["ple_wg"][l]).transpose(1, 0, 2).reshape(128, 8192)
        wpk[l, :, O_PWP:O_PWP + 2048] = _slabs_kn(inp["ple_wp"][l]).transpose(1, 0, 2).reshape(128, 2048)
    return wpk


def pack_vecs(inp):
    v = np.zeros((128, L * NV + 8), np.float32)

    def cm(a, n):
        return a.reshape(n, 128).T

    for l in range(L):
        b = l * NV
        v[:, b + V_G1:b + V_G1 + 8] = cm(inp["norm1_g"][l], 8)
        v[:, b + V_G2:b + V_G2 + 8] = cm(inp["norm2_g"][l], 8)
        v[:, b + V_GP:b + V_GP + 8] = cm(inp["ple_norm_g"][l], 8)
        v[:, b + V_BGP:b + V_BGP + 8] = cm(inp["ple_bg"][l], 8)
        v[:, b + V_DNCW:b + V_DNCW + 48] = inp["dn_conv_w"][l].reshape(4, 12, 128).transpose(2, 1, 0).reshape(128, 48)
        v[:, b + V_LCW:b + V_LCW + 16] = inp["lru_conv_w"][l].reshape(4, 4, 128).transpose(2, 1, 0).reshape(128, 16)
        v[:, b + V_LCB:b + V_LCB + 4] = cm(inp["lru_conv_b"][l], 4)
        v[:, b + V_LBA:b + V_LBA + 8] = inp["lru_ba"][l].reshape(2, 4, 128).transpose(2, 0, 1).reshape(128, 8)
        v[:, b + V_LBX:b + V_LBX + 8] = inp["lru_bx"][l].reshape(2, 4, 128).transpose(2, 0, 1).reshape(128, 8)
        v[:, b + V_LLAM:b + V_LLAM + 8] = inp["lru_lambda"][l].reshape(2, 4, 128).transpose(2, 0, 1).reshape(128, 8)
        v[:, b + V_LNG:b + V_LNG + 4] = cm(inp["lru_norm_g"][l], 4)
        v[:, b + V_DNG] = inp["dn_norm_g"][l]
        v[:, b + V_FCW:b + V_FCW + 66] = inp["ffn_conv_w"][l].reshape(3, NFC, 128).transpose(2, 1, 0).reshape(128, 66)
        v[:, b + V_FCB:b + V_FCB + 22] = cm(inp["ffn_conv_b"][l], NFC)
        v[:, b + V_ALOG:b + V_ALOG + 8] = np.broadcast_to(inp["dn_a_log"][l].reshape(1, 8), (128, 8))
        v[:, b + V_DTB:b + V_DTB + 8] = np.broadcast_to(inp["dn_dt_bias"][l].reshape(1, 8), (128, 8))
    v[:, L * NV:L * NV + 8] = cm(inp["final_g"], 8)
    return v


def kernel(**inputs):
    inp = {k: np.asarray(v) for k, v in inputs.items()}
    x = inp["x"]
    p = inp["p"]
    wpk = pack_weights(inp)
    vecs = pack_vecs(inp)
    xT = np.ascontiguousarray(x.transpose(0, 2, 1))
    pT = np.ascontiguousarray(p.transpose(0, 1, 3, 2))
    nc = build_nc()
    in_maps = []
    for c in range(NCORE):
        in_maps.append({
            "xT": xT[c * SEQ_PER_CORE:(c + 1) * SEQ_PER_CORE],
            "pT": np.ascontiguousarray(pT[:, c * SEQ_PER_CORE:(c + 1) * SEQ_PER_CORE]),
            "wpk": wpk,
            "vecs": vecs,
        })
    res = run_bass_kernel_spmd(nc, in_maps, core_ids=list(range(NCORE)))
    yT = np.concatenate([r["yT"] for r in res.results], axis=0)
    return np.ascontiguousarray(yT.transpose(0, 2, 1)).astype(np.float32)
```
